# Optimizing a Trainium2 kernel written in Bass

```python
import math
import jax, jax.numpy as jnp
from jax import lax
import numpy as np

D_MODEL = 1024
BATCH = 1
SEQ = 16384
DEPTH = 1

EPS = 1e-6
SSM_WIDTH = 512
SSM_GROUP = 16
SSM_GROUPS = SSM_WIDTH // SSM_GROUP
SSM_STATE = 64
DT_MIN = 1e-3
DT_MAX = 1e-1
NSA_HEADS = 8
NSA_KV_HEADS = 2
GQA_RATIO = NSA_HEADS // NSA_KV_HEADS
HEAD_DIM = 64
NSA_WIDTH = NSA_HEADS * HEAD_DIM
KV_WIDTH = NSA_KV_HEADS * HEAD_DIM
CMP_LEN = 32
CMP_STRIDE = 16
CMP_HIDDEN = 256
SLC_LEN = 64
N_SEL = 16
N_LOCAL = 2
WINDOW = 512
Q_BLOCK = 128
BIG = 1e4
REL_BUCKETS = 32
REL_MAX_DIST = 128
D_FF = -(-8 * D_MODEL // (3 * 256)) * 256

IN_SIZES = [SSM_WIDTH, NSA_WIDTH] + [KV_WIDTH] * 6 + [3 * NSA_HEADS, 2 * D_MODEL]
IN_COLS = sum(IN_SIZES)
IN_SPLITS = [int(v) for v in np.cumsum(IN_SIZES)[:-1]]

kernel_name = "hybrid_s5_nsa_gated_block"


def rmsnorm(x, g):
    x32 = x.astype(jnp.float32)
    y = x32 * lax.rsqrt(jnp.mean(x32 * x32, axis=-1, keepdims=True) + EPS)
    return (y * g.astype(jnp.float32)).astype(x.dtype)


def masked_softmax(logits, mask):
    l = jnp.where(mask, logits.astype(jnp.float32), -1e30)
    m = jnp.max(l, axis=-1, keepdims=True)
    p = jnp.exp(l - m) * mask
    return p / jnp.maximum(jnp.sum(p, axis=-1, keepdims=True), 1e-30)


def t5_bucket(dist):
    n = jnp.maximum(dist, 0)
    max_exact = REL_BUCKETS // 2
    nf = jnp.maximum(n, 1).astype(jnp.float32)
    large = max_exact + (jnp.log(nf / max_exact) / math.log(REL_MAX_DIST / max_exact)
                         * (REL_BUCKETS - max_exact)).astype(jnp.int32)
    large = jnp.minimum(large, REL_BUCKETS - 1)
    return jnp.where(n < max_exact, n, large)


def head_bias(rel_bias, dist):
    b = rel_bias.astype(jnp.float32)[t5_bucket(dist)]
    b = b.reshape(dist.shape + (NSA_KV_HEADS, GQA_RATIO))
    return jnp.transpose(b, (2, 3, 0, 1))


def s5_mixer(u, a_re, a_im, log_dt, b_re, b_im, c_re, c_im, d, w_glu):
    bsz, s, _ = u.shape
    f32 = jnp.float32
    lam = lax.complex(a_re.astype(f32), a_im.astype(f32))
    dt = jnp.exp(log_dt.astype(f32))[:, None]
    lam_bar = jnp.exp(lam * dt)
    b = lax.complex(b_re.astype(f32), b_im.astype(f32))
    b_bar = ((lam_bar - 1.0) / lam)[..., None] * b
    c = lax.complex(c_re.astype(f32), c_im.astype(f32))
    ug = u.astype(f32).reshape(bsz, s, SSM_GROUPS, SSM_GROUP)
    bu = jnp.einsum("bsgc,gpc->bsgp", ug.astype(jnp.complex64), b_bar)
    a = jnp.broadcast_to(lam_bar, bu.shape)

    def combine(e1, e2):
        a1, x1 = e1
        a2, x2 = e2
        return a1 * a2, a2 * x1 + x2

    _, states = lax.associative_scan(combine, (a, bu), axis=1)
    y = jnp.einsum("bsgp,gcp->bsgc", states, c).real \
        + d.astype(f32).reshape(SSM_GROUPS, SSM_GROUP) * ug
    y = y.reshape(bsz, s, SSM_WIDTH).astype(u.dtype)
    z = jax.nn.gelu(y)
    return z * jax.nn.sigmoid(z @ w_glu)


def compress(k, pos, w1, w2):
    bsz, s = k.shape[0], k.shape[1]
    n_cmp = (s - CMP_LEN) // CMP_STRIDE + 1
    idx = np.arange(n_cmp)[:, None] * CMP_STRIDE + np.arange(CMP_LEN)[None, :]
    blocks = k[:, idx] + pos[None, None, :, None, :]
    blocks = jnp.transpose(blocks, (0, 1, 3, 2, 4)).reshape(bsz, n_cmp, NSA_KV_HEADS, CMP_LEN * HEAD_DIM)
    return jax.nn.gelu(blocks @ w1) @ w2


def nsa_mixer(q, kc, vc, ks, vs, kw, vw, gates, rel_bias):
    bsz, s = q.shape[0], q.shape[1]
    n_cmp = kc.shape[1]
    n_slc = s // SLC_LEN
    n_sel = min(N_SEL, n_slc)
    n_qblk = s // Q_BLOCK
    cmp_end = jnp.asarray(np.arange(n_cmp) * CMP_STRIDE + CMP_LEN - 1, dtype=jnp.int32)
    ratio = SLC_LEN // CMP_STRIDE
    front = CMP_LEN // CMP_STRIDE - 1
    w_ov = [float(v) for v in np.convolve(np.ones(ratio), np.ones(CMP_LEN // CMP_STRIDE))]
    back = ratio * n_slc + len(w_ov) - 1 - front - n_cmp
    ks_t = jnp.transpose(ks.reshape(bsz, n_slc, SLC_LEN, NSA_KV_HEADS, HEAD_DIM), (0, 3, 1, 2, 4))
    vs_t = jnp.transpose(vs.reshape(bsz, n_slc, SLC_LEN, NSA_KV_HEADS, HEAD_DIM), (0, 3, 1, 2, 4))
    pad_w = ((0, 0), (WINDOW, 0), (0, 0), (0, 0))
    kw_pad = jnp.pad(kw, pad_w)
    vw_pad = jnp.pad(vw, pad_w)
    tab_t = rel_bias.astype(jnp.float32).T.reshape(NSA_KV_HEADS, GQA_RATIO, REL_BUCKETS)
    bi = jnp.arange(bsz).reshape(bsz, 1, 1, 1)
    gi = jnp.arange(NSA_KV_HEADS).reshape(1, NSA_KV_HEADS, 1, 1)
    gi6 = jnp.arange(NSA_KV_HEADS).reshape(1, NSA_KV_HEADS, 1, 1, 1, 1)
    ri6 = jnp.arange(GQA_RATIO).reshape(1, 1, GQA_RATIO, 1, 1, 1)
    blk = jnp.arange(n_slc)

    def block(i):
        s0 = i * Q_BLOCK
        t = s0 + jnp.arange(Q_BLOCK)
        qb = lax.dynamic_slice_in_dim(q, s0, Q_BLOCK, axis=1)
        gb = lax.dynamic_slice_in_dim(gates, s0, Q_BLOCK, axis=1)
        dist_c = t[:, None] - cmp_end[None, :]
        logit_c = jnp.einsum("bqgrd,bngd->bgrqn", qb, kc) + head_bias(rel_bias, dist_c)
        p_c = masked_softmax(logit_c, dist_c >= 0)
        o_c = jnp.einsum("bgrqn,bngd->bqgrd", p_c.astype(vc.dtype), vc)
        imp = jnp.pad(p_c.sum(axis=2), ((0, 0), (0, 0), (0, 0), (front, back)))
        p_slc = sum(w_ov[o] * imp[..., o:o + ratio * n_slc:ratio] for o in range(len(w_ov)))
        cur = t // SLC_LEN
        valid = blk[None, :] <= cur[:, None]
        forced = valid & ((blk[None, :] == 0) | (blk[None, :] >= cur[:, None] - (N_LOCAL - 1)))
        score = jnp.where(forced, BIG, jnp.where(valid, p_slc, -BIG))
        _, sel = lax.top_k(score, n_sel)
        ks_g = ks_t[bi, gi, sel]
        vs_g = vs_t[bi, gi, sel]
        pos_s = sel[..., None] * SLC_LEN + jnp.arange(SLC_LEN)
        dist_s = t[None, None, :, None, None] - pos_s
        bias_s = tab_t[gi6, ri6, t5_bucket(dist_s)[:, :, None]]
        logit_s = jnp.einsum("bqgrd,bgqksd->bgrqks", qb, ks_g) + bias_s
        kflat = n_sel * SLC_LEN
        p_s = masked_softmax(logit_s.reshape(bsz, NSA_KV_HEADS, GQA_RATIO, Q_BLOCK, kflat),
                             (dist_s >= 0).reshape(bsz, NSA_KV_HEADS, 1, Q_BLOCK, kflat))
        o_s = jnp.einsum("bgrqk,bgqkd->bqgrd", p_s.astype(vs.dtype),
                         vs_g.reshape(bsz, NSA_KV_HEADS, Q_BLOCK, kflat, HEAD_DIM))
        kwb = lax.dynamic_slice_in_dim(kw_pad, s0, WINDOW + Q_BLOCK, axis=1)
        vwb = lax.dynamic_slice_in_dim(vw_pad, s0, WINDOW + Q_BLOCK, axis=1)
        pos_w = s0 - WINDOW + jnp.arange(WINDOW + Q_BLOCK)
        dist_w = t[:, None] - pos_w[None, :]
        mask_w = (dist_w >= 0) & (dist_w < WINDOW) & (pos_w[None, :] >= 0)
        logit_w = jnp.einsum("bqgrd,bkgd->bgrqk", qb, kwb) + head_bias(rel_bias, dist_w)
        p_w = masked_softmax(logit_w, mask_w)
        o_w = jnp.einsum("bgrqk,bkgd->bqgrd", p_w.astype(vw.dtype), vwb)
        o = gb[..., 0:1] * o_c + gb[..., 1:2] * o_s + gb[..., 2:3] * o_w
        return o.reshape(bsz, Q_BLOCK, NSA_WIDTH)

    out = lax.map(block, jnp.arange(n_qblk))
    return jnp.transpose(out, (1, 0, 2, 3)).reshape(bsz, s, NSA_WIDTH)


def setup_inputs(seed: int = 0) -> dict:
    key = jax.random.key(seed)
    ks = jax.random.split(key, 32)
    nrm = lambda k, shape, scale: jax.random.normal(k, shape, jnp.float32) * scale
    L = DEPTH
    n_idx = jnp.arange(SSM_STATE, dtype=jnp.float32)
    return {
        "x": nrm(ks[0], (BATCH, SEQ, D_MODEL), 1.0),
        "norm_mix_g": 1.0 + nrm(ks[1], (L, D_MODEL), 0.01),
        "w_in": nrm(ks[2], (L, D_MODEL, IN_COLS), D_MODEL ** -0.5),
        "ssm_a_re": -0.5 + nrm(ks[3], (L, SSM_GROUPS, SSM_STATE), 0.01),
        "ssm_a_im": math.pi * n_idx + nrm(ks[4], (L, SSM_GROUPS, SSM_STATE), 0.01),
        "ssm_log_dt": jax.random.uniform(ks[5], (L, SSM_GROUPS), jnp.float32,
                                         math.log(DT_MIN), math.log(DT_MAX)),
        "ssm_b_re": nrm(ks[6], (L, SSM_GROUPS, SSM_STATE, SSM_GROUP), (2 * SSM_GROUP) ** -0.5),
        "ssm_b_im": nrm(ks[7], (L, SSM_GROUPS, SSM_STATE, SSM_GROUP), (2 * SSM_GROUP) ** -0.5),
        "ssm_c_re": nrm(ks[8], (L, SSM_GROUPS, SSM_GROUP, SSM_STATE), SSM_STATE ** -0.5),
        "ssm_c_im": nrm(ks[9], (L, SSM_GROUPS, SSM_GROUP, SSM_STATE), SSM_STATE ** -0.5),
        "ssm_d": nrm(ks[10], (L, SSM_WIDTH), 1.0),
        "ssm_w_glu": nrm(ks[11], (L, SSM_WIDTH, SSM_WIDTH), SSM_WIDTH ** -0.5),
        "w_up_ssm": nrm(ks[12], (L, SSM_WIDTH, D_MODEL), SSM_WIDTH ** -0.5),
        "cmp_pos_k": nrm(ks[13], (L, CMP_LEN, HEAD_DIM), 0.1),
        "cmp_pos_v": nrm(ks[14], (L, CMP_LEN, HEAD_DIM), 0.1),
        "cmp_w1_k": nrm(ks[15], (L, CMP_LEN * HEAD_DIM, CMP_HIDDEN), (CMP_LEN * HEAD_DIM) ** -0.5),
        "cmp_w2_k": nrm(ks[16], (L, CMP_HIDDEN, HEAD_DIM), CMP_HIDDEN ** -0.5),
        "cmp_w1_v": nrm(ks[17], (L, CMP_LEN * HEAD_DIM, CMP_HIDDEN), (CMP_LEN * HEAD_DIM) ** -0.5),
        "cmp_w2_v": nrm(ks[18], (L, CMP_HIDDEN, HEAD_DIM), CMP_HIDDEN ** -0.5),
        "rel_bias": nrm(ks[19], (REL_BUCKETS, NSA_HEADS), 0.5),
        "w_up_nsa": nrm(ks[20], (L, NSA_WIDTH, D_MODEL), NSA_WIDTH ** -0.5),
        "w_out": nrm(ks[21], (L, D_MODEL, D_MODEL), D_MODEL ** -0.5),
        "norm_ffn_g": 1.0 + nrm(ks[22], (L, D_MODEL), 0.01),
        "w_ffn_gate": nrm(ks[23], (L, D_MODEL, D_FF), D_MODEL ** -0.5),
        "w_ffn_up": nrm(ks[24], (L, D_MODEL, D_FF), D_MODEL ** -0.5),
        "w_ffn_down": nrm(ks[25], (L, D_FF, D_MODEL), D_FF ** -0.5),
        "norm_final_g": 1.0 + nrm(ks[26], (D_MODEL,), 0.01),
    }


def reference(x, norm_mix_g, w_in, ssm_a_re, ssm_a_im, ssm_log_dt, ssm_b_re, ssm_b_im,
              ssm_c_re, ssm_c_im, ssm_d, ssm_w_glu, w_up_ssm, cmp_pos_k, cmp_pos_v,
              cmp_w1_k, cmp_w2_k, cmp_w1_v, cmp_w2_v, rel_bias, w_up_nsa, w_out,
              norm_ffn_g, w_ffn_gate, w_ffn_up, w_ffn_down, norm_final_g):
    bsz, s, _ = x.shape
    kv_shape = (bsz, s, NSA_KV_HEADS, HEAD_DIM)
    for l in range(DEPTH):
        h = rmsnorm(x, norm_mix_g[l])
        proj = h @ w_in[l]
        u, q, kc_r, vc_r, ks_r, vs_r, kw_r, vw_r, g_nsa, g_br = jnp.split(proj, IN_SPLITS, axis=-1)
        y_a = s5_mixer(u, ssm_a_re[l], ssm_a_im[l], ssm_log_dt[l], ssm_b_re[l], ssm_b_im[l],
                       ssm_c_re[l], ssm_c_im[l], ssm_d[l], ssm_w_glu[l]) @ w_up_ssm[l]
        qh = q.reshape(bsz, s, NSA_KV_HEADS, GQA_RATIO, HEAD_DIM) * (HEAD_DIM ** -0.5)
        kc = compress(kc_r.reshape(kv_shape), cmp_pos_k[l], cmp_w1_k[l], cmp_w2_k[l])
        vc = compress(vc_r.reshape(kv_shape), cmp_pos_v[l], cmp_w1_v[l], cmp_w2_v[l])
        gates = jax.nn.sigmoid(g_nsa).reshape(bsz, s, NSA_KV_HEADS, GQA_RATIO, 3)
        y_b = nsa_mixer(qh, kc, vc, ks_r.reshape(kv_shape), vs_r.reshape(kv_shape),
                        kw_r.reshape(kv_shape), vw_r.reshape(kv_shape), gates, rel_bias) @ w_up_nsa[l]
        g_a, g_b = jnp.split(jax.nn.sigmoid(g_br), 2, axis=-1)
        x = x + (g_a * y_a + g_b * y_b) @ w_out[l]
        h = rmsnorm(x, norm_ffn_g[l])
        x = x + (jax.nn.silu(h @ w_ffn_gate[l]) * (h @ w_ffn_up[l])) @ w_ffn_down[l]
    return rmsnorm(x, norm_final_g)
```

```python
import contextlib
import numpy as np
import ml_dtypes
import concourse.bass as bass
import concourse.mybir as mybir
from concourse.bass_utils import run_bass_kernel_spmd

F32 = mybir.dt.float32
BF16 = mybir.dt.bfloat16
AF = mybir.ActivationFunctionType
ALU = mybir.AluOpType
AX = mybir.AxisListType

BARRIERS = True
NCORES = 8
S = 16384
D = 1024
NQ = 16
TQ = 128
NTOK = NQ * TQ
DFF = 2816
NFT = DFF // 128
EPS = 1e-6
INC = 3864


def AP(t, off, dims):
    return bass.AP(t, off, [list(d) for d in dims])


class Tok:
    __slots__ = ("w", "r")

    def __init__(self):
        self.w = None
        self.r = {}


class KB:
    def __init__(self, nc):
        self.nc = nc
        self.E = {"pe": nc.tensor, "act": nc.scalar, "dve": nc.vector, "pool": nc.gpsimd, "sp": nc.sync}
        self.csem = {e: nc.alloc_semaphore("c_" + e) for e in ("pe", "act", "dve", "pool")}
        self.ccnt = {e: 0 for e in self.csem}
        self.NDS = 28
        self.dsem = {e: [nc.alloc_semaphore("d_%s%d" % (e, i)) for i in range(self.NDS)]
                     for e in ("sp", "pool", "act")}
        self.dcnt = {e: 0 for e in self.dsem}
        self.seen = {e: {} for e in self.E}
        self.nwait = 0
        self.xsem = {}

    def collective(self, name, kind, in_ap, out_ap, reads, writes):
        self._deps("pool", reads, writes)
        sem = self.nc.alloc_semaphore("x_" + name)
        self.xsem[name] = sem
        ins = self.nc.gpsimd.collective_compute(kind, ALU.bypass, replica_groups=[list(range(NCORES))],
                                                ins=[in_ap], outs=[out_ap])
        ins.then_inc(sem)
        self._mark((("x", name), 1), reads, writes)

    def _wait(self, e, key, val):
        if key[0] == "c" and key[1] == e and e == "pe":
            return
        if self.seen[e].get(key, 0) >= val:
            return
        if key[0] == "x":
            sem = self.xsem[key[1]]
        else:
            sem = self.csem[key[1]] if key[0] == "c" else self.dsem[key[1]][key[2]]
        self.E[e].wait_ge(sem, val)
        self.seen[e][key] = val
        self.nwait += 1

    def _deps(self, e, reads, writes):
        for t in reads:
            if t.w is not None:
                self._wait(e, *t.w)
        for t in writes:
            if t.w is not None:
                self._wait(e, *t.w)
            for k, v in t.r.items():
                self._wait(e, k, v)

    def _mark(self, me, reads, writes):
        for t in reads:
            t.r[me[0]] = me[1]
        for t in writes:
            t.w = me
            t.r = {}

    def op(self, e, fn, reads=(), writes=()):
        self._deps(e, reads, writes)
        ins = fn(self.E[e])
        self.ccnt[e] += 1
        ins.then_inc(self.csem[e], 1)
        self._mark((("c", e), self.ccnt[e]), reads, writes)

    def dma(self, e, out, in_, reads=(), writes=(), **kw):
        self._deps(e, reads, writes)
        i = self.dcnt[e]
        self.dcnt[e] += 1
        slot = i % self.NDS
        ins = self.E[e].dma_start(out=out, in_=in_, **kw)
        ins.then_inc(self.dsem[e][slot], 16)
        self._mark((("d", e, slot), 16 * (i // self.NDS + 1)), reads, writes)

    def barrier(self):
        for e in self.E:
            for o in self.csem:
                if self.ccnt[o] > 0:
                    self._wait(e, ("c", o), self.ccnt[o])
            for q in self.dsem:
                n = self.dcnt[q]
                for slot in range(self.NDS):
                    cnt = (n - slot + self.NDS - 1) // self.NDS
                    if cnt > 0:
                        self._wait(e, ("d", q, slot), 16 * cnt)

    def finish(self, toks):
        for t in toks:
            if t.w is not None:
                self._wait("sp", *t.w)


class _Stop(Exception):
    pass


class Rot:
    def __init__(self, aps):
        self.aps = aps
        self.toks = [Tok() for _ in aps]
        self.i = 0

    def next(self):
        k = self.i % len(self.aps)
        self.i += 1
        return self.aps[k], self.toks[k]


def build_program(debug=None, debug_stop=None):
    nc = bass.Bass("TRN2", target_bir_lowering=False)
    kb = KB(nc)
    es = contextlib.ExitStack()

    def dram_in(name, shape, dt=F32):
        return nc.dram_tensor(name, list(shape), dt, kind="ExternalInput")

    x_own = dram_in("x_own", [NTOK, D])
    g_mix = dram_in("norm_mix_g", [1, D])
    g_ffn = dram_in("norm_ffn_g", [1, D])
    g_fin = dram_in("norm_final_g", [1, D])
    w_gate = dram_in("w_ffn_gate", [D, DFF])
    w_up = dram_in("w_ffn_up", [D, DFF])
    w_down = dram_in("w_ffn_down", [DFF, D])
    identb_d = dram_in("ident_bf", [128, 128], BF16)
    out_d = nc.dram_tensor("out", [NTOK, D], F32, kind="ExternalOutput")
    x1_d = nc.dram_tensor("x1_scr", [NTOK, D], F32, kind="ExternalOutput" if debug == "dump" else "Internal")

    w_in = dram_in("w_in", [D, INC])
    ssm_a_re = dram_in("ssm_a_re", [32, 64]); ssm_a_im = dram_in("ssm_a_im", [32, 64])
    ssm_log_dt = dram_in("ssm_log_dt", [1, 32])
    ssm_b_re = dram_in("ssm_b_re", [32, 64, 16]); ssm_b_im = dram_in("ssm_b_im", [32, 64, 16])
    ssm_c_re = dram_in("ssm_c_re", [32, 16, 64]); ssm_c_im = dram_in("ssm_c_im", [32, 16, 64])
    ssm_d = dram_in("ssm_d", [1, 512])
    w_glu = dram_in("ssm_w_glu", [512, 512])
    identf_d = dram_in("ident_f", [128, 128])
    onehot_d = dram_in("onehot_r", [128, 8])
    x_all = dram_in("x_all", [S, D])
    ksT_d = nc.dram_tensor("ksT_scr", [128, S], BF16, kind="Internal")
    kcR_d = nc.dram_tensor("kcR_scr", [128, S + 16], BF16, kind="Internal")
    vcR_d = nc.dram_tensor("vcR_scr", [128, S + 16], BF16, kind="Internal")
    vs_d = nc.dram_tensor("vs_scr", [S, 128], BF16, kind="Internal")
    zs_d = nc.dram_tensor("zs_scr", [128, 8192], F32, kind="Internal")
    ut_d = nc.dram_tensor("ut_scr", [128, 8192], BF16, kind="Internal")
    t_zsd = Tok(); t_utd = Tok()
    x_prev = dram_in("x_prev", [NQ * 512, D])
    vprev_d = dram_in("vprev", [128, 64])
    rel_bias_d = dram_in("rel_bias", [32, 8])
    ohrev_d = dram_in("ohrev", [32, 256]); ohfwd_d = dram_in("ohfwd", [32, 256])
    antij_d = dram_in("antij", [128, 128]); m4_d = dram_in("m4", [128, 128])
    tabR_d = nc.dram_tensor("tabR_scr", [8, 384], F32, kind="Internal")
    tabF_d = nc.dram_tensor("tabF_scr", [8, 416], F32, kind="Internal")
    cmp_w1_k = dram_in("cmp_w1_k", [2048, 256]); cmp_w1_v = dram_in("cmp_w1_v", [2048, 256])
    cmp_w2_k = dram_in("cmp_w2_k", [256, 64]); cmp_w2_v = dram_in("cmp_w2_v", [256, 64])
    cmp_pos_k = dram_in("cmp_pos_k", [32, 64]); cmp_pos_v = dram_in("cmp_pos_v", [32, 64])
    wide_d = dram_in("wide64", [128, 4096], BF16)
    keepblk_d = dram_in("keepblk", [128, 32])
    cand_d = dram_in("cand", [NQ * 128, 256], BF16); forced_d = dram_in("forced", [NQ * 128, 256], BF16)
    expn_d = dram_in("expn", [NQ * 256, 128], BF16)
    selc_d = dram_in("selc", [NQ * 17, 1024], BF16)
    acc_d = nc.dram_tensor("acc_scr", [NTOK, 512], F32, kind="ExternalOutput" if debug == "dump" else "Internal")
    w_out_d = dram_in("w_out", [D, D]); w_up_ssm = dram_in("w_up_ssm", [512, D]); w_up_nsa = dram_in("w_up_nsa", [512, D])
    t_accd = Tok(); x1_toks = []
    t_BW = Tok(); t_BnA = Tok(); t_Kc = Tok(); t_Vc = Tok(); t_Q = Tok(); t_gates = Tok(); t_KsN = Tok(); t_VsN = Tok(); t_oT = Tok()
    zg_d = nc.dram_tensor("zg_scr", [128, 4 * NTOK], BF16, kind="ExternalOutput" if debug in ("s5", "dump") else "Internal")
    t_zg = Tok()

    def sb(name, shape, dt):
        return es.enter_context(nc.sbuf_tensor(name, list(shape), dt))

    def ps(name, shape, dt):
        return es.enter_context(nc.psum_tensor(name, list(shape), dt))

    ident = sb("ident", [128, 128], BF16)
    t_ident = Tok()
    kb.dma("sp", ident[:], identb_d.ap(), writes=[t_ident])

    identF = sb("identF", [128, 128], F32)
    t_identF = Tok()
    kb.dma("sp", identF[:], identf_d.ap(), writes=[t_identF])
    epsT = sb("epsT", [128, 1], F32)
    t_eps = Tok()
    kb.op("dve", lambda e: e.memset(epsT[:], EPS), writes=[t_eps])

    def load_gain(name, src):
        t = sb(name, [128, D], F32)
        tk = Tok()
        kb.dma("sp", t[:], AP(src, 0, [[0, 128], [1, D]]), writes=[tk])
        return t, tk

    def rmsnorm(xap, tx, gt, tg, hout, th, scr):
        junk, tjunk, ss, tss = scr
        kb.op("act", lambda e: e.activation(out=junk[:], in_=xap, func=AF.Square, accum_out=ss[:, 0:1]),
              reads=[tx], writes=[tjunk, tss])
        kb.op("act", lambda e: e.activation(out=ss[:, 1:2], in_=ss[:, 0:1], func=AF.Sqrt, scale=1.0 / D,
                                            bias=epsT[:, 0:1]), reads=[tss, t_eps], writes=[tss])
        kb.op("dve", lambda e: e.reciprocal(out=ss[:, 2:3], in_=ss[:, 1:2]), reads=[tss], writes=[tss])
        kb.op("dve", lambda e: e.scalar_tensor_tensor(out=hout, in0=xap, scalar=ss[:, 2:3], in1=gt[:],
                                                      op0=ALU.mult, op1=ALU.mult),
              reads=[tx, tss, tg], writes=[th])

    def V(fn, r=(), w=()):
        kb.op("dve", fn, r, w)

    def A(fn, r=(), w=()):
        kb.op("act", fn, r, w)

    def P(fn, r=(), w=()):
        kb.op("pe", fn, r, w)

    def G(fn, r=(), w=()):
        kb.op("pool", fn, r, w)

    def load_norm_T(src, row0, ntile, X, ret_x=False):
        xb, txb = X["x"].next()
        kb.dma("sp", xb[:, 0:ntile, :], AP(src, row0 * D, [[D, 128], [128 * D, ntile], [1, D]]), writes=[txb])
        hT, thT = X["hT"].next()
        for a in range(ntile):
            ss, tss = X["ss"].next()
            hb, thb = X["h"].next()
            rmsnorm(xb[:, a, :], txb, X["g"], X["tg"], hb[:], thb, (X["junk"], X["tjunk"], ss, tss))
            pt, tpt = X["pT"].next()
            for k in range(8):
                P(lambda e, k=k: e.transpose(out=pt[:, k, :], in_=hb[:, k * 128:(k + 1) * 128], identity=ident[:]),
                  [thb, t_ident], [tpt])
            V(lambda e: e.tensor_copy(hT[:, :, a * 128:(a + 1) * 128], pt[:]), [tpt], [thT])
        if ret_x:
            return hT, thT, xb, txb
        return hT, thT

    nctx = [0]

    def norm_ctx(st, xtiles, gsrc):
        nctx[0] += 1
        pre = "c%d" % nctx[0]

        def a_(name, shape, dt):
            return st.enter_context(nc.sbuf_tensor(pre + name, list(shape), dt))
        X = {}
        X["x"] = Rot([a_("n_x%d" % i, [128, xtiles, D], F32) for i in range(2)])
        X["hT"] = Rot([a_("n_hT%d" % i, [128, 8, 128 * xtiles], BF16) for i in range(2)])
        X["h"] = Rot([a_("n_h%d" % i, [128, D], BF16) for i in range(2)])
        X["ss"] = Rot([a_("n_ss%d" % i, [128, 4], F32) for i in range(4)])
        X["junk"] = a_("n_junk", [128, D], BF16)
        X["tjunk"] = Tok()
        X["g"] = a_("n_g", [128, D], F32)
        X["tg"] = Tok()
        kb.dma("sp", X["g"][:], AP(gsrc, 0, [[0, 128], [1, D]]), writes=[X["tg"]])
        X["pT"] = Rot([st.enter_context(nc.psum_tensor(pre + "n_pT%d" % i, [128, 8, 128], BF16)) for i in range(2)])
        return X

    def ck(name):
        if debug_stop == name:
            raise _Stop()

    kv_toks = []

    def phase_all(UT, tUT, Zs, tZ, Eg, tEg, z_matmuls, recur, zv):
        UTW = 8192
        ZW = 8192
        with contextlib.ExitStack() as sa:
            X = norm_ctx(sa, 2, g_mix)
            WA = sa.enter_context(nc.sbuf_tensor("WA", [128, 8, 1024], BF16)); tWA = Tok()
            for k in range(8):
                for (c0, s0, n) in ((0, 0, 512), (512, 1280, 128), (640, 1024, 128), (768, 1152, 128), (896, 1408, 128)):
                    kb.dma("pool", WA[:, k, c0:c0 + n], AP(w_in, k * 128 * INC + s0, [[INC, 128], [1, n]]), writes=[tWA])
            pP = Rot([sa.enter_context(nc.psum_tensor("a_pP%d" % i, [128, 512], F32)) for i in range(2)])
            fmS = Rot([sa.enter_context(nc.sbuf_tensor("a_fm%d" % i, [128, 256], BF16)) for i in range(3)])
            vsS = Rot([sa.enter_context(nc.sbuf_tensor("a_vs%d" % i, [128, 128], BF16)) for i in range(2)])
            zpad = sa.enter_context(nc.sbuf_tensor("a_zpad", [128, 16], BF16)); tzp = Tok()
            V(lambda e: e.memset(zpad[:], 0.0), [], [tzp])
            for dst in (kcR_d, vcR_d):
                tk_ = Tok(); kv_toks.append(tk_)
                kb.dma("pool", AP(dst, S, [[S + 16, 128], [1, 16]]), zpad[:], reads=[tzp], writes=[tk_])
            n_ev = 0
            for cc in range(S // 256):
                sc, q = cc // 8, cc % 8
                hT, thT = load_norm_T(x_all, cc * 256, 2, X)
                for T in range(4):
                    pp, tpp = pP.next()
                    for k in range(8):
                        P(lambda e, k=k: e.matmul(pp[:, 0:256], lhsT=WA[:, k, T * 128:(T + 1) * 128], rhs=hT[:, k, :],
                                                  start=(k == 0), stop=(k == 7)), [tWA, thT], [tpp])
                    o_ = AP(UT, T * 2048 + q * 32, [[UTW, 128], [256, 8], [1, 32]])
                    i_ = AP(pp, 0, [[512, 128], [1, 8], [8, 32]])
                    n_ev += 1
                    if n_ev % 2 == 0:
                        V(lambda e: e.tensor_copy(o_, i_), [tpp], [tUT])
                    else:
                        A(lambda e: e.copy(out=o_, in_=i_), [tpp], [tUT])
                for (c0, dst) in ((512, ksT_d), (640, kcR_d), (768, vcR_d)):
                    pp, tpp = pP.next()
                    for k in range(8):
                        P(lambda e, k=k: e.matmul(pp[:, 0:256], lhsT=WA[:, k, c0:c0 + 128], rhs=hT[:, k, :],
                                                  start=(k == 0), stop=(k == 7)), [tWA, thT], [tpp])
                    f_, tf_ = fmS.next()
                    n_ev += 1
                    if n_ev % 2 == 0:
                        V(lambda e: e.tensor_copy(f_[:], pp[:, 0:256]), [tpp], [tf_])
                    else:
                        A(lambda e: e.copy(out=f_[:], in_=pp[:, 0:256]), [tpp], [tf_])
                    tk_ = Tok(); kv_toks.append(tk_)
                    W_ = dst.shape[1]
                    kb.dma("pool", AP(dst, cc * 256, [[W_, 128], [1, 256]]), f_[:], reads=[tf_], writes=[tk_])
                for a in range(2):
                    pp, tpp = pP.next()
                    for k in range(8):
                        P(lambda e, k=k: e.matmul(pp[:, 0:128], lhsT=hT[:, k, a * 128:(a + 1) * 128], rhs=WA[:, k, 896:1024],
                                                  start=(k == 0), stop=(k == 7)), [tWA, thT], [tpp])
                    v_, tv_ = vsS.next()
                    V(lambda e: e.tensor_copy(v_[:], pp[:, 0:128]), [tpp], [tv_])
                    tk_ = Tok(); kv_toks.append(tk_)
                    kb.dma("pool", AP(vs_d, (cc * 256 + a * 128) * 128, [[128, 128], [1, 128]]), v_[:],
                           reads=[tv_], writes=[tk_])
                if q == 7:
                    z_matmuls()
                    recur()
                    for ri in range(2):
                        d_ = AP(Eg, ri * 256 + 2 * sc, [[4096, 128], [16, 16], [1, 2], [512, 8]])
                        s_ = AP(Zs, ri * 4096 + 15, [[ZW, 128], [256, 16], [128, 2], [16, 8]])
                        V(lambda e, d_=d_, s_=s_: e.tensor_copy(d_, s_), [tZ], [tEg])
            kb.barrier()

    def phase_s5():
        PI = float(np.pi)
        with contextlib.ExitStack() as s5:
            def a_(name, shape, dt):
                return s5.enter_context(nc.sbuf_tensor(name, list(shape), dt))
            UT = a_("UT", [128, 4, 8, 256], BF16); tUT = Tok()
            UTW = 4 * 8 * 256
            Wgl = a_("s5_Wgl", [128, 4, 512], BF16); tWgl = Tok()
            for k4 in range(4):
                kb.dma("pool", Wgl[:, k4, :], AP(w_glu, k4 * 128 * 512, [[512, 128], [1, 512]]), writes=[tWgl])
            ck("u")
            NS = 40
            spt = a_("s5_sp", [128, NS, 16], F32); tS = Tok()
            SPW = NS * 16

            def sl(i):
                return spt[:, i, :]

            def slb(i, n=16):
                return AP(spt, i * 16, [[SPW, 128], [1, 16], [0, n]])
            (aR, aI, DT, XR, ANG, MAG, T1, T2, SINV, COSV, CFR, CFI, DEN, M1, T3, T4) = range(16)
            PWR, PWI = 16, 25
            kb.dma("sp", sl(aR), AP(ssm_a_re, 0, [[1, 128], [128, 16]]), writes=[tS], allow_slow_non_contiguous=True)
            kb.dma("sp", sl(aI), AP(ssm_a_im, 0, [[1, 128], [128, 16]]), writes=[tS], allow_slow_non_contiguous=True)
            for g2 in range(2):
                kb.dma("sp", spt[64 * g2:64 * g2 + 64, DT, :], AP(ssm_log_dt, g2, [[0, 64], [2, 16]]), writes=[tS],
                       allow_slow_non_contiguous=True)
            Br = a_("s5_Br", [128, 16, 16], F32); Bi = a_("s5_Bi", [128, 16, 16], F32)
            Cr = a_("s5_Cr", [128, 16, 16], F32); Ci = a_("s5_Ci", [128, 16, 16], F32)
            BBr = a_("s5_BBr", [128, 16, 16], F32); BBi = a_("s5_BBi", [128, 16, 16], F32)
            TA = a_("s5_TA", [128, 16, 16], F32); TB = a_("s5_TB", [128, 16, 16], F32)
            TRe = a_("s5_TRe", [128, 16, 16], F32); TIm = a_("s5_TIm", [128, 16, 16], F32)
            dcol = a_("s5_dcol", [128, 4], F32)
            kb.dma("sp", Br[:], AP(ssm_b_re, 0, [[16, 128], [2048, 16], [1, 16]]), writes=[tS])
            kb.dma("sp", Bi[:], AP(ssm_b_im, 0, [[16, 128], [2048, 16], [1, 16]]), writes=[tS])
            tCs = []
            for g2 in range(2):
                for pair in range(16):
                    for (dst_, src_) in ((Cr, ssm_c_re), (Ci, ssm_c_im)):
                        tk_ = Tok(); tCs.append(tk_)
                        kb.dma("sp", dst_[64 * g2:64 * g2 + 64, pair, :],
                               AP(src_, (2 * pair + g2) * 1024, [[1, 64], [64, 16]]),
                               writes=[tk_], allow_slow_non_contiguous=True)
            jn = a_("s5_join", [128, 2], F32)
            V(lambda e: e.memset(jn[:], 0.0), tCs, [tS])
            kb.dma("sp", dcol[:], AP(ssm_d, 0, [[1, 128], [128, 4]]), writes=[tS], allow_slow_non_contiguous=True)

            def vv(out, a, b, op):
                V(lambda e: e.tensor_tensor(out=out, in0=a, in1=b, op=op), [tS], [tS])

            def vs(out, a, s1, op0, s2=None, op1=None):
                if op1 is None:
                    V(lambda e: e.tensor_scalar(out=out, in0=a, scalar1=s1, scalar2=None, op0=op0), [tS], [tS])
                else:
                    V(lambda e: e.tensor_scalar(out=out, in0=a, scalar1=s1, scalar2=s2, op0=op0, op1=op1), [tS], [tS])

            def cmul(orr, oi, ar, ai, br, bi, t1, t2, sign=1.0):
                vv(t1, ar, br, ALU.mult); vv(t2, ai, bi, ALU.mult); vv(orr, t1, t2, ALU.subtract)
                vv(t1, ar, bi, ALU.mult); vv(t2, ai, br, ALU.mult); vv(oi, t1, t2, ALU.add)

            A(lambda e: e.activation(out=sl(DT), in_=sl(DT), func=AF.Exp), [tS], [tS])
            vv(sl(XR), sl(aR), sl(DT), ALU.mult)
            vv(sl(ANG), sl(aI), sl(DT), ALU.mult)
            A(lambda e: e.activation(out=sl(MAG), in_=sl(XR), func=AF.Exp), [tS], [tS])

            def sin_of(dst, shift):
                vs(sl(T1), sl(ANG), shift, ALU.add)
                vs(sl(T3), sl(T1), 1.0, ALU.mult)
                for m in (1, 3, 5, 7, 9):
                    vs(sl(T2), sl(T1), m * PI, ALU.is_ge, -2.0 * PI, ALU.mult)
                    vv(sl(T3), sl(T3), sl(T2), ALU.add)
                A(lambda e: e.activation(out=sl(dst), in_=sl(T3), func=AF.Sin), [tS], [tS])
            sin_of(SINV, 0.0)
            sin_of(COSV, PI / 2)
            V(lambda e: e.memset(sl(PWR + 0), 1.0), [tS], [tS])
            V(lambda e: e.memset(sl(PWI + 0), 0.0), [tS], [tS])
            vv(sl(PWR + 1), sl(MAG), sl(COSV), ALU.mult)
            vv(sl(PWI + 1), sl(MAG), sl(SINV), ALU.mult)
            for k in range(2, 9):
                cmul(sl(PWR + k), sl(PWI + k), sl(PWR + k - 1), sl(PWI + k - 1), sl(PWR + 1), sl(PWI + 1), sl(T1), sl(T2))
            SQ = a_("s5_sq", [128, 10, 16], F32)
            V(lambda e: e.tensor_copy(SQ[:, 0, :], sl(PWR + 8)), [tS], [tS])
            V(lambda e: e.tensor_copy(SQ[:, 5, :], sl(PWI + 8)), [tS], [tS])
            for q in range(1, 5):
                cmul(SQ[:, q, :], SQ[:, 5 + q, :], SQ[:, q - 1, :], SQ[:, 4 + q, :], SQ[:, q - 1, :], SQ[:, 4 + q, :],
                     sl(T1), sl(T2))
            Q128 = a_("s5_q128", [128, 18, 16], F32)
            V(lambda e: e.memset(Q128[:, 0, :], 1.0), [tS], [tS])
            V(lambda e: e.memset(Q128[:, 9, :], 0.0), [tS], [tS])
            for r in range(1, 9):
                cmul(Q128[:, r, :], Q128[:, 9 + r, :], Q128[:, r - 1, :], Q128[:, 8 + r, :], SQ[:, 4, :], SQ[:, 9, :],
                     sl(T1), sl(T2))
            vv(sl(T1), sl(aR), sl(aR), ALU.mult); vv(sl(T2), sl(aI), sl(aI), ALU.mult)
            vv(sl(DEN), sl(T1), sl(T2), ALU.add)
            V(lambda e: e.reciprocal(out=sl(DEN), in_=sl(DEN)), [tS], [tS])
            vs(sl(M1), sl(PWR + 1), -1.0, ALU.add)
            vv(sl(T1), sl(M1), sl(aR), ALU.mult); vv(sl(T2), sl(PWI + 1), sl(aI), ALU.mult)
            vv(sl(T3), sl(T1), sl(T2), ALU.add); vv(sl(CFR), sl(T3), sl(DEN), ALU.mult)
            vv(sl(T1), sl(PWI + 1), sl(aR), ALU.mult); vv(sl(T2), sl(M1), sl(aI), ALU.mult)
            vv(sl(T3), sl(T1), sl(T2), ALU.subtract); vv(sl(CFI), sl(T3), sl(DEN), ALU.mult)
            cmul(BBr[:], BBi[:], slb(CFR), slb(CFI), Br[:], Bi[:], TA[:], TB[:])

            def scatter(dst_t, pair_stride, base_off, src_t, neg=False):
                for g2 in range(2):
                    d_ = AP(dst_t, (64 * g2) * dst_t_pstride[id(dst_t)] + base_off + 16 * g2,
                            [[dst_t_pstride[id(dst_t)], 64], [2 * pair_stride, 8], [pair_stride + 32, 2], [1, 16]])
                    s_ = AP(src_t, (64 * g2) * 256, [[256, 64], [32, 8], [16, 2], [1, 16]])
                    if neg:
                        V(lambda e: e.tensor_scalar(out=d_, in0=s_, scalar1=-1.0, scalar2=None, op0=ALU.mult), [tS], [tS])
                    else:
                        V(lambda e: e.tensor_copy(d_, s_), [tS], [tS])
            dst_t_pstride = {}
            SPt = a_("s5_SPt", [128, 16, 2, 64], BF16); dst_t_pstride[id(SPt)] = 16 * 2 * 64
            V(lambda e: e.memset(SPt[:], 0.0), [tS], [tS])
            Zs = a_("s5_Zs", [128, 2, 16, 256], F32); tZ = Tok()
            ZW = 2 * 16 * 256
            Eg = a_("s5_Eg", [128, 8, 2, 256], F32); tEg = Tok()
            RT = a_("s5_rt", [128, 4, 256], F32)

            def zv(ri, bb):
                return AP(Zs, ri * 4096 + bb, [[ZW, 128], [256, 16], [16, 16]])

            def lb(t_, idx):
                return AP(t_, idx * 16, [[t_pstride[id(t_)], 128], [1, 16], [0, 16]])
            t_pstride = {id(SQ): 160, id(Q128): 288, id(spt): SPW}

            def rt(i):
                return AP(RT, i * 256, [[1024, 128], [16, 16], [1, 16]])

            def zz(out, a, b, op, r, w):
                V(lambda e: e.tensor_tensor(out=out, in0=a, in1=b, op=op), r, w)
            L8r, L8i = lb(SQ, 0), lb(SQ, 5)
            with contextlib.ExitStack() as sz:
                Wz = sz.enter_context(nc.sbuf_tensor("s5_Wz", [128, 4, 2, 8, 2, 128], BF16)); tWz = Tok()
                pz = Rot([sz.enter_context(nc.psum_tensor("s5_pz%d" % i, [128, 4, 128], BF16)) for i in range(2)])
                pZ = Rot([sz.enter_context(nc.psum_tensor("s5_pZ%d" % i, [128, 256], F32)) for i in range(2)])
                G(lambda e: e.memset(Wz[:], 0.0), [], [tWz])
                for k in range(8):
                    cmul(TRe[:], TIm[:], BBr[:], BBi[:], slb(PWR + k), slb(PWI + k), TA[:], TB[:])
                    scatter(SPt, 128, 0, TRe); scatter(SPt, 128, 64, TIm)
                    for quad in range(8):
                        T, h2 = quad // 2, quad % 2
                        pz_, tpz = pz.next()
                        for pq in range(2):
                            for ri in range(2):
                                P(lambda e, pq=pq, ri=ri: e.transpose(out=pz_[64 * h2:64 * h2 + 64, pq * 2 + ri, :],
                                                                      in_=SPt[:, 2 * quad + pq, ri, :], identity=ident[:]),
                                  [tS, t_ident], [tpz])
                        o_ = AP(Wz, (64 * h2) * 16384 + T * 4096 + (7 - k) * 256,
                                [[16384, 64], [2048, 2], [128, 2], [1, 128]])
                        i_ = AP(pz_, (64 * h2) * 512, [[512, 64], [256, 2], [128, 2], [1, 128]])
                        A(lambda e: e.copy(out=o_, in_=i_), [tpz], [tWz])
                def z_matmuls():
                    for pair in range(16):
                        quad, pq = pair // 2, pair % 2
                        T, h2 = quad // 2, quad % 2
                        for ri in range(2):
                            pZ_, tpZ = pZ.next()
                            for ip in range(8):
                                P(lambda e, ip=ip: e.matmul(pZ_[:], lhsT=Wz[64 * h2:64 * h2 + 64, T, pq, ip, ri, :],
                                                            rhs=UT[64 * h2:64 * h2 + 64, T, ip, :],
                                                            start=(ip == 0), stop=(ip == 7)), [tWz, tUT], [tpZ])
                            if ri == 0:
                                V(lambda e: e.tensor_copy(Zs[:, ri, pair, :], pZ_[:]), [tpZ], [tZ])
                            else:
                                A(lambda e: e.copy(out=Zs[:, ri, pair, :], in_=pZ_[:]), [tpZ], [tZ])
                def recur():
                    for bb in range(1, 16):
                        zz(rt(0), zv(0, bb - 1), L8r, ALU.mult, [tZ, tS], [tZ])
                        zz(rt(1), zv(1, bb - 1), L8i, ALU.mult, [tZ, tS], [tZ])
                        zz(rt(2), zv(1, bb - 1), L8r, ALU.mult, [tZ, tS], [tZ])
                        zz(rt(3), zv(0, bb - 1), L8i, ALU.mult, [tZ, tS], [tZ])
                        zz(zv(0, bb), zv(0, bb), rt(0), ALU.add, [tZ], [tZ])
                        zz(zv(0, bb), zv(0, bb), rt(1), ALU.subtract, [tZ], [tZ])
                        zz(zv(1, bb), zv(1, bb), rt(2), ALU.add, [tZ], [tZ])
                        zz(zv(1, bb), zv(1, bb), rt(3), ALU.add, [tZ], [tZ])
                phase_all(UT, tUT, Zs, tZ, Eg, tEg, z_matmuls, recur, zv)
                with contextlib.ExitStack() as su:
                    X = norm_ctx(su, 2, g_mix)
                    WU = su.enter_context(nc.sbuf_tensor("WU", [128, 8, 512], BF16)); tWU = Tok()
                    for k in range(8):
                        kb.dma("pool", WU[:, k, :], AP(w_in, k * 128 * INC, [[INC, 128], [1, 512]]), writes=[tWU])
                    pP = Rot([su.enter_context(nc.psum_tensor("u_pP%d" % i, [128, 512], F32)) for i in range(2)])
                    for st in range(8):
                        hT, thT = load_norm_T(x_own, st * 256, 2, X)
                        for T in range(4):
                            pp, tpp = pP.next()
                            for k in range(8):
                                P(lambda e, k=k: e.matmul(pp[:, 0:256], lhsT=WU[:, k, T * 128:(T + 1) * 128], rhs=hT[:, k, :],
                                                          start=(k == 0), stop=(k == 7)), [tWU, thT], [tpp])
                            o_ = AP(UT, T * 2048 + st * 32, [[UTW, 128], [256, 8], [1, 32]])
                            i_ = AP(pp, 0, [[512, 128], [1, 8], [8, 32]])
                            if T % 2 == 0:
                                V(lambda e: e.tensor_copy(o_, i_), [tpp], [tUT])
                            else:
                                A(lambda e: e.copy(out=o_, in_=i_), [tpp], [tUT])
                    kb.barrier()
                z_matmuls()
                recur()
                kb.barrier()
            ck("z")
            if BARRIERS: kb.barrier()
            B4 = a_("s5_B4", [128, 16, 2, 64], BF16); dst_t_pstride[id(B4)] = 16 * 2 * 64
            Wc0 = a_("s5_Wc0", [128, 16, 2, 64], BF16); dst_t_pstride[id(Wc0)] = 16 * 2 * 64
            Wc = a_("s5_Wc", [128, 16, 8, 2, 64], BF16); dst_t_pstride[id(Wc)] = 16 * 8 * 2 * 64
            Kblk = a_("s5_Kblk", [128, 4, 8, 128], BF16)
            for t_ in (B4, Wc0, Kblk):
                V(lambda e, t_=t_: e.memset(t_[:], 0.0), [tS], [tS])
            G(lambda e: e.memset(Wc[:], 0.0), [tS], [tS])
            scatter(B4, 128, 0, BBr); scatter(B4, 128, 64, BBi)
            for k in range(0, 9):
                cmul(TRe[:], TIm[:], Cr[:], Ci[:], slb(PWR + k), slb(PWI + k), TA[:], TB[:])
                if k == 0:
                    scatter(Wc0, 128, 0, TRe); scatter(Wc0, 128, 64, TIm, neg=True)
                else:
                    scatter(Wc, 1024, (k - 1) * 128, TRe); scatter(Wc, 1024, (k - 1) * 128 + 64, TIm, neg=True)
            identf = a_("s5_identf", [128, 128], F32)
            kb.dma("sp", identf[:], identf_d.ap(), writes=[tS])
            ck("tab")
            with contextlib.ExitStack() as sk:
                pK = Rot([sk.enter_context(nc.psum_tensor("s5_pK%d" % i, [128, 128], F32)) for i in range(2)])
                for T in range(4):
                    for tau in range(8):
                        pk, tpk = pK.next()
                        for h2 in range(2):
                            for pq in range(2):
                                pair = 2 * (2 * T + h2) + pq
                                for ri in range(2):
                                    rhs_ = Wc0[:, pair, ri, :] if tau == 0 else Wc[:, pair, tau - 1, ri, :]
                                    P(lambda e, h2=h2, pair=pair, ri=ri, rhs_=rhs_, pq=pq: e.matmul(
                                        pk[64 * h2:64 * h2 + 64, 64 * h2:64 * h2 + 64], lhsT=B4[:, pair, ri, :], rhs=rhs_,
                                        start=(pq == 0 and ri == 0), stop=(pq == 1 and ri == 1)), [tS], [tpk])
                        for h2 in range(2):
                            sl_ = slice(64 * h2, 64 * h2 + 64)
                            if tau == 0:
                                V(lambda e, sl_=sl_: e.scalar_tensor_tensor(out=Kblk[sl_, T, 0, sl_], in0=identf[sl_, sl_],
                                                                            scalar=dcol[sl_, T:T + 1], in1=pk[sl_, sl_],
                                                                            op0=ALU.mult, op1=ALU.add), [tpk, tS], [tS])
                            else:
                                V(lambda e, sl_=sl_: e.tensor_copy(Kblk[sl_, T, tau, sl_], pk[sl_, sl_]), [tpk], [tS])
            ck("kblk")
            if BARRIERS: kb.barrier()
            Xp = a_("s5_Xp", [128, 2, 16, 256], BF16); tXp = Tok()
            with contextlib.ExitStack() as sc:
                def c_(name, shape, dt):
                    return sc.enter_context(nc.sbuf_tensor(name, list(shape), dt))
                Dd = c_("s5_D", [128, 9, 2, 256], F32); tD = Tok()
                Gg = c_("s5_G", [128, 2, 16, 16], F32)
                Cw = c_("s5_Cw", [128, 2, 256], F32)
                Cn = c_("s5_Cn", [128, 2, 256], F32)
                oh = c_("s5_oh", [128, 8], F32)
                kb.dma("sp", oh[:], onehot_d.ap(), writes=[tD])

                def dv(r, ri):
                    return AP(Dd, (r * 2 + ri) * 256, [[9 * 512, 128], [16, 16], [1, 16]])

                def ev(r, ri):
                    return AP(Eg, (r * 2 + ri) * 256, [[8 * 512, 128], [16, 16], [1, 16]])
                L128r, L128i = lb(SQ, 4), lb(SQ, 9)
                for ri in range(2):
                    V(lambda e, ri=ri: e.memset(dv(0, ri), 0.0), [], [tD])
                for r in range(8):
                    zz(rt(0), dv(r, 0), L128r, ALU.mult, [tD, tS], [tD]); zz(rt(1), dv(r, 1), L128i, ALU.mult, [tD, tS], [tD])
                    zz(rt(2), dv(r, 1), L128r, ALU.mult, [tD, tS], [tD]); zz(rt(3), dv(r, 0), L128i, ALU.mult, [tD, tS], [tD])
                    zz(dv(r + 1, 0), rt(0), rt(1), ALU.subtract, [tD], [tD])
                    zz(dv(r + 1, 0), dv(r + 1, 0), ev(r, 0), ALU.add, [tD, tEg], [tD])
                    zz(dv(r + 1, 1), rt(2), rt(3), ALU.add, [tD], [tD])
                    zz(dv(r + 1, 1), dv(r + 1, 1), ev(r, 1), ALU.add, [tD, tEg], [tD])
                def gv(ri, j):
                    return Gg[:, ri, :, j]

                def d8(ri, j):
                    return AP(Dd, (8 * 2 + ri) * 256 + j, [[9 * 512, 128], [16, 16]])
                Lkr, Lki = Q128[:, 8, :], Q128[:, 17, :]
                V(lambda e: e.memset(Gg[:, :, :, 0], 0.0), [], [tD])
                for j in range(15):
                    zz(sl(T1), gv(0, j), Lkr, ALU.mult, [tD, tS], [tS]); zz(sl(T2), gv(1, j), Lki, ALU.mult, [tD, tS], [tS])
                    zz(sl(T3), gv(1, j), Lkr, ALU.mult, [tD, tS], [tS]); zz(sl(T4), gv(0, j), Lki, ALU.mult, [tD, tS], [tS])
                    zz(sl(T1), sl(T1), sl(T2), ALU.subtract, [tS], [tS])
                    zz(gv(0, j + 1), sl(T1), d8(0, j), ALU.add, [tS, tD], [tD])
                    zz(sl(T3), sl(T3), sl(T4), ALU.add, [tS], [tS])
                    zz(gv(1, j + 1), sl(T3), d8(1, j), ALU.add, [tS, tD], [tD])
                Gr = AP(Gg, 0, [[512, 128], [16, 16], [1, 16]]); Gi = AP(Gg, 256, [[512, 128], [16, 16], [1, 16]])
                cw = [AP(Cw, ri * 256, [[512, 128], [16, 16], [1, 16]]) for ri in range(2)]
                for ri in range(2):
                    V(lambda e, ri=ri: e.memset(cw[ri], 0.0), [], [tD])
                for r in range(8):
                    qr, qi = lb(Q128, r), lb(Q128, 9 + r)
                    zz(rt(0), Gr, qr, ALU.mult, [tD, tS], [tD]); zz(rt(1), Gi, qi, ALU.mult, [tD, tS], [tD])
                    zz(rt(2), Gi, qr, ALU.mult, [tD, tS], [tD]); zz(rt(3), Gr, qi, ALU.mult, [tD, tS], [tD])
                    zz(rt(0), rt(0), rt(1), ALU.subtract, [tD], [tD]); zz(rt(0), rt(0), dv(r, 0), ALU.add, [tD], [tD])
                    zz(rt(2), rt(2), rt(3), ALU.add, [tD], [tD]); zz(rt(2), rt(2), dv(r, 1), ALU.add, [tD], [tD])
                    for ri, src in ((0, rt(0)), (1, rt(2))):
                        V(lambda e, ri=ri, src=src: e.scalar_tensor_tensor(out=cw[ri], in0=src, scalar=oh[:, r:r + 1],
                                                                           in1=cw[ri], op0=ALU.mult, op1=ALU.add),
                          [tD], [tD])
                cn = [AP(Cn, ri * 256, [[512, 128], [16, 16], [1, 16]]) for ri in range(2)]

                def xpv(ri, bb):
                    return AP(Xp, ri * 4096 + bb, [[ZW, 128], [256, 16], [16, 16]])
                for bb in range(16):
                    for ri in range(2):
                        if bb == 0:
                            V(lambda e, ri=ri: e.tensor_copy(xpv(ri, 0), cw[ri]), [tD], [tXp])
                        else:
                            zz(xpv(ri, bb), zv(ri, bb - 1), cw[ri], ALU.add, [tZ, tD], [tXp])
                    if bb < 15:
                        zz(rt(0), cw[0], L8r, ALU.mult, [tD, tS], [tD]); zz(rt(1), cw[1], L8i, ALU.mult, [tD, tS], [tD])
                        zz(rt(2), cw[1], L8r, ALU.mult, [tD, tS], [tD]); zz(rt(3), cw[0], L8i, ALU.mult, [tD, tS], [tD])
                        zz(cn[0], rt(0), rt(1), ALU.subtract, [tD], [tD]); zz(cn[1], rt(2), rt(3), ALU.add, [tD], [tD])
                        for ri in range(2):
                            V(lambda e, ri=ri: e.tensor_copy(cw[ri], cn[ri]), [tD], [tD])
            ck("car")
            if BARRIERS: kb.barrier()
            with contextlib.ExitStack() as sy_:
                def y_(name, shape, dt):
                    return sy_.enter_context(nc.sbuf_tensor(name, list(shape), dt))
                zT = y_("s5_zT", [128, 4, 2048], BF16); tzT = Tok()
                pY = Rot([sy_.enter_context(nc.psum_tensor("s5_pY%d" % i, [128, 256], F32)) for i in range(4)])
                for T in range(4):
                    for i in range(8):
                        py, tpy = pY.next()
                        for h2 in range(2):
                            hs = slice(64 * h2, 64 * h2 + 64)
                            for ip in range(i + 1):
                                P(lambda e, ip=ip, hs=hs: e.matmul(py[hs, :], lhsT=Kblk[:, T, i - ip, hs], rhs=UT[:, T, ip, :],
                                                                   start=(ip == 0), stop=False), [tS, tUT], [tpy])
                            n_ = 0
                            for pq in range(2):
                                pair = 2 * (2 * T + h2) + pq
                                for ri in range(2):
                                    n_ += 1
                                    P(lambda e, hs=hs, pair=pair, ri=ri, n_=n_: e.matmul(
                                        py[hs, :], lhsT=Wc[:, pair, i, ri, :], rhs=Xp[:, ri, pair, :],
                                        start=False, stop=(n_ == 4)), [tS, tXp], [tpy])
                        o_ = AP(zT, T * 2048 + i, [[8192, 128], [8, 256]])
                        A(lambda e: e.activation(out=o_, in_=py[:], func=AF.Gelu_apprx_tanh), [tpy], [tzT])
                ck("y")
                if BARRIERS: kb.barrier()
                pG = Rot([sy_.enter_context(nc.psum_tensor("s5_pG%d" % i, [128, 512], F32)) for i in range(2)])
                sg = Rot([y_("s5_sg%d" % i, [128, 512], F32) for i in range(2)])
                zg = Rot([y_("s5_zg%d" % i, [128, 512], BF16) for i in range(2)])
                for ch in range(4):
                    for m in range(4):
                        pg, tpg = pG.next()
                        for k4 in range(4):
                            P(lambda e, k4=k4: e.matmul(pg[:], lhsT=Wgl[:, k4, m * 128:(m + 1) * 128],
                                                        rhs=zT[:, k4, ch * 512:(ch + 1) * 512],
                                                        start=(k4 == 0), stop=(k4 == 3)), [tWgl, tzT], [tpg])
                        s_, ts_ = sg.next()
                        A(lambda e: e.activation(out=s_[:], in_=pg[:], func=AF.Sigmoid), [tpg], [ts_])
                        z_, tz_ = zg.next()
                        V(lambda e: e.tensor_tensor(out=z_[:], in0=s_[:], in1=zT[:, m, ch * 512:(ch + 1) * 512],
                                                    op=ALU.mult), [ts_, tzT], [tz_])
                        kb.dma("sp", AP(zg_d, m * 2048 + ch * 512, [[4 * 2048, 128], [1, 512]]), z_[:],
                               reads=[tz_], writes=[t_zg])


    NEG = -30000.0

    def phase_tables():
        with contextlib.ExitStack() as st:
            def a_(name, shape, dt):
                return st.enter_context(nc.sbuf_tensor(name, list(shape), dt))
            tT = Tok()
            relb = a_("t_relb", [32, 8], F32); rl = a_("t_rl", [32, 8], F32)
            ohr = a_("t_ohr", [32, 256], F32); ohf = a_("t_ohf", [32, 256], F32)
            antiJ = a_("t_antiJ", [128, 128], F32)
            kb.dma("sp", relb[:], rel_bias_d.ap(), writes=[tT])
            kb.dma("sp", rl[:], AP(rel_bias_d, 31 * 8, [[0, 32], [1, 8]]), writes=[tT])
            kb.dma("sp", ohr[:], ohrev_d.ap(), writes=[tT])
            kb.dma("sp", ohf[:], ohfwd_d.ap(), writes=[tT])
            kb.dma("sp", antiJ[:], antij_d.ap(), writes=[tT])
            V(lambda e: e.tensor_tensor(out=relb[:], in0=relb[:], in1=rl[:], op=ALU.subtract), [tT], [tT])
            pt = st.enter_context(nc.psum_tensor("t_pt", [8, 512], F32)); tpt = Tok()
            P(lambda e: e.matmul(pt[:, 0:256], lhsT=relb[:], rhs=ohr[:], start=True, stop=True), [tT], [tpt])
            P(lambda e: e.matmul(pt[:, 256:512], lhsT=relb[:], rhs=ohf[:], start=True, stop=True), [tT], [tpt])
            rowR = a_("t_rowR", [8, 384], F32); rowF = a_("t_rowF", [8, 416], F32)
            V(lambda e: e.memset(rowR[:], NEG), [tT], [tT])
            V(lambda e: e.memset(rowF[:], NEG), [tT], [tT])
            V(lambda e: e.tensor_copy(rowR[:, 0:256], pt[:, 0:256]), [tpt, tT], [tT])
            V(lambda e: e.tensor_copy(rowF[:, 160:416], pt[:, 256:512]), [tpt, tT], [tT])
            tD_ = Tok()
            kb.dma("sp", tabR_d.ap(), rowR[:], reads=[tT], writes=[tD_])
            kb.dma("sp", tabF_d.ap(), rowF[:], reads=[tT], writes=[tD_])
            Hk = a_("t_Hk", [128, 2, 8, 128], F32); tH = Tok()
            kb.dma("sp", Hk[:, 0, :, :], AP(tabR_d, 128, [[1, 128], [384, 8], [1, 128]]), reads=[tD_], writes=[tH])
            kb.dma("sp", Hk[:, 1, :, :], AP(tabR_d, 0, [[1, 128], [384, 8], [1, 128]]), reads=[tD_], writes=[tH])
            pb = Rot([st.enter_context(nc.psum_tensor("t_pb%d" % i, [128, 128], F32)) for i in range(2)])
            for d_ in range(2):
                for h in range(8):
                    p_, tp_ = pb.next()
                    P(lambda e: e.matmul(p_[:], lhsT=Hk[:, d_, h, :], rhs=antiJ[:], start=True, stop=True), [tH, tT], [tp_])
                    V(lambda e: e.tensor_copy(BW[:, d_, h, :], p_[:]), [tp_], [t_BW])
            kb.dma("sp", M4[:], m4_d.ap(), writes=[t_BW])
            V(lambda e: e.memset(BnA[:], 1.0), [], [t_BnA])
            kb.dma("pool", BnA[0:16, :, :], AP(tabF_d, 17, [[16, 16], [416, 8], [1, 128]]), reads=[tD_], writes=[t_BnA])

    def phase_compress():
        with contextlib.ExitStack() as sc:
            def a_(name, shape, dt):
                return sc.enter_context(nc.sbuf_tensor(name, list(shape), dt))
            tW = Tok()
            W1s = a_("c_W1s", [128, 2, 16, 256], BF16)
            W2 = a_("c_W2", [128, 2, 2, 64], BF16)
            PosS = a_("c_PosS", [128, 2, 16], BF16)
            for w, (w1, w2, pos) in enumerate(((cmp_w1_k, cmp_w2_k, cmp_pos_k), (cmp_w1_v, cmp_w2_v, cmp_pos_v))):
                for lp in range(16):
                    kb.dma("pool", W1s[:, w, lp, :], AP(w1, lp * 128 * 256, [[256, 128], [1, 256]]), writes=[tW])
                kb.dma("pool", W2[:, w, :, :], AP(w2, 0, [[64, 128], [128 * 64, 2], [1, 64]]), writes=[tW])
                kb.dma("pool", PosS[:, w, :], AP(pos, 0, [[1, 128], [128, 16]]), writes=[tW], allow_slow_non_contiguous=True)
            biasW = a_("c_biasW", [128, 4], F32); tB = Tok()
            pB = sc.enter_context(nc.psum_tensor("c_pB", [128, 4], F32)); tpB = Tok()
            for w in range(2):
                for ht in range(2):
                    for lp in range(16):
                        P(lambda e, lp=lp: e.matmul(pB[:, w * 2 + ht:w * 2 + ht + 1], lhsT=W1s[:, w, lp, ht * 128:(ht + 1) * 128],
                                                    rhs=PosS[:, w, lp:lp + 1], start=(lp == 0), stop=(lp == 15)), [tW], [tpB])
            V(lambda e: e.tensor_copy(biasW[:], pB[:]), [tpB], [tB])
            CW = 2 * 2 * 2080
            CRs = a_("c_CRs", [128, 2, 2, 2080], BF16); tCR = Tok()
            G1 = a_("c_G1", [128, 2, 2, 2, 128], BF16); tG1 = Tok()
            pH = Rot([sc.enter_context(nc.psum_tensor("c_pH%d" % i, [128, 128], F32)) for i in range(2)])
            pO = Rot([sc.enter_context(nc.psum_tensor("c_pO%d" % i, [128, 128], F32)) for i in range(2)])
            V(lambda e: e.memset(Vc_aug[:], 1.0), [], [t_Vc])
            for nt in range(8):
                t0 = 2048 * nt
                for w, src in enumerate((kcR_d, vcR_d)):
                    for g in range(2):
                        kb.dma("sp", CRs[0:64, w, g, 0:2064], AP(src, (64 * g) * (S + 16) + t0, [[S + 16, 64], [1, 2064]]),
                               reads=kv_toks, writes=[tCR])
                        kb.dma("sp", CRs[64:128, w, g, 0:2063], AP(src, (64 * g) * (S + 16) + t0 + 1, [[S + 16, 64], [1, 2063]]),
                               reads=kv_toks, writes=[tCR])
                for w in range(2):
                    for g in range(2):
                        for ht in range(2):
                            ph, tph = pH.next()
                            for lp in range(16):
                                rhs_ = AP(CRs, (w * 2 + g) * 2080 + 2 * lp, [[CW, 128], [16, 128]])
                                P(lambda e, lp=lp, rhs_=rhs_: e.matmul(ph[:], lhsT=W1s[:, w, lp, ht * 128:(ht + 1) * 128], rhs=rhs_,
                                                                       start=(lp == 0), stop=(lp == 15)), [tW, tCR], [tph])
                            A(lambda e: e.activation(out=G1[:, w, g, ht, :], in_=ph[:], func=AF.Gelu_apprx_tanh,
                                                     bias=biasW[:, w * 2 + ht:w * 2 + ht + 1]), [tph, tB], [tG1])
                po, tpo = pO.next()
                for g in range(2):
                    for ht in range(2):
                        P(lambda e, ht=ht: e.matmul(po[64 * g:64 * g + 64, :], lhsT=W2[:, 0, ht, :], rhs=G1[:, 0, g, ht, :],
                                                    start=(ht == 0), stop=(ht == 1)), [tW, tG1], [tpo])
                V(lambda e: e.tensor_copy(KcT[:, nt * 128:(nt + 1) * 128], po[:]), [tpo], [t_Kc])
                po, tpo = pO.next()
                for g in range(2):
                    for ht in range(2):
                        P(lambda e, ht=ht: e.matmul(po[:, 64 * g:64 * g + 64], lhsT=G1[:, 1, g, ht, :], rhs=W2[:, 1, ht, :],
                                                    start=(ht == 0), stop=(ht == 1)), [tW, tG1], [tpo])
                V(lambda e: e.tensor_copy(Vc_aug[:, nt, :, 0:64], po[:].rearrange("p (g d) -> p g d", g=2)), [tpo], [t_Vc])

    def attn_tile(ps_, tps_, kT_ap, q_ap, deps_r, bias_ap, bias_tok, P_rot, Sb_rot):
        P(lambda e: e.matmul(ps_[:], lhsT=kT_ap, rhs=q_ap, start=True, stop=True), deps_r, [tps_])
        p_, tp_ = P_rot.next()
        if bias_ap is not None:
            sb_, tsb_ = Sb_rot.next()
            V(lambda e: e.tensor_tensor(out=sb_[:], in0=ps_[:], in1=bias_ap, op=ALU.add), [tps_, bias_tok], [tsb_])
            A(lambda e: e.activation(out=p_[:], in_=sb_[:], func=AF.Exp), [tsb_], [tp_])
        else:
            A(lambda e: e.activation(out=p_[:], in_=ps_[:], func=AF.Exp), [tps_], [tp_])
        return p_, tp_

    def combine(po_, tpo_, gcol0, gstride, j, g, acc_, tacc_, first, cf_rot):
        cf, tcf = cf_rot.next()
        rs_ = AP(po_, 64, [[int(np.prod(list(po_.shape)[1:])), 128], [65, 4]])
        V(lambda e: e.tensor_scalar(out=cf[:, 0:4], in0=rs_, scalar1=1e-30, scalar2=None, op0=ALU.max), [tpo_], [tcf])
        V(lambda e: e.reciprocal(out=cf[:, 4:8], in_=cf[:, 0:4]), [tcf], [tcf])
        g_ = AP(gates, j * 24 + 12 * g + gcol0, [[16 * 24, 128], [3, 4]])
        V(lambda e: e.tensor_tensor(out=cf[:, 8:12], in0=cf[:, 4:8], in1=g_, op=ALU.mult), [tcf, t_gates], [tcf])
        for r in range(4):
            h = 4 * g + r
            o_ = acc_[:, h * 64:(h + 1) * 64]
            if first:
                V(lambda e, r=r, o_=o_: e.tensor_scalar(out=o_, in0=po_[:, r * 65:r * 65 + 64], scalar1=cf[:, 8 + r:9 + r],
                                                        scalar2=None, op0=ALU.mult), [tpo_, tcf], [tacc_])
            else:
                V(lambda e, r=r, o_=o_: e.scalar_tensor_tensor(out=o_, in0=po_[:, r * 65:r * 65 + 64], scalar=cf[:, 8 + r:9 + r],
                                                               in1=o_, op0=ALU.mult, op1=ALU.add), [tpo_, tcf], [tacc_])

    def pv_T(poT, tpoT, Pm, tPm, v_ap, tv, first, last):
        P(lambda e: e.matmul(poT[0:65, :], lhsT=v_ap, rhs=Pm[:], start=first, stop=last), [tPm, tv], [tpoT])

    def finish_o(poT, tpoT, po, tpo, osb, tosb):
        V(lambda e: e.tensor_copy(osb[0:65, :], poT[0:65, :]), [tpoT], [tosb])
        for r in range(4):
            P(lambda e, r=r: e.transpose(out=po[:, r * 65:(r + 1) * 65], in_=osb[0:65, r * 128:(r + 1) * 128],
                                         identity=identF[0:65, 0:65]), [tosb, t_identF], [tpo])

    def phase1():
        with contextlib.ExitStack() as s1:
            def a_(name, shape, dt):
                return s1.enter_context(nc.sbuf_tensor(name, list(shape), dt))
            X = norm_ctx(s1, 2, g_mix)
            WinA = a_("WinA", [128, 8, 1048], BF16); tWin = Tok()
            for k in range(8):
                base = k * 128 * INC
                for r in range(4):
                    kb.dma("pool", WinA[:, k, r * 128:(r + 1) * 128].rearrange("p (g d) -> p g d", g=2),
                           AP(w_in, base + 512 + 64 * r, [[INC, 128], [256, 2], [1, 64]]), writes=[tWin])
                for (c0, s0, n) in ((512, 1536, 128), (640, 1280, 128), (768, 1664, 128), (896, 1408, 128), (1024, 1792, 24)):
                    kb.dma("pool", WinA[:, k, c0:c0 + n], AP(w_in, base + s0, [[INC, 128], [1, n]]), writes=[tWin])
            vprev = a_("vprev_sb", [128, 64], F32); tvp = Tok()
            kb.dma("sp", vprev[:], vprev_d.ap(), writes=[tvp])
            KwT = [a_("KwT%d" % i, [128, 5, 128], BF16) for i in range(2)]; tKw = [Tok(), Tok()]
            Vw = [a_("Vw%d" % i, [128, 5, 2, 65], BF16) for i in range(2)]; tVw = [Tok(), Tok()]
            pP = Rot([s1.enter_context(nc.psum_tensor("p1_pP%d" % i, [128, 512], F32)) for i in range(2)])
            pS = Rot([s1.enter_context(nc.psum_tensor("p1_pS%d" % i, [128, 512], F32)) for i in range(2)])
            pO = Rot([s1.enter_context(nc.psum_tensor("p1_pO%d" % i, [128, 512], F32)) for i in range(1)])
            poT = s1.enter_context(nc.psum_tensor("p1_poT", [128, 512], F32)); tpoT = Tok()
            osb = a_("p1_osb", [128, 512], F32); tosb = Tok()
            M4r = a_("p1_M4r", [128, 512], F32)
            for r_ in range(4):
                V(lambda e, r_=r_: e.tensor_copy(M4r[:, r_ * 128:(r_ + 1) * 128], M4[:]), [t_BW], [t_BW])
            Pr = Rot([a_("p1_P%d" % i, [128, 512], BF16) for i in range(2)])
            Sbr = Rot([a_("p1_Sb%d" % i, [128, 512], F32) for i in range(2)])
            cfr = Rot([a_("p1_cf%d" % i, [128, 12], F32) for i in range(2)])
            accw = Rot([a_("p1_acc%d" % i, [128, 512], F32) for i in range(2)])
            V(lambda e: e.memset(VsN[:], 1.0), [], [t_VsN])
            nev = [0]

            def evac(o_, i_, rd, wr, scale=None):
                nev[0] += 1
                if scale is not None:
                    A(lambda e: e.activation(out=o_, in_=i_, func=AF.Copy, scale=scale), rd, wr)
                else:
                    V(lambda e: e.tensor_copy(o_, i_), rd, wr)

            def fm(col0, hT, thT, c0, n):
                pp, tpp = pP.next()
                for k in range(8):
                    P(lambda e, k=k: e.matmul(pp[:, 0:n], lhsT=WinA[:, k, col0:col0 + 128], rhs=hT[:, k, c0:c0 + n],
                                              start=(k == 0), stop=(k == 7)), [tWin, thT], [tpp])
                return pp, tpp

            def tm(col0, ncol, hT, thT, a):
                pp, tpp = pP.next()
                for k in range(8):
                    P(lambda e, k=k: e.matmul(pp[:, 0:ncol], lhsT=hT[:, k, a * 128:(a + 1) * 128], rhs=WinA[:, k, col0:col0 + ncol],
                                              start=(k == 0), stop=(k == 7)), [tWin, thT], [tpp])
                return pp, tpp
            ck("p1w")
            for oc in range(8):
                hT, thT = load_norm_T(x_own, oc * 256, 2, X)
                for r in range(4):
                    pp, tpp = fm(r * 128, hT, thT, 0, 256)
                    evac(AP(Qall, (2 * oc) * 512 + r * 128, [[16 * 512, 128], [512, 2], [1, 128]]),
                         AP(pp, 0, [[512, 128], [128, 2], [1, 128]]), [tpp], [t_Q], scale=0.125)
                ck("p1a")
                pp, tpp = fm(512, hT, thT, 0, 256)
                ck("p1a1")
                for a in range(2):
                    evac(KwT[a][:, 4, :], pp[:, a * 128:(a + 1) * 128], [tpp], [tKw[a]])
                ck("p1a2")
                pp, tpp = fm(640, hT, thT, 0, 256)
                evac(AP(KsN, (2 * oc) * 256 + 128, [[16 * 256, 128], [256, 2], [1, 128]]),
                     AP(pp, 0, [[512, 128], [128, 2], [1, 128]]), [tpp], [t_KsN])
                ck("p1b")
                for a in range(2):
                    j = 2 * oc + a
                    pp, tpp = tm(768, 256, hT, thT, a)
                    evac(Vw[a][:, 4, :, 0:64], pp[:, 0:128].rearrange("p (g d) -> p g d", g=2), [tpp], [tVw[a]])
                    V(lambda e: e.memset(Vw[a][:, 4, :, 64:65], 1.0), [], [tVw[a]])
                    evac(VsN[:, j, 1, :, 0:64], pp[:, 128:256].rearrange("p (g d) -> p g d", g=2), [tpp], [t_VsN])
                    ck("p1c")
                    pp, tpp = tm(1024, 24, hT, thT, a)
                    A(lambda e: e.activation(out=gates[:, j, :], in_=pp[:, 0:24], func=AF.Sigmoid), [tpp], [t_gates])
                ck("p1own")
                for a in range(2):
                    j = 2 * oc + a
                    for pc in range(2):
                        hP, thP = load_norm_T(x_prev, j * 512 + pc * 256, 2, X)
                        pp, tpp = fm(512, hP, thP, 0, 256)
                        evac(KwT[a][:, 2 * pc:2 * pc + 2, :], pp[:, 0:256].rearrange("p (t k) -> p t k", t=2), [tpp], [tKw[a]])
                        for a2 in range(2):
                            p_ = 2 * pc + a2
                            last = (p_ == 3)
                            pp, tpp = tm(768, 256 if last else 128, hP, thP, a2)
                            evac(Vw[a][:, p_, :, 0:64], pp[:, 0:128].rearrange("p (g d) -> p g d", g=2), [tpp], [tVw[a]])
                            vcol = AP(vprev, j * 4 + p_, [[64, 128], [0, 2], [1, 1]])
                            V(lambda e: e.tensor_copy(Vw[a][:, p_, :, 64:65], vcol), [tvp], [tVw[a]])
                            if last:
                                evac(VsN[:, j, 0, :, 0:64], pp[:, 128:256].rearrange("p (g d) -> p g d", g=2), [tpp], [t_VsN])
                                V(lambda e: e.tensor_copy(VsN[:, j, 0, :, 64:65], vcol), [tvp], [t_VsN])
                        if pc == 1:
                            pp, tpp = fm(640, hP, thP, 128, 128)
                            evac(KsN[:, j, 0, :], pp[:, 0:128], [tpp], [t_KsN])
                    ck("p1prev")
                    ac, tac = accw.next()
                    for g in range(2):
                        po, tpo = pO.next()
                        for p_ in range(5):
                            dl = 4 - p_
                            ps_, tps_ = pS.next()
                            if dl == 0:
                                b_ap, b_tok = BW[:, 0, 4 * g:4 * g + 4, :].rearrange("p h q -> p (h q)"), t_BW
                            elif dl == 1:
                                b_ap, b_tok = BW[:, 1, 4 * g:4 * g + 4, :].rearrange("p h q -> p (h q)"), t_BW
                            elif dl == 4:
                                b_ap, b_tok = M4r[:], t_BW
                            else:
                                b_ap, b_tok = None, None
                            if b_ap is not None and dl != 4:
                                pass
                            Pm, tPm = attn_tile(ps_, tps_, KwT[a][64 * g:64 * g + 64, p_, :],
                                                Qall[64 * g:64 * g + 64, j, :, :].rearrange("p r q -> p (r q)"),
                                                [tKw[a], t_Q], b_ap, b_tok, Pr, Sbr)
                            pv_T(poT, tpoT, Pm, tPm, Vw[a][:, p_, g, :], tVw[a], p_ == 0, p_ == 4)
                        finish_o(poT, tpoT, po, tpo, osb, tosb)
                        combine(po, tpo, 2, 3, j, g, ac, tac, True, cfr)
                    kb.dma("sp", acc_d.ap()[j * 128:(j + 1) * 128, :], ac[:], reads=[tac], writes=[t_accd])
                    ck("p1win")

    def phase2():
        with contextlib.ExitStack() as s2:
            def a_(name, shape, dt):
                return s2.enter_context(nc.sbuf_tensor(name, list(shape), dt))
            KsT_all = a_("KsT_all", [128, S], BF16); tKs = Tok()
            for i in range(8):
                kb.dma("sp", KsT_all[:, i * 2048:(i + 1) * 2048], AP(ksT_d, i * 2048, [[S, 128], [1, 2048]]),
                       reads=kv_toks, writes=[tKs])
            Vs_all = a_("Vs_all", [128, 128, 2, 65], BF16); tVs = Tok()
            G(lambda e: e.memset(Vs_all[:], 1.0), [], [tVs])
            for i in range(8):
                for g in range(2):
                    kb.dma("sp", Vs_all[:, i * 16:(i + 1) * 16, g, 0:64],
                           AP(vs_d, i * 16 * 16384 + 64 * g, [[128, 128], [16384, 16], [1, 64]]), reads=kv_toks, writes=[tVs])
            wide = a_("wide_sb", [128, 4096], BF16); tc_ = Tok()
            kb.dma("sp", wide[:], wide_d.ap(), writes=[tc_])
            keepb = a_("keepb", [128, 32], F32)
            kb.dma("sp", keepb[:], keepblk_d.ap(), writes=[tc_])
            candr = Rot([a_("cand%d" % i, [128, 256], BF16) for i in range(2)])
            forcr = Rot([a_("forc%d" % i, [128, 256], BF16) for i in range(2)])
            expnr = Rot([a_("expn%d" % i, [128, 2, 128], BF16) for i in range(2)])
            selcr = Rot([a_("selc%d" % i, [17, 1024], BF16) for i in range(2)])
            accr = Rot([a_("p2_acc%d" % i, [128, 512], F32) for i in range(2)])
            accb = a_("p2_accb", [128, 512], BF16); taccb = Tok()
            imp = a_("p2_imp", [128, 1028], F32); timp = Tok()
            pq = Rot([a_("p2_pq%d" % i, [128, 1024], F32) for i in range(2)])
            rs = Rot([a_("p2_rs%d" % i, [128, 4], F32) for i in range(4)])
            sS = a_("p2_s", [128, 256], F32); sS2 = a_("p2_s2", [128, 256], F32); tS_ = Tok()
            m8 = a_("p2_m8", [128, 16], F32)
            selg = a_("p2_selg", [128, 256], BF16); tselg = Tok()
            selT = a_("p2_selT", [128, 2, 2, 128], BF16); tselT = Tok()
            selTk = a_("p2_selTk", [128, 2, 2, 128], BF16)
            Pr = Rot([a_("p2_P%d" % i, [128, 512], BF16) for i in range(3)])
            Pmr = Rot([a_("p2_Pm%d" % i, [128, 512], BF16) for i in range(3)])
            Sbr = Rot([a_("p2_Sb%d" % i, [128, 512], F32) for i in range(2)])
            cfr = Rot([a_("p2_cf%d" % i, [128, 12], F32) for i in range(2)])
            pS = Rot([s2.enter_context(nc.psum_tensor("p2_pS%d" % i, [128, 512], F32)) for i in range(2)])
            pI = Rot([s2.enter_context(nc.psum_tensor("p2_pI%d" % i, [128, 512], F32)) for i in range(2)])
            pMt = s2.enter_context(nc.psum_tensor("p2_pM", [128, 2, 128], F32))
            pM = Rot([pMt[:, 0, :], pMt[:, 1, :]])
            pO = Rot([s2.enter_context(nc.psum_tensor("p2_pO%d" % i, [128, 260], F32)) for i in range(1)])
            pT = s2.enter_context(nc.psum_tensor("p2_pT", [128, 4, 128], BF16)); tpT = Tok()
            poT = s2.enter_context(nc.psum_tensor("p2_poT", [128, 512], F32)); tpoT = Tok()
            osb = a_("p2_osb", [128, 512], F32); tosb = Tok()
            V(lambda e: e.memset(imp[:], 0.0), [], [timp])

            def pv(po, tpo, Pm, tPm, v_ap, tv, first, last):
                pv_T(poT, tpoT, Pm, tPm, v_ap, tv, first, last)
                if last:
                    finish_o(poT, tpoT, po, tpo, osb, tosb)

            def pm_b(pm):
                i = 0 if pm is pM.aps[0] else 1
                return AP(pMt, i * 128, [[256, 128], [0, 4], [1, 128]])

            def masked(Pt, tPt, pm, tpm):
                Pm, tPm = Pmr.next()
                V(lambda e: e.tensor_tensor(out=Pm[:].rearrange("p (r q) -> p r q", r=4), in0=Pt[:].rearrange("p (r q) -> p r q", r=4),
                                            in1=pm.rearrange("p (o q) -> p o q", o=1).broadcast_to([128, 4, 128]) if False else pm_b(pm), op=ALU.mult), [tPt, tpm], [tPm])
                return Pm, tPm
            for j in range(NQ):
                Wb_ = 16 * (j + 1)
                NCc = 64 * (j + 1)
                ntc = (NCc + 127) // 128
                nbt = (Wb_ + 127) // 128
                cd, tcd = candr.next(); fc, tfc = forcr.next(); ex, tex = expnr.next(); sc_, tsc = selcr.next()
                kb.dma("sp", cd[:], AP(cand_d, j * 128 * 256, [[256, 128], [1, 256]]), writes=[tcd])
                kb.dma("sp", fc[:], AP(forced_d, j * 128 * 256, [[256, 128], [1, 256]]), writes=[tfc])
                kb.dma("sp", ex[:], AP(expn_d, j * 256 * 128, [[128, 128], [128 * 128, 2], [1, 128]]), writes=[tex])
                kb.dma("sp", sc_[:], AP(selc_d, j * 17 * 1024, [[1024, 17], [1, 1024]]), writes=[tsc])
                ac, tac = accr.next()
                kb.dma("sp", ac[:], acc_d.ap()[j * 128:(j + 1) * 128, :], reads=[t_accd], writes=[tac])
                for g in range(2):
                    qg = Qall[64 * g:64 * g + 64, j, :, :].rearrange("p r q -> p (r q)")
                    for r in range(4):
                        h = 4 * g + r
                        pq_, tpq = pq.next(); rs_, trs = rs.next()
                        nch = 0
                        for c0 in range(0, NCc, 512):
                            n = min(512, NCc - c0)
                            pi_, tpi = pI.next()
                            P(lambda e: e.matmul(pi_[:, 0:n], lhsT=Qall[64 * g:64 * g + 64, j, r, :], rhs=KcT[64 * g:64 * g + 64, c0:c0 + n],
                                                 start=True, stop=False), [t_Q, t_Kc], [tpi])
                            P(lambda e: e.matmul(pi_[:, 0:n], lhsT=BnA[0:17, h, :], rhs=sc_[0:17, c0:c0 + n],
                                                 start=False, stop=True), [t_BnA, tsc], [tpi])
                            A(lambda e, nch=nch: e.activation(out=pq_[:, c0:c0 + n], in_=pi_[:, 0:n], func=AF.Exp,
                                                              accum_out=rs_[:, nch:nch + 1]), [tpi], [tpq, trs])
                            nch += 1
                        if nch == 2:
                            V(lambda e: e.tensor_tensor(out=rs_[:, 0:1], in0=rs_[:, 0:1], in1=rs_[:, 1:2], op=ALU.add), [trs], [trs])
                        V(lambda e: e.tensor_scalar(out=rs_[:, 2:3], in0=rs_[:, 0:1], scalar1=1e-30, scalar2=None, op0=ALU.max), [trs], [trs])
                        V(lambda e: e.reciprocal(out=rs_[:, 3:4], in_=rs_[:, 2:3]), [trs], [trs])
                        if r == 0:
                            V(lambda e: e.tensor_scalar(out=imp[:, 1:1 + NCc], in0=pq_[:, 0:NCc], scalar1=rs_[:, 3:4], scalar2=None,
                                                        op0=ALU.mult), [tpq, trs], [timp])
                        else:
                            V(lambda e: e.scalar_tensor_tensor(out=imp[:, 1:1 + NCc], in0=pq_[:, 0:NCc], scalar=rs_[:, 3:4],
                                                               in1=imp[:, 1:1 + NCc], op0=ALU.mult, op1=ALU.add), [tpq, trs], [timp])

                    def iv(o):
                        return AP(imp, o, [[1028, 128], [4, Wb_]])
                    s_ = sS[:, 0:Wb_]
                    V(lambda e: e.tensor_tensor(out=s_, in0=iv(1), in1=iv(2), op=ALU.add), [timp], [tS_])
                    V(lambda e: e.tensor_tensor(out=s_, in0=s_, in1=iv(3), op=ALU.add), [timp, tS_], [tS_])
                    V(lambda e: e.scalar_tensor_tensor(out=s_, in0=s_, scalar=2.0, in1=iv(0), op0=ALU.mult, op1=ALU.add), [timp, tS_], [tS_])
                    V(lambda e: e.tensor_tensor(out=s_, in0=s_, in1=iv(4), op=ALU.add), [timp, tS_], [tS_])
                    V(lambda e: e.tensor_tensor(out=s_, in0=s_, in1=cd[:, 0:Wb_], op=ALU.mult), [tS_, tcd], [tS_])
                    V(lambda e: e.max(out=m8[:, 0:8], in_=s_), [tS_], [tS_])
                    V(lambda e: e.match_replace(out=sS2[:, 0:Wb_], in_to_replace=m8[:, 0:8], in_values=s_, imm_value=-1.0), [tS_], [tS_])
                    V(lambda e: e.max(out=m8[:, 8:16], in_=sS2[:, 0:Wb_]), [tS_], [tS_])
                    V(lambda e: e.tensor_scalar(out=s_, in0=s_, scalar1=m8[:, 12:13], scalar2=None, op0=ALU.is_ge), [tS_], [tS_])
                    V(lambda e: e.tensor_tensor(out=s_, in0=s_, in1=cd[:, 0:Wb_], op=ALU.mult), [tS_, tcd], [tS_])
                    if Wb_ < 256:
                        V(lambda e: e.memset(selg[:, Wb_:256], 0.0), [], [tselg])
                    V(lambda e: e.tensor_tensor(out=selg[:, 0:Wb_], in0=s_, in1=fc[:, 0:Wb_], op=ALU.add), [tS_, tfc], [tselg])
                    for bt in range(nbt):
                        P(lambda e, bt=bt: e.transpose(out=pT[:, bt, :], in_=selg[:, bt * 128:(bt + 1) * 128], identity=ident[:]),
                          [tselg, t_ident], [tpT])
                        V(lambda e, bt=bt: e.tensor_copy(selT[:, g, bt, :], pT[:, bt, :]), [tpT], [tselT])
                        V(lambda e, bt=bt: e.tensor_scalar(out=selTk[:, g, bt, :], in0=pT[:, bt, :], scalar1=keepb[:, j * 2 + bt:j * 2 + bt + 1],
                                                           scalar2=None, op0=ALU.mult), [tpT, tc_], [tselT])
                    po, tpo = pO.next()
                    for nt in range(ntc):
                        ps_, tps_ = pS.next()
                        P(lambda e: e.matmul(ps_[:], lhsT=KcT[64 * g:64 * g + 64, nt * 128:(nt + 1) * 128], rhs=qg,
                                             start=True, stop=False), [t_Kc, t_Q], [tps_])
                        P(lambda e: e.matmul(ps_[:], lhsT=sc_[0:17, nt * 128:(nt + 1) * 128],
                                             rhs=BnA[0:17, 4 * g:4 * g + 4, :].rearrange("p h q -> p (h q)"),
                                             start=False, stop=True), [t_BnA, tsc], [tps_])
                        Pt, tPt = Pr.next()
                        A(lambda e: e.activation(out=Pt[:], in_=ps_[:], func=AF.Exp), [tps_], [tPt])
                        pv(po, tpo, Pt, tPt, Vc_aug[:, nt, g, :], t_Vc, nt == 0, nt == ntc - 1)
                    combine(po, tpo, 0, 3, j, g, ac, tac, False, cfr)
                    po, tpo = pO.next()
                    ntile = 8 * (j + 1)
                    for qb in range(ntile):
                        bt, p0 = divmod(2 * qb, 128)
                        h2, pi2 = p0 // 64, (p0 % 64) // 2
                        ps_, tps_ = pS.next()
                        P(lambda e: e.matmul(ps_[:], lhsT=KsT_all[64 * g:64 * g + 64, qb * 128:(qb + 1) * 128], rhs=qg,
                                             start=True, stop=True), [tKs, t_Q], [tps_])
                        pm, tpm = pM.next()
                        P(lambda e: e.matmul(pm, lhsT=wide[64 * h2:64 * h2 + 64, pi2 * 128:(pi2 + 1) * 128],
                                             rhs=selTk[64 * h2:64 * h2 + 64, g, bt, :], start=True, stop=True), [tc_, tselT], [tpm])
                        Pt, tPt = Pr.next()
                        A(lambda e: e.activation(out=Pt[:], in_=ps_[:], func=AF.Exp), [tps_], [tPt])
                        Pm, tPm = masked(Pt, tPt, pm, tpm)
                        pv(po, tpo, Pm, tPm, Vs_all[:, qb, g, :], tVs, qb == 0, False)
                    ps_, tps_ = pS.next()
                    b_ap = BW[:, 1, 4 * g:4 * g + 4, :].rearrange("p h q -> p (h q)")
                    Pt, tPt = attn_tile(ps_, tps_, KsN[64 * g:64 * g + 64, j, 0, :], qg, [t_KsN, t_Q], b_ap, t_BW, Pr, Sbr)
                    pm, tpm = pM.next()
                    for bt in range(nbt):
                        P(lambda e, bt=bt: e.matmul(pm, lhsT=ex[:, bt, :], rhs=selT[:, g, bt, :], start=(bt == 0), stop=(bt == nbt - 1)),
                          [tex, tselT], [tpm])
                    Pm, tPm = masked(Pt, tPt, pm, tpm)
                    pv(po, tpo, Pm, tPm, VsN[:, j, 0, g, :], t_VsN, False, False)
                    ps_, tps_ = pS.next()
                    b_ap = BW[:, 0, 4 * g:4 * g + 4, :].rearrange("p h q -> p (h q)")
                    Pt, tPt = attn_tile(ps_, tps_, KsN[64 * g:64 * g + 64, j, 1, :], qg, [t_KsN, t_Q], b_ap, t_BW, Pr, Sbr)
                    pv(po, tpo, Pt, tPt, VsN[:, j, 1, g, :], t_VsN, False, True)
                    combine(po, tpo, 1, 3, j, g, ac, tac, False, cfr)
                V(lambda e: e.tensor_copy(accb[:], ac[:]), [tac], [taccb])
                for t in range(4):
                    P(lambda e, t=t: e.transpose(out=pT[:, t, :], in_=accb[:, t * 128:(t + 1) * 128], identity=ident[:]),
                      [taccb, t_ident], [tpT])
                V(lambda e: e.tensor_copy(oT[:, :, j * 128:(j + 1) * 128], pT[:]), [tpT], [t_oT])

    def phase3a():
        with contextlib.ExitStack() as s3:
            def a_(name, shape, dt):
                return s3.enter_context(nc.sbuf_tensor(name, list(shape), dt))
            X = norm_ctx(s3, 2, g_mix)
            tW = Tok()
            Wb = a_("Wbr", [128, 8, 2048], BF16)
            Wus = a_("Wus", [128, 4, 1024], BF16); Wun = a_("Wun", [128, 4, 1024], BF16)
            Wo = a_("Wo", [128, 8, 1024], BF16)
            for k in range(8):
                kb.dma("pool", Wb[:, k, :], AP(w_in, k * 128 * INC + 1816, [[INC, 128], [1, 2048]]), writes=[tW])
                kb.dma("pool", Wo[:, k, :], AP(w_out_d, k * 128 * 1024, [[1024, 128], [1, 1024]]), writes=[tW])
            for k in range(4):
                kb.dma("pool", Wus[:, k, :], AP(w_up_ssm, k * 128 * 1024, [[1024, 128], [1, 1024]]), writes=[tW])
                kb.dma("pool", Wun[:, k, :], AP(w_up_nsa, k * 128 * 1024, [[1024, 128], [1, 1024]]), writes=[tW])
            zgr = Rot([a_("p3_zg%d" % i, [128, 4, 256], BF16) for i in range(2)])
            mix = a_("p3_mix", [128, 8, 256], BF16); tmix = Tok()
            sga = Rot([a_("p3_sa%d" % i, [128, 2, 256], F32) for i in range(2)])
            tt = Rot([a_("p3_t%d" % i, [128, 2, 256], F32) for i in range(2)])
            x1r = Rot([a_("p3_x1%d" % i, [128, 1024], F32) for i in range(2)])
            pA = Rot([s3.enter_context(nc.psum_tensor("p3_pA%d" % i, [128, 2, 256], F32)) for i in range(2)])
            pY = Rot([s3.enter_context(nc.psum_tensor("p3_pY%d" % i, [128, 2, 256], F32)) for i in range(2)])
            pX = Rot([s3.enter_context(nc.psum_tensor("p3_pX%d" % i, [128, 512], F32)) for i in range(2)])
            for oc in range(8):
                hT, thT, xb, txb = load_norm_T(x_own, oc * 256, 2, X, ret_x=True)
                zc, tzc = zgr.next()
                kb.dma("sp", zc[:], AP(zg_d, oc * 256, [[4 * NTOK, 128], [NTOK, 4], [1, 256]]), reads=[t_zg], writes=[tzc])
                for m in range(8):
                    pa, tpa = pA.next(); py, tpy = pY.next()
                    for k in range(8):
                        P(lambda e, k=k: e.matmul(pa[:, 0, :], lhsT=Wb[:, k, m * 128:(m + 1) * 128], rhs=hT[:, k, :],
                                                  start=(k == 0), stop=(k == 7)), [tW, thT], [tpa])
                    for k in range(8):
                        P(lambda e, k=k: e.matmul(pa[:, 1, :], lhsT=Wb[:, k, 1024 + m * 128:1024 + (m + 1) * 128], rhs=hT[:, k, :],
                                                  start=(k == 0), stop=(k == 7)), [tW, thT], [tpa])
                    for k in range(4):
                        P(lambda e, k=k: e.matmul(py[:, 0, :], lhsT=Wus[:, k, m * 128:(m + 1) * 128], rhs=zc[:, k, :],
                                                  start=(k == 0), stop=(k == 3)), [tW, tzc], [tpy])
                    for k in range(4):
                        P(lambda e, k=k: e.matmul(py[:, 1, :], lhsT=Wun[:, k, m * 128:(m + 1) * 128], rhs=oT[:, k, oc * 256:(oc + 1) * 256],
                                                  start=(k == 0), stop=(k == 3)), [tW, t_oT], [tpy])
                    sa, tsa = sga.next(); t_, tt_ = tt.next()
                    A(lambda e: e.activation(out=sa[:], in_=pa[:], func=AF.Sigmoid), [tpa], [tsa])
                    V(lambda e: e.tensor_tensor(out=t_[:], in0=sa[:], in1=py[:], op=ALU.mult), [tsa, tpy], [tt_])
                    V(lambda e: e.tensor_tensor(out=mix[:, m, :], in0=t_[:, 0, :], in1=t_[:, 1, :], op=ALU.add), [tt_], [tmix])
                for a in range(2):
                    x1, tx1 = x1r.next()
                    for hf in range(2):
                        px, tpx = pX.next()
                        for m in range(8):
                            P(lambda e, m=m: e.matmul(px[:], lhsT=mix[:, m, a * 128:(a + 1) * 128], rhs=Wo[:, m, hf * 512:(hf + 1) * 512],
                                                      start=(m == 0), stop=(m == 7)), [tmix, tW], [tpx])
                        V(lambda e: e.tensor_tensor(out=x1[:, hf * 512:(hf + 1) * 512], in0=px[:], in1=xb[:, a, hf * 512:(hf + 1) * 512],
                                                    op=ALU.add), [tpx, txb], [tx1])
                    tk_ = Tok(); x1_toks.append(tk_)
                    kb.dma("sp", x1_d.ap()[oc * 256 + a * 128: oc * 256 + (a + 1) * 128, :], x1[:], reads=[tx1], writes=[tk_])

    def phase_ffn(src_d):
        with contextlib.ExitStack() as fs:
            def fsb(name, shape, dt):
                return fs.enter_context(nc.sbuf_tensor(name, list(shape), dt))

            def fps(name, shape, dt):
                return fs.enter_context(nc.psum_tensor(name, list(shape), dt))
            gF, tgF = load_gain("g_ffn_t", g_ffn) if False else (None, None)
            gF = fsb("gF", [128, D], F32); tgF = Tok()
            kb.dma("sp", gF[:], AP(g_ffn, 0, [[0, 128], [1, D]]), writes=[tgF])
            gL = fsb("gL", [128, D], F32); tgL = Tok()
            kb.dma("sp", gL[:], AP(g_fin, 0, [[0, 128], [1, D]]), writes=[tgL])
            Wg = fsb("Wg", [128, 8, DFF], BF16); tWg = Tok()
            Wu = fsb("Wu", [128, 8, DFF], BF16); tWu = Tok()
            Wd = fsb("Wd", [128, NFT, D], BF16); tWd = Tok()
            for k in range(8):
                kb.dma("pool", Wg[:, k, :], w_gate.ap()[k * 128:(k + 1) * 128, :], writes=[tWg])
                kb.dma("pool", Wu[:, k, :], w_up.ap()[k * 128:(k + 1) * 128, :], writes=[tWu])
            for m in range(NFT):
                kb.dma("pool", Wd[:, m, :], w_down.ap()[m * 128:(m + 1) * 128, :], writes=[tWd])
            NT = 256
            xt = [fsb("f_x%d" % i, [128, 2, D], F32) for i in range(2)]
            xr = Rot(xt)
            h2 = fsb("f_h2", [128, D], BF16); th2 = Tok()
            h2T = fsb("f_h2T", [128, 8, NT], BF16); th2T = Tok()
            aT = fsb("f_aT", [128, NFT, NT], BF16); taT = Tok()
            sg = Rot([fsb("f_sg%d" % i, [128, NT], F32) for i in range(2)])
            x2 = fsb("f_x2", [128, 2, D], F32); tx2 = Tok()
            oo = Rot([fsb("f_o%d" % i, [128, D], F32) for i in range(2)])
            junk = fsb("f_junk", [128, D], BF16); tjunk = Tok()
            ssr = Rot([fsb("f_ss%d" % i, [128, 4], F32) for i in range(4)])
            pT = Rot([fps("f_pT%d" % i, [128, 8, 128], BF16) for i in range(2)])
            pGU = Rot([fps("f_pGU%d" % i, [128, 2, NT], F32) for i in range(2)])
            pD = Rot([fps("f_pD%d" % i, [128, 512], F32) for i in range(2)])
            for tt in range(NTOK // NT):
                xa, tx = xr.next()
                kb.dma("sp", xa[:], AP(src_d, tt * NT * D, [[D, 128], [128 * D, 2], [1, D]]), reads=x1_toks, writes=[tx])
                for a in range(2):
                    ss, tss = ssr.next()
                    rmsnorm(xa[:, a, :], tx, gF, tgF, h2[:], th2, (junk, tjunk, ss, tss))
                    pt, tpt = pT.next()
                    for k in range(8):
                        kb.op("pe", lambda e, k=k: e.transpose(out=pt[:, k, :], in_=h2[:, k * 128:(k + 1) * 128],
                                                               identity=ident[:]),
                              reads=[th2, t_ident], writes=[tpt])
                    kb.op("act", lambda e: e.copy(out=h2T[:, :, a * 128:(a + 1) * 128], in_=pt[:]),
                          reads=[tpt], writes=[th2T])
                for m in range(NFT):
                    pg, tpg = pGU.next()
                    for k in range(8):
                        kb.op("pe", lambda e, k=k: e.matmul(pg[:, 0, :], lhsT=Wg[:, k, m * 128:(m + 1) * 128],
                                                            rhs=h2T[:, k, :], start=(k == 0), stop=(k == 7)),
                              reads=[tWg, th2T], writes=[tpg])
                    for k in range(8):
                        kb.op("pe", lambda e, k=k: e.matmul(pg[:, 1, :], lhsT=Wu[:, k, m * 128:(m + 1) * 128],
                                                            rhs=h2T[:, k, :], start=(k == 0), stop=(k == 7)),
                              reads=[tWu, th2T], writes=[tpg])
                    s_, ts_ = sg.next()
                    kb.op("act", lambda e: e.activation(out=s_[:], in_=pg[:, 0, :], func=AF.Silu),
                          reads=[tpg], writes=[ts_])
                    kb.op("dve", lambda e: e.tensor_tensor(out=aT[:, m, :], in0=s_[:], in1=pg[:, 1, :], op=ALU.mult),
                          reads=[ts_, tpg], writes=[taT])
                for a in range(2):
                    for hf in range(2):
                        pd, tpd = pD.next()
                        for m in range(NFT):
                            kb.op("pe", lambda e, m=m: e.matmul(pd[:], lhsT=aT[:, m, a * 128:(a + 1) * 128],
                                                                rhs=Wd[:, m, hf * 512:(hf + 1) * 512],
                                                                start=(m == 0), stop=(m == NFT - 1)),
                                  reads=[taT, tWd], writes=[tpd])
                        kb.op("dve", lambda e: e.tensor_tensor(out=x2[:, a, hf * 512:(hf + 1) * 512], in0=pd[:],
                                                               in1=xa[:, a, hf * 512:(hf + 1) * 512], op=ALU.add),
                              reads=[tpd, tx], writes=[tx2])
                    ss, tss = ssr.next()
                    o_, to_ = oo.next()
                    rmsnorm(x2[:, a, :], tx2, gL, tgL, o_[:], to_, (junk, tjunk, ss, tss))
                    kb.dma("sp", out_d.ap()[tt * NT + a * 128: tt * NT + (a + 1) * 128, :], o_[:],
                           reads=[to_], writes=[t_out])


    t_out = Tok()
    try:
        phase_s5()
    except _Stop:
        return nc
    if debug == "s5":
        kb.finish([t_zg])
        es.close()
        return nc
    es_o = contextlib.ExitStack()
    oT = es_o.enter_context(nc.sbuf_tensor("oT", [128, 4, NTOK], BF16))
    es2 = contextlib.ExitStack()

    def p_(name, shape, dt):
        return es2.enter_context(nc.sbuf_tensor(name, list(shape), dt))
    BW = p_("BW", [128, 2, 8, 128], F32); M4 = p_("M4", [128, 128], F32); BnA = p_("BnA", [17, 8, 128], BF16)
    KcT = p_("KcT", [128, 1024], BF16); Vc_aug = p_("Vc_aug", [128, 8, 2, 65], BF16)
    Qall = p_("Qall", [128, NQ, 4, 128], BF16); gates = p_("gates", [128, NQ, 24], F32)
    KsN = p_("KsN", [128, NQ, 2, 128], BF16); VsN = p_("VsN", [128, NQ, 2, 2, 65], BF16)
    kb.barrier()
    try:
        phase_tables()
        kb.barrier()
        ck("tables")
        phase_compress()
        kb.barrier()
        ck("compress")
        phase1()
        kb.barrier()
        if debug == "dump":
            for nm, t_, tk_, dt_ in (("dbg_bw", BW, t_BW, F32), ("dbg_m4", M4, t_BW, F32), ("dbg_gates", gates, t_gates, F32),
                                     ("dbg_q", Qall, t_Q, BF16), ("dbg_ksn", KsN, t_KsN, BF16), ("dbg_vsn", VsN, t_VsN, BF16),
                                     ("dbg_bna", BnA, t_BnA, BF16), ("dbg_kct", KcT, t_Kc, BF16), ("dbg_vc", Vc_aug, t_Vc, BF16)):
                shp = list(t_.shape)
                n_ = int(np.prod(shp[1:]))
                dd_ = nc.dram_tensor(nm, [shp[0], n_], dt_, kind="ExternalOutput")
                kb.dma("sp", dd_.ap(), AP(t_, 0, [[n_, shp[0]], [1, n_]]), reads=[tk_], writes=[t_out])
        ck("phase1")
        phase2()
        kb.barrier()
        if debug == "dump":
            dd_ = nc.dram_tensor("dbg_oT", [128, 4 * NTOK], BF16, kind="ExternalOutput")
            kb.dma("sp", dd_.ap(), AP(oT, 0, [[4 * NTOK, 128], [1, 4 * NTOK]]), reads=[t_oT], writes=[t_out])
        ck("phase2")
    except _Stop:
        return nc
    es2.close()
    try:
        phase3a()
        kb.barrier()
        ck("phase3a")
    except _Stop:
        return nc
    es_o.close()
    phase_ffn(x1_d)
    kb.finish([t_out])
    es.close()
    return nc


_PROG = {}


def _bf(a):
    return np.ascontiguousarray(a).astype(ml_dtypes.bfloat16)


def make_in_maps(inp):
    x = np.asarray(inp["x"], np.float32)[0]
    xq = x.reshape(S // TQ, TQ, D)
    ident = np.eye(128, dtype=np.float32)
    f = lambda k: np.ascontiguousarray(np.asarray(inp[k], np.float32)[0])
    shared = {
        "norm_mix_g": np.asarray(inp["norm_mix_g"], np.float32).reshape(1, D),
        "norm_ffn_g": np.asarray(inp["norm_ffn_g"], np.float32).reshape(1, D),
        "norm_final_g": np.asarray(inp["norm_final_g"], np.float32).reshape(1, D),
        "w_ffn_gate": f("w_ffn_gate"), "w_ffn_up": f("w_ffn_up"), "w_ffn_down": f("w_ffn_down"),
        "w_in": f("w_in"),
        "ssm_a_re": f("ssm_a_re"), "ssm_a_im": f("ssm_a_im"),
        "ssm_log_dt": np.asarray(inp["ssm_log_dt"], np.float32).reshape(1, 32),
        "ssm_b_re": f("ssm_b_re"), "ssm_b_im": f("ssm_b_im"), "ssm_c_re": f("ssm_c_re"), "ssm_c_im": f("ssm_c_im"),
        "ssm_d": np.asarray(inp["ssm_d"], np.float32).reshape(1, 512),
        "ssm_w_glu": f("ssm_w_glu"),
        "ident_bf": _bf(ident), "ident_f": ident,
    }
    def bucket(n):
        n = np.maximum(n, 0)
        nf = np.maximum(n, 1).astype(np.float32)
        large = 16 + (np.log(nf / np.float32(16)) / np.float32(np.log(8.0)) * np.float32(16)).astype(np.int32)
        large = np.minimum(large, 31)
        return np.where(n < 16, n, large)
    dd = np.arange(256)
    ohf = (bucket(dd)[None, :] == np.arange(32)[:, None]).astype(np.float32)
    ohr = (bucket(255 - dd)[None, :] == np.arange(32)[:, None]).astype(np.float32)
    pp = np.arange(128)
    antij = (pp[:, None] + pp[None, :] == 127).astype(np.float32)
    m4 = np.where(pp[None, :] >= pp[:, None], np.float32(-30000.0), np.float32(0.0)).astype(np.float32)
    mm = np.arange(4096)
    wide = ((pp[:, None] % 64) == (2 * (mm[None, :] // 128) + (mm[None, :] % 128) // 64)).astype(np.float32)
    shared.update({
        "rel_bias": np.asarray(inp["rel_bias"], np.float32), "ohrev": ohr, "ohfwd": ohf, "antij": antij, "m4": m4,
        "cmp_w1_k": f("cmp_w1_k"), "cmp_w1_v": f("cmp_w1_v"), "cmp_w2_k": f("cmp_w2_k"), "cmp_w2_v": f("cmp_w2_v"),
        "cmp_pos_k": f("cmp_pos_k"), "cmp_pos_v": f("cmp_pos_v"), "wide64": _bf(wide),
        "w_out": f("w_out"), "w_up_ssm": f("w_up_ssm"), "w_up_nsa": f("w_up_nsa"), "x_all": x,
    })
    blk = np.arange(256)
    in_maps = []
    for c in range(NCORES):
        m = dict(shared)
        xp = np.zeros((NQ, 512, D), np.float32)
        vp = np.zeros((128, 64), np.float32)
        cand = np.zeros((NQ, 128, 256), np.float32); forced = np.zeros((NQ, 128, 256), np.float32)
        expn = np.zeros((NQ, 256, 128), np.float32); selc = np.zeros((NQ, 17, 1024), np.float32)
        keepb = np.zeros((128, 32), np.float32)
        for j in range(NQ):
            qb = 8 * j + c
            lo = 128 * qb - 512
            s0 = max(lo, 0)
            xp[j, s0 - lo:] = x[s0:128 * qb]
            for a in range(4):
                vp[:, j * 4 + a] = ((lo + a * 128 + pp) >= 0)
            cur = 2 * qb + (pp >= 64).astype(np.int64)
            valid = blk[None, :] <= cur[:, None]
            frc = valid & ((blk[None, :] == 0) | (blk[None, :] >= cur[:, None] - 1))
            cand[j] = valid & ~frc
            forced[j] = frc
            for hh in range(2):
                b_ = 2 * qb - 2 + hh
                if b_ >= 0:
                    expn[j, b_, 64 * hh:64 * hh + 64] = 1.0
            for mp in range(16):
                n = 8 * qb - 8 + (15 - mp)
                if 0 <= n < 1024:
                    selc[j, mp, n] = 1.0
            selc[j, 16, min(8 * qb + 8, 1024):] = -30000.0
            for bt in range(2):
                keepb[:, j * 2 + bt] = ((bt * 128 + pp) <= 2 * qb - 3)
        m["x_prev"] = xp.reshape(NQ * 512, D); m["vprev"] = vp
        m["cand"] = _bf(cand.reshape(NQ * 128, 256)); m["forced"] = _bf(forced.reshape(NQ * 128, 256))
        m["expn"] = _bf(expn.reshape(NQ * 256, 128)); m["selc"] = _bf(selc.reshape(NQ * 17, 1024))
        m["keepblk"] = keepb
        m["x_own"] = np.ascontiguousarray(xq[c::NCORES].reshape(NTOK, D))
        oh = np.zeros((128, 8), np.float32); oh[:, c] = 1.0
        m["onehot_r"] = oh
        in_maps.append(m)
    return in_maps


def kernel(**inp):
    if "main" not in _PROG:
        _PROG["main"] = build_program()
    nc = _PROG["main"]
    in_maps = make_in_maps(inp)
    res = run_bass_kernel_spmd(nc, in_maps, core_ids=list(range(NCORES)))
    out = np.empty((S // TQ, TQ, D), np.float32)
    for c in range(NCORES):
        out[c::NCORES] = np.asarray(res.results[c]["out"], np.float32).reshape(NQ, TQ, D)
    return out.reshape(1, S, D)
```

```python
import contextlib
import numpy as np
import ml_dtypes
import concourse.bass as bass
import concourse.mybir as mybir
from concourse.bass_utils import run_bass_kernel_spmd

F32 = mybir.dt.float32
BF16 = mybir.dt.bfloat16
AF = mybir.ActivationFunctionType
ALU = mybir.AluOpType
AX = mybir.AxisListType

BARRIERS = True
NCORES = 8
S = 16384
D = 1024
NQ = 16
TQ = 128
NTOK = NQ * TQ
DFF = 2816
NFT = DFF // 128
EPS = 1e-6
INC = 3864


def AP(t, off, dims):
    return bass.AP(t, off, [list(d) for d in dims])


class Tok:
    __slots__ = ("w", "r")

    def __init__(self):
        self.w = None
        self.r = {}


class KB:
    def __init__(self, nc):
        self.nc = nc
        self.E = {"pe": nc.tensor, "act": nc.scalar, "dve": nc.vector, "pool": nc.gpsimd, "sp": nc.sync}
        self.csem = {e: nc.alloc_semaphore("c_" + e) for e in ("pe", "act", "dve", "pool")}
        self.ccnt = {e: 0 for e in self.csem}
        self.NDS = 28
        self.dsem = {e: [nc.alloc_semaphore("d_%s%d" % (e, i)) for i in range(self.NDS)]
                     for e in ("sp", "pool", "act")}
        self.dcnt = {e: 0 for e in self.dsem}
        self.seen = {e: {} for e in self.E}
        self.nwait = 0
        self.xsem = {}

    def collective(self, name, kind, in_ap, out_ap, reads, writes):
        self._deps("pool", reads, writes)
        sem = self.nc.alloc_semaphore("x_" + name)
        self.xsem[name] = sem
        ins = self.nc.gpsimd.collective_compute(kind, ALU.bypass, replica_groups=[list(range(NCORES))],
                                                ins=[in_ap], outs=[out_ap])
        ins.then_inc(sem)
        self._mark((("x", name), 1), reads, writes)

    def _wait(self, e, key, val):
        if key[0] == "c" and key[1] == e and e == "pe":
            return
        if self.seen[e].get(key, 0) >= val:
            return
        if key[0] == "x":
            sem = self.xsem[key[1]]
        else:
            sem = self.csem[key[1]] if key[0] == "c" else self.dsem[key[1]][key[2]]
        self.E[e].wait_ge(sem, val)
        self.seen[e][key] = val
        self.nwait += 1

    def _deps(self, e, reads, writes):
        for t in reads:
            if t.w is not None:
                self._wait(e, *t.w)
        for t in writes:
            if t.w is not None:
                self._wait(e, *t.w)
            for k, v in t.r.items():
                self._wait(e, k, v)

    def _mark(self, me, reads, writes):
        for t in reads:
            t.r[me[0]] = me[1]
        for t in writes:
            t.w = me
            t.r = {}

    def op(self, e, fn, reads=(), writes=()):
        self._deps(e, reads, writes)
        ins = fn(self.E[e])
        self.ccnt[e] += 1
        ins.then_inc(self.csem[e], 1)
        self._mark((("c", e), self.ccnt[e]), reads, writes)

    def dma(self, e, out, in_, reads=(), writes=(), **kw):
        self._deps(e, reads, writes)
        i = self.dcnt[e]
        self.dcnt[e] += 1
        slot = i % self.NDS
        ins = self.E[e].dma_start(out=out, in_=in_, **kw)
        ins.then_inc(self.dsem[e][slot], 16)
        self._mark((("d", e, slot), 16 * (i // self.NDS + 1)), reads, writes)

    def barrier(self):
        for e in self.E:
            for o in self.csem:
                if self.ccnt[o] > 0:
                    self._wait(e, ("c", o), self.ccnt[o])
            for q in self.dsem:
                n = self.dcnt[q]
                for slot in range(self.NDS):
                    cnt = (n - slot + self.NDS - 1) // self.NDS
                    if cnt > 0:
                        self._wait(e, ("d", q, slot), 16 * cnt)

    def finish(self, toks):
        for t in toks:
            if t.w is not None:
                self._wait("sp", *t.w)


class _Stop(Exception):
    pass


class Rot:
    def __init__(self, aps):
        self.aps = aps
        self.toks = [Tok() for _ in aps]
        self.i = 0

    def next(self):
        k = self.i % len(self.aps)
        self.i += 1
        return self.aps[k], self.toks[k]


def build_program(debug=None, debug_stop=None):
    nc = bass.Bass("TRN2", target_bir_lowering=False)
    kb = KB(nc)
    es = contextlib.ExitStack()

    def dram_in(name, shape, dt=F32):
        return nc.dram_tensor(name, list(shape), dt, kind="ExternalInput")

    x_own = dram_in("x_own", [NTOK, D])
    g_mix = dram_in("norm_mix_g", [1, D])
    g_ffn = dram_in("norm_ffn_g", [1, D])
    g_fin = dram_in("norm_final_g", [1, D])
    w_gate = dram_in("w_ffn_gate", [D, DFF])
    w_up = dram_in("w_ffn_up", [D, DFF])
    w_down = dram_in("w_ffn_down", [DFF, D])
    identb_d = dram_in("ident_bf", [128, 128], BF16)
    out_d = nc.dram_tensor("out", [NTOK, D], F32, kind="ExternalOutput")
    x1_d = nc.dram_tensor("x1_scr", [NTOK, D], F32, kind="ExternalOutput" if debug == "dump" else "Internal")

    w_in = dram_in("w_in", [D, INC])
    ssm_a_re = dram_in("ssm_a_re", [32, 64]); ssm_a_im = dram_in("ssm_a_im", [32, 64])
    ssm_log_dt = dram_in("ssm_log_dt", [1, 32])
    ssm_b_re = dram_in("ssm_b_re", [32, 64, 16]); ssm_b_im = dram_in("ssm_b_im", [32, 64, 16])
    ssm_c_re = dram_in("ssm_c_re", [32, 16, 64]); ssm_c_im = dram_in("ssm_c_im", [32, 16, 64])
    ssm_d = dram_in("ssm_d", [1, 512])
    w_glu = dram_in("ssm_w_glu", [512, 512])
    identf_d = dram_in("ident_f", [128, 128])
    onehot_d = dram_in("onehot_r", [128, 8])
    x_all = dram_in("x_all", [S, D])
    ksT_d = nc.dram_tensor("ksT_scr", [128, S], BF16, kind="Internal")
    kcR_d = nc.dram_tensor("kcR_scr", [128, S + 16], BF16, kind="Internal")
    vcR_d = nc.dram_tensor("vcR_scr", [128, S + 16], BF16, kind="Internal")
    vs_d = nc.dram_tensor("vs_scr", [S, 128], BF16, kind="Internal")
    zs_d = nc.dram_tensor("zs_scr", [128, 8192], F32, kind="Internal")
    ut_d = nc.dram_tensor("ut_scr", [128, 8192], BF16, kind="Internal")
    t_zsd = Tok(); t_utd = Tok()
    x_prev = dram_in("x_prev", [NQ * 512, D])
    vprev_d = dram_in("vprev", [128, 64])
    rel_bias_d = dram_in("rel_bias", [32, 8])
    ohrev_d = dram_in("ohrev", [32, 256]); ohfwd_d = dram_in("ohfwd", [32, 256])
    antij_d = dram_in("antij", [128, 128]); m4_d = dram_in("m4", [128, 128])
    tabR_d = nc.dram_tensor("tabR_scr", [8, 384], F32, kind="Internal")
    tabF_d = nc.dram_tensor("tabF_scr", [8, 416], F32, kind="Internal")
    cmp_w1_k = dram_in("cmp_w1_k", [2048, 256]); cmp_w1_v = dram_in("cmp_w1_v", [2048, 256])
    cmp_w2_k = dram_in("cmp_w2_k", [256, 64]); cmp_w2_v = dram_in("cmp_w2_v", [256, 64])
    cmp_pos_k = dram_in("cmp_pos_k", [32, 64]); cmp_pos_v = dram_in("cmp_pos_v", [32, 64])
    wide_d = dram_in("wide64", [128, 4096], BF16)
    keepblk_d = dram_in("keepblk", [128, 32])
    cand_d = dram_in("cand", [NQ * 128, 256], BF16); forced_d = dram_in("forced", [NQ * 128, 256], BF16)
    expn_d = dram_in("expn", [NQ * 256, 128], BF16)
    selc_d = dram_in("selc", [NQ * 17, 1024], BF16)
    acc_d = nc.dram_tensor("acc_scr", [NTOK, 512], F32, kind="ExternalOutput" if debug == "dump" else "Internal")
    w_out_d = dram_in("w_out", [D, D]); w_up_ssm = dram_in("w_up_ssm", [512, D]); w_up_nsa = dram_in("w_up_nsa", [512, D])
    t_accd = Tok(); x1_toks = []
    t_BW = Tok(); t_BnA = Tok(); t_Kc = Tok(); t_Vc = Tok(); t_Q = Tok(); t_gates = Tok(); t_KsN = Tok(); t_VsN = Tok(); t_oT = Tok()
    zg_d = nc.dram_tensor("zg_scr", [128, 4 * NTOK], BF16, kind="ExternalOutput" if debug in ("s5", "dump") else "Internal")
    t_zg = Tok()

    def sb(name, shape, dt):
        return es.enter_context(nc.sbuf_tensor(name, list(shape), dt))

    def ps(name, shape, dt):
        return es.enter_context(nc.psum_tensor(name, list(shape), dt))

    ident = sb("ident", [128, 128], BF16)
    t_ident = Tok()
    kb.dma("sp", ident[:], identb_d.ap(), writes=[t_ident])

    identF = sb("identF", [128, 128], F32)
    t_identF = Tok()
    kb.dma("sp", identF[:], identf_d.ap(), writes=[t_identF])
    epsT = sb("epsT", [128, 1], F32)
    t_eps = Tok()
    kb.op("dve", lambda e: e.memset(epsT[:], EPS), writes=[t_eps])

    def load_gain(name, src):
        t = sb(name, [128, D], F32)
        tk = Tok()
        kb.dma("sp", t[:], AP(src, 0, [[0, 128], [1, D]]), writes=[tk])
        return t, tk

    def rmsnorm(xap, tx, gt, tg, hout, th, scr):
        junk, tjunk, ss, tss = scr
        kb.op("act", lambda e: e.activation(out=junk[:], in_=xap, func=AF.Square, accum_out=ss[:, 0:1]),
              reads=[tx], writes=[tjunk, tss])
        kb.op("act", lambda e: e.activation(out=ss[:, 1:2], in_=ss[:, 0:1], func=AF.Sqrt, scale=1.0 / D,
                                            bias=epsT[:, 0:1]), reads=[tss, t_eps], writes=[tss])
        kb.op("dve", lambda e: e.reciprocal(out=ss[:, 2:3], in_=ss[:, 1:2]), reads=[tss], writes=[tss])
        kb.op("dve", lambda e: e.scalar_tensor_tensor(out=hout, in0=xap, scalar=ss[:, 2:3], in1=gt[:],
                                                      op0=ALU.mult, op1=ALU.mult),
              reads=[tx, tss, tg], writes=[th])

    def V(fn, r=(), w=()):
        kb.op("dve", fn, r, w)

    def A(fn, r=(), w=()):
        kb.op("act", fn, r, w)

    def P(fn, r=(), w=()):
        kb.op("pe", fn, r, w)

    def G(fn, r=(), w=()):
        kb.op("pool", fn, r, w)

    def load_norm_T(src, row0, ntile, X, ret_x=False):
        xb, txb = X["x"].next()
        kb.dma("sp", xb[:, 0:ntile, :], AP(src, row0 * D, [[D, 128], [128 * D, ntile], [1, D]]), writes=[txb])
        hT, thT = X["hT"].next()
        for a in range(ntile):
            ss, tss = X["ss"].next()
            hb, thb = X["h"].next()
            rmsnorm(xb[:, a, :], txb, X["g"], X["tg"], hb[:], thb, (X["junk"], X["tjunk"], ss, tss))
            pt, tpt = X["pT"].next()
            for k in range(8):
                P(lambda e, k=k: e.transpose(out=pt[:, k, :], in_=hb[:, k * 128:(k + 1) * 128], identity=ident[:]),
                  [thb, t_ident], [tpt])
            V(lambda e: e.tensor_copy(hT[:, :, a * 128:(a + 1) * 128], pt[:]), [tpt], [thT])
        if ret_x:
            return hT, thT, xb, txb
        return hT, thT

    nctx = [0]

    def norm_ctx(st, xtiles, gsrc):
        nctx[0] += 1
        pre = "c%d" % nctx[0]

        def a_(name, shape, dt):
            return st.enter_context(nc.sbuf_tensor(pre + name, list(shape), dt))
        X = {}
        X["x"] = Rot([a_("n_x%d" % i, [128, xtiles, D], F32) for i in range(2)])
        X["hT"] = Rot([a_("n_hT%d" % i, [128, 8, 128 * xtiles], BF16) for i in range(2)])
        X["h"] = Rot([a_("n_h%d" % i, [128, D], BF16) for i in range(2)])
        X["ss"] = Rot([a_("n_ss%d" % i, [128, 4], F32) for i in range(4)])
        X["junk"] = a_("n_junk", [128, D], BF16)
        X["tjunk"] = Tok()
        X["g"] = a_("n_g", [128, D], F32)
        X["tg"] = Tok()
        kb.dma("sp", X["g"][:], AP(gsrc, 0, [[0, 128], [1, D]]), writes=[X["tg"]])
        X["pT"] = Rot([st.enter_context(nc.psum_tensor(pre + "n_pT%d" % i, [128, 8, 128], BF16)) for i in range(2)])
        return X

    def ck(name):
        if debug_stop == name:
            raise _Stop()

    kv_toks = []

    def phase_all(UT, tUT, Zs, tZ, Eg, tEg, z_matmuls, recur, zv):
        UTW = 8192
        ZW = 8192
        with contextlib.ExitStack() as sa:
            X = norm_ctx(sa, 2, g_mix)
            WA = sa.enter_context(nc.sbuf_tensor("WA", [128, 8, 1024], BF16)); tWA = Tok()
            for k in range(8):
                for (c0, s0, n) in ((0, 0, 512), (512, 1280, 128), (640, 1024, 128), (768, 1152, 128), (896, 1408, 128)):
                    kb.dma("pool", WA[:, k, c0:c0 + n], AP(w_in, k * 128 * INC + s0, [[INC, 128], [1, n]]), writes=[tWA])
            pP = Rot([sa.enter_context(nc.psum_tensor("a_pP%d" % i, [128, 512], F32)) for i in range(2)])
            fmS = Rot([sa.enter_context(nc.sbuf_tensor("a_fm%d" % i, [128, 256], BF16)) for i in range(3)])
            vsS = Rot([sa.enter_context(nc.sbuf_tensor("a_vs%d" % i, [128, 128], BF16)) for i in range(2)])
            zpad = sa.enter_context(nc.sbuf_tensor("a_zpad", [128, 16], BF16)); tzp = Tok()
            V(lambda e: e.memset(zpad[:], 0.0), [], [tzp])
            for dst in (kcR_d, vcR_d):
                tk_ = Tok(); kv_toks.append(tk_)
                kb.dma("pool", AP(dst, S, [[S + 16, 128], [1, 16]]), zpad[:], reads=[tzp], writes=[tk_])
            n_ev = 0
            for cc in range(S // 256):
                sc, q = cc // 8, cc % 8
                hT, thT = load_norm_T(x_all, cc * 256, 2, X)
                for T in range(4):
                    pp, tpp = pP.next()
                    for k in range(8):
                        P(lambda e, k=k: e.matmul(pp[:, 0:256], lhsT=WA[:, k, T * 128:(T + 1) * 128], rhs=hT[:, k, :],
                                                  start=(k == 0), stop=(k == 7)), [tWA, thT], [tpp])
                    o_ = AP(UT, T * 2048 + q * 32, [[UTW, 128], [256, 8], [1, 32]])
                    i_ = AP(pp, 0, [[512, 128], [1, 8], [8, 32]])
                    n_ev += 1
                    if n_ev % 2 == 0:
                        V(lambda e: e.tensor_copy(o_, i_), [tpp], [tUT])
                    else:
                        A(lambda e: e.copy(out=o_, in_=i_), [tpp], [tUT])
                for (c0, dst) in ((512, ksT_d), (640, kcR_d), (768, vcR_d)):
                    pp, tpp = pP.next()
                    for k in range(8):
                        P(lambda e, k=k: e.matmul(pp[:, 0:256], lhsT=WA[:, k, c0:c0 + 128], rhs=hT[:, k, :],
                                                  start=(k == 0), stop=(k == 7)), [tWA, thT], [tpp])
                    f_, tf_ = fmS.next()
                    n_ev += 1
                    if n_ev % 2 == 0:
                        V(lambda e: e.tensor_copy(f_[:], pp[:, 0:256]), [tpp], [tf_])
                    else:
                        A(lambda e: e.copy(out=f_[:], in_=pp[:, 0:256]), [tpp], [tf_])
                    tk_ = Tok(); kv_toks.append(tk_)
                    W_ = dst.shape[1]
                    kb.dma("pool", AP(dst, cc * 256, [[W_, 128], [1, 256]]), f_[:], reads=[tf_], writes=[tk_])
                for a in range(2):
                    pp, tpp = pP.next()
                    for k in range(8):
                        P(lambda e, k=k: e.matmul(pp[:, 0:128], lhsT=hT[:, k, a * 128:(a + 1) * 128], rhs=WA[:, k, 896:1024],
                                                  start=(k == 0), stop=(k == 7)), [tWA, thT], [tpp])
                    v_, tv_ = vsS.next()
                    V(lambda e: e.tensor_copy(v_[:], pp[:, 0:128]), [tpp], [tv_])
                    tk_ = Tok(); kv_toks.append(tk_)
                    kb.dma("pool", AP(vs_d, (cc * 256 + a * 128) * 128, [[128, 128], [1, 128]]), v_[:],
                           reads=[tv_], writes=[tk_])
                if q == 7:
                    z_matmuls()
                    recur()
                    for ri in range(2):
                        d_ = AP(Eg, ri * 256 + 2 * sc, [[4096, 128], [16, 16], [1, 2], [512, 8]])
                        s_ = AP(Zs, ri * 4096 + 15, [[ZW, 128], [256, 16], [128, 2], [16, 8]])
                        V(lambda e, d_=d_, s_=s_: e.tensor_copy(d_, s_), [tZ], [tEg])
            kb.barrier()

    def phase_s5():
        PI = float(np.pi)
        with contextlib.ExitStack() as s5:
            def a_(name, shape, dt):
                return s5.enter_context(nc.sbuf_tensor(name, list(shape), dt))
            UT = a_("UT", [128, 4, 8, 256], BF16); tUT = Tok()
            UTW = 4 * 8 * 256
            Wgl = a_("s5_Wgl", [128, 4, 512], BF16); tWgl = Tok()
            for k4 in range(4):
                kb.dma("pool", Wgl[:, k4, :], AP(w_glu, k4 * 128 * 512, [[512, 128], [1, 512]]), writes=[tWgl])
            ck("u")
            NS = 40
            spt = a_("s5_sp", [128, NS, 16], F32); tS = Tok()
            SPW = NS * 16

            def sl(i):
                return spt[:, i, :]

            def slb(i, n=16):
                return AP(spt, i * 16, [[SPW, 128], [1, 16], [0, n]])
            (aR, aI, DT, XR, ANG, MAG, T1, T2, SINV, COSV, CFR, CFI, DEN, M1, T3, T4) = range(16)
            PWR, PWI = 16, 25
            kb.dma("sp", sl(aR), AP(ssm_a_re, 0, [[1, 128], [128, 16]]), writes=[tS], allow_slow_non_contiguous=True)
            kb.dma("sp", sl(aI), AP(ssm_a_im, 0, [[1, 128], [128, 16]]), writes=[tS], allow_slow_non_contiguous=True)
            for g2 in range(2):
                kb.dma("sp", spt[64 * g2:64 * g2 + 64, DT, :], AP(ssm_log_dt, g2, [[0, 64], [2, 16]]), writes=[tS],
                       allow_slow_non_contiguous=True)
            Br = a_("s5_Br", [128, 16, 16], F32); Bi = a_("s5_Bi", [128, 16, 16], F32)
            Cr = a_("s5_Cr", [128, 16, 16], F32); Ci = a_("s5_Ci", [128, 16, 16], F32)
            BBr = a_("s5_BBr", [128, 16, 16], F32); BBi = a_("s5_BBi", [128, 16, 16], F32)
            TA = a_("s5_TA", [128, 16, 16], F32); TB = a_("s5_TB", [128, 16, 16], F32)
            TRe = a_("s5_TRe", [128, 16, 16], F32); TIm = a_("s5_TIm", [128, 16, 16], F32)
            dcol = a_("s5_dcol", [128, 4], F32)
            kb.dma("sp", Br[:], AP(ssm_b_re, 0, [[16, 128], [2048, 16], [1, 16]]), writes=[tS])
            kb.dma("sp", Bi[:], AP(ssm_b_im, 0, [[16, 128], [2048, 16], [1, 16]]), writes=[tS])
            tCs = []
            for g2 in range(2):
                for pair in range(16):
                    for (dst_, src_) in ((Cr, ssm_c_re), (Ci, ssm_c_im)):
                        tk_ = Tok(); tCs.append(tk_)
                        kb.dma("sp", dst_[64 * g2:64 * g2 + 64, pair, :],
                               AP(src_, (2 * pair + g2) * 1024, [[1, 64], [64, 16]]),
                               writes=[tk_], allow_slow_non_contiguous=True)
            jn = a_("s5_join", [128, 2], F32)
            V(lambda e: e.memset(jn[:], 0.0), tCs, [tS])
            kb.dma("sp", dcol[:], AP(ssm_d, 0, [[1, 128], [128, 4]]), writes=[tS], allow_slow_non_contiguous=True)

            def vv(out, a, b, op):
                V(lambda e: e.tensor_tensor(out=out, in0=a, in1=b, op=op), [tS], [tS])

            def vs(out, a, s1, op0, s2=None, op1=None):
                if op1 is None:
                    V(lambda e: e.tensor_scalar(out=out, in0=a, scalar1=s1, scalar2=None, op0=op0), [tS], [tS])
                else:
                    V(lambda e: e.tensor_scalar(out=out, in0=a, scalar1=s1, scalar2=s2, op0=op0, op1=op1), [tS], [tS])

            def cmul(orr, oi, ar, ai, br, bi, t1, t2, sign=1.0):
                vv(t1, ar, br, ALU.mult); vv(t2, ai, bi, ALU.mult); vv(orr, t1, t2, ALU.subtract)
                vv(t1, ar, bi, ALU.mult); vv(t2, ai, br, ALU.mult); vv(oi, t1, t2, ALU.add)

            A(lambda e: e.activation(out=sl(DT), in_=sl(DT), func=AF.Exp), [tS], [tS])
            vv(sl(XR), sl(aR), sl(DT), ALU.mult)
            vv(sl(ANG), sl(aI), sl(DT), ALU.mult)
            A(lambda e: e.activation(out=sl(MAG), in_=sl(XR), func=AF.Exp), [tS], [tS])

            def sin_of(dst, shift):
                vs(sl(T1), sl(ANG), shift, ALU.add)
                vs(sl(T3), sl(T1), 1.0, ALU.mult)
                for m in (1, 3, 5, 7, 9):
                    vs(sl(T2), sl(T1), m * PI, ALU.is_ge, -2.0 * PI, ALU.mult)
                    vv(sl(T3), sl(T3), sl(T2), ALU.add)
                A(lambda e: e.activation(out=sl(dst), in_=sl(T3), func=AF.Sin), [tS], [tS])
            sin_of(SINV, 0.0)
            sin_of(COSV, PI / 2)
            V(lambda e: e.memset(sl(PWR + 0), 1.0), [tS], [tS])
            V(lambda e: e.memset(sl(PWI + 0), 0.0), [tS], [tS])
            vv(sl(PWR + 1), sl(MAG), sl(COSV), ALU.mult)
            vv(sl(PWI + 1), sl(MAG), sl(SINV), ALU.mult)
            for k in range(2, 9):
                cmul(sl(PWR + k), sl(PWI + k), sl(PWR + k - 1), sl(PWI + k - 1), sl(PWR + 1), sl(PWI + 1), sl(T1), sl(T2))
            SQ = a_("s5_sq", [128, 10, 16], F32)
            V(lambda e: e.tensor_copy(SQ[:, 0, :], sl(PWR + 8)), [tS], [tS])
            V(lambda e: e.tensor_copy(SQ[:, 5, :], sl(PWI + 8)), [tS], [tS])
            for q in range(1, 5):
                cmul(SQ[:, q, :], SQ[:, 5 + q, :], SQ[:, q - 1, :], SQ[:, 4 + q, :], SQ[:, q - 1, :], SQ[:, 4 + q, :],
                     sl(T1), sl(T2))
            Q128 = a_("s5_q128", [128, 18, 16], F32)
            V(lambda e: e.memset(Q128[:, 0, :], 1.0), [tS], [tS])
            V(lambda e: e.memset(Q128[:, 9, :], 0.0), [tS], [tS])
            for r in range(1, 9):
                cmul(Q128[:, r, :], Q128[:, 9 + r, :], Q128[:, r - 1, :], Q128[:, 8 + r, :], SQ[:, 4, :], SQ[:, 9, :],
                     sl(T1), sl(T2))
            vv(sl(T1), sl(aR), sl(aR), ALU.mult); vv(sl(T2), sl(aI), sl(aI), ALU.mult)
            vv(sl(DEN), sl(T1), sl(T2), ALU.add)
            V(lambda e: e.reciprocal(out=sl(DEN), in_=sl(DEN)), [tS], [tS])
            vs(sl(M1), sl(PWR + 1), -1.0, ALU.add)
            vv(sl(T1), sl(M1), sl(aR), ALU.mult); vv(sl(T2), sl(PWI + 1), sl(aI), ALU.mult)
            vv(sl(T3), sl(T1), sl(T2), ALU.add); vv(sl(CFR), sl(T3), sl(DEN), ALU.mult)
            vv(sl(T1), sl(PWI + 1), sl(aR), ALU.mult); vv(sl(T2), sl(M1), sl(aI), ALU.mult)
            vv(sl(T3), sl(T1), sl(T2), ALU.subtract); vv(sl(CFI), sl(T3), sl(DEN), ALU.mult)
            cmul(BBr[:], BBi[:], slb(CFR), slb(CFI), Br[:], Bi[:], TA[:], TB[:])

            def scatter(dst_t, pair_stride, base_off, src_t, neg=False):
                for g2 in range(2):
                    d_ = AP(dst_t, (64 * g2) * dst_t_pstride[id(dst_t)] + base_off + 16 * g2,
                            [[dst_t_pstride[id(dst_t)], 64], [2 * pair_stride, 8], [pair_stride + 32, 2], [1, 16]])
                    s_ = AP(src_t, (64 * g2) * 256, [[256, 64], [32, 8], [16, 2], [1, 16]])
                    if neg:
                        V(lambda e: e.tensor_scalar(out=d_, in0=s_, scalar1=-1.0, scalar2=None, op0=ALU.mult), [tS], [tS])
                    else:
                        V(lambda e: e.tensor_copy(d_, s_), [tS], [tS])
            dst_t_pstride = {}
            SPt = a_("s5_SPt", [128, 16, 2, 64], BF16); dst_t_pstride[id(SPt)] = 16 * 2 * 64
            V(lambda e: e.memset(SPt[:], 0.0), [tS], [tS])
            Zs = a_("s5_Zs", [128, 2, 16, 256], F32); tZ = Tok()
            ZW = 2 * 16 * 256
            Eg = a_("s5_Eg", [128, 8, 2, 256], F32); tEg = Tok()
            RT = a_("s5_rt", [128, 4, 256], F32)

            def zv(ri, bb):
                return AP(Zs, ri * 4096 + bb, [[ZW, 128], [256, 16], [16, 16]])

            def lb(t_, idx):
                return AP(t_, idx * 16, [[t_pstride[id(t_)], 128], [1, 16], [0, 16]])
            t_pstride = {id(SQ): 160, id(Q128): 288, id(spt): SPW}

            def rt(i):
                return AP(RT, i * 256, [[1024, 128], [16, 16], [1, 16]])

            def zz(out, a, b, op, r, w):
                V(lambda e: e.tensor_tensor(out=out, in0=a, in1=b, op=op), r, w)
            L8r, L8i = lb(SQ, 0), lb(SQ, 5)
            with contextlib.ExitStack() as sz:
                Wz = sz.enter_context(nc.sbuf_tensor("s5_Wz", [128, 4, 2, 8, 2, 128], BF16)); tWz = Tok()
                pz = Rot([sz.enter_context(nc.psum_tensor("s5_pz%d" % i, [128, 4, 128], BF16)) for i in range(2)])
                pZ = Rot([sz.enter_context(nc.psum_tensor("s5_pZ%d" % i, [128, 256], F32)) for i in range(2)])
                G(lambda e: e.memset(Wz[:], 0.0), [], [tWz])
                for k in range(8):
                    cmul(TRe[:], TIm[:], BBr[:], BBi[:], slb(PWR + k), slb(PWI + k), TA[:], TB[:])
                    scatter(SPt, 128, 0, TRe); scatter(SPt, 128, 64, TIm)
                    for quad in range(8):
                        T, h2 = quad // 2, quad % 2
                        pz_, tpz = pz.next()
                        for pq in range(2):
                            for ri in range(2):
                                P(lambda e, pq=pq, ri=ri: e.transpose(out=pz_[64 * h2:64 * h2 + 64, pq * 2 + ri, :],
                                                                      in_=SPt[:, 2 * quad + pq, ri, :], identity=ident[:]),
                                  [tS, t_ident], [tpz])
                        o_ = AP(Wz, (64 * h2) * 16384 + T * 4096 + (7 - k) * 256,
                                [[16384, 64], [2048, 2], [128, 2], [1, 128]])
                        i_ = AP(pz_, (64 * h2) * 512, [[512, 64], [256, 2], [128, 2], [1, 128]])
                        A(lambda e: e.copy(out=o_, in_=i_), [tpz], [tWz])
                def z_matmuls():
                    for pair in range(16):
                        quad, pq = pair // 2, pair % 2
                        T, h2 = quad // 2, quad % 2
                        for ri in range(2):
                            pZ_, tpZ = pZ.next()
                            for ip in range(8):
                                P(lambda e, ip=ip: e.matmul(pZ_[:], lhsT=Wz[64 * h2:64 * h2 + 64, T, pq, ip, ri, :],
                                                            rhs=UT[64 * h2:64 * h2 + 64, T, ip, :],
                                                            start=(ip == 0), stop=(ip == 7)), [tWz, tUT], [tpZ])
                            if ri == 0:
                                V(lambda e: e.tensor_copy(Zs[:, ri, pair, :], pZ_[:]), [tpZ], [tZ])
                            else:
                                A(lambda e: e.copy(out=Zs[:, ri, pair, :], in_=pZ_[:]), [tpZ], [tZ])
                def recur():
                    for bb in range(1, 16):
                        zz(rt(0), zv(0, bb - 1), L8r, ALU.mult, [tZ, tS], [tZ])
                        zz(rt(1), zv(1, bb - 1), L8i, ALU.mult, [tZ, tS], [tZ])
                        zz(rt(2), zv(1, bb - 1), L8r, ALU.mult, [tZ, tS], [tZ])
                        zz(rt(3), zv(0, bb - 1), L8i, ALU.mult, [tZ, tS], [tZ])
                        zz(zv(0, bb), zv(0, bb), rt(0), ALU.add, [tZ], [tZ])
                        zz(zv(0, bb), zv(0, bb), rt(1), ALU.subtract, [tZ], [tZ])
                        zz(zv(1, bb), zv(1, bb), rt(2), ALU.add, [tZ], [tZ])
                        zz(zv(1, bb), zv(1, bb), rt(3), ALU.add, [tZ], [tZ])
                phase_all(UT, tUT, Zs, tZ, Eg, tEg, z_matmuls, recur, zv)
                with contextlib.ExitStack() as su:
                    X = norm_ctx(su, 2, g_mix)
                    WU = su.enter_context(nc.sbuf_tensor("WU", [128, 8, 512], BF16)); tWU = Tok()
                    for k in range(8):
                        kb.dma("pool", WU[:, k, :], AP(w_in, k * 128 * INC, [[INC, 128], [1, 512]]), writes=[tWU])
                    pP = Rot([su.enter_context(nc.psum_tensor("u_pP%d" % i, [128, 512], F32)) for i in range(2)])
                    for st in range(8):
                        hT, thT = load_norm_T(x_own, st * 256, 2, X)
                        for T in range(4):
                            pp, tpp = pP.next()
                            for k in range(8):
                                P(lambda e, k=k: e.matmul(pp[:, 0:256], lhsT=WU[:, k, T * 128:(T + 1) * 128], rhs=hT[:, k, :],
                                                          start=(k == 0), stop=(k == 7)), [tWU, thT], [tpp])
                            o_ = AP(UT, T * 2048 + st * 32, [[UTW, 128], [256, 8], [1, 32]])
                            i_ = AP(pp, 0, [[512, 128], [1, 8], [8, 32]])
                            if T % 2 == 0:
                                V(lambda e: e.tensor_copy(o_, i_), [tpp], [tUT])
                            else:
                                A(lambda e: e.copy(out=o_, in_=i_), [tpp], [tUT])
                    kb.barrier()
                z_matmuls()
                recur()
                kb.barrier()
            ck("z")
            if BARRIERS: kb.barrier()
            B4 = a_("s5_B4", [128, 16, 2, 64], BF16); dst_t_pstride[id(B4)] = 16 * 2 * 64
            Wc0 = a_("s5_Wc0", [128, 16, 2, 64], BF16); dst_t_pstride[id(Wc0)] = 16 * 2 * 64
            Wc = a_("s5_Wc", [128, 16, 8, 2, 64], BF16); dst_t_pstride[id(Wc)] = 16 * 8 * 2 * 64
            Kblk = a_("s5_Kblk", [128, 4, 8, 128], BF16)
            for t_ in (B4, Wc0, Kblk):
                V(lambda e, t_=t_: e.memset(t_[:], 0.0), [tS], [tS])
            G(lambda e: e.memset(Wc[:], 0.0), [tS], [tS])
            scatter(B4, 128, 0, BBr); scatter(B4, 128, 64, BBi)
            for k in range(0, 9):
                cmul(TRe[:], TIm[:], Cr[:], Ci[:], slb(PWR + k), slb(PWI + k), TA[:], TB[:])
                if k == 0:
                    scatter(Wc0, 128, 0, TRe); scatter(Wc0, 128, 64, TIm, neg=True)
                else:
                    scatter(Wc, 1024, (k - 1) * 128, TRe); scatter(Wc, 1024, (k - 1) * 128 + 64, TIm, neg=True)
            identf = a_("s5_identf", [128, 128], F32)
            kb.dma("sp", identf[:], identf_d.ap(), writes=[tS])
            ck("tab")
            with contextlib.ExitStack() as sk:
                pK = Rot([sk.enter_context(nc.psum_tensor("s5_pK%d" % i, [128, 128], F32)) for i in range(2)])
                for T in range(4):
                    for tau in range(8):
                        pk, tpk = pK.next()
                        for h2 in range(2):
                            for pq in range(2):
                                pair = 2 * (2 * T + h2) + pq
                                for ri in range(2):
                                    rhs_ = Wc0[:, pair, ri, :] if tau == 0 else Wc[:, pair, tau - 1, ri, :]
                                    P(lambda e, h2=h2, pair=pair, ri=ri, rhs_=rhs_, pq=pq: e.matmul(
                                        pk[64 * h2:64 * h2 + 64, 64 * h2:64 * h2 + 64], lhsT=B4[:, pair, ri, :], rhs=rhs_,
                                        start=(pq == 0 and ri == 0), stop=(pq == 1 and ri == 1)), [tS], [tpk])
                        for h2 in range(2):
                            sl_ = slice(64 * h2, 64 * h2 + 64)
                            if tau == 0:
                                V(lambda e, sl_=sl_: e.scalar_tensor_tensor(out=Kblk[sl_, T, 0, sl_], in0=identf[sl_, sl_],
                                                                            scalar=dcol[sl_, T:T + 1], in1=pk[sl_, sl_],
                                                                            op0=ALU.mult, op1=ALU.add), [tpk, tS], [tS])
                            else:
                                V(lambda e, sl_=sl_: e.tensor_copy(Kblk[sl_, T, tau, sl_], pk[sl_, sl_]), [tpk], [tS])
            ck("kblk")
            if BARRIERS: kb.barrier()
            Xp = a_("s5_Xp", [128, 2, 16, 256], BF16); tXp = Tok()
            with contextlib.ExitStack() as sc:
                def c_(name, shape, dt):
                    return sc.enter_context(nc.sbuf_tensor(name, list(shape), dt))
                Dd = c_("s5_D", [128, 9, 2, 256], F32); tD = Tok()
                Gg = c_("s5_G", [128, 2, 16, 16], F32)
                Cw = c_("s5_Cw", [128, 2, 256], F32)
                Cn = c_("s5_Cn", [128, 2, 256], F32)
                oh = c_("s5_oh", [128, 8], F32)
                kb.dma("sp", oh[:], onehot_d.ap(), writes=[tD])

                def dv(r, ri):
                    return AP(Dd, (r * 2 + ri) * 256, [[9 * 512, 128], [16, 16], [1, 16]])

                def ev(r, ri):
                    return AP(Eg, (r * 2 + ri) * 256, [[8 * 512, 128], [16, 16], [1, 16]])
                L128r, L128i = lb(SQ, 4), lb(SQ, 9)
                for ri in range(2):
                    V(lambda e, ri=ri: e.memset(dv(0, ri), 0.0), [], [tD])
                for r in range(8):
                    zz(rt(0), dv(r, 0), L128r, ALU.mult, [tD, tS], [tD]); zz(rt(1), dv(r, 1), L128i, ALU.mult, [tD, tS], [tD])
                    zz(rt(2), dv(r, 1), L128r, ALU.mult, [tD, tS], [tD]); zz(rt(3), dv(r, 0), L128i, ALU.mult, [tD, tS], [tD])
                    zz(dv(r + 1, 0), rt(0), rt(1), ALU.subtract, [tD], [tD])
                    zz(dv(r + 1, 0), dv(r + 1, 0), ev(r, 0), ALU.add, [tD, tEg], [tD])
                    zz(dv(r + 1, 1), rt(2), rt(3), ALU.add, [tD], [tD])
                    zz(dv(r + 1, 1), dv(r + 1, 1), ev(r, 1), ALU.add, [tD, tEg], [tD])
                def gv(ri, j):
                    return Gg[:, ri, :, j]

                def d8(ri, j):
                    return AP(Dd, (8 * 2 + ri) * 256 + j, [[9 * 512, 128], [16, 16]])
                Lkr, Lki = Q128[:, 8, :], Q128[:, 17, :]
                V(lambda e: e.memset(Gg[:, :, :, 0], 0.0), [], [tD])
                for j in range(15):
                    zz(sl(T1), gv(0, j), Lkr, ALU.mult, [tD, tS], [tS]); zz(sl(T2), gv(1, j), Lki, ALU.mult, [tD, tS], [tS])
                    zz(sl(T3), gv(1, j), Lkr, ALU.mult, [tD, tS], [tS]); zz(sl(T4), gv(0, j), Lki, ALU.mult, [tD, tS], [tS])
                    zz(sl(T1), sl(T1), sl(T2), ALU.subtract, [tS], [tS])
                    zz(gv(0, j + 1), sl(T1), d8(0, j), ALU.add, [tS, tD], [tD])
                    zz(sl(T3), sl(T3), sl(T4), ALU.add, [tS], [tS])
                    zz(gv(1, j + 1), sl(T3), d8(1, j), ALU.add, [tS, tD], [tD])
                Gr = AP(Gg, 0, [[512, 128], [16, 16], [1, 16]]); Gi = AP(Gg, 256, [[512, 128], [16, 16], [1, 16]])
                cw = [AP(Cw, ri * 256, [[512, 128], [16, 16], [1, 16]]) for ri in range(2)]
                for ri in range(2):
                    V(lambda e, ri=ri: e.memset(cw[ri], 0.0), [], [tD])
                for r in range(8):
                    qr, qi = lb(Q128, r), lb(Q128, 9 + r)
                    zz(rt(0), Gr, qr, ALU.mult, [tD, tS], [tD]); zz(rt(1), Gi, qi, ALU.mult, [tD, tS], [tD])
                    zz(rt(2), Gi, qr, ALU.mult, [tD, tS], [tD]); zz(rt(3), Gr, qi, ALU.mult, [tD, tS], [tD])
                    zz(rt(0), rt(0), rt(1), ALU.subtract, [tD], [tD]); zz(rt(0), rt(0), dv(r, 0), ALU.add, [tD], [tD])
                    zz(rt(2), rt(2), rt(3), ALU.add, [tD], [tD]); zz(rt(2), rt(2), dv(r, 1), ALU.add, [tD], [tD])
                    for ri, src in ((0, rt(0)), (1, rt(2))):
                        V(lambda e, ri=ri, src=src: e.scalar_tensor_tensor(out=cw[ri], in0=src, scalar=oh[:, r:r + 1],
                                                                           in1=cw[ri], op0=ALU.mult, op1=ALU.add),
                          [tD], [tD])
                cn = [AP(Cn, ri * 256, [[512, 128], [16, 16], [1, 16]]) for ri in range(2)]

                def xpv(ri, bb):
                    return AP(Xp, ri * 4096 + bb, [[ZW, 128], [256, 16], [16, 16]])
                for bb in range(16):
                    for ri in range(2):
                        if bb == 0:
                            V(lambda e, ri=ri: e.tensor_copy(xpv(ri, 0), cw[ri]), [tD], [tXp])
                        else:
                            zz(xpv(ri, bb), zv(ri, bb - 1), cw[ri], ALU.add, [tZ, tD], [tXp])
                    if bb < 15:
                        zz(rt(0), cw[0], L8r, ALU.mult, [tD, tS], [tD]); zz(rt(1), cw[1], L8i, ALU.mult, [tD, tS], [tD])
                        zz(rt(2), cw[1], L8r, ALU.mult, [tD, tS], [tD]); zz(rt(3), cw[0], L8i, ALU.mult, [tD, tS], [tD])
                        zz(cn[0], rt(0), rt(1), ALU.subtract, [tD], [tD]); zz(cn[1], rt(2), rt(3), ALU.add, [tD], [tD])
                        for ri in range(2):
                            V(lambda e, ri=ri: e.tensor_copy(cw[ri], cn[ri]), [tD], [tD])
            ck("car")
            if BARRIERS: kb.barrier()
            with contextlib.ExitStack() as sy_:
                def y_(name, shape, dt):
                    return sy_.enter_context(nc.sbuf_tensor(name, list(shape), dt))
                zT = y_("s5_zT", [128, 4, 2048], BF16); tzT = Tok()
                pY = Rot([sy_.enter_context(nc.psum_tensor("s5_pY%d" % i, [128, 256], F32)) for i in range(4)])
                for T in range(4):
                    for i in range(8):
                        py, tpy = pY.next()
                        for h2 in range(2):
                            hs = slice(64 * h2, 64 * h2 + 64)
                            for ip in range(i + 1):
                                P(lambda e, ip=ip, hs=hs: e.matmul(py[hs, :], lhsT=Kblk[:, T, i - ip, hs], rhs=UT[:, T, ip, :],
                                                                   start=(ip == 0), stop=False), [tS, tUT], [tpy])
                            n_ = 0
                            for pq in range(2):
                                pair = 2 * (2 * T + h2) + pq
                                for ri in range(2):
                                    n_ += 1
                                    P(lambda e, hs=hs, pair=pair, ri=ri, n_=n_: e.matmul(
                                        py[hs, :], lhsT=Wc[:, pair, i, ri, :], rhs=Xp[:, ri, pair, :],
                                        start=False, stop=(n_ == 4)), [tS, tXp], [tpy])
                        o_ = AP(zT, T * 2048 + i, [[8192, 128], [8, 256]])
                        A(lambda e: e.activation(out=o_, in_=py[:], func=AF.Gelu_apprx_tanh), [tpy], [tzT])
                ck("y")
                if BARRIERS: kb.barrier()
                pG = Rot([sy_.enter_context(nc.psum_tensor("s5_pG%d" % i, [128, 512], F32)) for i in range(2)])
                sg = Rot([y_("s5_sg%d" % i, [128, 512], F32) for i in range(2)])
                zg = Rot([y_("s5_zg%d" % i, [128, 512], BF16) for i in range(2)])
                for ch in range(4):
                    for m in range(4):
                        pg, tpg = pG.next()
                        for k4 in range(4):
                            P(lambda e, k4=k4: e.matmul(pg[:], lhsT=Wgl[:, k4, m * 128:(m + 1) * 128],
                                                        rhs=zT[:, k4, ch * 512:(ch + 1) * 512],
                                                        start=(k4 == 0), stop=(k4 == 3)), [tWgl, tzT], [tpg])
                        s_, ts_ = sg.next()
                        A(lambda e: e.activation(out=s_[:], in_=pg[:], func=AF.Sigmoid), [tpg], [ts_])
                        z_, tz_ = zg.next()
                        V(lambda e: e.tensor_tensor(out=z_[:], in0=s_[:], in1=zT[:, m, ch * 512:(ch + 1) * 512],
                                                    op=ALU.mult), [ts_, tzT], [tz_])
                        kb.dma("sp", AP(zg_d, m * 2048 + ch * 512, [[4 * 2048, 128], [1, 512]]), z_[:],
                               reads=[tz_], writes=[t_zg])


    NEG = -30000.0

    def phase_tables():
        with contextlib.ExitStack() as st:
            def a_(name, shape, dt):
                return st.enter_context(nc.sbuf_tensor(name, list(shape), dt))
            tT = Tok()
            relb = a_("t_relb", [32, 8], F32); rl = a_("t_rl", [32, 8], F32)
            ohr = a_("t_ohr", [32, 256], F32); ohf = a_("t_ohf", [32, 256], F32)
            antiJ = a_("t_antiJ", [128, 128], F32)
            kb.dma("sp", relb[:], rel_bias_d.ap(), writes=[tT])
            kb.dma("sp", rl[:], AP(rel_bias_d, 31 * 8, [[0, 32], [1, 8]]), writes=[tT])
            kb.dma("sp", ohr[:], ohrev_d.ap(), writes=[tT])
            kb.dma("sp", ohf[:], ohfwd_d.ap(), writes=[tT])
            kb.dma("sp", antiJ[:], antij_d.ap(), writes=[tT])
            V(lambda e: e.tensor_tensor(out=relb[:], in0=relb[:], in1=rl[:], op=ALU.subtract), [tT], [tT])
            pt = st.enter_context(nc.psum_tensor("t_pt", [8, 512], F32)); tpt = Tok()
            P(lambda e: e.matmul(pt[:, 0:256], lhsT=relb[:], rhs=ohr[:], start=True, stop=True), [tT], [tpt])
            P(lambda e: e.matmul(pt[:, 256:512], lhsT=relb[:], rhs=ohf[:], start=True, stop=True), [tT], [tpt])
            rowR = a_("t_rowR", [8, 384], F32); rowF = a_("t_rowF", [8, 416], F32)
            V(lambda e: e.memset(rowR[:], NEG), [tT], [tT])
            V(lambda e: e.memset(rowF[:], NEG), [tT], [tT])
            V(lambda e: e.tensor_copy(rowR[:, 0:256], pt[:, 0:256]), [tpt, tT], [tT])
            V(lambda e: e.tensor_copy(rowF[:, 160:416], pt[:, 256:512]), [tpt, tT], [tT])
            tD_ = Tok()
            kb.dma("sp", tabR_d.ap(), rowR[:], reads=[tT], writes=[tD_])
            kb.dma("sp", tabF_d.ap(), rowF[:], reads=[tT], writes=[tD_])
            Hk = a_("t_Hk", [128, 2, 8, 128], F32); tH = Tok()
            kb.dma("sp", Hk[:, 0, :, :], AP(tabR_d, 128, [[1, 128], [384, 8], [1, 128]]), reads=[tD_], writes=[tH])
            kb.dma("sp", Hk[:, 1, :, :], AP(tabR_d, 0, [[1, 128], [384, 8], [1, 128]]), reads=[tD_], writes=[tH])
            pb = Rot([st.enter_context(nc.psum_tensor("t_pb%d" % i, [128, 128], F32)) for i in range(2)])
            for d_ in range(2):
                for h in range(8):
                    p_, tp_ = pb.next()
                    P(lambda e: e.matmul(p_[:], lhsT=Hk[:, d_, h, :], rhs=antiJ[:], start=True, stop=True), [tH, tT], [tp_])
                    V(lambda e: e.tensor_copy(BW[:, d_, h, :], p_[:]), [tp_], [t_BW])
            kb.dma("sp", M4[:], m4_d.ap(), writes=[t_BW])
            V(lambda e: e.memset(BnA[:], 1.0), [], [t_BnA])
            kb.dma("pool", BnA[0:16, :, :], AP(tabF_d, 17, [[16, 16], [416, 8], [1, 128]]), reads=[tD_], writes=[t_BnA])

    def phase_compress():
        with contextlib.ExitStack() as sc:
            def a_(name, shape, dt):
                return sc.enter_context(nc.sbuf_tensor(name, list(shape), dt))
            tW = Tok()
            W1s = a_("c_W1s", [128, 2, 16, 256], BF16)
            W2 = a_("c_W2", [128, 2, 2, 64], BF16)
            PosS = a_("c_PosS", [128, 2, 16], BF16)
            for w, (w1, w2, pos) in enumerate(((cmp_w1_k, cmp_w2_k, cmp_pos_k), (cmp_w1_v, cmp_w2_v, cmp_pos_v))):
                for lp in range(16):
                    kb.dma("pool", W1s[:, w, lp, :], AP(w1, lp * 128 * 256, [[256, 128], [1, 256]]), writes=[tW])
                kb.dma("pool", W2[:, w, :, :], AP(w2, 0, [[64, 128], [128 * 64, 2], [1, 64]]), writes=[tW])
                kb.dma("pool", PosS[:, w, :], AP(pos, 0, [[1, 128], [128, 16]]), writes=[tW], allow_slow_non_contiguous=True)
            biasW = a_("c_biasW", [128, 4], F32); tB = Tok()
            pB = sc.enter_context(nc.psum_tensor("c_pB", [128, 4], F32)); tpB = Tok()
            for w in range(2):
                for ht in range(2):
                    for lp in range(16):
                        P(lambda e, lp=lp: e.matmul(pB[:, w * 2 + ht:w * 2 + ht + 1], lhsT=W1s[:, w, lp, ht * 128:(ht + 1) * 128],
                                                    rhs=PosS[:, w, lp:lp + 1], start=(lp == 0), stop=(lp == 15)), [tW], [tpB])
            V(lambda e: e.tensor_copy(biasW[:], pB[:]), [tpB], [tB])
            CW = 2 * 2 * 2080
            CRs = a_("c_CRs", [128, 2, 2, 2080], BF16); tCR = Tok()
            G1 = a_("c_G1", [128, 2, 2, 2, 128], BF16); tG1 = Tok()
            pH = Rot([sc.enter_context(nc.psum_tensor("c_pH%d" % i, [128, 128], F32)) for i in range(2)])
            pO = Rot([sc.enter_context(nc.psum_tensor("c_pO%d" % i, [128, 128], F32)) for i in range(2)])
            V(lambda e: e.memset(Vc_aug[:], 1.0), [], [t_Vc])
            for nt in range(8):
                t0 = 2048 * nt
                for w, src in enumerate((kcR_d, vcR_d)):
                    for g in range(2):
                        kb.dma("sp", CRs[0:64, w, g, 0:2064], AP(src, (64 * g) * (S + 16) + t0, [[S + 16, 64], [1, 2064]]),
                               reads=kv_toks, writes=[tCR])
                        kb.dma("sp", CRs[64:128, w, g, 0:2063], AP(src, (64 * g) * (S + 16) + t0 + 1, [[S + 16, 64], [1, 2063]]),
                               reads=kv_toks, writes=[tCR])
                for w in range(2):
                    for g in range(2):
                        for ht in range(2):
                            ph, tph = pH.next()
                            for lp in range(16):
                                rhs_ = AP(CRs, (w * 2 + g) * 2080 + 2 * lp, [[CW, 128], [16, 128]])
                                P(lambda e, lp=lp, rhs_=rhs_: e.matmul(ph[:], lhsT=W1s[:, w, lp, ht * 128:(ht + 1) * 128], rhs=rhs_,
                                                                       start=(lp == 0), stop=(lp == 15)), [tW, tCR], [tph])
                            A(lambda e: e.activation(out=G1[:, w, g, ht, :], in_=ph[:], func=AF.Gelu_apprx_tanh,
                                                     bias=biasW[:, w * 2 + ht:w * 2 + ht + 1]), [tph, tB], [tG1])
                po, tpo = pO.next()
                for g in range(2):
                    for ht in range(2):
                        P(lambda e, ht=ht: e.matmul(po[64 * g:64 * g + 64, :], lhsT=W2[:, 0, ht, :], rhs=G1[:, 0, g, ht, :],
                                                    start=(ht == 0), stop=(ht == 1)), [tW, tG1], [tpo])
                V(lambda e: e.tensor_copy(KcT[:, nt * 128:(nt + 1) * 128], po[:]), [tpo], [t_Kc])
                po, tpo = pO.next()
                for g in range(2):
                    for ht in range(2):
                        P(lambda e, ht=ht: e.matmul(po[:, 64 * g:64 * g + 64], lhsT=G1[:, 1, g, ht, :], rhs=W2[:, 1, ht, :],
                                                    start=(ht == 0), stop=(ht == 1)), [tW, tG1], [tpo])
                V(lambda e: e.tensor_copy(Vc_aug[:, nt, :, 0:64], po[:].rearrange("p (g d) -> p g d", g=2)), [tpo], [t_Vc])

    def attn_tile(ps_, tps_, kT_ap, q_ap, deps_r, bias_ap, bias_tok, P_rot, Sb_rot):
        P(lambda e: e.matmul(ps_[:], lhsT=kT_ap, rhs=q_ap, start=True, stop=True), deps_r, [tps_])
        p_, tp_ = P_rot.next()
        if bias_ap is not None:
            sb_, tsb_ = Sb_rot.next()
            V(lambda e: e.tensor_tensor(out=sb_[:], in0=ps_[:], in1=bias_ap, op=ALU.add), [tps_, bias_tok], [tsb_])
            A(lambda e: e.activation(out=p_[:], in_=sb_[:], func=AF.Exp), [tsb_], [tp_])
        else:
            A(lambda e: e.activation(out=p_[:], in_=ps_[:], func=AF.Exp), [tps_], [tp_])
        return p_, tp_

    def combine(po_, tpo_, gcol0, gstride, j, g, acc_, tacc_, first, cf_rot):
        cf, tcf = cf_rot.next()
        rs_ = AP(po_, 64, [[int(np.prod(list(po_.shape)[1:])), 128], [65, 4]])
        V(lambda e: e.tensor_scalar(out=cf[:, 0:4], in0=rs_, scalar1=1e-30, scalar2=None, op0=ALU.max), [tpo_], [tcf])
        V(lambda e: e.reciprocal(out=cf[:, 4:8], in_=cf[:, 0:4]), [tcf], [tcf])
        g_ = AP(gates, j * 24 + 12 * g + gcol0, [[16 * 24, 128], [3, 4]])
        V(lambda e: e.tensor_tensor(out=cf[:, 8:12], in0=cf[:, 4:8], in1=g_, op=ALU.mult), [tcf, t_gates], [tcf])
        for r in range(4):
            h = 4 * g + r
            o_ = acc_[:, h * 64:(h + 1) * 64]
            if first:
                V(lambda e, r=r, o_=o_: e.tensor_scalar(out=o_, in0=po_[:, r * 65:r * 65 + 64], scalar1=cf[:, 8 + r:9 + r],
                                                        scalar2=None, op0=ALU.mult), [tpo_, tcf], [tacc_])
            else:
                V(lambda e, r=r, o_=o_: e.scalar_tensor_tensor(out=o_, in0=po_[:, r * 65:r * 65 + 64], scalar=cf[:, 8 + r:9 + r],
                                                               in1=o_, op0=ALU.mult, op1=ALU.add), [tpo_, tcf], [tacc_])

    def pv_T(poT, tpoT, Pm, tPm, v_ap, tv, first, last):
        P(lambda e: e.matmul(poT[0:65, :], lhsT=v_ap, rhs=Pm[:], start=first, stop=last), [tPm, tv], [tpoT])

    def finish_o(poT, tpoT, po, tpo, osb, tosb):
        V(lambda e: e.tensor_copy(osb[0:65, :], poT[0:65, :]), [tpoT], [tosb])
        for r in range(4):
            P(lambda e, r=r: e.transpose(out=po[:, r * 65:(r + 1) * 65], in_=osb[0:65, r * 128:(r + 1) * 128],
                                         identity=identF[0:65, 0:65]), [tosb, t_identF], [tpo])

    def phase1():
        with contextlib.ExitStack() as s1:
            def a_(name, shape, dt):
                return s1.enter_context(nc.sbuf_tensor(name, list(shape), dt))
            X = norm_ctx(s1, 2, g_mix)
            WinA = a_("WinA", [128, 8, 1048], BF16); tWin = Tok()
            for k in range(8):
                base = k * 128 * INC
                for r in range(4):
                    kb.dma("pool", WinA[:, k, r * 128:(r + 1) * 128].rearrange("p (g d) -> p g d", g=2),
                           AP(w_in, base + 512 + 64 * r, [[INC, 128], [256, 2], [1, 64]]), writes=[tWin])
                for (c0, s0, n) in ((512, 1536, 128), (640, 1280, 128), (768, 1664, 128), (896, 1408, 128), (1024, 1792, 24)):
                    kb.dma("pool", WinA[:, k, c0:c0 + n], AP(w_in, base + s0, [[INC, 128], [1, n]]), writes=[tWin])
            vprev = a_("vprev_sb", [128, 64], F32); tvp = Tok()
            kb.dma("sp", vprev[:], vprev_d.ap(), writes=[tvp])
            KwT = [a_("KwT%d" % i, [128, 5, 128], BF16) for i in range(2)]; tKw = [Tok(), Tok()]
            Vw = [a_("Vw%d" % i, [128, 5, 2, 65], BF16) for i in range(2)]; tVw = [Tok(), Tok()]
            pP = Rot([s1.enter_context(nc.psum_tensor("p1_pP%d" % i, [128, 512], F32)) for i in range(2)])
            pS = Rot([s1.enter_context(nc.psum_tensor("p1_pS%d" % i, [128, 512], F32)) for i in range(2)])
            pO = Rot([s1.enter_context(nc.psum_tensor("p1_pO%d" % i, [128, 512], F32)) for i in range(1)])
            poT = s1.enter_context(nc.psum_tensor("p1_poT", [128, 512], F32)); tpoT = Tok()
            osb = a_("p1_osb", [128, 512], F32); tosb = Tok()
            M4r = a_("p1_M4r", [128, 512], F32)
            for r_ in range(4):
                V(lambda e, r_=r_: e.tensor_copy(M4r[:, r_ * 128:(r_ + 1) * 128], M4[:]), [t_BW], [t_BW])
            Pr = Rot([a_("p1_P%d" % i, [128, 512], BF16) for i in range(2)])
            Sbr = Rot([a_("p1_Sb%d" % i, [128, 512], F32) for i in range(2)])
            cfr = Rot([a_("p1_cf%d" % i, [128, 12], F32) for i in range(2)])
            accw = Rot([a_("p1_acc%d" % i, [128, 512], F32) for i in range(2)])
            V(lambda e: e.memset(VsN[:], 1.0), [], [t_VsN])
            nev = [0]

            def evac(o_, i_, rd, wr, scale=None):
                nev[0] += 1
                if scale is not None:
                    A(lambda e: e.activation(out=o_, in_=i_, func=AF.Copy, scale=scale), rd, wr)
                else:
                    V(lambda e: e.tensor_copy(o_, i_), rd, wr)

            def fm(col0, hT, thT, c0, n):
                pp, tpp = pP.next()
                for k in range(8):
                    P(lambda e, k=k: e.matmul(pp[:, 0:n], lhsT=WinA[:, k, col0:col0 + 128], rhs=hT[:, k, c0:c0 + n],
                                              start=(k == 0), stop=(k == 7)), [tWin, thT], [tpp])
                return pp, tpp

            def tm(col0, ncol, hT, thT, a):
                pp, tpp = pP.next()
                for k in range(8):
                    P(lambda e, k=k: e.matmul(pp[:, 0:ncol], lhsT=hT[:, k, a * 128:(a + 1) * 128], rhs=WinA[:, k, col0:col0 + ncol],
                                              start=(k == 0), stop=(k == 7)), [tWin, thT], [tpp])
                return pp, tpp
            ck("p1w")
            for oc in range(8):
                hT, thT = load_norm_T(x_own, oc * 256, 2, X)
                for r in range(4):
                    pp, tpp = fm(r * 128, hT, thT, 0, 256)
                    evac(AP(Qall, (2 * oc) * 512 + r * 128, [[16 * 512, 128], [512, 2], [1, 128]]),
                         AP(pp, 0, [[512, 128], [128, 2], [1, 128]]), [tpp], [t_Q], scale=0.125)
                ck("p1a")
                pp, tpp = fm(512, hT, thT, 0, 256)
                ck("p1a1")
                for a in range(2):
                    evac(KwT[a][:, 4, :], pp[:, a * 128:(a + 1) * 128], [tpp], [tKw[a]])
                ck("p1a2")
                pp, tpp = fm(640, hT, thT, 0, 256)
                evac(AP(KsN, (2 * oc) * 256 + 128, [[16 * 256, 128], [256, 2], [1, 128]]),
                     AP(pp, 0, [[512, 128], [128, 2], [1, 128]]), [tpp], [t_KsN])
                ck("p1b")
                for a in range(2):
                    j = 2 * oc + a
                    pp, tpp = tm(768, 256, hT, thT, a)
                    evac(Vw[a][:, 4, :, 0:64], pp[:, 0:128].rearrange("p (g d) -> p g d", g=2), [tpp], [tVw[a]])
                    V(lambda e: e.memset(Vw[a][:, 4, :, 64:65], 1.0), [], [tVw[a]])
                    evac(VsN[:, j, 1, :, 0:64], pp[:, 128:256].rearrange("p (g d) -> p g d", g=2), [tpp], [t_VsN])
                    ck("p1c")
                    pp, tpp = tm(1024, 24, hT, thT, a)
                    A(lambda e: e.activation(out=gates[:, j, :], in_=pp[:, 0:24], func=AF.Sigmoid), [tpp], [t_gates])
                ck("p1own")
                for a in range(2):
                    j = 2 * oc + a
                    for pc in range(2):
                        hP, thP = load_norm_T(x_prev, j * 512 + pc * 256, 2, X)
                        pp, tpp = fm(512, hP, thP, 0, 256)
                        evac(KwT[a][:, 2 * pc:2 * pc + 2, :], pp[:, 0:256].rearrange("p (t k) -> p t k", t=2), [tpp], [tKw[a]])
                        for a2 in range(2):
                            p_ = 2 * pc + a2
                            last = (p_ == 3)
                            pp, tpp = tm(768, 256 if last else 128, hP, thP, a2)
                            evac(Vw[a][:, p_, :, 0:64], pp[:, 0:128].rearrange("p (g d) -> p g d", g=2), [tpp], [tVw[a]])
                            vcol = AP(vprev, j * 4 + p_, [[64, 128], [0, 2], [1, 1]])
                            V(lambda e: e.tensor_copy(Vw[a][:, p_, :, 64:65], vcol), [tvp], [tVw[a]])
                            if last:
                                evac(VsN[:, j, 0, :, 0:64], pp[:, 128:256].rearrange("p (g d) -> p g d", g=2), [tpp], [t_VsN])
                                V(lambda e: e.tensor_copy(VsN[:, j, 0, :, 64:65], vcol), [tvp], [t_VsN])
                        if pc == 1:
                            pp, tpp = fm(640, hP, thP, 128, 128)
                            evac(KsN[:, j, 0, :], pp[:, 0:128], [tpp], [t_KsN])
                    ck("p1prev")
                    ac, tac = accw.next()
                    for g in range(2):
                        po, tpo = pO.next()
                        for p_ in range(5):
                            dl = 4 - p_
                            ps_, tps_ = pS.next()
                            if dl == 0:
                                b_ap, b_tok = BW[:, 0, 4 * g:4 * g + 4, :].rearrange("p h q -> p (h q)"), t_BW
                            elif dl == 1:
                                b_ap, b_tok = BW[:, 1, 4 * g:4 * g + 4, :].rearrange("p h q -> p (h q)"), t_BW
                            elif dl == 4:
                                b_ap, b_tok = M4r[:], t_BW
                            else:
                                b_ap, b_tok = None, None
                            if b_ap is not None and dl != 4:
                                pass
                            Pm, tPm = attn_tile(ps_, tps_, KwT[a][64 * g:64 * g + 64, p_, :],
                                                Qall[64 * g:64 * g + 64, j, :, :].rearrange("p r q -> p (r q)"),
                                                [tKw[a], t_Q], b_ap, b_tok, Pr, Sbr)
                            pv_T(poT, tpoT, Pm, tPm, Vw[a][:, p_, g, :], tVw[a], p_ == 0, p_ == 4)
                        finish_o(poT, tpoT, po, tpo, osb, tosb)
                        combine(po, tpo, 2, 3, j, g, ac, tac, True, cfr)
                    kb.dma("sp", acc_d.ap()[j * 128:(j + 1) * 128, :], ac[:], reads=[tac], writes=[t_accd])
                    ck("p1win")

    def phase2():
        with contextlib.ExitStack() as s2:
            def a_(name, shape, dt):
                return s2.enter_context(nc.sbuf_tensor(name, list(shape), dt))
            KsT_all = a_("KsT_all", [128, S], BF16); tKs = Tok()
            for i in range(8):
                kb.dma("sp", KsT_all[:, i * 2048:(i + 1) * 2048], AP(ksT_d, i * 2048, [[S, 128], [1, 2048]]),
                       reads=kv_toks, writes=[tKs])
            Vs_all = a_("Vs_all", [128, 128, 2, 65], BF16); tVs = Tok()
            G(lambda e: e.memset(Vs_all[:], 1.0), [], [tVs])
            for i in range(8):
                for g in range(2):
                    kb.dma("sp", Vs_all[:, i * 16:(i + 1) * 16, g, 0:64],
                           AP(vs_d, i * 16 * 16384 + 64 * g, [[128, 128], [16384, 16], [1, 64]]), reads=kv_toks, writes=[tVs])
            wide = a_("wide_sb", [128, 4096], BF16); tc_ = Tok()
            kb.dma("sp", wide[:], wide_d.ap(), writes=[tc_])
            keepb = a_("keepb", [128, 32], F32)
            kb.dma("sp", keepb[:], keepblk_d.ap(), writes=[tc_])
            candr = Rot([a_("cand%d" % i, [128, 256], BF16) for i in range(2)])
            forcr = Rot([a_("forc%d" % i, [128, 256], BF16) for i in range(2)])
            expnr = Rot([a_("expn%d" % i, [128, 2, 128], BF16) for i in range(2)])
            selcr = Rot([a_("selc%d" % i, [17, 1024], BF16) for i in range(2)])
            accr = Rot([a_("p2_acc%d" % i, [128, 512], F32) for i in range(2)])
            accb = a_("p2_accb", [128, 512], BF16); taccb = Tok()
            imp = a_("p2_imp", [128, 1028], F32); timp = Tok()
            pq = Rot([a_("p2_pq%d" % i, [128, 1024], F32) for i in range(2)])
            rs = Rot([a_("p2_rs%d" % i, [128, 4], F32) for i in range(4)])
            sS = a_("p2_s", [128, 256], F32); sS2 = a_("p2_s2", [128, 256], F32); tS_ = Tok()
            m8 = a_("p2_m8", [128, 16], F32)
            selg = a_("p2_selg", [128, 256], BF16); tselg = Tok()
            selT = a_("p2_selT", [128, 2, 2, 128], BF16); tselT = Tok()
            selTk = a_("p2_selTk", [128, 2, 2, 128], BF16)
            Pr = Rot([a_("p2_P%d" % i, [128, 512], BF16) for i in range(3)])
            Pmr = Rot([a_("p2_Pm%d" % i, [128, 512], BF16) for i in range(3)])
            Sbr = Rot([a_("p2_Sb%d" % i, [128, 512], F32) for i in range(2)])
            cfr = Rot([a_("p2_cf%d" % i, [128, 12], F32) for i in range(2)])
            pS = Rot([s2.enter_context(nc.psum_tensor("p2_pS%d" % i, [128, 512], F32)) for i in range(2)])
            pI = Rot([s2.enter_context(nc.psum_tensor("p2_pI%d" % i, [128, 512], F32)) for i in range(2)])
            pMt = s2.enter_context(nc.psum_tensor("p2_pM", [128, 2, 128], F32))
            pM = Rot([pMt[:, 0, :], pMt[:, 1, :]])
            pO = Rot([s2.enter_context(nc.psum_tensor("p2_pO%d" % i, [128, 260], F32)) for i in range(1)])
            pT = s2.enter_context(nc.psum_tensor("p2_pT", [128, 4, 128], BF16)); tpT = Tok()
            poT = s2.enter_context(nc.psum_tensor("p2_poT", [128, 512], F32)); tpoT = Tok()
            osb = a_("p2_osb", [128, 512], F32); tosb = Tok()
            V(lambda e: e.memset(imp[:], 0.0), [], [timp])

            def pv(po, tpo, Pm, tPm, v_ap, tv, first, last):
                pv_T(poT, tpoT, Pm, tPm, v_ap, tv, first, last)
                if last:
                    finish_o(poT, tpoT, po, tpo, osb, tosb)

            def pm_b(pm):
                i = 0 if pm is pM.aps[0] else 1
                return AP(pMt, i * 128, [[256, 128], [0, 4], [1, 128]])

            def masked(Pt, tPt, pm, tpm):
                Pm, tPm = Pmr.next()
                V(lambda e: e.tensor_tensor(out=Pm[:].rearrange("p (r q) -> p r q", r=4), in0=Pt[:].rearrange("p (r q) -> p r q", r=4),
                                            in1=pm.rearrange("p (o q) -> p o q", o=1).broadcast_to([128, 4, 128]) if False else pm_b(pm), op=ALU.mult), [tPt, tpm], [tPm])
                return Pm, tPm
            for j in range(NQ):
                Wb_ = 16 * (j + 1)
                NCc = 64 * (j + 1)
                ntc = (NCc + 127) // 128
                nbt = (Wb_ + 127) // 128
                cd, tcd = candr.next(); fc, tfc = forcr.next(); ex, tex = expnr.next(); sc_, tsc = selcr.next()
                kb.dma("sp", cd[:], AP(cand_d, j * 128 * 256, [[256, 128], [1, 256]]), writes=[tcd])
                kb.dma("sp", fc[:], AP(forced_d, j * 128 * 256, [[256, 128], [1, 256]]), writes=[tfc])
                kb.dma("sp", ex[:], AP(expn_d, j * 256 * 128, [[128, 128], [128 * 128, 2], [1, 128]]), writes=[tex])
                kb.dma("sp", sc_[:], AP(selc_d, j * 17 * 1024, [[1024, 17], [1, 1024]]), writes=[tsc])
                ac, tac = accr.next()
                kb.dma("sp", ac[:], acc_d.ap()[j * 128:(j + 1) * 128, :], reads=[t_accd], writes=[tac])
                for g in range(2):
                    qg = Qall[64 * g:64 * g + 64, j, :, :].rearrange("p r q -> p (r q)")
                    for r in range(4):
                        h = 4 * g + r
                        pq_, tpq = pq.next(); rs_, trs = rs.next()
                        nch = 0
                        for c0 in range(0, NCc, 512):
                            n = min(512, NCc - c0)
                            pi_, tpi = pI.next()
                            P(lambda e: e.matmul(pi_[:, 0:n], lhsT=Qall[64 * g:64 * g + 64, j, r, :], rhs=KcT[64 * g:64 * g + 64, c0:c0 + n],
                                                 start=True, stop=False), [t_Q, t_Kc], [tpi])
                            P(lambda e: e.matmul(pi_[:, 0:n], lhsT=BnA[0:17, h, :], rhs=sc_[0:17, c0:c0 + n],
                                                 start=False, stop=True), [t_BnA, tsc], [tpi])
                            A(lambda e, nch=nch: e.activation(out=pq_[:, c0:c0 + n], in_=pi_[:, 0:n], func=AF.Exp,
                                                              accum_out=rs_[:, nch:nch + 1]), [tpi], [tpq, trs])
                            nch += 1
                        if nch == 2:
                            V(lambda e: e.tensor_tensor(out=rs_[:, 0:1], in0=rs_[:, 0:1], in1=rs_[:, 1:2], op=ALU.add), [trs], [trs])
                        V(lambda e: e.tensor_scalar(out=rs_[:, 2:3], in0=rs_[:, 0:1], scalar1=1e-30, scalar2=None, op0=ALU.max), [trs], [trs])
                        V(lambda e: e.reciprocal(out=rs_[:, 3:4], in_=rs_[:, 2:3]), [trs], [trs])
                        if r == 0:
                            V(lambda e: e.tensor_scalar(out=imp[:, 1:1 + NCc], in0=pq_[:, 0:NCc], scalar1=rs_[:, 3:4], scalar2=None,
                                                        op0=ALU.mult), [tpq, trs], [timp])
                        else:
                            V(lambda e: e.scalar_tensor_tensor(out=imp[:, 1:1 + NCc], in0=pq_[:, 0:NCc], scalar=rs_[:, 3:4],
                                                               in1=imp[:, 1:1 + NCc], op0=ALU.mult, op1=ALU.add), [tpq, trs], [timp])

                    def iv(o):
                        return AP(imp, o, [[1028, 128], [4, Wb_]])
                    s_ = sS[:, 0:Wb_]
                    V(lambda e: e.tensor_tensor(out=s_, in0=iv(1), in1=iv(2), op=ALU.add), [timp], [tS_])
                    V(lambda e: e.tensor_tensor(out=s_, in0=s_, in1=iv(3), op=ALU.add), [timp, tS_], [tS_])
                    V(lambda e: e.scalar_tensor_tensor(out=s_, in0=s_, scalar=2.0, in1=iv(0), op0=ALU.mult, op1=ALU.add), [timp, tS_], [tS_])
                    V(lambda e: e.tensor_tensor(out=s_, in0=s_, in1=iv(4), op=ALU.add), [timp, tS_], [tS_])
                    V(lambda e: e.tensor_tensor(out=s_, in0=s_, in1=cd[:, 0:Wb_], op=ALU.mult), [tS_, tcd], [tS_])
                    V(lambda e: e.max(out=m8[:, 0:8], in_=s_), [tS_], [tS_])
                    V(lambda e: e.match_replace(out=sS2[:, 0:Wb_], in_to_replace=m8[:, 0:8], in_values=s_, imm_value=-1.0), [tS_], [tS_])
                    V(lambda e: e.max(out=m8[:, 8:16], in_=sS2[:, 0:Wb_]), [tS_], [tS_])
                    V(lambda e: e.tensor_scalar(out=s_, in0=s_, scalar1=m8[:, 12:13], scalar2=None, op0=ALU.is_ge), [tS_], [tS_])
                    V(lambda e: e.tensor_tensor(out=s_, in0=s_, in1=cd[:, 0:Wb_], op=ALU.mult), [tS_, tcd], [tS_])
                    if Wb_ < 256:
                        V(lambda e: e.memset(selg[:, Wb_:256], 0.0), [], [tselg])
                    V(lambda e: e.tensor_tensor(out=selg[:, 0:Wb_], in0=s_, in1=fc[:, 0:Wb_], op=ALU.add), [tS_, tfc], [tselg])
                    for bt in range(nbt):
                        P(lambda e, bt=bt: e.transpose(out=pT[:, bt, :], in_=selg[:, bt * 128:(bt + 1) * 128], identity=ident[:]),
                          [tselg, t_ident], [tpT])
                        V(lambda e, bt=bt: e.tensor_copy(selT[:, g, bt, :], pT[:, bt, :]), [tpT], [tselT])
                        V(lambda e, bt=bt: e.tensor_scalar(out=selTk[:, g, bt, :], in0=pT[:, bt, :], scalar1=keepb[:, j * 2 + bt:j * 2 + bt + 1],
                                                           scalar2=None, op0=ALU.mult), [tpT, tc_], [tselT])
                    po, tpo = pO.next()
                    for nt in range(ntc):
                        ps_, tps_ = pS.next()
                        P(lambda e: e.matmul(ps_[:], lhsT=KcT[64 * g:64 * g + 64, nt * 128:(nt + 1) * 128], rhs=qg,
                                             start=True, stop=False), [t_Kc, t_Q], [tps_])
                        P(lambda e: e.matmul(ps_[:], lhsT=sc_[0:17, nt * 128:(nt + 1) * 128],
                                             rhs=BnA[0:17, 4 * g:4 * g + 4, :].rearrange("p h q -> p (h q)"),
                                             start=False, stop=True), [t_BnA, tsc], [tps_])
                        Pt, tPt = Pr.next()
                        A(lambda e: e.activation(out=Pt[:], in_=ps_[:], func=AF.Exp), [tps_], [tPt])
                        pv(po, tpo, Pt, tPt, Vc_aug[:, nt, g, :], t_Vc, nt == 0, nt == ntc - 1)
                    combine(po, tpo, 0, 3, j, g, ac, tac, False, cfr)
                    po, tpo = pO.next()
                    ntile = 8 * (j + 1)
                    for qb in range(ntile):
                        bt, p0 = divmod(2 * qb, 128)
                        h2, pi2 = p0 // 64, (p0 % 64) // 2
                        ps_, tps_ = pS.next()
                        P(lambda e: e.matmul(ps_[:], lhsT=KsT_all[64 * g:64 * g + 64, qb * 128:(qb + 1) * 128], rhs=qg,
                                             start=True, stop=True), [tKs, t_Q], [tps_])
                        pm, tpm = pM.next()
                        P(lambda e: e.matmul(pm, lhsT=wide[64 * h2:64 * h2 + 64, pi2 * 128:(pi2 + 1) * 128],
                                             rhs=selTk[64 * h2:64 * h2 + 64, g, bt, :], start=True, stop=True), [tc_, tselT], [tpm])
                        Pt, tPt = Pr.next()
                        A(lambda e: e.activation(out=Pt[:], in_=ps_[:], func=AF.Exp), [tps_], [tPt])
                        Pm, tPm = masked(Pt, tPt, pm, tpm)
                        pv(po, tpo, Pm, tPm, Vs_all[:, qb, g, :], tVs, qb == 0, False)
                    ps_, tps_ = pS.next()
                    b_ap = BW[:, 1, 4 * g:4 * g + 4, :].rearrange("p h q -> p (h q)")
                    Pt, tPt = attn_tile(ps_, tps_, KsN[64 * g:64 * g + 64, j, 0, :], qg, [t_KsN, t_Q], b_ap, t_BW, Pr, Sbr)
                    pm, tpm = pM.next()
                    for bt in range(nbt):
                        P(lambda e, bt=bt: e.matmul(pm, lhsT=ex[:, bt, :], rhs=selT[:, g, bt, :], start=(bt == 0), stop=(bt == nbt - 1)),
                          [tex, tselT], [tpm])
                    Pm, tPm = masked(Pt, tPt, pm, tpm)
                    pv(po, tpo, Pm, tPm, VsN[:, j, 0, g, :], t_VsN, False, False)
                    ps_, tps_ = pS.next()
                    b_ap = BW[:, 0, 4 * g:4 * g + 4, :].rearrange("p h q -> p (h q)")
                    Pt, tPt = attn_tile(ps_, tps_, KsN[64 * g:64 * g + 64, j, 1, :], qg, [t_KsN, t_Q], b_ap, t_BW, Pr, Sbr)
                    pv(po, tpo, Pt, tPt, VsN[:, j, 1, g, :], t_VsN, False, True)
                    combine(po, tpo, 1, 3, j, g, ac, tac, False, cfr)
                V(lambda e: e.tensor_copy(accb[:], ac[:]), [tac], [taccb])
                for t in range(4):
                    P(lambda e, t=t: e.transpose(out=pT[:, t, :], in_=accb[:, t * 128:(t + 1) * 128], identity=ident[:]),
                      [taccb, t_ident], [tpT])
                V(lambda e: e.tensor_copy(oT[:, :, j * 128:(j + 1) * 128], pT[:]), [tpT], [t_oT])

    def phase3a():
        with contextlib.ExitStack() as s3:
            def a_(name, shape, dt):
                return s3.enter_context(nc.sbuf_tensor(name, list(shape), dt))
            X = norm_ctx(s3, 2, g_mix)
            tW = Tok()
            Wb = a_("Wbr", [128, 8, 2048], BF16)
            Wus = a_("Wus", [128, 4, 1024], BF16); Wun = a_("Wun", [128, 4, 1024], BF16)
            Wo = a_("Wo", [128, 8, 1024], BF16)
            for k in range(8):
                kb.dma("pool", Wb[:, k, :], AP(w_in, k * 128 * INC + 1816, [[INC, 128], [1, 2048]]), writes=[tW])
                kb.dma("pool", Wo[:, k, :], AP(w_out_d, k * 128 * 1024, [[1024, 128], [1, 1024]]), writes=[tW])
            for k in range(4):
                kb.dma("pool", Wus[:, k, :], AP(w_up_ssm, k * 128 * 1024, [[1024, 128], [1, 1024]]), writes=[tW])
                kb.dma("pool", Wun[:, k, :], AP(w_up_nsa, k * 128 * 1024, [[1024, 128], [1, 1024]]), writes=[tW])
            zgr = Rot([a_("p3_zg%d" % i, [128, 4, 256], BF16) for i in range(2)])
            mix = a_("p3_mix", [128, 8, 256], BF16); tmix = Tok()
            sga = Rot([a_("p3_sa%d" % i, [128, 2, 256], F32) for i in range(2)])
            tt = Rot([a_("p3_t%d" % i, [128, 2, 256], F32) for i in range(2)])
            x1r = Rot([a_("p3_x1%d" % i, [128, 1024], F32) for i in range(2)])
            pA = Rot([s3.enter_context(nc.psum_tensor("p3_pA%d" % i, [128, 2, 256], F32)) for i in range(2)])
            pY = Rot([s3.enter_context(nc.psum_tensor("p3_pY%d" % i, [128, 2, 256], F32)) for i in range(2)])
            pX = Rot([s3.enter_context(nc.psum_tensor("p3_pX%d" % i, [128, 512], F32)) for i in range(2)])
            for oc in range(8):
                hT, thT, xb, txb = load_norm_T(x_own, oc * 256, 2, X, ret_x=True)
                zc, tzc = zgr.next()
                kb.dma("sp", zc[:], AP(zg_d, oc * 256, [[4 * NTOK, 128], [NTOK, 4], [1, 256]]), reads=[t_zg], writes=[tzc])
                for m in range(8):
                    pa, tpa = pA.next(); py, tpy = pY.next()
                    for k in range(8):
                        P(lambda e, k=k: e.matmul(pa[:, 0, :], lhsT=Wb[:, k, m * 128:(m + 1) * 128], rhs=hT[:, k, :],
                                                  start=(k == 0), stop=(k == 7)), [tW, thT], [tpa])
                    for k in range(8):
                        P(lambda e, k=k: e.matmul(pa[:, 1, :], lhsT=Wb[:, k, 1024 + m * 128:1024 + (m + 1) * 128], rhs=hT[:, k, :],
                                                  start=(k == 0), stop=(k == 7)), [tW, thT], [tpa])
                    for k in range(4):
                        P(lambda e, k=k: e.matmul(py[:, 0, :], lhsT=Wus[:, k, m * 128:(m + 1) * 128], rhs=zc[:, k, :],
                                                  start=(k == 0), stop=(k == 3)), [tW, tzc], [tpy])
                    for k in range(4):
                        P(lambda e, k=k: e.matmul(py[:, 1, :], lhsT=Wun[:, k, m * 128:(m + 1) * 128], rhs=oT[:, k, oc * 256:(oc + 1) * 256],
                                                  start=(k == 0), stop=(k == 3)), [tW, t_oT], [tpy])
                    sa, tsa = sga.next(); t_, tt_ = tt.next()
                    A(lambda e: e.activation(out=sa[:], in_=pa[:], func=AF.Sigmoid), [tpa], [tsa])
                    V(lambda e: e.tensor_tensor(out=t_[:], in0=sa[:], in1=py[:], op=ALU.mult), [tsa, tpy], [tt_])
                    V(lambda e: e.tensor_tensor(out=mix[:, m, :], in0=t_[:, 0, :], in1=t_[:, 1, :], op=ALU.add), [tt_], [tmix])
                for a in range(2):
                    x1, tx1 = x1r.next()
                    for hf in range(2):
                        px, tpx = pX.next()
                        for m in range(8):
                            P(lambda e, m=m: e.matmul(px[:], lhsT=mix[:, m, a * 128:(a + 1) * 128], rhs=Wo[:, m, hf * 512:(hf + 1) * 512],
                                                      start=(m == 0), stop=(m == 7)), [tmix, tW], [tpx])
                        V(lambda e: e.tensor_tensor(out=x1[:, hf * 512:(hf + 1) * 512], in0=px[:], in1=xb[:, a, hf * 512:(hf + 1) * 512],
                                                    op=ALU.add), [tpx, txb], [tx1])
                    tk_ = Tok(); x1_toks.append(tk_)
                    kb.dma("sp", x1_d.ap()[oc * 256 + a * 128: oc * 256 + (a + 1) * 128, :], x1[:], reads=[tx1], writes=[tk_])

    def phase_ffn(src_d):
        with contextlib.ExitStack() as fs:
            def fsb(name, shape, dt):
                return fs.enter_context(nc.sbuf_tensor(name, list(shape), dt))

            def fps(name, shape, dt):
                return fs.enter_context(nc.psum_tensor(name, list(shape), dt))
            gF, tgF = load_gain("g_ffn_t", g_ffn) if False else (None, None)
            gF = fsb("gF", [128, D], F32); tgF = Tok()
            kb.dma("sp", gF[:], AP(g_ffn, 0, [[0, 128], [1, D]]), writes=[tgF])
            gL = fsb("gL", [128, D], F32); tgL = Tok()
            kb.dma("sp", gL[:], AP(g_fin, 0, [[0, 128], [1, D]]), writes=[tgL])
            Wg = fsb("Wg", [128, 8, DFF], BF16); tWg = Tok()
            Wu = fsb("Wu", [128, 8, DFF], BF16); tWu = Tok()
            Wd = fsb("Wd", [128, NFT, D], BF16); tWd = Tok()
            lWg, lWu, lWd = [], [], []
            for k in range(8):
                t1_ = Tok(); lWg.append(t1_)
                kb.dma("pool", Wg[:, k, :], w_gate.ap()[k * 128:(k + 1) * 128, :], writes=[t1_])
                t2_ = Tok(); lWu.append(t2_)
                kb.dma("pool", Wu[:, k, :], w_up.ap()[k * 128:(k + 1) * 128, :], writes=[t2_])
            for m in range(NFT):
                t3_ = Tok(); lWd.append(t3_)
                kb.dma("pool", Wd[:, m, :], w_down.ap()[m * 128:(m + 1) * 128, :], writes=[t3_])
            NT = 256
            xt = [fsb("f_x%d" % i, [128, 2, D], F32) for i in range(2)]
            xr = Rot(xt)
            h2 = fsb("f_h2", [128, D], BF16); th2 = Tok()
            h2T = fsb("f_h2T", [128, 8, NT], BF16); th2T = Tok()
            aT = fsb("f_aT", [128, NFT, NT], BF16); taT = Tok()
            sg = Rot([fsb("f_sg%d" % i, [128, NT], F32) for i in range(2)])
            x2 = fsb("f_x2", [128, 2, D], F32); tx2 = Tok()
            oo = Rot([fsb("f_o%d" % i, [128, D], F32) for i in range(2)])
            junk = fsb("f_junk", [128, D], BF16); tjunk = Tok()
            ssr = Rot([fsb("f_ss%d" % i, [128, 4], F32) for i in range(4)])
            pT = Rot([fps("f_pT%d" % i, [128, 8, 128], BF16) for i in range(2)])
            pGU = Rot([fps("f_pGU%d" % i, [128, 2, NT], F32) for i in range(2)])
            pD = Rot([fps("f_pD%d" % i, [128, 512], F32) for i in range(2)])
            for tt in range(NTOK // NT):
                xa, tx = xr.next()
                kb.dma("sp", xa[:], AP(src_d, tt * NT * D, [[D, 128], [128 * D, 2], [1, D]]), reads=x1_toks, writes=[tx])
                for a in range(2):
                    ss, tss = ssr.next()
                    rmsnorm(xa[:, a, :], tx, gF, tgF, h2[:], th2, (junk, tjunk, ss, tss))
                    pt, tpt = pT.next()
                    for k in range(8):
                        kb.op("pe", lambda e, k=k: e.transpose(out=pt[:, k, :], in_=h2[:, k * 128:(k + 1) * 128],
                                                               identity=ident[:]),
                              reads=[th2, t_ident], writes=[tpt])
                    kb.op("act", lambda e: e.copy(out=h2T[:, :, a * 128:(a + 1) * 128], in_=pt[:]),
                          reads=[tpt], writes=[th2T])
                for m in range(NFT):
                    pg, tpg = pGU.next()
                    for k in range(8):
                        kb.op("pe", lambda e, k=k: e.matmul(pg[:, 0, :], lhsT=Wg[:, k, m * 128:(m + 1) * 128],
                                                            rhs=h2T[:, k, :], start=(k == 0), stop=(k == 7)),
                              reads=lWg + [th2T], writes=[tpg])
                    for k in range(8):
                        kb.op("pe", lambda e, k=k: e.matmul(pg[:, 1, :], lhsT=Wu[:, k, m * 128:(m + 1) * 128],
                                                            rhs=h2T[:, k, :], start=(k == 0), stop=(k == 7)),
                              reads=lWu + [th2T], writes=[tpg])
                    s_, ts_ = sg.next()
                    kb.op("act", lambda e: e.activation(out=s_[:], in_=pg[:, 0, :], func=AF.Silu),
                          reads=[tpg], writes=[ts_])
                    kb.op("dve", lambda e: e.tensor_tensor(out=aT[:, m, :], in0=s_[:], in1=pg[:, 1, :], op=ALU.mult),
                          reads=[ts_, tpg], writes=[taT])
                for a in range(2):
                    for hf in range(2):
                        pd, tpd = pD.next()
                        for m in range(NFT):
                            kb.op("pe", lambda e, m=m: e.matmul(pd[:], lhsT=aT[:, m, a * 128:(a + 1) * 128],
                                                                rhs=Wd[:, m, hf * 512:(hf + 1) * 512],
                                                                start=(m == 0), stop=(m == NFT - 1)),
                                  reads=[taT] + lWd, writes=[tpd])
                        kb.op("dve", lambda e: e.tensor_tensor(out=x2[:, a, hf * 512:(hf + 1) * 512], in0=pd[:],
                                                               in1=xa[:, a, hf * 512:(hf + 1) * 512], op=ALU.add),
                              reads=[tpd, tx], writes=[tx2])
                    ss, tss = ssr.next()
                    o_, to_ = oo.next()
                    rmsnorm(x2[:, a, :], tx2, gL, tgL, o_[:], to_, (junk, tjunk, ss, tss))
                    kb.dma("sp", out_d.ap()[tt * NT + a * 128: tt * NT + (a + 1) * 128, :], o_[:],
                           reads=[to_], writes=[t_out])


    t_out = Tok()
    try:
        phase_s5()
    except _Stop:
        return nc
    if debug == "s5":
        kb.finish([t_zg])
        es.close()
        return nc
    es_o = contextlib.ExitStack()
    oT = es_o.enter_context(nc.sbuf_tensor("oT", [128, 4, NTOK], BF16))
    es2 = contextlib.ExitStack()

    def p_(name, shape, dt):
        return es2.enter_context(nc.sbuf_tensor(name, list(shape), dt))
    BW = p_("BW", [128, 2, 8, 128], F32); M4 = p_("M4", [128, 128], F32); BnA = p_("BnA", [17, 8, 128], BF16)
    KcT = p_("KcT", [128, 1024], BF16); Vc_aug = p_("Vc_aug", [128, 8, 2, 65], BF16)
    Qall = p_("Qall", [128, NQ, 4, 128], BF16); gates = p_("gates", [128, NQ, 24], F32)
    KsN = p_("KsN", [128, NQ, 2, 128], BF16); VsN = p_("VsN", [128, NQ, 2, 2, 65], BF16)
    kb.barrier()
    try:
        phase_tables()
        kb.barrier()
        ck("tables")
        phase_compress()
        kb.barrier()
        ck("compress")
        phase1()
        kb.barrier()
        if debug == "dump":
            for nm, t_, tk_, dt_ in (("dbg_bw", BW, t_BW, F32), ("dbg_m4", M4, t_BW, F32), ("dbg_gates", gates, t_gates, F32),
                                     ("dbg_q", Qall, t_Q, BF16), ("dbg_ksn", KsN, t_KsN, BF16), ("dbg_vsn", VsN, t_VsN, BF16),
                                     ("dbg_bna", BnA, t_BnA, BF16), ("dbg_kct", KcT, t_Kc, BF16), ("dbg_vc", Vc_aug, t_Vc, BF16)):
                shp = list(t_.shape)
                n_ = int(np.prod(shp[1:]))
                dd_ = nc.dram_tensor(nm, [shp[0], n_], dt_, kind="ExternalOutput")
                kb.dma("sp", dd_.ap(), AP(t_, 0, [[n_, shp[0]], [1, n_]]), reads=[tk_], writes=[t_out])
        ck("phase1")
        phase2()
        kb.barrier()
        if debug == "dump":
            dd_ = nc.dram_tensor("dbg_oT", [128, 4 * NTOK], BF16, kind="ExternalOutput")
            kb.dma("sp", dd_.ap(), AP(oT, 0, [[4 * NTOK, 128], [1, 4 * NTOK]]), reads=[t_oT], writes=[t_out])
        ck("phase2")
    except _Stop:
        return nc
    es2.close()
    try:
        phase3a()
        kb.barrier()
        ck("phase3a")
    except _Stop:
        return nc
    es_o.close()
    phase_ffn(x1_d)
    kb.finish([t_out])
    es.close()
    return nc


_PROG = {}


def _bf(a):
    return np.ascontiguousarray(a).astype(ml_dtypes.bfloat16)


def make_in_maps(inp):
    x = np.asarray(inp["x"], np.float32)[0]
    xq = x.reshape(S // TQ, TQ, D)
    ident = np.eye(128, dtype=np.float32)
    f = lambda k: np.ascontiguousarray(np.asarray(inp[k], np.float32)[0])
    shared = {
        "norm_mix_g": np.asarray(inp["norm_mix_g"], np.float32).reshape(1, D),
        "norm_ffn_g": np.asarray(inp["norm_ffn_g"], np.float32).reshape(1, D),
        "norm_final_g": np.asarray(inp["norm_final_g"], np.float32).reshape(1, D),
        "w_ffn_gate": f("w_ffn_gate"), "w_ffn_up": f("w_ffn_up"), "w_ffn_down": f("w_ffn_down"),
        "w_in": f("w_in"),
        "ssm_a_re": f("ssm_a_re"), "ssm_a_im": f("ssm_a_im"),
        "ssm_log_dt": np.asarray(inp["ssm_log_dt"], np.float32).reshape(1, 32),
        "ssm_b_re": f("ssm_b_re"), "ssm_b_im": f("ssm_b_im"), "ssm_c_re": f("ssm_c_re"), "ssm_c_im": f("ssm_c_im"),
        "ssm_d": np.asarray(inp["ssm_d"], np.float32).reshape(1, 512),
        "ssm_w_glu": f("ssm_w_glu"),
        "ident_bf": _bf(ident), "ident_f": ident,
    }
    def bucket(n):
        n = np.maximum(n, 0)
        nf = np.maximum(n, 1).astype(np.float32)
        large = 16 + (np.log(nf / np.float32(16)) / np.float32(np.log(8.0)) * np.float32(16)).astype(np.int32)
        large = np.minimum(large, 31)
        return np.where(n < 16, n, large)
    dd = np.arange(256)
    ohf = (bucket(dd)[None, :] == np.arange(32)[:, None]).astype(np.float32)
    ohr = (bucket(255 - dd)[None, :] == np.arange(32)[:, None]).astype(np.float32)
    pp = np.arange(128)
    antij = (pp[:, None] + pp[None, :] == 127).astype(np.float32)
    m4 = np.where(pp[None, :] >= pp[:, None], np.float32(-30000.0), np.float32(0.0)).astype(np.float32)
    mm = np.arange(4096)
    wide = ((pp[:, None] % 64) == (2 * (mm[None, :] // 128) + (mm[None, :] % 128) // 64)).astype(np.float32)
    shared.update({
        "rel_bias": np.asarray(inp["rel_bias"], np.float32), "ohrev": ohr, "ohfwd": ohf, "antij": antij, "m4": m4,
        "cmp_w1_k": f("cmp_w1_k"), "cmp_w1_v": f("cmp_w1_v"), "cmp_w2_k": f("cmp_w2_k"), "cmp_w2_v": f("cmp_w2_v"),
        "cmp_pos_k": f("cmp_pos_k"), "cmp_pos_v": f("cmp_pos_v"), "wide64": _bf(wide),
        "w_out": f("w_out"), "w_up_ssm": f("w_up_ssm"), "w_up_nsa": f("w_up_nsa"), "x_all": x,
    })
    blk = np.arange(256)
    in_maps = []
    for c in range(NCORES):
        m = dict(shared)
        xp = np.zeros((NQ, 512, D), np.float32)
        vp = np.zeros((128, 64), np.float32)
        cand = np.zeros((NQ, 128, 256), np.float32); forced = np.zeros((NQ, 128, 256), np.float32)
        expn = np.zeros((NQ, 256, 128), np.float32); selc = np.zeros((NQ, 17, 1024), np.float32)
        keepb = np.zeros((128, 32), np.float32)
        for j in range(NQ):
            qb = 8 * j + c
            lo = 128 * qb - 512
            s0 = max(lo, 0)
            xp[j, s0 - lo:] = x[s0:128 * qb]
            for a in range(4):
                vp[:, j * 4 + a] = ((lo + a * 128 + pp) >= 0)
            cur = 2 * qb + (pp >= 64).astype(np.int64)
            valid = blk[None, :] <= cur[:, None]
            frc = valid & ((blk[None, :] == 0) | (blk[None, :] >= cur[:, None] - 1))
            cand[j] = valid & ~frc
            forced[j] = frc
            for hh in range(2):
                b_ = 2 * qb - 2 + hh
                if b_ >= 0:
                    expn[j, b_, 64 * hh:64 * hh + 64] = 1.0
            for mp in range(16):
                n = 8 * qb - 8 + (15 - mp)
                if 0 <= n < 1024:
                    selc[j, mp, n] = 1.0
            selc[j, 16, min(8 * qb + 8, 1024):] = -30000.0
            for bt in range(2):
                keepb[:, j * 2 + bt] = ((bt * 128 + pp) <= 2 * qb - 3)
        m["x_prev"] = xp.reshape(NQ * 512, D); m["vprev"] = vp
        m["cand"] = _bf(cand.reshape(NQ * 128, 256)); m["forced"] = _bf(forced.reshape(NQ * 128, 256))
        m["expn"] = _bf(expn.reshape(NQ * 256, 128)); m["selc"] = _bf(selc.reshape(NQ * 17, 1024))
        m["keepblk"] = keepb
        m["x_own"] = np.ascontiguousarray(xq[c::NCORES].reshape(NTOK, D))
        oh = np.zeros((128, 8), np.float32); oh[:, c] = 1.0
        m["onehot_r"] = oh
        in_maps.append(m)
    return in_maps


def kernel(**inp):
    if "main" not in _PROG:
        _PROG["main"] = build_program()
    nc = _PROG["main"]
    in_maps = make_in_maps(inp)
    res = run_bass_kernel_spmd(nc, in_maps, core_ids=list(range(NCORES)))
    out = np.empty((S // TQ, TQ, D), np.float32)
    for c in range(NCORES):
        out[c::NCORES] = np.asarray(res.results[c]["out"], np.float32).reshape(NQ, TQ, D)
    return out.reshape(1, S, D)
```

```python
import contextlib
import numpy as np
import ml_dtypes
import concourse.bass as bass
import concourse.mybir as mybir
from concourse.bass_utils import run_bass_kernel_spmd

F32 = mybir.dt.float32
BF16 = mybir.dt.bfloat16
AF = mybir.ActivationFunctionType
ALU = mybir.AluOpType
AX = mybir.AxisListType

BARRIERS = True
NCORES = 8
S = 16384
D = 1024
NQ = 16
TQ = 128
NTOK = NQ * TQ
DFF = 2816
NFT = DFF // 128
EPS = 1e-6
INC = 3864


def AP(t, off, dims):
    return bass.AP(t, off, [list(d) for d in dims])


class Tok:
    __slots__ = ("w", "r", "dw")

    def __init__(self):
        self.w = None
        self.r = {}
        self.dw = {}


class KB:
    def __init__(self, nc):
        self.nc = nc
        self.E = {"pe": nc.tensor, "act": nc.scalar, "dve": nc.vector, "pool": nc.gpsimd, "sp": nc.sync}
        self.csem = {e: nc.alloc_semaphore("c_" + e) for e in ("pe", "act", "dve", "pool")}
        self.ccnt = {e: 0 for e in self.csem}
        self.NDS = 28
        self.dsem = {e: [nc.alloc_semaphore("d_%s%d" % (e, i)) for i in range(self.NDS)]
                     for e in ("sp", "pool", "act")}
        self.dcnt = {e: 0 for e in self.dsem}
        self.seen = {e: {} for e in self.E}
        self.nwait = 0
        self.xsem = {}

    def collective(self, name, kind, in_ap, out_ap, reads, writes):
        self._deps("pool", reads, writes)
        sem = self.nc.alloc_semaphore("x_" + name)
        self.xsem[name] = sem
        ins = self.nc.gpsimd.collective_compute(kind, ALU.bypass, replica_groups=[list(range(NCORES))],
                                                ins=[in_ap], outs=[out_ap])
        ins.then_inc(sem)
        self._mark((("x", name), 1), reads, writes)

    def _wait(self, e, key, val):
        if key[0] == "c" and key[1] == e and e == "pe":
            return
        if self.seen[e].get(key, 0) >= val:
            return
        if key[0] == "x":
            sem = self.xsem[key[1]]
        else:
            sem = self.csem[key[1]] if key[0] == "c" else self.dsem[key[1]][key[2]]
        self.E[e].wait_ge(sem, val)
        self.seen[e][key] = val
        self.nwait += 1

    def _deps(self, e, reads, writes, dma_write=False):
        for t in reads:
            if t.w is not None:
                self._wait(e, *t.w)
            for k, v in t.dw.items():
                self._wait(e, k, v)
        for t in writes:
            if t.w is not None:
                self._wait(e, *t.w)
            if not dma_write:
                for k, v in t.dw.items():
                    self._wait(e, k, v)
            for k, v in t.r.items():
                self._wait(e, k, v)

    def _mark(self, me, reads, writes, dma_write=False):
        for t in reads:
            t.r[me[0]] = me[1]
        for t in writes:
            if dma_write:
                t.dw[me[0]] = me[1]
            else:
                t.w = me
                t.dw = {}
            t.r = {}

    def op(self, e, fn, reads=(), writes=()):
        self._deps(e, reads, writes)
        ins = fn(self.E[e])
        self.ccnt[e] += 1
        ins.then_inc(self.csem[e], 1)
        self._mark((("c", e), self.ccnt[e]), reads, writes)

    def dma(self, e, out, in_, reads=(), writes=(), **kw):
        self._deps(e, reads, writes, dma_write=True)
        i = self.dcnt[e]
        self.dcnt[e] += 1
        slot = i % self.NDS
        ins = self.E[e].dma_start(out=out, in_=in_, **kw)
        ins.then_inc(self.dsem[e][slot], 16)
        self._mark((("d", e, slot), 16 * (i // self.NDS + 1)), reads, writes, dma_write=True)

    def barrier(self):
        for e in self.E:
            for o in self.csem:
                if self.ccnt[o] > 0:
                    self._wait(e, ("c", o), self.ccnt[o])
            for q in self.dsem:
                n = self.dcnt[q]
                for slot in range(self.NDS):
                    cnt = (n - slot + self.NDS - 1) // self.NDS
                    if cnt > 0:
                        self._wait(e, ("d", q, slot), 16 * cnt)

    def finish(self, toks):
        for t in toks:
            if t.w is not None:
                self._wait("sp", *t.w)
            for k, v in t.dw.items():
                self._wait("sp", k, v)


class _Stop(Exception):
    pass


class Rot:
    def __init__(self, aps):
        self.aps = aps
        self.toks = [Tok() for _ in aps]
        self.i = 0

    def next(self):
        k = self.i % len(self.aps)
        self.i += 1
        return self.aps[k], self.toks[k]


def build_program(debug=None, debug_stop=None):
    nc = bass.Bass("TRN2", target_bir_lowering=False)
    kb = KB(nc)
    es = contextlib.ExitStack()

    def dram_in(name, shape, dt=F32):
        return nc.dram_tensor(name, list(shape), dt, kind="ExternalInput")

    x_own = dram_in("x_own", [NTOK, D])
    g_mix = dram_in("norm_mix_g", [1, D])
    g_ffn = dram_in("norm_ffn_g", [1, D])
    g_fin = dram_in("norm_final_g", [1, D])
    w_gate = dram_in("w_ffn_gate", [D, DFF])
    w_up = dram_in("w_ffn_up", [D, DFF])
    w_down = dram_in("w_ffn_down", [DFF, D])
    identb_d = dram_in("ident_bf", [128, 128], BF16)
    out_d = nc.dram_tensor("out", [NTOK, D], F32, kind="ExternalOutput")
    x1_d = nc.dram_tensor("x1_scr", [NTOK, D], F32, kind="ExternalOutput" if debug == "dump" else "Internal")

    w_in = dram_in("w_in", [D, INC])
    ssm_a_re = dram_in("ssm_a_re", [32, 64]); ssm_a_im = dram_in("ssm_a_im", [32, 64])
    ssm_log_dt = dram_in("ssm_log_dt", [1, 32])
    ssm_b_re = dram_in("ssm_b_re", [32, 64, 16]); ssm_b_im = dram_in("ssm_b_im", [32, 64, 16])
    ssm_c_re = dram_in("ssm_c_re", [32, 16, 64]); ssm_c_im = dram_in("ssm_c_im", [32, 16, 64])
    ssm_d = dram_in("ssm_d", [1, 512])
    w_glu = dram_in("ssm_w_glu", [512, 512])
    identf_d = dram_in("ident_f", [128, 128])
    onehot_d = dram_in("onehot_r", [128, 8])
    x_all = dram_in("x_all", [S, D])
    ksT_d = nc.dram_tensor("ksT_scr", [128, S], BF16, kind="Internal")
    kcR_d = nc.dram_tensor("kcR_scr", [128, S + 16], BF16, kind="Internal")
    vcR_d = nc.dram_tensor("vcR_scr", [128, S + 16], BF16, kind="Internal")
    vs_d = nc.dram_tensor("vs_scr", [S, 128], BF16, kind="Internal")
    zs_d = nc.dram_tensor("zs_scr", [128, 8192], F32, kind="Internal")
    ut_d = nc.dram_tensor("ut_scr", [128, 8192], BF16, kind="Internal")
    t_zsd = Tok(); t_utd = Tok()
    x_prev = dram_in("x_prev", [NQ * 512, D])
    vprev_d = dram_in("vprev", [128, 64])
    rel_bias_d = dram_in("rel_bias", [32, 8])
    ohrev_d = dram_in("ohrev", [32, 256]); ohfwd_d = dram_in("ohfwd", [32, 256])
    antij_d = dram_in("antij", [128, 128]); m4_d = dram_in("m4", [128, 128])
    tabR_d = nc.dram_tensor("tabR_scr", [8, 384], F32, kind="Internal")
    tabF_d = nc.dram_tensor("tabF_scr", [8, 416], F32, kind="Internal")
    cmp_w1_k = dram_in("cmp_w1_k", [2048, 256]); cmp_w1_v = dram_in("cmp_w1_v", [2048, 256])
    cmp_w2_k = dram_in("cmp_w2_k", [256, 64]); cmp_w2_v = dram_in("cmp_w2_v", [256, 64])
    cmp_pos_k = dram_in("cmp_pos_k", [32, 64]); cmp_pos_v = dram_in("cmp_pos_v", [32, 64])
    wide_d = dram_in("wide64", [128, 4096], BF16)
    keepblk_d = dram_in("keepblk", [128, 32])
    cand_d = dram_in("cand", [NQ * 128, 256], BF16); forced_d = dram_in("forced", [NQ * 128, 256], BF16)
    expn_d = dram_in("expn", [NQ * 256, 128], BF16)
    selc_d = dram_in("selc", [NQ * 17, 1024], BF16)
    acc_d = nc.dram_tensor("acc_scr", [NTOK, 512], F32, kind="ExternalOutput" if debug == "dump" else "Internal")
    w_out_d = dram_in("w_out", [D, D]); w_up_ssm = dram_in("w_up_ssm", [512, D]); w_up_nsa = dram_in("w_up_nsa", [512, D])
    t_accd = Tok(); x1_toks = []
    t_BW = Tok(); t_BnA = Tok(); t_Kc = Tok(); t_Vc = Tok(); t_Q = Tok(); t_gates = Tok(); t_KsN = Tok(); t_VsN = Tok(); t_oT = Tok()
    zg_d = nc.dram_tensor("zg_scr", [128, 4 * NTOK], BF16, kind="ExternalOutput" if debug in ("s5", "dump") else "Internal")
    t_zg = Tok()

    def sb(name, shape, dt):
        return es.enter_context(nc.sbuf_tensor(name, list(shape), dt))

    def ps(name, shape, dt):
        return es.enter_context(nc.psum_tensor(name, list(shape), dt))

    ident = sb("ident", [128, 128], BF16)
    t_ident = Tok()
    kb.dma("sp", ident[:], identb_d.ap(), writes=[t_ident])

    identF = sb("identF", [128, 128], F32)
    t_identF = Tok()
    kb.dma("sp", identF[:], identf_d.ap(), writes=[t_identF])
    epsT = sb("epsT", [128, 1], F32)
    t_eps = Tok()
    kb.op("dve", lambda e: e.memset(epsT[:], EPS), writes=[t_eps])

    def load_gain(name, src):
        t = sb(name, [128, D], F32)
        tk = Tok()
        kb.dma("sp", t[:], AP(src, 0, [[0, 128], [1, D]]), writes=[tk])
        return t, tk

    def rmsnorm(xap, tx, gt, tg, hout, th, scr):
        junk, tjunk, ss, tss = scr
        kb.op("act", lambda e: e.activation(out=junk[:], in_=xap, func=AF.Square, accum_out=ss[:, 0:1]),
              reads=[tx], writes=[tjunk, tss])
        kb.op("act", lambda e: e.activation(out=ss[:, 1:2], in_=ss[:, 0:1], func=AF.Sqrt, scale=1.0 / D,
                                            bias=epsT[:, 0:1]), reads=[tss, t_eps], writes=[tss])
        kb.op("dve", lambda e: e.reciprocal(out=ss[:, 2:3], in_=ss[:, 1:2]), reads=[tss], writes=[tss])
        kb.op("dve", lambda e: e.scalar_tensor_tensor(out=hout, in0=xap, scalar=ss[:, 2:3], in1=gt[:],
                                                      op0=ALU.mult, op1=ALU.mult),
              reads=[tx, tss, tg], writes=[th])

    def V(fn, r=(), w=()):
        kb.op("dve", fn, r, w)

    def A(fn, r=(), w=()):
        kb.op("act", fn, r, w)

    def P(fn, r=(), w=()):
        kb.op("pe", fn, r, w)

    def G(fn, r=(), w=()):
        kb.op("pool", fn, r, w)

    def load_norm_T(src, row0, ntile, X, ret_x=False):
        xb, txb = X["x"].next()
        kb.dma("sp", xb[:, 0:ntile, :], AP(src, row0 * D, [[D, 128], [128 * D, ntile], [1, D]]), writes=[txb])
        hT, thT = X["hT"].next()
        for a in range(ntile):
            ss, tss = X["ss"].next()
            hb, thb = X["h"].next()
            rmsnorm(xb[:, a, :], txb, X["g"], X["tg"], hb[:], thb, (X["junk"], X["tjunk"], ss, tss))
            pt, tpt = X["pT"].next()
            for k in range(8):
                P(lambda e, k=k: e.transpose(out=pt[:, k, :], in_=hb[:, k * 128:(k + 1) * 128], identity=ident[:]),
                  [thb, t_ident], [tpt])
            V(lambda e: e.tensor_copy(hT[:, :, a * 128:(a + 1) * 128], pt[:]), [tpt], [thT])
        if ret_x:
            return hT, thT, xb, txb
        return hT, thT

    nctx = [0]

    def norm_ctx(st, xtiles, gsrc):
        nctx[0] += 1
        pre = "c%d" % nctx[0]

        def a_(name, shape, dt):
            return st.enter_context(nc.sbuf_tensor(pre + name, list(shape), dt))
        X = {}
        X["x"] = Rot([a_("n_x%d" % i, [128, xtiles, D], F32) for i in range(2)])
        X["hT"] = Rot([a_("n_hT%d" % i, [128, 8, 128 * xtiles], BF16) for i in range(2)])
        X["h"] = Rot([a_("n_h%d" % i, [128, D], BF16) for i in range(2)])
        X["ss"] = Rot([a_("n_ss%d" % i, [128, 4], F32) for i in range(4)])
        X["junk"] = a_("n_junk", [128, D], BF16)
        X["tjunk"] = Tok()
        X["g"] = a_("n_g", [128, D], F32)
        X["tg"] = Tok()
        kb.dma("sp", X["g"][:], AP(gsrc, 0, [[0, 128], [1, D]]), writes=[X["tg"]])
        X["pT"] = Rot([st.enter_context(nc.psum_tensor(pre + "n_pT%d" % i, [128, 8, 128], BF16)) for i in range(2)])
        return X

    def ck(name):
        if debug_stop == name:
            raise _Stop()

    kv_toks = []

    def phase_all(UT, tUT, Zs, tZ, Eg, tEg, z_matmuls, recur, zv):
        UTW = 8192
        ZW = 8192
        with contextlib.ExitStack() as sa:
            X = norm_ctx(sa, 2, g_mix)
            WA = sa.enter_context(nc.sbuf_tensor("WA", [128, 8, 1024], BF16)); tWA = Tok()
            for k in range(8):
                for (c0, s0, n) in ((0, 0, 512), (512, 1280, 128), (640, 1024, 128), (768, 1152, 128), (896, 1408, 128)):
                    kb.dma("pool", WA[:, k, c0:c0 + n], AP(w_in, k * 128 * INC + s0, [[INC, 128], [1, n]]), writes=[tWA])
            pP = Rot([sa.enter_context(nc.psum_tensor("a_pP%d" % i, [128, 512], F32)) for i in range(2)])
            fmS = Rot([sa.enter_context(nc.sbuf_tensor("a_fm%d" % i, [128, 256], BF16)) for i in range(3)])
            vsS = Rot([sa.enter_context(nc.sbuf_tensor("a_vs%d" % i, [128, 128], BF16)) for i in range(2)])
            zpad = sa.enter_context(nc.sbuf_tensor("a_zpad", [128, 16], BF16)); tzp = Tok()
            V(lambda e: e.memset(zpad[:], 0.0), [], [tzp])
            for dst in (kcR_d, vcR_d):
                tk_ = Tok(); kv_toks.append(tk_)
                kb.dma("pool", AP(dst, S, [[S + 16, 128], [1, 16]]), zpad[:], reads=[tzp], writes=[tk_])
            n_ev = 0
            for cc in range(S // 256):
                sc, q = cc // 8, cc % 8
                hT, thT = load_norm_T(x_all, cc * 256, 2, X)
                for T in range(4):
                    pp, tpp = pP.next()
                    for k in range(8):
                        P(lambda e, k=k: e.matmul(pp[:, 0:256], lhsT=WA[:, k, T * 128:(T + 1) * 128], rhs=hT[:, k, :],
                                                  start=(k == 0), stop=(k == 7)), [tWA, thT], [tpp])
                    o_ = AP(UT, T * 2048 + q * 32, [[UTW, 128], [256, 8], [1, 32]])
                    i_ = AP(pp, 0, [[512, 128], [1, 8], [8, 32]])
                    n_ev += 1
                    if n_ev % 2 == 0:
                        V(lambda e: e.tensor_copy(o_, i_), [tpp], [tUT])
                    else:
                        A(lambda e: e.copy(out=o_, in_=i_), [tpp], [tUT])
                for (c0, dst) in ((512, ksT_d), (640, kcR_d), (768, vcR_d)):
                    pp, tpp = pP.next()
                    for k in range(8):
                        P(lambda e, k=k: e.matmul(pp[:, 0:256], lhsT=WA[:, k, c0:c0 + 128], rhs=hT[:, k, :],
                                                  start=(k == 0), stop=(k == 7)), [tWA, thT], [tpp])
                    f_, tf_ = fmS.next()
                    n_ev += 1
                    if n_ev % 2 == 0:
                        V(lambda e: e.tensor_copy(f_[:], pp[:, 0:256]), [tpp], [tf_])
                    else:
                        A(lambda e: e.copy(out=f_[:], in_=pp[:, 0:256]), [tpp], [tf_])
                    tk_ = Tok(); kv_toks.append(tk_)
                    W_ = dst.shape[1]
                    kb.dma("pool", AP(dst, cc * 256, [[W_, 128], [1, 256]]), f_[:], reads=[tf_], writes=[tk_])
                for a in range(2):
                    pp, tpp = pP.next()
                    for k in range(8):
                        P(lambda e, k=k: e.matmul(pp[:, 0:128], lhsT=hT[:, k, a * 128:(a + 1) * 128], rhs=WA[:, k, 896:1024],
                                                  start=(k == 0), stop=(k == 7)), [tWA, thT], [tpp])
                    v_, tv_ = vsS.next()
                    V(lambda e: e.tensor_copy(v_[:], pp[:, 0:128]), [tpp], [tv_])
                    tk_ = Tok(); kv_toks.append(tk_)
                    kb.dma("pool", AP(vs_d, (cc * 256 + a * 128) * 128, [[128, 128], [1, 128]]), v_[:],
                           reads=[tv_], writes=[tk_])
                if q == 7:
                    z_matmuls()
                    recur()
                    for ri in range(2):
                        d_ = AP(Eg, ri * 256 + 2 * sc, [[4096, 128], [16, 16], [1, 2], [512, 8]])
                        s_ = AP(Zs, ri * 4096 + 15, [[ZW, 128], [256, 16], [128, 2], [16, 8]])
                        V(lambda e, d_=d_, s_=s_: e.tensor_copy(d_, s_), [tZ], [tEg])
            kb.barrier()

    def phase_s5():
        PI = float(np.pi)
        with contextlib.ExitStack() as s5:
            def a_(name, shape, dt):
                return s5.enter_context(nc.sbuf_tensor(name, list(shape), dt))
            UT = a_("UT", [128, 4, 8, 256], BF16); tUT = Tok()
            UTW = 4 * 8 * 256
            Wgl = a_("s5_Wgl", [128, 4, 512], BF16); tWgl = Tok()
            for k4 in range(4):
                kb.dma("pool", Wgl[:, k4, :], AP(w_glu, k4 * 128 * 512, [[512, 128], [1, 512]]), writes=[tWgl])
            ck("u")
            NS = 40
            spt = a_("s5_sp", [128, NS, 16], F32); tS = Tok()
            SPW = NS * 16

            def sl(i):
                return spt[:, i, :]

            def slb(i, n=16):
                return AP(spt, i * 16, [[SPW, 128], [1, 16], [0, n]])
            (aR, aI, DT, XR, ANG, MAG, T1, T2, SINV, COSV, CFR, CFI, DEN, M1, T3, T4) = range(16)
            PWR, PWI = 16, 25
            kb.dma("sp", sl(aR), AP(ssm_a_re, 0, [[1, 128], [128, 16]]), writes=[tS], allow_slow_non_contiguous=True)
            kb.dma("sp", sl(aI), AP(ssm_a_im, 0, [[1, 128], [128, 16]]), writes=[tS], allow_slow_non_contiguous=True)
            for g2 in range(2):
                kb.dma("sp", spt[64 * g2:64 * g2 + 64, DT, :], AP(ssm_log_dt, g2, [[0, 64], [2, 16]]), writes=[tS],
                       allow_slow_non_contiguous=True)
            Br = a_("s5_Br", [128, 16, 16], F32); Bi = a_("s5_Bi", [128, 16, 16], F32)
            Cr = a_("s5_Cr", [128, 16, 16], F32); Ci = a_("s5_Ci", [128, 16, 16], F32)
            BBr = a_("s5_BBr", [128, 16, 16], F32); BBi = a_("s5_BBi", [128, 16, 16], F32)
            TA = a_("s5_TA", [128, 16, 16], F32); TB = a_("s5_TB", [128, 16, 16], F32)
            TRe = a_("s5_TRe", [128, 16, 16], F32); TIm = a_("s5_TIm", [128, 16, 16], F32)
            dcol = a_("s5_dcol", [128, 4], F32)
            kb.dma("sp", Br[:], AP(ssm_b_re, 0, [[16, 128], [2048, 16], [1, 16]]), writes=[tS])
            kb.dma("sp", Bi[:], AP(ssm_b_im, 0, [[16, 128], [2048, 16], [1, 16]]), writes=[tS])
            tCs = []
            for g2 in range(2):
                for pair in range(16):
                    for (dst_, src_) in ((Cr, ssm_c_re), (Ci, ssm_c_im)):
                        tk_ = Tok(); tCs.append(tk_)
                        kb.dma("sp", dst_[64 * g2:64 * g2 + 64, pair, :],
                               AP(src_, (2 * pair + g2) * 1024, [[1, 64], [64, 16]]),
                               writes=[tk_], allow_slow_non_contiguous=True)
            jn = a_("s5_join", [128, 2], F32)
            V(lambda e: e.memset(jn[:], 0.0), tCs, [tS])
            kb.dma("sp", dcol[:], AP(ssm_d, 0, [[1, 128], [128, 4]]), writes=[tS], allow_slow_non_contiguous=True)

            def vv(out, a, b, op):
                V(lambda e: e.tensor_tensor(out=out, in0=a, in1=b, op=op), [tS], [tS])

            def vs(out, a, s1, op0, s2=None, op1=None):
                if op1 is None:
                    V(lambda e: e.tensor_scalar(out=out, in0=a, scalar1=s1, scalar2=None, op0=op0), [tS], [tS])
                else:
                    V(lambda e: e.tensor_scalar(out=out, in0=a, scalar1=s1, scalar2=s2, op0=op0, op1=op1), [tS], [tS])

            def cmul(orr, oi, ar, ai, br, bi, t1, t2, sign=1.0):
                vv(t1, ar, br, ALU.mult); vv(t2, ai, bi, ALU.mult); vv(orr, t1, t2, ALU.subtract)
                vv(t1, ar, bi, ALU.mult); vv(t2, ai, br, ALU.mult); vv(oi, t1, t2, ALU.add)

            A(lambda e: e.activation(out=sl(DT), in_=sl(DT), func=AF.Exp), [tS], [tS])
            vv(sl(XR), sl(aR), sl(DT), ALU.mult)
            vv(sl(ANG), sl(aI), sl(DT), ALU.mult)
            A(lambda e: e.activation(out=sl(MAG), in_=sl(XR), func=AF.Exp), [tS], [tS])

            def sin_of(dst, shift):
                vs(sl(T1), sl(ANG), shift, ALU.add)
                vs(sl(T3), sl(T1), 1.0, ALU.mult)
                for m in (1, 3, 5, 7, 9):
                    vs(sl(T2), sl(T1), m * PI, ALU.is_ge, -2.0 * PI, ALU.mult)
                    vv(sl(T3), sl(T3), sl(T2), ALU.add)
                A(lambda e: e.activation(out=sl(dst), in_=sl(T3), func=AF.Sin), [tS], [tS])
            sin_of(SINV, 0.0)
            sin_of(COSV, PI / 2)
            V(lambda e: e.memset(sl(PWR + 0), 1.0), [tS], [tS])
            V(lambda e: e.memset(sl(PWI + 0), 0.0), [tS], [tS])
            vv(sl(PWR + 1), sl(MAG), sl(COSV), ALU.mult)
            vv(sl(PWI + 1), sl(MAG), sl(SINV), ALU.mult)
            for k in range(2, 9):
                cmul(sl(PWR + k), sl(PWI + k), sl(PWR + k - 1), sl(PWI + k - 1), sl(PWR + 1), sl(PWI + 1), sl(T1), sl(T2))
            SQ = a_("s5_sq", [128, 10, 16], F32)
            V(lambda e: e.tensor_copy(SQ[:, 0, :], sl(PWR + 8)), [tS], [tS])
            V(lambda e: e.tensor_copy(SQ[:, 5, :], sl(PWI + 8)), [tS], [tS])
            for q in range(1, 5):
                cmul(SQ[:, q, :], SQ[:, 5 + q, :], SQ[:, q - 1, :], SQ[:, 4 + q, :], SQ[:, q - 1, :], SQ[:, 4 + q, :],
                     sl(T1), sl(T2))
            Q128 = a_("s5_q128", [128, 18, 16], F32)
            V(lambda e: e.memset(Q128[:, 0, :], 1.0), [tS], [tS])
            V(lambda e: e.memset(Q128[:, 9, :], 0.0), [tS], [tS])
            for r in range(1, 9):
                cmul(Q128[:, r, :], Q128[:, 9 + r, :], Q128[:, r - 1, :], Q128[:, 8 + r, :], SQ[:, 4, :], SQ[:, 9, :],
                     sl(T1), sl(T2))
            vv(sl(T1), sl(aR), sl(aR), ALU.mult); vv(sl(T2), sl(aI), sl(aI), ALU.mult)
            vv(sl(DEN), sl(T1), sl(T2), ALU.add)
            V(lambda e: e.reciprocal(out=sl(DEN), in_=sl(DEN)), [tS], [tS])
            vs(sl(M1), sl(PWR + 1), -1.0, ALU.add)
            vv(sl(T1), sl(M1), sl(aR), ALU.mult); vv(sl(T2), sl(PWI + 1), sl(aI), ALU.mult)
            vv(sl(T3), sl(T1), sl(T2), ALU.add); vv(sl(CFR), sl(T3), sl(DEN), ALU.mult)
            vv(sl(T1), sl(PWI + 1), sl(aR), ALU.mult); vv(sl(T2), sl(M1), sl(aI), ALU.mult)
            vv(sl(T3), sl(T1), sl(T2), ALU.subtract); vv(sl(CFI), sl(T3), sl(DEN), ALU.mult)
            cmul(BBr[:], BBi[:], slb(CFR), slb(CFI), Br[:], Bi[:], TA[:], TB[:])

            def scatter(dst_t, pair_stride, base_off, src_t, neg=False):
                for g2 in range(2):
                    d_ = AP(dst_t, (64 * g2) * dst_t_pstride[id(dst_t)] + base_off + 16 * g2,
                            [[dst_t_pstride[id(dst_t)], 64], [2 * pair_stride, 8], [pair_stride + 32, 2], [1, 16]])
                    s_ = AP(src_t, (64 * g2) * 256, [[256, 64], [32, 8], [16, 2], [1, 16]])
                    if neg:
                        V(lambda e: e.tensor_scalar(out=d_, in0=s_, scalar1=-1.0, scalar2=None, op0=ALU.mult), [tS], [tS])
                    else:
                        V(lambda e: e.tensor_copy(d_, s_), [tS], [tS])
            dst_t_pstride = {}
            SPt = a_("s5_SPt", [128, 16, 2, 64], BF16); dst_t_pstride[id(SPt)] = 16 * 2 * 64
            V(lambda e: e.memset(SPt[:], 0.0), [tS], [tS])
            Zs = a_("s5_Zs", [128, 2, 16, 256], F32); tZ = Tok()
            ZW = 2 * 16 * 256
            Eg = a_("s5_Eg", [128, 8, 2, 256], F32); tEg = Tok()
            RT = a_("s5_rt", [128, 4, 256], F32)

            def zv(ri, bb):
                return AP(Zs, ri * 4096 + bb, [[ZW, 128], [256, 16], [16, 16]])

            def lb(t_, idx):
                return AP(t_, idx * 16, [[t_pstride[id(t_)], 128], [1, 16], [0, 16]])
            t_pstride = {id(SQ): 160, id(Q128): 288, id(spt): SPW}

            def rt(i):
                return AP(RT, i * 256, [[1024, 128], [16, 16], [1, 16]])

            def zz(out, a, b, op, r, w):
                V(lambda e: e.tensor_tensor(out=out, in0=a, in1=b, op=op), r, w)
            L8r, L8i = lb(SQ, 0), lb(SQ, 5)
            with contextlib.ExitStack() as sz:
                Wz = sz.enter_context(nc.sbuf_tensor("s5_Wz", [128, 4, 2, 8, 2, 128], BF16)); tWz = Tok()
                pz = Rot([sz.enter_context(nc.psum_tensor("s5_pz%d" % i, [128, 4, 128], BF16)) for i in range(2)])
                pZ = Rot([sz.enter_context(nc.psum_tensor("s5_pZ%d" % i, [128, 256], F32)) for i in range(2)])
                G(lambda e: e.memset(Wz[:], 0.0), [], [tWz])
                for k in range(8):
                    cmul(TRe[:], TIm[:], BBr[:], BBi[:], slb(PWR + k), slb(PWI + k), TA[:], TB[:])
                    scatter(SPt, 128, 0, TRe); scatter(SPt, 128, 64, TIm)
                    for quad in range(8):
                        T, h2 = quad // 2, quad % 2
                        pz_, tpz = pz.next()
                        for pq in range(2):
                            for ri in range(2):
                                P(lambda e, pq=pq, ri=ri: e.transpose(out=pz_[64 * h2:64 * h2 + 64, pq * 2 + ri, :],
                                                                      in_=SPt[:, 2 * quad + pq, ri, :], identity=ident[:]),
                                  [tS, t_ident], [tpz])
                        o_ = AP(Wz, (64 * h2) * 16384 + T * 4096 + (7 - k) * 256,
                                [[16384, 64], [2048, 2], [128, 2], [1, 128]])
                        i_ = AP(pz_, (64 * h2) * 512, [[512, 64], [256, 2], [128, 2], [1, 128]])
                        A(lambda e: e.copy(out=o_, in_=i_), [tpz], [tWz])
                def z_matmuls():
                    for pair in range(16):
                        quad, pq = pair // 2, pair % 2
                        T, h2 = quad // 2, quad % 2
                        for ri in range(2):
                            pZ_, tpZ = pZ.next()
                            for ip in range(8):
                                P(lambda e, ip=ip: e.matmul(pZ_[:], lhsT=Wz[64 * h2:64 * h2 + 64, T, pq, ip, ri, :],
                                                            rhs=UT[64 * h2:64 * h2 + 64, T, ip, :],
                                                            start=(ip == 0), stop=(ip == 7)), [tWz, tUT], [tpZ])
                            if ri == 0:
                                V(lambda e: e.tensor_copy(Zs[:, ri, pair, :], pZ_[:]), [tpZ], [tZ])
                            else:
                                A(lambda e: e.copy(out=Zs[:, ri, pair, :], in_=pZ_[:]), [tpZ], [tZ])
                def recur():
                    for bb in range(1, 16):
                        zz(rt(0), zv(0, bb - 1), L8r, ALU.mult, [tZ, tS], [tZ])
                        zz(rt(1), zv(1, bb - 1), L8i, ALU.mult, [tZ, tS], [tZ])
                        zz(rt(2), zv(1, bb - 1), L8r, ALU.mult, [tZ, tS], [tZ])
                        zz(rt(3), zv(0, bb - 1), L8i, ALU.mult, [tZ, tS], [tZ])
                        zz(zv(0, bb), zv(0, bb), rt(0), ALU.add, [tZ], [tZ])
                        zz(zv(0, bb), zv(0, bb), rt(1), ALU.subtract, [tZ], [tZ])
                        zz(zv(1, bb), zv(1, bb), rt(2), ALU.add, [tZ], [tZ])
                        zz(zv(1, bb), zv(1, bb), rt(3), ALU.add, [tZ], [tZ])
                phase_all(UT, tUT, Zs, tZ, Eg, tEg, z_matmuls, recur, zv)
                with contextlib.ExitStack() as su:
                    X = norm_ctx(su, 2, g_mix)
                    WU = su.enter_context(nc.sbuf_tensor("WU", [128, 8, 512], BF16)); tWU = Tok()
                    for k in range(8):
                        kb.dma("pool", WU[:, k, :], AP(w_in, k * 128 * INC, [[INC, 128], [1, 512]]), writes=[tWU])
                    pP = Rot([su.enter_context(nc.psum_tensor("u_pP%d" % i, [128, 512], F32)) for i in range(2)])
                    for st in range(8):
                        hT, thT = load_norm_T(x_own, st * 256, 2, X)
                        for T in range(4):
                            pp, tpp = pP.next()
                            for k in range(8):
                                P(lambda e, k=k: e.matmul(pp[:, 0:256], lhsT=WU[:, k, T * 128:(T + 1) * 128], rhs=hT[:, k, :],
                                                          start=(k == 0), stop=(k == 7)), [tWU, thT], [tpp])
                            o_ = AP(UT, T * 2048 + st * 32, [[UTW, 128], [256, 8], [1, 32]])
                            i_ = AP(pp, 0, [[512, 128], [1, 8], [8, 32]])
                            if T % 2 == 0:
                                V(lambda e: e.tensor_copy(o_, i_), [tpp], [tUT])
                            else:
                                A(lambda e: e.copy(out=o_, in_=i_), [tpp], [tUT])
                    kb.barrier()
                z_matmuls()
                recur()
                kb.barrier()
            ck("z")
            if BARRIERS: kb.barrier()
            B4 = a_("s5_B4", [128, 16, 2, 64], BF16); dst_t_pstride[id(B4)] = 16 * 2 * 64
            Wc0 = a_("s5_Wc0", [128, 16, 2, 64], BF16); dst_t_pstride[id(Wc0)] = 16 * 2 * 64
            Wc = a_("s5_Wc", [128, 16, 8, 2, 64], BF16); dst_t_pstride[id(Wc)] = 16 * 8 * 2 * 64
            Kblk = a_("s5_Kblk", [128, 4, 8, 128], BF16)
            for t_ in (B4, Wc0, Kblk):
                V(lambda e, t_=t_: e.memset(t_[:], 0.0), [tS], [tS])
            G(lambda e: e.memset(Wc[:], 0.0), [tS], [tS])
            scatter(B4, 128, 0, BBr); scatter(B4, 128, 64, BBi)
            for k in range(0, 9):
                cmul(TRe[:], TIm[:], Cr[:], Ci[:], slb(PWR + k), slb(PWI + k), TA[:], TB[:])
                if k == 0:
                    scatter(Wc0, 128, 0, TRe); scatter(Wc0, 128, 64, TIm, neg=True)
                else:
                    scatter(Wc, 1024, (k - 1) * 128, TRe); scatter(Wc, 1024, (k - 1) * 128 + 64, TIm, neg=True)
            identf = a_("s5_identf", [128, 128], F32)
            kb.dma("sp", identf[:], identf_d.ap(), writes=[tS])
            ck("tab")
            with contextlib.ExitStack() as sk:
                pK = Rot([sk.enter_context(nc.psum_tensor("s5_pK%d" % i, [128, 128], F32)) for i in range(2)])
                for T in range(4):
                    for tau in range(8):
                        pk, tpk = pK.next()
                        for h2 in range(2):
                            for pq in range(2):
                                pair = 2 * (2 * T + h2) + pq
                                for ri in range(2):
                                    rhs_ = Wc0[:, pair, ri, :] if tau == 0 else Wc[:, pair, tau - 1, ri, :]
                                    P(lambda e, h2=h2, pair=pair, ri=ri, rhs_=rhs_, pq=pq: e.matmul(
                                        pk[64 * h2:64 * h2 + 64, 64 * h2:64 * h2 + 64], lhsT=B4[:, pair, ri, :], rhs=rhs_,
                                        start=(pq == 0 and ri == 0), stop=(pq == 1 and ri == 1)), [tS], [tpk])
                        for h2 in range(2):
                            sl_ = slice(64 * h2, 64 * h2 + 64)
                            if tau == 0:
                                V(lambda e, sl_=sl_: e.scalar_tensor_tensor(out=Kblk[sl_, T, 0, sl_], in0=identf[sl_, sl_],
                                                                            scalar=dcol[sl_, T:T + 1], in1=pk[sl_, sl_],
                                                                            op0=ALU.mult, op1=ALU.add), [tpk, tS], [tS])
                            else:
                                V(lambda e, sl_=sl_: e.tensor_copy(Kblk[sl_, T, tau, sl_], pk[sl_, sl_]), [tpk], [tS])
            ck("kblk")
            if BARRIERS: kb.barrier()
            Xp = a_("s5_Xp", [128, 2, 16, 256], BF16); tXp = Tok()
            with contextlib.ExitStack() as sc:
                def c_(name, shape, dt):
                    return sc.enter_context(nc.sbuf_tensor(name, list(shape), dt))
                Dd = c_("s5_D", [128, 9, 2, 256], F32); tD = Tok()
                Gg = c_("s5_G", [128, 2, 16, 16], F32)
                Cw = c_("s5_Cw", [128, 2, 256], F32)
                Cn = c_("s5_Cn", [128, 2, 256], F32)
                oh = c_("s5_oh", [128, 8], F32)
                kb.dma("sp", oh[:], onehot_d.ap(), writes=[tD])

                def dv(r, ri):
                    return AP(Dd, (r * 2 + ri) * 256, [[9 * 512, 128], [16, 16], [1, 16]])

                def ev(r, ri):
                    return AP(Eg, (r * 2 + ri) * 256, [[8 * 512, 128], [16, 16], [1, 16]])
                L128r, L128i = lb(SQ, 4), lb(SQ, 9)
                for ri in range(2):
                    V(lambda e, ri=ri: e.memset(dv(0, ri), 0.0), [], [tD])
                for r in range(8):
                    zz(rt(0), dv(r, 0), L128r, ALU.mult, [tD, tS], [tD]); zz(rt(1), dv(r, 1), L128i, ALU.mult, [tD, tS], [tD])
                    zz(rt(2), dv(r, 1), L128r, ALU.mult, [tD, tS], [tD]); zz(rt(3), dv(r, 0), L128i, ALU.mult, [tD, tS], [tD])
                    zz(dv(r + 1, 0), rt(0), rt(1), ALU.subtract, [tD], [tD])
                    zz(dv(r + 1, 0), dv(r + 1, 0), ev(r, 0), ALU.add, [tD, tEg], [tD])
                    zz(dv(r + 1, 1), rt(2), rt(3), ALU.add, [tD], [tD])
                    zz(dv(r + 1, 1), dv(r + 1, 1), ev(r, 1), ALU.add, [tD, tEg], [tD])
                def gv(ri, j):
                    return Gg[:, ri, :, j]

                def d8(ri, j):
                    return AP(Dd, (8 * 2 + ri) * 256 + j, [[9 * 512, 128], [16, 16]])
                Lkr, Lki = Q128[:, 8, :], Q128[:, 17, :]
                V(lambda e: e.memset(Gg[:, :, :, 0], 0.0), [], [tD])
                for j in range(15):
                    zz(sl(T1), gv(0, j), Lkr, ALU.mult, [tD, tS], [tS]); zz(sl(T2), gv(1, j), Lki, ALU.mult, [tD, tS], [tS])
                    zz(sl(T3), gv(1, j), Lkr, ALU.mult, [tD, tS], [tS]); zz(sl(T4), gv(0, j), Lki, ALU.mult, [tD, tS], [tS])
                    zz(sl(T1), sl(T1), sl(T2), ALU.subtract, [tS], [tS])
                    zz(gv(0, j + 1), sl(T1), d8(0, j), ALU.add, [tS, tD], [tD])
                    zz(sl(T3), sl(T3), sl(T4), ALU.add, [tS], [tS])
                    zz(gv(1, j + 1), sl(T3), d8(1, j), ALU.add, [tS, tD], [tD])
                Gr = AP(Gg, 0, [[512, 128], [16, 16], [1, 16]]); Gi = AP(Gg, 256, [[512, 128], [16, 16], [1, 16]])
                cw = [AP(Cw, ri * 256, [[512, 128], [16, 16], [1, 16]]) for ri in range(2)]
                for ri in range(2):
                    V(lambda e, ri=ri: e.memset(cw[ri], 0.0), [], [tD])
                for r in range(8):
                    qr, qi = lb(Q128, r), lb(Q128, 9 + r)
                    zz(rt(0), Gr, qr, ALU.mult, [tD, tS], [tD]); zz(rt(1), Gi, qi, ALU.mult, [tD, tS], [tD])
                    zz(rt(2), Gi, qr, ALU.mult, [tD, tS], [tD]); zz(rt(3), Gr, qi, ALU.mult, [tD, tS], [tD])
                    zz(rt(0), rt(0), rt(1), ALU.subtract, [tD], [tD]); zz(rt(0), rt(0), dv(r, 0), ALU.add, [tD], [tD])
                    zz(rt(2), rt(2), rt(3), ALU.add, [tD], [tD]); zz(rt(2), rt(2), dv(r, 1), ALU.add, [tD], [tD])
                    for ri, src in ((0, rt(0)), (1, rt(2))):
                        V(lambda e, ri=ri, src=src: e.scalar_tensor_tensor(out=cw[ri], in0=src, scalar=oh[:, r:r + 1],
                                                                           in1=cw[ri], op0=ALU.mult, op1=ALU.add),
                          [tD], [tD])
                cn = [AP(Cn, ri * 256, [[512, 128], [16, 16], [1, 16]]) for ri in range(2)]

                def xpv(ri, bb):
                    return AP(Xp, ri * 4096 + bb, [[ZW, 128], [256, 16], [16, 16]])
                for bb in range(16):
                    for ri in range(2):
                        if bb == 0:
                            V(lambda e, ri=ri: e.tensor_copy(xpv(ri, 0), cw[ri]), [tD], [tXp])
                        else:
                            zz(xpv(ri, bb), zv(ri, bb - 1), cw[ri], ALU.add, [tZ, tD], [tXp])
                    if bb < 15:
                        zz(rt(0), cw[0], L8r, ALU.mult, [tD, tS], [tD]); zz(rt(1), cw[1], L8i, ALU.mult, [tD, tS], [tD])
                        zz(rt(2), cw[1], L8r, ALU.mult, [tD, tS], [tD]); zz(rt(3), cw[0], L8i, ALU.mult, [tD, tS], [tD])
                        zz(cn[0], rt(0), rt(1), ALU.subtract, [tD], [tD]); zz(cn[1], rt(2), rt(3), ALU.add, [tD], [tD])
                        for ri in range(2):
                            V(lambda e, ri=ri: e.tensor_copy(cw[ri], cn[ri]), [tD], [tD])
            ck("car")
            if BARRIERS: kb.barrier()
            with contextlib.ExitStack() as sy_:
                def y_(name, shape, dt):
                    return sy_.enter_context(nc.sbuf_tensor(name, list(shape), dt))
                zT = y_("s5_zT", [128, 4, 2048], BF16); tzT = Tok()
                pY = Rot([sy_.enter_context(nc.psum_tensor("s5_pY%d" % i, [128, 256], F32)) for i in range(4)])
                for T in range(4):
                    for i in range(8):
                        py, tpy = pY.next()
                        for h2 in range(2):
                            hs = slice(64 * h2, 64 * h2 + 64)
                            for ip in range(i + 1):
                                P(lambda e, ip=ip, hs=hs: e.matmul(py[hs, :], lhsT=Kblk[:, T, i - ip, hs], rhs=UT[:, T, ip, :],
                                                                   start=(ip == 0), stop=False), [tS, tUT], [tpy])
                            n_ = 0
                            for pq in range(2):
                                pair = 2 * (2 * T + h2) + pq
                                for ri in range(2):
                                    n_ += 1
                                    P(lambda e, hs=hs, pair=pair, ri=ri, n_=n_: e.matmul(
                                        py[hs, :], lhsT=Wc[:, pair, i, ri, :], rhs=Xp[:, ri, pair, :],
                                        start=False, stop=(n_ == 4)), [tS, tXp], [tpy])
                        o_ = AP(zT, T * 2048 + i, [[8192, 128], [8, 256]])
                        A(lambda e: e.activation(out=o_, in_=py[:], func=AF.Gelu_apprx_tanh), [tpy], [tzT])
                ck("y")
                if BARRIERS: kb.barrier()
                pG = Rot([sy_.enter_context(nc.psum_tensor("s5_pG%d" % i, [128, 512], F32)) for i in range(2)])
                sg = Rot([y_("s5_sg%d" % i, [128, 512], F32) for i in range(2)])
                zg = Rot([y_("s5_zg%d" % i, [128, 512], BF16) for i in range(2)])
                for ch in range(4):
                    for m in range(4):
                        pg, tpg = pG.next()
                        for k4 in range(4):
                            P(lambda e, k4=k4: e.matmul(pg[:], lhsT=Wgl[:, k4, m * 128:(m + 1) * 128],
                                                        rhs=zT[:, k4, ch * 512:(ch + 1) * 512],
                                                        start=(k4 == 0), stop=(k4 == 3)), [tWgl, tzT], [tpg])
                        s_, ts_ = sg.next()
                        A(lambda e: e.activation(out=s_[:], in_=pg[:], func=AF.Sigmoid), [tpg], [ts_])
                        z_, tz_ = zg.next()
                        V(lambda e: e.tensor_tensor(out=z_[:], in0=s_[:], in1=zT[:, m, ch * 512:(ch + 1) * 512],
                                                    op=ALU.mult), [ts_, tzT], [tz_])
                        kb.dma("sp", AP(zg_d, m * 2048 + ch * 512, [[4 * 2048, 128], [1, 512]]), z_[:],
                               reads=[tz_], writes=[t_zg])


    NEG = -30000.0

    def phase_tables():
        with contextlib.ExitStack() as st:
            def a_(name, shape, dt):
                return st.enter_context(nc.sbuf_tensor(name, list(shape), dt))
            tT = Tok()
            relb = a_("t_relb", [32, 8], F32); rl = a_("t_rl", [32, 8], F32)
            ohr = a_("t_ohr", [32, 256], F32); ohf = a_("t_ohf", [32, 256], F32)
            antiJ = a_("t_antiJ", [128, 128], F32)
            kb.dma("sp", relb[:], rel_bias_d.ap(), writes=[tT])
            kb.dma("sp", rl[:], AP(rel_bias_d, 31 * 8, [[0, 32], [1, 8]]), writes=[tT])
            kb.dma("sp", ohr[:], ohrev_d.ap(), writes=[tT])
            kb.dma("sp", ohf[:], ohfwd_d.ap(), writes=[tT])
            kb.dma("sp", antiJ[:], antij_d.ap(), writes=[tT])
            V(lambda e: e.tensor_tensor(out=relb[:], in0=relb[:], in1=rl[:], op=ALU.subtract), [tT], [tT])
            pt = st.enter_context(nc.psum_tensor("t_pt", [8, 512], F32)); tpt = Tok()
            P(lambda e: e.matmul(pt[:, 0:256], lhsT=relb[:], rhs=ohr[:], start=True, stop=True), [tT], [tpt])
            P(lambda e: e.matmul(pt[:, 256:512], lhsT=relb[:], rhs=ohf[:], start=True, stop=True), [tT], [tpt])
            rowR = a_("t_rowR", [8, 384], F32); rowF = a_("t_rowF", [8, 416], F32)
            V(lambda e: e.memset(rowR[:], NEG), [tT], [tT])
            V(lambda e: e.memset(rowF[:], NEG), [tT], [tT])
            V(lambda e: e.tensor_copy(rowR[:, 0:256], pt[:, 0:256]), [tpt, tT], [tT])
            V(lambda e: e.tensor_copy(rowF[:, 160:416], pt[:, 256:512]), [tpt, tT], [tT])
            tD_ = Tok()
            kb.dma("sp", tabR_d.ap(), rowR[:], reads=[tT], writes=[tD_])
            kb.dma("sp", tabF_d.ap(), rowF[:], reads=[tT], writes=[tD_])
            Hk = a_("t_Hk", [128, 2, 8, 128], F32); tH = Tok()
            kb.dma("sp", Hk[:, 0, :, :], AP(tabR_d, 128, [[1, 128], [384, 8], [1, 128]]), reads=[tD_], writes=[tH])
            kb.dma("sp", Hk[:, 1, :, :], AP(tabR_d, 0, [[1, 128], [384, 8], [1, 128]]), reads=[tD_], writes=[tH])
            pb = Rot([st.enter_context(nc.psum_tensor("t_pb%d" % i, [128, 128], F32)) for i in range(2)])
            for d_ in range(2):
                for h in range(8):
                    p_, tp_ = pb.next()
                    P(lambda e: e.matmul(p_[:], lhsT=Hk[:, d_, h, :], rhs=antiJ[:], start=True, stop=True), [tH, tT], [tp_])
                    V(lambda e: e.tensor_copy(BW[:, d_, h, :], p_[:]), [tp_], [t_BW])
            kb.dma("sp", M4[:], m4_d.ap(), writes=[t_BW])
            V(lambda e: e.memset(BnA[:], 1.0), [], [t_BnA])
            kb.dma("pool", BnA[0:16, :, :], AP(tabF_d, 17, [[16, 16], [416, 8], [1, 128]]), reads=[tD_], writes=[t_BnA])

    def phase_compress():
        with contextlib.ExitStack() as sc:
            def a_(name, shape, dt):
                return sc.enter_context(nc.sbuf_tensor(name, list(shape), dt))
            tW = Tok()
            W1s = a_("c_W1s", [128, 2, 16, 256], BF16)
            W2 = a_("c_W2", [128, 2, 2, 64], BF16)
            PosS = a_("c_PosS", [128, 2, 16], BF16)
            for w, (w1, w2, pos) in enumerate(((cmp_w1_k, cmp_w2_k, cmp_pos_k), (cmp_w1_v, cmp_w2_v, cmp_pos_v))):
                for lp in range(16):
                    kb.dma("pool", W1s[:, w, lp, :], AP(w1, lp * 128 * 256, [[256, 128], [1, 256]]), writes=[tW])
                kb.dma("pool", W2[:, w, :, :], AP(w2, 0, [[64, 128], [128 * 64, 2], [1, 64]]), writes=[tW])
                kb.dma("pool", PosS[:, w, :], AP(pos, 0, [[1, 128], [128, 16]]), writes=[tW], allow_slow_non_contiguous=True)
            biasW = a_("c_biasW", [128, 4], F32); tB = Tok()
            pB = sc.enter_context(nc.psum_tensor("c_pB", [128, 4], F32)); tpB = Tok()
            for w in range(2):
                for ht in range(2):
                    for lp in range(16):
                        P(lambda e, lp=lp: e.matmul(pB[:, w * 2 + ht:w * 2 + ht + 1], lhsT=W1s[:, w, lp, ht * 128:(ht + 1) * 128],
                                                    rhs=PosS[:, w, lp:lp + 1], start=(lp == 0), stop=(lp == 15)), [tW], [tpB])
            V(lambda e: e.tensor_copy(biasW[:], pB[:]), [tpB], [tB])
            CW = 2 * 2 * 2080
            CRs = a_("c_CRs", [128, 2, 2, 2080], BF16); tCR = Tok()
            G1 = a_("c_G1", [128, 2, 2, 2, 128], BF16); tG1 = Tok()
            pH = Rot([sc.enter_context(nc.psum_tensor("c_pH%d" % i, [128, 128], F32)) for i in range(2)])
            pO = Rot([sc.enter_context(nc.psum_tensor("c_pO%d" % i, [128, 128], F32)) for i in range(2)])
            V(lambda e: e.memset(Vc_aug[:], 1.0), [], [t_Vc])
            for nt in range(8):
                t0 = 2048 * nt
                for w, src in enumerate((kcR_d, vcR_d)):
                    for g in range(2):
                        kb.dma("sp", CRs[0:64, w, g, 0:2064], AP(src, (64 * g) * (S + 16) + t0, [[S + 16, 64], [1, 2064]]),
                               reads=kv_toks, writes=[tCR])
                        kb.dma("sp", CRs[64:128, w, g, 0:2063], AP(src, (64 * g) * (S + 16) + t0 + 1, [[S + 16, 64], [1, 2063]]),
                               reads=kv_toks, writes=[tCR])
                for w in range(2):
                    for g in range(2):
                        for ht in range(2):
                            ph, tph = pH.next()
                            for lp in range(16):
                                rhs_ = AP(CRs, (w * 2 + g) * 2080 + 2 * lp, [[CW, 128], [16, 128]])
                                P(lambda e, lp=lp, rhs_=rhs_: e.matmul(ph[:], lhsT=W1s[:, w, lp, ht * 128:(ht + 1) * 128], rhs=rhs_,
                                                                       start=(lp == 0), stop=(lp == 15)), [tW, tCR], [tph])
                            A(lambda e: e.activation(out=G1[:, w, g, ht, :], in_=ph[:], func=AF.Gelu_apprx_tanh,
                                                     bias=biasW[:, w * 2 + ht:w * 2 + ht + 1]), [tph, tB], [tG1])
                po, tpo = pO.next()
                for g in range(2):
                    for ht in range(2):
                        P(lambda e, ht=ht: e.matmul(po[64 * g:64 * g + 64, :], lhsT=W2[:, 0, ht, :], rhs=G1[:, 0, g, ht, :],
                                                    start=(ht == 0), stop=(ht == 1)), [tW, tG1], [tpo])
                V(lambda e: e.tensor_copy(KcT[:, nt * 128:(nt + 1) * 128], po[:]), [tpo], [t_Kc])
                po, tpo = pO.next()
                for g in range(2):
                    for ht in range(2):
                        P(lambda e, ht=ht: e.matmul(po[:, 64 * g:64 * g + 64], lhsT=G1[:, 1, g, ht, :], rhs=W2[:, 1, ht, :],
                                                    start=(ht == 0), stop=(ht == 1)), [tW, tG1], [tpo])
                V(lambda e: e.tensor_copy(Vc_aug[:, nt, :, 0:64], po[:].rearrange("p (g d) -> p g d", g=2)), [tpo], [t_Vc])

    def attn_tile(ps_, tps_, kT_ap, q_ap, deps_r, bias_ap, bias_tok, P_rot, Sb_rot):
        P(lambda e: e.matmul(ps_[:], lhsT=kT_ap, rhs=q_ap, start=True, stop=True), deps_r, [tps_])
        p_, tp_ = P_rot.next()
        if bias_ap is not None:
            sb_, tsb_ = Sb_rot.next()
            V(lambda e: e.tensor_tensor(out=sb_[:], in0=ps_[:], in1=bias_ap, op=ALU.add), [tps_, bias_tok], [tsb_])
            A(lambda e: e.activation(out=p_[:], in_=sb_[:], func=AF.Exp), [tsb_], [tp_])
        else:
            A(lambda e: e.activation(out=p_[:], in_=ps_[:], func=AF.Exp), [tps_], [tp_])
        return p_, tp_

    def combine(po_, tpo_, gcol0, gstride, j, g, acc_, tacc_, first, cf_rot):
        cf, tcf = cf_rot.next()
        rs_ = AP(po_, 64, [[int(np.prod(list(po_.shape)[1:])), 128], [65, 4]])
        V(lambda e: e.tensor_scalar(out=cf[:, 0:4], in0=rs_, scalar1=1e-30, scalar2=None, op0=ALU.max), [tpo_], [tcf])
        V(lambda e: e.reciprocal(out=cf[:, 4:8], in_=cf[:, 0:4]), [tcf], [tcf])
        g_ = AP(gates, j * 24 + 12 * g + gcol0, [[16 * 24, 128], [3, 4]])
        V(lambda e: e.tensor_tensor(out=cf[:, 8:12], in0=cf[:, 4:8], in1=g_, op=ALU.mult), [tcf, t_gates], [tcf])
        for r in range(4):
            h = 4 * g + r
            o_ = acc_[:, h * 64:(h + 1) * 64]
            if first:
                V(lambda e, r=r, o_=o_: e.tensor_scalar(out=o_, in0=po_[:, r * 65:r * 65 + 64], scalar1=cf[:, 8 + r:9 + r],
                                                        scalar2=None, op0=ALU.mult), [tpo_, tcf], [tacc_])
            else:
                V(lambda e, r=r, o_=o_: e.scalar_tensor_tensor(out=o_, in0=po_[:, r * 65:r * 65 + 64], scalar=cf[:, 8 + r:9 + r],
                                                               in1=o_, op0=ALU.mult, op1=ALU.add), [tpo_, tcf], [tacc_])

    def pv_T(poT, tpoT, Pm, tPm, v_ap, tv, first, last):
        P(lambda e: e.matmul(poT[0:65, :], lhsT=v_ap, rhs=Pm[:], start=first, stop=last), [tPm, tv], [tpoT])

    def finish_o(poT, tpoT, po, tpo, osb, tosb):
        V(lambda e: e.tensor_copy(osb[0:65, :], poT[0:65, :]), [tpoT], [tosb])
        for r in range(4):
            P(lambda e, r=r: e.transpose(out=po[:, r * 65:(r + 1) * 65], in_=osb[0:65, r * 128:(r + 1) * 128],
                                         identity=identF[0:65, 0:65]), [tosb, t_identF], [tpo])

    def phase1():
        with contextlib.ExitStack() as s1:
            def a_(name, shape, dt):
                return s1.enter_context(nc.sbuf_tensor(name, list(shape), dt))
            X = norm_ctx(s1, 2, g_mix)
            WinA = a_("WinA", [128, 8, 1048], BF16); tWin = Tok()
            for k in range(8):
                base = k * 128 * INC
                for r in range(4):
                    kb.dma("pool", WinA[:, k, r * 128:(r + 1) * 128].rearrange("p (g d) -> p g d", g=2),
                           AP(w_in, base + 512 + 64 * r, [[INC, 128], [256, 2], [1, 64]]), writes=[tWin])
                for (c0, s0, n) in ((512, 1536, 128), (640, 1280, 128), (768, 1664, 128), (896, 1408, 128), (1024, 1792, 24)):
                    kb.dma("pool", WinA[:, k, c0:c0 + n], AP(w_in, base + s0, [[INC, 128], [1, n]]), writes=[tWin])
            vprev = a_("vprev_sb", [128, 64], F32); tvp = Tok()
            kb.dma("sp", vprev[:], vprev_d.ap(), writes=[tvp])
            KwT = [a_("KwT%d" % i, [128, 5, 128], BF16) for i in range(2)]; tKw = [Tok(), Tok()]
            Vw = [a_("Vw%d" % i, [128, 5, 2, 65], BF16) for i in range(2)]; tVw = [Tok(), Tok()]
            pP = Rot([s1.enter_context(nc.psum_tensor("p1_pP%d" % i, [128, 512], F32)) for i in range(2)])
            pS = Rot([s1.enter_context(nc.psum_tensor("p1_pS%d" % i, [128, 512], F32)) for i in range(2)])
            pO = Rot([s1.enter_context(nc.psum_tensor("p1_pO%d" % i, [128, 512], F32)) for i in range(1)])
            poT = s1.enter_context(nc.psum_tensor("p1_poT", [128, 512], F32)); tpoT = Tok()
            osb = a_("p1_osb", [128, 512], F32); tosb = Tok()
            M4r = a_("p1_M4r", [128, 512], F32)
            for r_ in range(4):
                V(lambda e, r_=r_: e.tensor_copy(M4r[:, r_ * 128:(r_ + 1) * 128], M4[:]), [t_BW], [t_BW])
            Pr = Rot([a_("p1_P%d" % i, [128, 512], BF16) for i in range(2)])
            Sbr = Rot([a_("p1_Sb%d" % i, [128, 512], F32) for i in range(2)])
            cfr = Rot([a_("p1_cf%d" % i, [128, 12], F32) for i in range(2)])
            accw = Rot([a_("p1_acc%d" % i, [128, 512], F32) for i in range(2)])
            V(lambda e: e.memset(VsN[:], 1.0), [], [t_VsN])
            nev = [0]

            def evac(o_, i_, rd, wr, scale=None):
                nev[0] += 1
                if scale is not None:
                    A(lambda e: e.activation(out=o_, in_=i_, func=AF.Copy, scale=scale), rd, wr)
                else:
                    V(lambda e: e.tensor_copy(o_, i_), rd, wr)

            def fm(col0, hT, thT, c0, n):
                pp, tpp = pP.next()
                for k in range(8):
                    P(lambda e, k=k: e.matmul(pp[:, 0:n], lhsT=WinA[:, k, col0:col0 + 128], rhs=hT[:, k, c0:c0 + n],
                                              start=(k == 0), stop=(k == 7)), [tWin, thT], [tpp])
                return pp, tpp

            def tm(col0, ncol, hT, thT, a):
                pp, tpp = pP.next()
                for k in range(8):
                    P(lambda e, k=k: e.matmul(pp[:, 0:ncol], lhsT=hT[:, k, a * 128:(a + 1) * 128], rhs=WinA[:, k, col0:col0 + ncol],
                                              start=(k == 0), stop=(k == 7)), [tWin, thT], [tpp])
                return pp, tpp
            ck("p1w")
            for oc in range(8):
                hT, thT = load_norm_T(x_own, oc * 256, 2, X)
                for r in range(4):
                    pp, tpp = fm(r * 128, hT, thT, 0, 256)
                    evac(AP(Qall, (2 * oc) * 512 + r * 128, [[16 * 512, 128], [512, 2], [1, 128]]),
                         AP(pp, 0, [[512, 128], [128, 2], [1, 128]]), [tpp], [t_Q], scale=0.125)
                ck("p1a")
                pp, tpp = fm(512, hT, thT, 0, 256)
                ck("p1a1")
                for a in range(2):
                    evac(KwT[a][:, 4, :], pp[:, a * 128:(a + 1) * 128], [tpp], [tKw[a]])
                ck("p1a2")
                pp, tpp = fm(640, hT, thT, 0, 256)
                evac(AP(KsN, (2 * oc) * 256 + 128, [[16 * 256, 128], [256, 2], [1, 128]]),
                     AP(pp, 0, [[512, 128], [128, 2], [1, 128]]), [tpp], [t_KsN])
                ck("p1b")
                for a in range(2):
                    j = 2 * oc + a
                    pp, tpp = tm(768, 256, hT, thT, a)
                    evac(Vw[a][:, 4, :, 0:64], pp[:, 0:128].rearrange("p (g d) -> p g d", g=2), [tpp], [tVw[a]])
                    V(lambda e: e.memset(Vw[a][:, 4, :, 64:65], 1.0), [], [tVw[a]])
                    evac(VsN[:, j, 1, :, 0:64], pp[:, 128:256].rearrange("p (g d) -> p g d", g=2), [tpp], [t_VsN])
                    ck("p1c")
                    pp, tpp = tm(1024, 24, hT, thT, a)
                    A(lambda e: e.activation(out=gates[:, j, :], in_=pp[:, 0:24], func=AF.Sigmoid), [tpp], [t_gates])
                ck("p1own")
                for a in range(2):
                    j = 2 * oc + a
                    for pc in range(2):
                        hP, thP = load_norm_T(x_prev, j * 512 + pc * 256, 2, X)
                        pp, tpp = fm(512, hP, thP, 0, 256)
                        evac(KwT[a][:, 2 * pc:2 * pc + 2, :], pp[:, 0:256].rearrange("p (t k) -> p t k", t=2), [tpp], [tKw[a]])
                        for a2 in range(2):
                            p_ = 2 * pc + a2
                            last = (p_ == 3)
                            pp, tpp = tm(768, 256 if last else 128, hP, thP, a2)
                            evac(Vw[a][:, p_, :, 0:64], pp[:, 0:128].rearrange("p (g d) -> p g d", g=2), [tpp], [tVw[a]])
                            vcol = AP(vprev, j * 4 + p_, [[64, 128], [0, 2], [1, 1]])
                            V(lambda e: e.tensor_copy(Vw[a][:, p_, :, 64:65], vcol), [tvp], [tVw[a]])
                            if last:
                                evac(VsN[:, j, 0, :, 0:64], pp[:, 128:256].rearrange("p (g d) -> p g d", g=2), [tpp], [t_VsN])
                                V(lambda e: e.tensor_copy(VsN[:, j, 0, :, 64:65], vcol), [tvp], [t_VsN])
                        if pc == 1:
                            pp, tpp = fm(640, hP, thP, 128, 128)
                            evac(KsN[:, j, 0, :], pp[:, 0:128], [tpp], [t_KsN])
                    ck("p1prev")
                    ac, tac = accw.next()
                    for g in range(2):
                        po, tpo = pO.next()
                        for p_ in range(5):
                            dl = 4 - p_
                            ps_, tps_ = pS.next()
                            if dl == 0:
                                b_ap, b_tok = BW[:, 0, 4 * g:4 * g + 4, :].rearrange("p h q -> p (h q)"), t_BW
                            elif dl == 1:
                                b_ap, b_tok = BW[:, 1, 4 * g:4 * g + 4, :].rearrange("p h q -> p (h q)"), t_BW
                            elif dl == 4:
                                b_ap, b_tok = M4r[:], t_BW
                            else:
                                b_ap, b_tok = None, None
                            if b_ap is not None and dl != 4:
                                pass
                            Pm, tPm = attn_tile(ps_, tps_, KwT[a][64 * g:64 * g + 64, p_, :],
                                                Qall[64 * g:64 * g + 64, j, :, :].rearrange("p r q -> p (r q)"),
                                                [tKw[a], t_Q], b_ap, b_tok, Pr, Sbr)
                            pv_T(poT, tpoT, Pm, tPm, Vw[a][:, p_, g, :], tVw[a], p_ == 0, p_ == 4)
                        finish_o(poT, tpoT, po, tpo, osb, tosb)
                        combine(po, tpo, 2, 3, j, g, ac, tac, True, cfr)
                    kb.dma("sp", acc_d.ap()[j * 128:(j + 1) * 128, :], ac[:], reads=[tac], writes=[t_accd])
                    ck("p1win")

    def phase2():
        with contextlib.ExitStack() as s2:
            def a_(name, shape, dt):
                return s2.enter_context(nc.sbuf_tensor(name, list(shape), dt))
            KsT_all = a_("KsT_all", [128, S], BF16); tKs = Tok()
            for i in range(8):
                kb.dma("sp", KsT_all[:, i * 2048:(i + 1) * 2048], AP(ksT_d, i * 2048, [[S, 128], [1, 2048]]),
                       reads=kv_toks, writes=[tKs])
            Vs_all = a_("Vs_all", [128, 128, 2, 65], BF16); tVs = Tok()
            G(lambda e: e.memset(Vs_all[:], 1.0), [], [tVs])
            for i in range(8):
                for g in range(2):
                    kb.dma("sp", Vs_all[:, i * 16:(i + 1) * 16, g, 0:64],
                           AP(vs_d, i * 16 * 16384 + 64 * g, [[128, 128], [16384, 16], [1, 64]]), reads=kv_toks, writes=[tVs])
            wide = a_("wide_sb", [128, 4096], BF16); tc_ = Tok()
            kb.dma("sp", wide[:], wide_d.ap(), writes=[tc_])
            keepb = a_("keepb", [128, 32], F32)
            kb.dma("sp", keepb[:], keepblk_d.ap(), writes=[tc_])
            candr = Rot([a_("cand%d" % i, [128, 256], BF16) for i in range(2)])
            forcr = Rot([a_("forc%d" % i, [128, 256], BF16) for i in range(2)])
            expnr = Rot([a_("expn%d" % i, [128, 2, 128], BF16) for i in range(2)])
            selcr = Rot([a_("selc%d" % i, [17, 1024], BF16) for i in range(2)])
            accr = Rot([a_("p2_acc%d" % i, [128, 512], F32) for i in range(2)])
            accb = a_("p2_accb", [128, 512], BF16); taccb = Tok()
            imp = a_("p2_imp", [128, 1028], F32); timp = Tok()
            pq = Rot([a_("p2_pq%d" % i, [128, 1024], F32) for i in range(2)])
            rs = Rot([a_("p2_rs%d" % i, [128, 4], F32) for i in range(4)])
            sS = a_("p2_s", [128, 256], F32); sS2 = a_("p2_s2", [128, 256], F32); tS_ = Tok()
            m8 = a_("p2_m8", [128, 16], F32)
            selg = a_("p2_selg", [128, 256], BF16); tselg = Tok()
            selT = a_("p2_selT", [128, 2, 2, 128], BF16); tselT = Tok()
            selTk = a_("p2_selTk", [128, 2, 2, 128], BF16)
            Pr = Rot([a_("p2_P%d" % i, [128, 512], BF16) for i in range(3)])
            Pmr = Rot([a_("p2_Pm%d" % i, [128, 512], BF16) for i in range(3)])
            Sbr = Rot([a_("p2_Sb%d" % i, [128, 512], F32) for i in range(2)])
            cfr = Rot([a_("p2_cf%d" % i, [128, 12], F32) for i in range(2)])
            pS = Rot([s2.enter_context(nc.psum_tensor("p2_pS%d" % i, [128, 512], F32)) for i in range(2)])
            pI = Rot([s2.enter_context(nc.psum_tensor("p2_pI%d" % i, [128, 512], F32)) for i in range(2)])
            pMt = s2.enter_context(nc.psum_tensor("p2_pM", [128, 2, 128], F32))
            pM = Rot([pMt[:, 0, :], pMt[:, 1, :]])
            pO = Rot([s2.enter_context(nc.psum_tensor("p2_pO%d" % i, [128, 260], F32)) for i in range(1)])
            pT = s2.enter_context(nc.psum_tensor("p2_pT", [128, 4, 128], BF16)); tpT = Tok()
            poT = s2.enter_context(nc.psum_tensor("p2_poT", [128, 512], F32)); tpoT = Tok()
            osb = a_("p2_osb", [128, 512], F32); tosb = Tok()
            V(lambda e: e.memset(imp[:], 0.0), [], [timp])

            def pv(po, tpo, Pm, tPm, v_ap, tv, first, last):
                pv_T(poT, tpoT, Pm, tPm, v_ap, tv, first, last)
                if last:
                    finish_o(poT, tpoT, po, tpo, osb, tosb)

            def pm_b(pm):
                i = 0 if pm is pM.aps[0] else 1
                return AP(pMt, i * 128, [[256, 128], [0, 4], [1, 128]])

            def masked(Pt, tPt, pm, tpm):
                Pm, tPm = Pmr.next()
                V(lambda e: e.tensor_tensor(out=Pm[:].rearrange("p (r q) -> p r q", r=4), in0=Pt[:].rearrange("p (r q) -> p r q", r=4),
                                            in1=pm.rearrange("p (o q) -> p o q", o=1).broadcast_to([128, 4, 128]) if False else pm_b(pm), op=ALU.mult), [tPt, tpm], [tPm])
                return Pm, tPm
            for j in range(NQ):
                Wb_ = 16 * (j + 1)
                NCc = 64 * (j + 1)
                ntc = (NCc + 127) // 128
                nbt = (Wb_ + 127) // 128
                cd, tcd = candr.next(); fc, tfc = forcr.next(); ex, tex = expnr.next(); sc_, tsc = selcr.next()
                kb.dma("sp", cd[:], AP(cand_d, j * 128 * 256, [[256, 128], [1, 256]]), writes=[tcd])
                kb.dma("sp", fc[:], AP(forced_d, j * 128 * 256, [[256, 128], [1, 256]]), writes=[tfc])
                kb.dma("sp", ex[:], AP(expn_d, j * 256 * 128, [[128, 128], [128 * 128, 2], [1, 128]]), writes=[tex])
                kb.dma("sp", sc_[:], AP(selc_d, j * 17 * 1024, [[1024, 17], [1, 1024]]), writes=[tsc])
                ac, tac = accr.next()
                kb.dma("sp", ac[:], acc_d.ap()[j * 128:(j + 1) * 128, :], reads=[t_accd], writes=[tac])
                for g in range(2):
                    qg = Qall[64 * g:64 * g + 64, j, :, :].rearrange("p r q -> p (r q)")
                    for r in range(4):
                        h = 4 * g + r
                        pq_, tpq = pq.next(); rs_, trs = rs.next()
                        nch = 0
                        for c0 in range(0, NCc, 512):
                            n = min(512, NCc - c0)
                            pi_, tpi = pI.next()
                            P(lambda e: e.matmul(pi_[:, 0:n], lhsT=Qall[64 * g:64 * g + 64, j, r, :], rhs=KcT[64 * g:64 * g + 64, c0:c0 + n],
                                                 start=True, stop=False), [t_Q, t_Kc], [tpi])
                            P(lambda e: e.matmul(pi_[:, 0:n], lhsT=BnA[0:17, h, :], rhs=sc_[0:17, c0:c0 + n],
                                                 start=False, stop=True), [t_BnA, tsc], [tpi])
                            A(lambda e, nch=nch: e.activation(out=pq_[:, c0:c0 + n], in_=pi_[:, 0:n], func=AF.Exp,
                                                              accum_out=rs_[:, nch:nch + 1]), [tpi], [tpq, trs])
                            nch += 1
                        if nch == 2:
                            V(lambda e: e.tensor_tensor(out=rs_[:, 0:1], in0=rs_[:, 0:1], in1=rs_[:, 1:2], op=ALU.add), [trs], [trs])
                        V(lambda e: e.tensor_scalar(out=rs_[:, 2:3], in0=rs_[:, 0:1], scalar1=1e-30, scalar2=None, op0=ALU.max), [trs], [trs])
                        V(lambda e: e.reciprocal(out=rs_[:, 3:4], in_=rs_[:, 2:3]), [trs], [trs])
                        if r == 0:
                            V(lambda e: e.tensor_scalar(out=imp[:, 1:1 + NCc], in0=pq_[:, 0:NCc], scalar1=rs_[:, 3:4], scalar2=None,
                                                        op0=ALU.mult), [tpq, trs], [timp])
                        else:
                            V(lambda e: e.scalar_tensor_tensor(out=imp[:, 1:1 + NCc], in0=pq_[:, 0:NCc], scalar=rs_[:, 3:4],
                                                               in1=imp[:, 1:1 + NCc], op0=ALU.mult, op1=ALU.add), [tpq, trs], [timp])

                    def iv(o):
                        return AP(imp, o, [[1028, 128], [4, Wb_]])
                    s_ = sS[:, 0:Wb_]
                    V(lambda e: e.tensor_tensor(out=s_, in0=iv(1), in1=iv(2), op=ALU.add), [timp], [tS_])
                    V(lambda e: e.tensor_tensor(out=s_, in0=s_, in1=iv(3), op=ALU.add), [timp, tS_], [tS_])
                    V(lambda e: e.scalar_tensor_tensor(out=s_, in0=s_, scalar=2.0, in1=iv(0), op0=ALU.mult, op1=ALU.add), [timp, tS_], [tS_])
                    V(lambda e: e.tensor_tensor(out=s_, in0=s_, in1=iv(4), op=ALU.add), [timp, tS_], [tS_])
                    V(lambda e: e.tensor_tensor(out=s_, in0=s_, in1=cd[:, 0:Wb_], op=ALU.mult), [tS_, tcd], [tS_])
                    V(lambda e: e.max(out=m8[:, 0:8], in_=s_), [tS_], [tS_])
                    V(lambda e: e.match_replace(out=sS2[:, 0:Wb_], in_to_replace=m8[:, 0:8], in_values=s_, imm_value=-1.0), [tS_], [tS_])
                    V(lambda e: e.max(out=m8[:, 8:16], in_=sS2[:, 0:Wb_]), [tS_], [tS_])
                    V(lambda e: e.tensor_scalar(out=s_, in0=s_, scalar1=m8[:, 12:13], scalar2=None, op0=ALU.is_ge), [tS_], [tS_])
                    V(lambda e: e.tensor_tensor(out=s_, in0=s_, in1=cd[:, 0:Wb_], op=ALU.mult), [tS_, tcd], [tS_])
                    if Wb_ < 256:
                        V(lambda e: e.memset(selg[:, Wb_:256], 0.0), [], [tselg])
                    V(lambda e: e.tensor_tensor(out=selg[:, 0:Wb_], in0=s_, in1=fc[:, 0:Wb_], op=ALU.add), [tS_, tfc], [tselg])
                    for bt in range(nbt):
                        P(lambda e, bt=bt: e.transpose(out=pT[:, bt, :], in_=selg[:, bt * 128:(bt + 1) * 128], identity=ident[:]),
                          [tselg, t_ident], [tpT])
                        V(lambda e, bt=bt: e.tensor_copy(selT[:, g, bt, :], pT[:, bt, :]), [tpT], [tselT])
                        V(lambda e, bt=bt: e.tensor_scalar(out=selTk[:, g, bt, :], in0=pT[:, bt, :], scalar1=keepb[:, j * 2 + bt:j * 2 + bt + 1],
                                                           scalar2=None, op0=ALU.mult), [tpT, tc_], [tselT])
                    po, tpo = pO.next()
                    for nt in range(ntc):
                        ps_, tps_ = pS.next()
                        P(lambda e: e.matmul(ps_[:], lhsT=KcT[64 * g:64 * g + 64, nt * 128:(nt + 1) * 128], rhs=qg,
                                             start=True, stop=False), [t_Kc, t_Q], [tps_])
                        P(lambda e: e.matmul(ps_[:], lhsT=sc_[0:17, nt * 128:(nt + 1) * 128],
                                             rhs=BnA[0:17, 4 * g:4 * g + 4, :].rearrange("p h q -> p (h q)"),
                                             start=False, stop=True), [t_BnA, tsc], [tps_])
                        Pt, tPt = Pr.next()
                        A(lambda e: e.activation(out=Pt[:], in_=ps_[:], func=AF.Exp), [tps_], [tPt])
                        pv(po, tpo, Pt, tPt, Vc_aug[:, nt, g, :], t_Vc, nt == 0, nt == ntc - 1)
                    combine(po, tpo, 0, 3, j, g, ac, tac, False, cfr)
                    po, tpo = pO.next()
                    ntile = 8 * (j + 1)
                    for qb in range(ntile):
                        bt, p0 = divmod(2 * qb, 128)
                        h2, pi2 = p0 // 64, (p0 % 64) // 2
                        ps_, tps_ = pS.next()
                        P(lambda e: e.matmul(ps_[:], lhsT=KsT_all[64 * g:64 * g + 64, qb * 128:(qb + 1) * 128], rhs=qg,
                                             start=True, stop=True), [tKs, t_Q], [tps_])
                        pm, tpm = pM.next()
                        P(lambda e: e.matmul(pm, lhsT=wide[64 * h2:64 * h2 + 64, pi2 * 128:(pi2 + 1) * 128],
                                             rhs=selTk[64 * h2:64 * h2 + 64, g, bt, :], start=True, stop=True), [tc_, tselT], [tpm])
                        Pt, tPt = Pr.next()
                        A(lambda e: e.activation(out=Pt[:], in_=ps_[:], func=AF.Exp), [tps_], [tPt])
                        Pm, tPm = masked(Pt, tPt, pm, tpm)
                        pv(po, tpo, Pm, tPm, Vs_all[:, qb, g, :], tVs, qb == 0, False)
                    ps_, tps_ = pS.next()
                    b_ap = BW[:, 1, 4 * g:4 * g + 4, :].rearrange("p h q -> p (h q)")
                    Pt, tPt = attn_tile(ps_, tps_, KsN[64 * g:64 * g + 64, j, 0, :], qg, [t_KsN, t_Q], b_ap, t_BW, Pr, Sbr)
                    pm, tpm = pM.next()
                    for bt in range(nbt):
                        P(lambda e, bt=bt: e.matmul(pm, lhsT=ex[:, bt, :], rhs=selT[:, g, bt, :], start=(bt == 0), stop=(bt == nbt - 1)),
                          [tex, tselT], [tpm])
                    Pm, tPm = masked(Pt, tPt, pm, tpm)
                    pv(po, tpo, Pm, tPm, VsN[:, j, 0, g, :], t_VsN, False, False)
                    ps_, tps_ = pS.next()
                    b_ap = BW[:, 0, 4 * g:4 * g + 4, :].rearrange("p h q -> p (h q)")
                    Pt, tPt = attn_tile(ps_, tps_, KsN[64 * g:64 * g + 64, j, 1, :], qg, [t_KsN, t_Q], b_ap, t_BW, Pr, Sbr)
                    pv(po, tpo, Pt, tPt, VsN[:, j, 1, g, :], t_VsN, False, True)
                    combine(po, tpo, 1, 3, j, g, ac, tac, False, cfr)
                V(lambda e: e.tensor_copy(accb[:], ac[:]), [tac], [taccb])
                for t in range(4):
                    P(lambda e, t=t: e.transpose(out=pT[:, t, :], in_=accb[:, t * 128:(t + 1) * 128], identity=ident[:]),
                      [taccb, t_ident], [tpT])
                V(lambda e: e.tensor_copy(oT[:, :, j * 128:(j + 1) * 128], pT[:]), [tpT], [t_oT])

    def phase3a():
        with contextlib.ExitStack() as s3:
            def a_(name, shape, dt):
                return s3.enter_context(nc.sbuf_tensor(name, list(shape), dt))
            X = norm_ctx(s3, 2, g_mix)
            tW = Tok()
            Wb = a_("Wbr", [128, 8, 2048], BF16)
            Wus = a_("Wus", [128, 4, 1024], BF16); Wun = a_("Wun", [128, 4, 1024], BF16)
            Wo = a_("Wo", [128, 8, 1024], BF16)
            for k in range(8):
                kb.dma("pool", Wb[:, k, :], AP(w_in, k * 128 * INC + 1816, [[INC, 128], [1, 2048]]), writes=[tW])
                kb.dma("pool", Wo[:, k, :], AP(w_out_d, k * 128 * 1024, [[1024, 128], [1, 1024]]), writes=[tW])
            for k in range(4):
                kb.dma("pool", Wus[:, k, :], AP(w_up_ssm, k * 128 * 1024, [[1024, 128], [1, 1024]]), writes=[tW])
                kb.dma("pool", Wun[:, k, :], AP(w_up_nsa, k * 128 * 1024, [[1024, 128], [1, 1024]]), writes=[tW])
            zgr = Rot([a_("p3_zg%d" % i, [128, 4, 256], BF16) for i in range(2)])
            mix = a_("p3_mix", [128, 8, 256], BF16); tmix = Tok()
            sga = Rot([a_("p3_sa%d" % i, [128, 2, 256], F32) for i in range(2)])
            tt = Rot([a_("p3_t%d" % i, [128, 2, 256], F32) for i in range(2)])
            x1r = Rot([a_("p3_x1%d" % i, [128, 1024], F32) for i in range(2)])
            pA = Rot([s3.enter_context(nc.psum_tensor("p3_pA%d" % i, [128, 2, 256], F32)) for i in range(2)])
            pY = Rot([s3.enter_context(nc.psum_tensor("p3_pY%d" % i, [128, 2, 256], F32)) for i in range(2)])
            pX = Rot([s3.enter_context(nc.psum_tensor("p3_pX%d" % i, [128, 512], F32)) for i in range(2)])
            for oc in range(8):
                hT, thT, xb, txb = load_norm_T(x_own, oc * 256, 2, X, ret_x=True)
                zc, tzc = zgr.next()
                kb.dma("sp", zc[:], AP(zg_d, oc * 256, [[4 * NTOK, 128], [NTOK, 4], [1, 256]]), reads=[t_zg], writes=[tzc])
                for m in range(8):
                    pa, tpa = pA.next(); py, tpy = pY.next()
                    for k in range(8):
                        P(lambda e, k=k: e.matmul(pa[:, 0, :], lhsT=Wb[:, k, m * 128:(m + 1) * 128], rhs=hT[:, k, :],
                                                  start=(k == 0), stop=(k == 7)), [tW, thT], [tpa])
                    for k in range(8):
                        P(lambda e, k=k: e.matmul(pa[:, 1, :], lhsT=Wb[:, k, 1024 + m * 128:1024 + (m + 1) * 128], rhs=hT[:, k, :],
                                                  start=(k == 0), stop=(k == 7)), [tW, thT], [tpa])
                    for k in range(4):
                        P(lambda e, k=k: e.matmul(py[:, 0, :], lhsT=Wus[:, k, m * 128:(m + 1) * 128], rhs=zc[:, k, :],
                                                  start=(k == 0), stop=(k == 3)), [tW, tzc], [tpy])
                    for k in range(4):
                        P(lambda e, k=k: e.matmul(py[:, 1, :], lhsT=Wun[:, k, m * 128:(m + 1) * 128], rhs=oT[:, k, oc * 256:(oc + 1) * 256],
                                                  start=(k == 0), stop=(k == 3)), [tW, t_oT], [tpy])
                    sa, tsa = sga.next(); t_, tt_ = tt.next()
                    A(lambda e: e.activation(out=sa[:], in_=pa[:], func=AF.Sigmoid), [tpa], [tsa])
                    V(lambda e: e.tensor_tensor(out=t_[:], in0=sa[:], in1=py[:], op=ALU.mult), [tsa, tpy], [tt_])
                    V(lambda e: e.tensor_tensor(out=mix[:, m, :], in0=t_[:, 0, :], in1=t_[:, 1, :], op=ALU.add), [tt_], [tmix])
                for a in range(2):
                    x1, tx1 = x1r.next()
                    for hf in range(2):
                        px, tpx = pX.next()
                        for m in range(8):
                            P(lambda e, m=m: e.matmul(px[:], lhsT=mix[:, m, a * 128:(a + 1) * 128], rhs=Wo[:, m, hf * 512:(hf + 1) * 512],
                                                      start=(m == 0), stop=(m == 7)), [tmix, tW], [tpx])
                        V(lambda e: e.tensor_tensor(out=x1[:, hf * 512:(hf + 1) * 512], in0=px[:], in1=xb[:, a, hf * 512:(hf + 1) * 512],
                                                    op=ALU.add), [tpx, txb], [tx1])
                    tk_ = Tok(); x1_toks.append(tk_)
                    kb.dma("sp", x1_d.ap()[oc * 256 + a * 128: oc * 256 + (a + 1) * 128, :], x1[:], reads=[tx1], writes=[tk_])

    def phase_ffn(src_d):
        with contextlib.ExitStack() as fs:
            def fsb(name, shape, dt):
                return fs.enter_context(nc.sbuf_tensor(name, list(shape), dt))

            def fps(name, shape, dt):
                return fs.enter_context(nc.psum_tensor(name, list(shape), dt))
            gF, tgF = load_gain("g_ffn_t", g_ffn) if False else (None, None)
            gF = fsb("gF", [128, D], F32); tgF = Tok()
            kb.dma("sp", gF[:], AP(g_ffn, 0, [[0, 128], [1, D]]), writes=[tgF])
            gL = fsb("gL", [128, D], F32); tgL = Tok()
            kb.dma("sp", gL[:], AP(g_fin, 0, [[0, 128], [1, D]]), writes=[tgL])
            Wg = fsb("Wg", [128, 8, DFF], BF16); tWg = Tok()
            Wu = fsb("Wu", [128, 8, DFF], BF16); tWu = Tok()
            Wd = fsb("Wd", [128, NFT, D], BF16); tWd = Tok()
            lWg, lWu, lWd = [], [], []
            for k in range(8):
                t1_ = Tok(); lWg.append(t1_)
                kb.dma("pool", Wg[:, k, :], w_gate.ap()[k * 128:(k + 1) * 128, :], writes=[t1_])
                t2_ = Tok(); lWu.append(t2_)
                kb.dma("pool", Wu[:, k, :], w_up.ap()[k * 128:(k + 1) * 128, :], writes=[t2_])
            for m in range(NFT):
                t3_ = Tok(); lWd.append(t3_)
                kb.dma("pool", Wd[:, m, :], w_down.ap()[m * 128:(m + 1) * 128, :], writes=[t3_])
            NT = 256
            xt = [fsb("f_x%d" % i, [128, 2, D], F32) for i in range(2)]
            xr = Rot(xt)
            h2 = fsb("f_h2", [128, D], BF16); th2 = Tok()
            h2T = fsb("f_h2T", [128, 8, NT], BF16); th2T = Tok()
            aT = fsb("f_aT", [128, NFT, NT], BF16); taT = Tok()
            sg = Rot([fsb("f_sg%d" % i, [128, NT], F32) for i in range(2)])
            x2 = fsb("f_x2", [128, 2, D], F32); tx2 = Tok()
            oo = Rot([fsb("f_o%d" % i, [128, D], F32) for i in range(2)])
            junk = fsb("f_junk", [128, D], BF16); tjunk = Tok()
            ssr = Rot([fsb("f_ss%d" % i, [128, 4], F32) for i in range(4)])
            pT = Rot([fps("f_pT%d" % i, [128, 8, 128], BF16) for i in range(2)])
            pGU = Rot([fps("f_pGU%d" % i, [128, 2, NT], F32) for i in range(2)])
            pD = Rot([fps("f_pD%d" % i, [128, 512], F32) for i in range(2)])
            for tt in range(NTOK // NT):
                xa, tx = xr.next()
                kb.dma("sp", xa[:], AP(src_d, tt * NT * D, [[D, 128], [128 * D, 2], [1, D]]), reads=x1_toks, writes=[tx])
                for a in range(2):
                    ss, tss = ssr.next()
                    rmsnorm(xa[:, a, :], tx, gF, tgF, h2[:], th2, (junk, tjunk, ss, tss))
                    pt, tpt = pT.next()
                    for k in range(8):
                        kb.op("pe", lambda e, k=k: e.transpose(out=pt[:, k, :], in_=h2[:, k * 128:(k + 1) * 128],
                                                               identity=ident[:]),
                              reads=[th2, t_ident], writes=[tpt])
                    kb.op("act", lambda e: e.copy(out=h2T[:, :, a * 128:(a + 1) * 128], in_=pt[:]),
                          reads=[tpt], writes=[th2T])
                for m in range(NFT):
                    pg, tpg = pGU.next()
                    for k in range(8):
                        kb.op("pe", lambda e, k=k: e.matmul(pg[:, 0, :], lhsT=Wg[:, k, m * 128:(m + 1) * 128],
                                                            rhs=h2T[:, k, :], start=(k == 0), stop=(k == 7)),
                              reads=lWg + [th2T], writes=[tpg])
                    for k in range(8):
                        kb.op("pe", lambda e, k=k: e.matmul(pg[:, 1, :], lhsT=Wu[:, k, m * 128:(m + 1) * 128],
                                                            rhs=h2T[:, k, :], start=(k == 0), stop=(k == 7)),
                              reads=lWu + [th2T], writes=[tpg])
                    s_, ts_ = sg.next()
                    kb.op("act", lambda e: e.activation(out=s_[:], in_=pg[:, 0, :], func=AF.Silu),
                          reads=[tpg], writes=[ts_])
                    kb.op("dve", lambda e: e.tensor_tensor(out=aT[:, m, :], in0=s_[:], in1=pg[:, 1, :], op=ALU.mult),
                          reads=[ts_, tpg], writes=[taT])
                for a in range(2):
                    for hf in range(2):
                        pd, tpd = pD.next()
                        for m in range(NFT):
                            kb.op("pe", lambda e, m=m: e.matmul(pd[:], lhsT=aT[:, m, a * 128:(a + 1) * 128],
                                                                rhs=Wd[:, m, hf * 512:(hf + 1) * 512],
                                                                start=(m == 0), stop=(m == NFT - 1)),
                                  reads=[taT] + lWd, writes=[tpd])
                        kb.op("dve", lambda e: e.tensor_tensor(out=x2[:, a, hf * 512:(hf + 1) * 512], in0=pd[:],
                                                               in1=xa[:, a, hf * 512:(hf + 1) * 512], op=ALU.add),
                              reads=[tpd, tx], writes=[tx2])
                    ss, tss = ssr.next()
                    o_, to_ = oo.next()
                    rmsnorm(x2[:, a, :], tx2, gL, tgL, o_[:], to_, (junk, tjunk, ss, tss))
                    kb.dma("sp", out_d.ap()[tt * NT + a * 128: tt * NT + (a + 1) * 128, :], o_[:],
                           reads=[to_], writes=[t_out])


    t_out = Tok()
    try:
        phase_s5()
    except _Stop:
        return nc
    if debug == "s5":
        kb.finish([t_zg])
        es.close()
        return nc
    es_o = contextlib.ExitStack()
    oT = es_o.enter_context(nc.sbuf_tensor("oT", [128, 4, NTOK], BF16))
    es2 = contextlib.ExitStack()

    def p_(name, shape, dt):
        return es2.enter_context(nc.sbuf_tensor(name, list(shape), dt))
    BW = p_("BW", [128, 2, 8, 128], F32); M4 = p_("M4", [128, 128], F32); BnA = p_("BnA", [17, 8, 128], BF16)
    KcT = p_("KcT", [128, 1024], BF16); Vc_aug = p_("Vc_aug", [128, 8, 2, 65], BF16)
    Qall = p_("Qall", [128, NQ, 4, 128], BF16); gates = p_("gates", [128, NQ, 24], F32)
    KsN = p_("KsN", [128, NQ, 2, 128], BF16); VsN = p_("VsN", [128, NQ, 2, 2, 65], BF16)
    kb.barrier()
    try:
        phase_tables()
        kb.barrier()
        ck("tables")
        phase_compress()
        kb.barrier()
        ck("compress")
        phase1()
        kb.barrier()
        if debug == "dump":
            for nm, t_, tk_, dt_ in (("dbg_bw", BW, t_BW, F32), ("dbg_m4", M4, t_BW, F32), ("dbg_gates", gates, t_gates, F32),
                                     ("dbg_q", Qall, t_Q, BF16), ("dbg_ksn", KsN, t_KsN, BF16), ("dbg_vsn", VsN, t_VsN, BF16),
                                     ("dbg_bna", BnA, t_BnA, BF16), ("dbg_kct", KcT, t_Kc, BF16), ("dbg_vc", Vc_aug, t_Vc, BF16)):
                shp = list(t_.shape)
                n_ = int(np.prod(shp[1:]))
                dd_ = nc.dram_tensor(nm, [shp[0], n_], dt_, kind="ExternalOutput")
                kb.dma("sp", dd_.ap(), AP(t_, 0, [[n_, shp[0]], [1, n_]]), reads=[tk_], writes=[t_out])
        ck("phase1")
        phase2()
        kb.barrier()
        if debug == "dump":
            dd_ = nc.dram_tensor("dbg_oT", [128, 4 * NTOK], BF16, kind="ExternalOutput")
            kb.dma("sp", dd_.ap(), AP(oT, 0, [[4 * NTOK, 128], [1, 4 * NTOK]]), reads=[t_oT], writes=[t_out])
        ck("phase2")
    except _Stop:
        return nc
    es2.close()
    try:
        phase3a()
        kb.barrier()
        ck("phase3a")
    except _Stop:
        return nc
    es_o.close()
    phase_ffn(x1_d)
    kb.finish([t_out])
    es.close()
    return nc


_PROG = {}


def _bf(a):
    return np.ascontiguousarray(a).astype(ml_dtypes.bfloat16)


def make_in_maps(inp):
    x = np.asarray(inp["x"], np.float32)[0]
    xq = x.reshape(S // TQ, TQ, D)
    ident = np.eye(128, dtype=np.float32)
    f = lambda k: np.ascontiguousarray(np.asarray(inp[k], np.float32)[0])
    shared = {
        "norm_mix_g": np.asarray(inp["norm_mix_g"], np.float32).reshape(1, D),
        "norm_ffn_g": np.asarray(inp["norm_ffn_g"], np.float32).reshape(1, D),
        "norm_final_g": np.asarray(inp["norm_final_g"], np.float32).reshape(1, D),
        "w_ffn_gate": f("w_ffn_gate"), "w_ffn_up": f("w_ffn_up"), "w_ffn_down": f("w_ffn_down"),
        "w_in": f("w_in"),
        "ssm_a_re": f("ssm_a_re"), "ssm_a_im": f("ssm_a_im"),
        "ssm_log_dt": np.asarray(inp["ssm_log_dt"], np.float32).reshape(1, 32),
        "ssm_b_re": f("ssm_b_re"), "ssm_b_im": f("ssm_b_im"), "ssm_c_re": f("ssm_c_re"), "ssm_c_im": f("ssm_c_im"),
        "ssm_d": np.asarray(inp["ssm_d"], np.float32).reshape(1, 512),
        "ssm_w_glu": f("ssm_w_glu"),
        "ident_bf": _bf(ident), "ident_f": ident,
    }
    def bucket(n):
        n = np.maximum(n, 0)
        nf = np.maximum(n, 1).astype(np.float32)
        large = 16 + (np.log(nf / np.float32(16)) / np.float32(np.log(8.0)) * np.float32(16)).astype(np.int32)
        large = np.minimum(large, 31)
        return np.where(n < 16, n, large)
    dd = np.arange(256)
    ohf = (bucket(dd)[None, :] == np.arange(32)[:, None]).astype(np.float32)
    ohr = (bucket(255 - dd)[None, :] == np.arange(32)[:, None]).astype(np.float32)
    pp = np.arange(128)
    antij = (pp[:, None] + pp[None, :] == 127).astype(np.float32)
    m4 = np.where(pp[None, :] >= pp[:, None], np.float32(-30000.0), np.float32(0.0)).astype(np.float32)
    mm = np.arange(4096)
    wide = ((pp[:, None] % 64) == (2 * (mm[None, :] // 128) + (mm[None, :] % 128) // 64)).astype(np.float32)
    shared.update({
        "rel_bias": np.asarray(inp["rel_bias"], np.float32), "ohrev": ohr, "ohfwd": ohf, "antij": antij, "m4": m4,
        "cmp_w1_k": f("cmp_w1_k"), "cmp_w1_v": f("cmp_w1_v"), "cmp_w2_k": f("cmp_w2_k"), "cmp_w2_v": f("cmp_w2_v"),
        "cmp_pos_k": f("cmp_pos_k"), "cmp_pos_v": f("cmp_pos_v"), "wide64": _bf(wide),
        "w_out": f("w_out"), "w_up_ssm": f("w_up_ssm"), "w_up_nsa": f("w_up_nsa"), "x_all": x,
    })
    blk = np.arange(256)
    in_maps = []
    for c in range(NCORES):
        m = dict(shared)
        xp = np.zeros((NQ, 512, D), np.float32)
        vp = np.zeros((128, 64), np.float32)
        cand = np.zeros((NQ, 128, 256), np.float32); forced = np.zeros((NQ, 128, 256), np.float32)
        expn = np.zeros((NQ, 256, 128), np.float32); selc = np.zeros((NQ, 17, 1024), np.float32)
        keepb = np.zeros((128, 32), np.float32)
        for j in range(NQ):
            qb = 8 * j + c
            lo = 128 * qb - 512
            s0 = max(lo, 0)
            xp[j, s0 - lo:] = x[s0:128 * qb]
            for a in range(4):
                vp[:, j * 4 + a] = ((lo + a * 128 + pp) >= 0)
            cur = 2 * qb + (pp >= 64).astype(np.int64)
            valid = blk[None, :] <= cur[:, None]
            frc = valid & ((blk[None, :] == 0) | (blk[None, :] >= cur[:, None] - 1))
            cand[j] = valid & ~frc
            forced[j] = frc
            for hh in range(2):
                b_ = 2 * qb - 2 + hh
                if b_ >= 0:
                    expn[j, b_, 64 * hh:64 * hh + 64] = 1.0
            for mp in range(16):
                n = 8 * qb - 8 + (15 - mp)
                if 0 <= n < 1024:
                    selc[j, mp, n] = 1.0
            selc[j, 16, min(8 * qb + 8, 1024):] = -30000.0
            for bt in range(2):
                keepb[:, j * 2 + bt] = ((bt * 128 + pp) <= 2 * qb - 3)
        m["x_prev"] = xp.reshape(NQ * 512, D); m["vprev"] = vp
        m["cand"] = _bf(cand.reshape(NQ * 128, 256)); m["forced"] = _bf(forced.reshape(NQ * 128, 256))
        m["expn"] = _bf(expn.reshape(NQ * 256, 128)); m["selc"] = _bf(selc.reshape(NQ * 17, 1024))
        m["keepblk"] = keepb
        m["x_own"] = np.ascontiguousarray(xq[c::NCORES].reshape(NTOK, D))
        oh = np.zeros((128, 8), np.float32); oh[:, c] = 1.0
        m["onehot_r"] = oh
        in_maps.append(m)
    return in_maps


def kernel(**inp):
    if "main" not in _PROG:
        _PROG["main"] = build_program()
    nc = _PROG["main"]
    in_maps = make_in_maps(inp)
    res = run_bass_kernel_spmd(nc, in_maps, core_ids=list(range(NCORES)))
    out = np.empty((S // TQ, TQ, D), np.float32)
    for c in range(NCORES):
        out[c::NCORES] = np.asarray(res.results[c]["out"], np.float32).reshape(NQ, TQ, D)
    return out.reshape(1, S, D)
```

```python
import contextlib
import numpy as np
import ml_dtypes
import concourse.bass as bass
import concourse.mybir as mybir
from concourse.bass_utils import run_bass_kernel_spmd

F32 = mybir.dt.float32
BF16 = mybir.dt.bfloat16
AF = mybir.ActivationFunctionType
ALU = mybir.AluOpType
AX = mybir.AxisListType

BARRIERS = True
NCORES = 8
S = 16384
D = 1024
NQ = 16
TQ = 128
NTOK = NQ * TQ
DFF = 2816
NFT = DFF // 128
EPS = 1e-6
INC = 3864


def AP(t, off, dims):
    return bass.AP(t, off, [list(d) for d in dims])


class Tok:
    __slots__ = ("w", "r", "dw")

    def __init__(self):
        self.w = None
        self.r = {}
        self.dw = {}


class KB:
    def __init__(self, nc):
        self.nc = nc
        self.E = {"pe": nc.tensor, "act": nc.scalar, "dve": nc.vector, "pool": nc.gpsimd, "sp": nc.sync}
        self.csem = {e: nc.alloc_semaphore("c_" + e) for e in ("pe", "act", "dve", "pool")}
        self.ccnt = {e: 0 for e in self.csem}
        self.NDS = 28
        self.dsem = {e: [nc.alloc_semaphore("d_%s%d" % (e, i)) for i in range(self.NDS)]
                     for e in ("sp", "pool", "act")}
        self.dcnt = {e: 0 for e in self.dsem}
        self.seen = {e: {} for e in self.E}
        self.nwait = 0
        self.xsem = {}

    def collective(self, name, kind, in_ap, out_ap, reads, writes):
        self._deps("pool", reads, writes)
        sem = self.nc.alloc_semaphore("x_" + name)
        self.xsem[name] = sem
        ins = self.nc.gpsimd.collective_compute(kind, ALU.bypass, replica_groups=[list(range(NCORES))],
                                                ins=[in_ap], outs=[out_ap])
        ins.then_inc(sem)
        self._mark((("x", name), 1), reads, writes)

    def _wait(self, e, key, val, raw=True):
        if key[0] == "c" and key[1] == e and (e == "pe" or not raw):
            return
        if self.seen[e].get(key, 0) >= val:
            return
        if key[0] == "x":
            sem = self.xsem[key[1]]
        else:
            sem = self.csem[key[1]] if key[0] == "c" else self.dsem[key[1]][key[2]]
        self.E[e].wait_ge(sem, val)
        self.seen[e][key] = val
        self.nwait += 1

    def _deps(self, e, reads, writes, dma_write=False):
        for t in reads:
            if t.w is not None:
                self._wait(e, *t.w)
            for k, v in t.dw.items():
                self._wait(e, k, v)
        for t in writes:
            if t.w is not None:
                self._wait(e, t.w[0], t.w[1], raw=False)
            if not dma_write:
                for k, v in t.dw.items():
                    self._wait(e, k, v)
            for k, v in t.r.items():
                self._wait(e, k, v, raw=False)

    def _mark(self, me, reads, writes, dma_write=False):
        for t in reads:
            t.r[me[0]] = me[1]
        for t in writes:
            if dma_write:
                t.dw[me[0]] = me[1]
            else:
                t.w = me
                t.dw = {}
            t.r = {}

    def op(self, e, fn, reads=(), writes=()):
        self._deps(e, reads, writes)
        ins = fn(self.E[e])
        self.ccnt[e] += 1
        ins.then_inc(self.csem[e], 1)
        self._mark((("c", e), self.ccnt[e]), reads, writes)

    def dma(self, e, out, in_, reads=(), writes=(), **kw):
        self._deps(e, reads, writes, dma_write=True)
        i = self.dcnt[e]
        self.dcnt[e] += 1
        slot = i % self.NDS
        ins = self.E[e].dma_start(out=out, in_=in_, **kw)
        ins.then_inc(self.dsem[e][slot], 16)
        self._mark((("d", e, slot), 16 * (i // self.NDS + 1)), reads, writes, dma_write=True)

    def barrier(self):
        for e in self.E:
            for o in self.csem:
                if self.ccnt[o] > 0:
                    self._wait(e, ("c", o), self.ccnt[o])
            for q in self.dsem:
                n = self.dcnt[q]
                for slot in range(self.NDS):
                    cnt = (n - slot + self.NDS - 1) // self.NDS
                    if cnt > 0:
                        self._wait(e, ("d", q, slot), 16 * cnt)

    def finish(self, toks):
        for t in toks:
            if t.w is not None:
                self._wait("sp", *t.w)
            for k, v in t.dw.items():
                self._wait("sp", k, v)


class _Stop(Exception):
    pass


class Rot:
    def __init__(self, aps):
        self.aps = aps
        self.toks = [Tok() for _ in aps]
        self.i = 0

    def next(self):
        k = self.i % len(self.aps)
        self.i += 1
        return self.aps[k], self.toks[k]


def build_program(debug=None, debug_stop=None):
    nc = bass.Bass("TRN2", target_bir_lowering=False)
    kb = KB(nc)
    es = contextlib.ExitStack()

    def dram_in(name, shape, dt=F32):
        return nc.dram_tensor(name, list(shape), dt, kind="ExternalInput")

    x_own = dram_in("x_own", [NTOK, D])
    g_mix = dram_in("norm_mix_g", [1, D])
    g_ffn = dram_in("norm_ffn_g", [1, D])
    g_fin = dram_in("norm_final_g", [1, D])
    w_gate = dram_in("w_ffn_gate", [D, DFF])
    w_up = dram_in("w_ffn_up", [D, DFF])
    w_down = dram_in("w_ffn_down", [DFF, D])
    identb_d = dram_in("ident_bf", [128, 128], BF16)
    out_d = nc.dram_tensor("out", [NTOK, D], F32, kind="ExternalOutput")
    x1_d = nc.dram_tensor("x1_scr", [NTOK, D], F32, kind="ExternalOutput" if debug == "dump" else "Internal")

    w_in = dram_in("w_in", [D, INC])
    ssm_a_re = dram_in("ssm_a_re", [32, 64]); ssm_a_im = dram_in("ssm_a_im", [32, 64])
    ssm_log_dt = dram_in("ssm_log_dt", [1, 32])
    ssm_b_re = dram_in("ssm_b_re", [32, 64, 16]); ssm_b_im = dram_in("ssm_b_im", [32, 64, 16])
    ssm_c_re = dram_in("ssm_c_re", [32, 16, 64]); ssm_c_im = dram_in("ssm_c_im", [32, 16, 64])
    ssm_d = dram_in("ssm_d", [1, 512])
    w_glu = dram_in("ssm_w_glu", [512, 512])
    identf_d = dram_in("ident_f", [128, 128])
    onehot_d = dram_in("onehot_r", [128, 8])
    x_all = dram_in("x_all", [S, D])
    ksT_d = nc.dram_tensor("ksT_scr", [128, S], BF16, kind="Internal")
    kcR_d = nc.dram_tensor("kcR_scr", [128, S + 16], BF16, kind="Internal")
    vcR_d = nc.dram_tensor("vcR_scr", [128, S + 16], BF16, kind="Internal")
    vs_d = nc.dram_tensor("vs_scr", [S, 128], BF16, kind="Internal")
    zs_d = nc.dram_tensor("zs_scr", [128, 8192], F32, kind="Internal")
    ut_d = nc.dram_tensor("ut_scr", [128, 8192], BF16, kind="Internal")
    t_zsd = Tok(); t_utd = Tok()
    x_prev = dram_in("x_prev", [NQ * 512, D])
    vprev_d = dram_in("vprev", [128, 64])
    rel_bias_d = dram_in("rel_bias", [32, 8])
    ohrev_d = dram_in("ohrev", [32, 256]); ohfwd_d = dram_in("ohfwd", [32, 256])
    antij_d = dram_in("antij", [128, 128]); m4_d = dram_in("m4", [128, 128])
    tabR_d = nc.dram_tensor("tabR_scr", [8, 384], F32, kind="Internal")
    tabF_d = nc.dram_tensor("tabF_scr", [8, 416], F32, kind="Internal")
    cmp_w1_k = dram_in("cmp_w1_k", [2048, 256]); cmp_w1_v = dram_in("cmp_w1_v", [2048, 256])
    cmp_w2_k = dram_in("cmp_w2_k", [256, 64]); cmp_w2_v = dram_in("cmp_w2_v", [256, 64])
    cmp_pos_k = dram_in("cmp_pos_k", [32, 64]); cmp_pos_v = dram_in("cmp_pos_v", [32, 64])
    wide_d = dram_in("wide64", [128, 4096], BF16)
    keepblk_d = dram_in("keepblk", [128, 32])
    cand_d = dram_in("cand", [NQ * 128, 256], BF16); forced_d = dram_in("forced", [NQ * 128, 256], BF16)
    expn_d = dram_in("expn", [NQ * 256, 128], BF16)
    selc_d = dram_in("selc", [NQ * 17, 1024], BF16)
    acc_d = nc.dram_tensor("acc_scr", [NTOK, 512], F32, kind="ExternalOutput" if debug == "dump" else "Internal")
    w_out_d = dram_in("w_out", [D, D]); w_up_ssm = dram_in("w_up_ssm", [512, D]); w_up_nsa = dram_in("w_up_nsa", [512, D])
    t_accd = Tok(); x1_toks = []
    t_BW = Tok(); t_BnA = Tok(); t_Kc = Tok(); t_Vc = Tok(); t_Q = Tok(); t_gates = Tok(); t_KsN = Tok(); t_VsN = Tok(); t_oT = Tok()
    zg_d = nc.dram_tensor("zg_scr", [128, 4 * NTOK], BF16, kind="ExternalOutput" if debug in ("s5", "dump") else "Internal")
    t_zg = Tok()

    def sb(name, shape, dt):
        return es.enter_context(nc.sbuf_tensor(name, list(shape), dt))

    def ps(name, shape, dt):
        return es.enter_context(nc.psum_tensor(name, list(shape), dt))

    ident = sb("ident", [128, 128], BF16)
    t_ident = Tok()
    kb.dma("sp", ident[:], identb_d.ap(), writes=[t_ident])

    identF = sb("identF", [128, 128], F32)
    t_identF = Tok()
    kb.dma("sp", identF[:], identf_d.ap(), writes=[t_identF])
    epsT = sb("epsT", [128, 1], F32)
    t_eps = Tok()
    kb.op("dve", lambda e: e.memset(epsT[:], EPS), writes=[t_eps])

    def load_gain(name, src):
        t = sb(name, [128, D], F32)
        tk = Tok()
        kb.dma("sp", t[:], AP(src, 0, [[0, 128], [1, D]]), writes=[tk])
        return t, tk

    def rmsnorm(xap, tx, gt, tg, hout, th, scr):
        junk, tjunk, ss, tss = scr
        kb.op("act", lambda e: e.activation(out=junk[:], in_=xap, func=AF.Square, accum_out=ss[:, 0:1]),
              reads=[tx], writes=[tjunk, tss])
        kb.op("act", lambda e: e.activation(out=ss[:, 1:2], in_=ss[:, 0:1], func=AF.Sqrt, scale=1.0 / D,
                                            bias=epsT[:, 0:1]), reads=[tss, t_eps], writes=[tss])
        kb.op("dve", lambda e: e.reciprocal(out=ss[:, 2:3], in_=ss[:, 1:2]), reads=[tss], writes=[tss])
        kb.op("dve", lambda e: e.scalar_tensor_tensor(out=hout, in0=xap, scalar=ss[:, 2:3], in1=gt[:],
                                                      op0=ALU.mult, op1=ALU.mult),
              reads=[tx, tss, tg], writes=[th])

    def V(fn, r=(), w=()):
        kb.op("dve", fn, r, w)

    def A(fn, r=(), w=()):
        kb.op("act", fn, r, w)

    def P(fn, r=(), w=()):
        kb.op("pe", fn, r, w)

    def G(fn, r=(), w=()):
        kb.op("pool", fn, r, w)

    def load_norm_T(src, row0, ntile, X, ret_x=False):
        xb, txb = X["x"].next()
        kb.dma("sp", xb[:, 0:ntile, :], AP(src, row0 * D, [[D, 128], [128 * D, ntile], [1, D]]), writes=[txb])
        hT, thT = X["hT"].next()
        for a in range(ntile):
            ss, tss = X["ss"].next()
            hb, thb = X["h"].next()
            rmsnorm(xb[:, a, :], txb, X["g"], X["tg"], hb[:], thb, (X["junk"], X["tjunk"], ss, tss))
            pt, tpt = X["pT"].next()
            for k in range(8):
                P(lambda e, k=k: e.transpose(out=pt[:, k, :], in_=hb[:, k * 128:(k + 1) * 128], identity=ident[:]),
                  [thb, t_ident], [tpt])
            V(lambda e: e.tensor_copy(hT[:, :, a * 128:(a + 1) * 128], pt[:]), [tpt], [thT])
        if ret_x:
            return hT, thT, xb, txb
        return hT, thT

    nctx = [0]

    def norm_ctx(st, xtiles, gsrc):
        nctx[0] += 1
        pre = "c%d" % nctx[0]

        def a_(name, shape, dt):
            return st.enter_context(nc.sbuf_tensor(pre + name, list(shape), dt))
        X = {}
        X["x"] = Rot([a_("n_x%d" % i, [128, xtiles, D], F32) for i in range(2)])
        X["hT"] = Rot([a_("n_hT%d" % i, [128, 8, 128 * xtiles], BF16) for i in range(2)])
        X["h"] = Rot([a_("n_h%d" % i, [128, D], BF16) for i in range(2)])
        X["ss"] = Rot([a_("n_ss%d" % i, [128, 4], F32) for i in range(4)])
        X["junk"] = a_("n_junk", [128, D], BF16)
        X["tjunk"] = Tok()
        X["g"] = a_("n_g", [128, D], F32)
        X["tg"] = Tok()
        kb.dma("sp", X["g"][:], AP(gsrc, 0, [[0, 128], [1, D]]), writes=[X["tg"]])
        X["pT"] = Rot([st.enter_context(nc.psum_tensor(pre + "n_pT%d" % i, [128, 8, 128], BF16)) for i in range(2)])
        return X

    def ck(name):
        if debug_stop == name:
            raise _Stop()

    kv_toks = []

    def phase_all(UT, tUT, Zs, tZ, Eg, tEg, z_matmuls, recur, zv):
        UTW = 8192
        ZW = 8192
        with contextlib.ExitStack() as sa:
            X = norm_ctx(sa, 2, g_mix)
            WA = sa.enter_context(nc.sbuf_tensor("WA", [128, 8, 1024], BF16)); tWA = Tok()
            for k in range(8):
                for (c0, s0, n) in ((0, 0, 512), (512, 1280, 128), (640, 1024, 128), (768, 1152, 128), (896, 1408, 128)):
                    kb.dma("pool", WA[:, k, c0:c0 + n], AP(w_in, k * 128 * INC + s0, [[INC, 128], [1, n]]), writes=[tWA])
            pP = Rot([sa.enter_context(nc.psum_tensor("a_pP%d" % i, [128, 512], F32)) for i in range(2)])
            fmS = Rot([sa.enter_context(nc.sbuf_tensor("a_fm%d" % i, [128, 256], BF16)) for i in range(3)])
            vsS = Rot([sa.enter_context(nc.sbuf_tensor("a_vs%d" % i, [128, 128], BF16)) for i in range(2)])
            zpad = sa.enter_context(nc.sbuf_tensor("a_zpad", [128, 16], BF16)); tzp = Tok()
            V(lambda e: e.memset(zpad[:], 0.0), [], [tzp])
            for dst in (kcR_d, vcR_d):
                tk_ = Tok(); kv_toks.append(tk_)
                kb.dma("pool", AP(dst, S, [[S + 16, 128], [1, 16]]), zpad[:], reads=[tzp], writes=[tk_])
            n_ev = 0
            for cc in range(S // 256):
                sc, q = cc // 8, cc % 8
                hT, thT = load_norm_T(x_all, cc * 256, 2, X)
                for T in range(4):
                    pp, tpp = pP.next()
                    for k in range(8):
                        P(lambda e, k=k: e.matmul(pp[:, 0:256], lhsT=WA[:, k, T * 128:(T + 1) * 128], rhs=hT[:, k, :],
                                                  start=(k == 0), stop=(k == 7)), [tWA, thT], [tpp])
                    o_ = AP(UT, T * 2048 + q * 32, [[UTW, 128], [256, 8], [1, 32]])
                    i_ = AP(pp, 0, [[512, 128], [1, 8], [8, 32]])
                    n_ev += 1
                    if n_ev % 2 == 0:
                        V(lambda e: e.tensor_copy(o_, i_), [tpp], [tUT])
                    else:
                        A(lambda e: e.copy(out=o_, in_=i_), [tpp], [tUT])
                for (c0, dst) in ((512, ksT_d), (640, kcR_d), (768, vcR_d)):
                    pp, tpp = pP.next()
                    for k in range(8):
                        P(lambda e, k=k: e.matmul(pp[:, 0:256], lhsT=WA[:, k, c0:c0 + 128], rhs=hT[:, k, :],
                                                  start=(k == 0), stop=(k == 7)), [tWA, thT], [tpp])
                    f_, tf_ = fmS.next()
                    n_ev += 1
                    if n_ev % 2 == 0:
                        V(lambda e: e.tensor_copy(f_[:], pp[:, 0:256]), [tpp], [tf_])
                    else:
                        A(lambda e: e.copy(out=f_[:], in_=pp[:, 0:256]), [tpp], [tf_])
                    tk_ = Tok(); kv_toks.append(tk_)
                    W_ = dst.shape[1]
                    kb.dma("pool", AP(dst, cc * 256, [[W_, 128], [1, 256]]), f_[:], reads=[tf_], writes=[tk_])
                for a in range(2):
                    pp, tpp = pP.next()
                    for k in range(8):
                        P(lambda e, k=k: e.matmul(pp[:, 0:128], lhsT=hT[:, k, a * 128:(a + 1) * 128], rhs=WA[:, k, 896:1024],
                                                  start=(k == 0), stop=(k == 7)), [tWA, thT], [tpp])
                    v_, tv_ = vsS.next()
                    V(lambda e: e.tensor_copy(v_[:], pp[:, 0:128]), [tpp], [tv_])
                    tk_ = Tok(); kv_toks.append(tk_)
                    kb.dma("pool", AP(vs_d, (cc * 256 + a * 128) * 128, [[128, 128], [1, 128]]), v_[:],
                           reads=[tv_], writes=[tk_])
                if q == 7:
                    z_matmuls()
                    recur()
                    for ri in range(2):
                        d_ = AP(Eg, ri * 256 + 2 * sc, [[4096, 128], [16, 16], [1, 2], [512, 8]])
                        s_ = AP(Zs, ri * 4096 + 15, [[ZW, 128], [256, 16], [128, 2], [16, 8]])
                        V(lambda e, d_=d_, s_=s_: e.tensor_copy(d_, s_), [tZ], [tEg])
            kb.barrier()

    def phase_s5():
        PI = float(np.pi)
        with contextlib.ExitStack() as s5:
            def a_(name, shape, dt):
                return s5.enter_context(nc.sbuf_tensor(name, list(shape), dt))
            UT = a_("UT", [128, 4, 8, 256], BF16); tUT = Tok()
            UTW = 4 * 8 * 256
            Wgl = a_("s5_Wgl", [128, 4, 512], BF16); tWgl = Tok()
            for k4 in range(4):
                kb.dma("pool", Wgl[:, k4, :], AP(w_glu, k4 * 128 * 512, [[512, 128], [1, 512]]), writes=[tWgl])
            ck("u")
            NS = 40
            spt = a_("s5_sp", [128, NS, 16], F32); tS = Tok()
            SPW = NS * 16

            def sl(i):
                return spt[:, i, :]

            def slb(i, n=16):
                return AP(spt, i * 16, [[SPW, 128], [1, 16], [0, n]])
            (aR, aI, DT, XR, ANG, MAG, T1, T2, SINV, COSV, CFR, CFI, DEN, M1, T3, T4) = range(16)
            PWR, PWI = 16, 25
            kb.dma("sp", sl(aR), AP(ssm_a_re, 0, [[1, 128], [128, 16]]), writes=[tS], allow_slow_non_contiguous=True)
            kb.dma("sp", sl(aI), AP(ssm_a_im, 0, [[1, 128], [128, 16]]), writes=[tS], allow_slow_non_contiguous=True)
            for g2 in range(2):
                kb.dma("sp", spt[64 * g2:64 * g2 + 64, DT, :], AP(ssm_log_dt, g2, [[0, 64], [2, 16]]), writes=[tS],
                       allow_slow_non_contiguous=True)
            Br = a_("s5_Br", [128, 16, 16], F32); Bi = a_("s5_Bi", [128, 16, 16], F32)
            Cr = a_("s5_Cr", [128, 16, 16], F32); Ci = a_("s5_Ci", [128, 16, 16], F32)
            BBr = a_("s5_BBr", [128, 16, 16], F32); BBi = a_("s5_BBi", [128, 16, 16], F32)
            TA = a_("s5_TA", [128, 16, 16], F32); TB = a_("s5_TB", [128, 16, 16], F32)
            TRe = a_("s5_TRe", [128, 16, 16], F32); TIm = a_("s5_TIm", [128, 16, 16], F32)
            dcol = a_("s5_dcol", [128, 4], F32)
            kb.dma("sp", Br[:], AP(ssm_b_re, 0, [[16, 128], [2048, 16], [1, 16]]), writes=[tS])
            kb.dma("sp", Bi[:], AP(ssm_b_im, 0, [[16, 128], [2048, 16], [1, 16]]), writes=[tS])
            tCs = []
            for g2 in range(2):
                for pair in range(16):
                    for (dst_, src_) in ((Cr, ssm_c_re), (Ci, ssm_c_im)):
                        tk_ = Tok(); tCs.append(tk_)
                        kb.dma("sp", dst_[64 * g2:64 * g2 + 64, pair, :],
                               AP(src_, (2 * pair + g2) * 1024, [[1, 64], [64, 16]]),
                               writes=[tk_], allow_slow_non_contiguous=True)
            jn = a_("s5_join", [128, 2], F32)
            V(lambda e: e.memset(jn[:], 0.0), tCs, [tS])
            kb.dma("sp", dcol[:], AP(ssm_d, 0, [[1, 128], [128, 4]]), writes=[tS], allow_slow_non_contiguous=True)

            def vv(out, a, b, op):
                V(lambda e: e.tensor_tensor(out=out, in0=a, in1=b, op=op), [tS], [tS])

            def vs(out, a, s1, op0, s2=None, op1=None):
                if op1 is None:
                    V(lambda e: e.tensor_scalar(out=out, in0=a, scalar1=s1, scalar2=None, op0=op0), [tS], [tS])
                else:
                    V(lambda e: e.tensor_scalar(out=out, in0=a, scalar1=s1, scalar2=s2, op0=op0, op1=op1), [tS], [tS])

            def cmul(orr, oi, ar, ai, br, bi, t1, t2, sign=1.0):
                vv(t1, ar, br, ALU.mult); vv(t2, ai, bi, ALU.mult); vv(orr, t1, t2, ALU.subtract)
                vv(t1, ar, bi, ALU.mult); vv(t2, ai, br, ALU.mult); vv(oi, t1, t2, ALU.add)

            A(lambda e: e.activation(out=sl(DT), in_=sl(DT), func=AF.Exp), [tS], [tS])
            vv(sl(XR), sl(aR), sl(DT), ALU.mult)
            vv(sl(ANG), sl(aI), sl(DT), ALU.mult)
            A(lambda e: e.activation(out=sl(MAG), in_=sl(XR), func=AF.Exp), [tS], [tS])

            def sin_of(dst, shift):
                vs(sl(T1), sl(ANG), shift, ALU.add)
                vs(sl(T3), sl(T1), 1.0, ALU.mult)
                for m in (1, 3, 5, 7, 9):
                    vs(sl(T2), sl(T1), m * PI, ALU.is_ge, -2.0 * PI, ALU.mult)
                    vv(sl(T3), sl(T3), sl(T2), ALU.add)
                A(lambda e: e.activation(out=sl(dst), in_=sl(T3), func=AF.Sin), [tS], [tS])
            sin_of(SINV, 0.0)
            sin_of(COSV, PI / 2)
            V(lambda e: e.memset(sl(PWR + 0), 1.0), [tS], [tS])
            V(lambda e: e.memset(sl(PWI + 0), 0.0), [tS], [tS])
            vv(sl(PWR + 1), sl(MAG), sl(COSV), ALU.mult)
            vv(sl(PWI + 1), sl(MAG), sl(SINV), ALU.mult)
            for k in range(2, 9):
                cmul(sl(PWR + k), sl(PWI + k), sl(PWR + k - 1), sl(PWI + k - 1), sl(PWR + 1), sl(PWI + 1), sl(T1), sl(T2))
            SQ = a_("s5_sq", [128, 10, 16], F32)
            V(lambda e: e.tensor_copy(SQ[:, 0, :], sl(PWR + 8)), [tS], [tS])
            V(lambda e: e.tensor_copy(SQ[:, 5, :], sl(PWI + 8)), [tS], [tS])
            for q in range(1, 5):
                cmul(SQ[:, q, :], SQ[:, 5 + q, :], SQ[:, q - 1, :], SQ[:, 4 + q, :], SQ[:, q - 1, :], SQ[:, 4 + q, :],
                     sl(T1), sl(T2))
            Q128 = a_("s5_q128", [128, 18, 16], F32)
            V(lambda e: e.memset(Q128[:, 0, :], 1.0), [tS], [tS])
            V(lambda e: e.memset(Q128[:, 9, :], 0.0), [tS], [tS])
            for r in range(1, 9):
                cmul(Q128[:, r, :], Q128[:, 9 + r, :], Q128[:, r - 1, :], Q128[:, 8 + r, :], SQ[:, 4, :], SQ[:, 9, :],
                     sl(T1), sl(T2))
            vv(sl(T1), sl(aR), sl(aR), ALU.mult); vv(sl(T2), sl(aI), sl(aI), ALU.mult)
            vv(sl(DEN), sl(T1), sl(T2), ALU.add)
            V(lambda e: e.reciprocal(out=sl(DEN), in_=sl(DEN)), [tS], [tS])
            vs(sl(M1), sl(PWR + 1), -1.0, ALU.add)
            vv(sl(T1), sl(M1), sl(aR), ALU.mult); vv(sl(T2), sl(PWI + 1), sl(aI), ALU.mult)
            vv(sl(T3), sl(T1), sl(T2), ALU.add); vv(sl(CFR), sl(T3), sl(DEN), ALU.mult)
            vv(sl(T1), sl(PWI + 1), sl(aR), ALU.mult); vv(sl(T2), sl(M1), sl(aI), ALU.mult)
            vv(sl(T3), sl(T1), sl(T2), ALU.subtract); vv(sl(CFI), sl(T3), sl(DEN), ALU.mult)
            cmul(BBr[:], BBi[:], slb(CFR), slb(CFI), Br[:], Bi[:], TA[:], TB[:])

            def scatter(dst_t, pair_stride, base_off, src_t, neg=False):
                for g2 in range(2):
                    d_ = AP(dst_t, (64 * g2) * dst_t_pstride[id(dst_t)] + base_off + 16 * g2,
                            [[dst_t_pstride[id(dst_t)], 64], [2 * pair_stride, 8], [pair_stride + 32, 2], [1, 16]])
                    s_ = AP(src_t, (64 * g2) * 256, [[256, 64], [32, 8], [16, 2], [1, 16]])
                    if neg:
                        V(lambda e: e.tensor_scalar(out=d_, in0=s_, scalar1=-1.0, scalar2=None, op0=ALU.mult), [tS], [tS])
                    else:
                        V(lambda e: e.tensor_copy(d_, s_), [tS], [tS])
            dst_t_pstride = {}
            SPt = a_("s5_SPt", [128, 16, 2, 64], BF16); dst_t_pstride[id(SPt)] = 16 * 2 * 64
            V(lambda e: e.memset(SPt[:], 0.0), [tS], [tS])
            Zs = a_("s5_Zs", [128, 2, 16, 256], F32); tZ = Tok()
            ZW = 2 * 16 * 256
            Eg = a_("s5_Eg", [128, 8, 2, 256], F32); tEg = Tok()
            RT = a_("s5_rt", [128, 4, 256], F32)

            def zv(ri, bb):
                return AP(Zs, ri * 4096 + bb, [[ZW, 128], [256, 16], [16, 16]])

            def lb(t_, idx):
                return AP(t_, idx * 16, [[t_pstride[id(t_)], 128], [1, 16], [0, 16]])
            t_pstride = {id(SQ): 160, id(Q128): 288, id(spt): SPW}

            def rt(i):
                return AP(RT, i * 256, [[1024, 128], [16, 16], [1, 16]])

            def zz(out, a, b, op, r, w):
                V(lambda e: e.tensor_tensor(out=out, in0=a, in1=b, op=op), r, w)
            L8r, L8i = lb(SQ, 0), lb(SQ, 5)
            with contextlib.ExitStack() as sz:
                Wz = sz.enter_context(nc.sbuf_tensor("s5_Wz", [128, 4, 2, 8, 2, 128], BF16)); tWz = Tok()
                pz = Rot([sz.enter_context(nc.psum_tensor("s5_pz%d" % i, [128, 4, 128], BF16)) for i in range(2)])
                pZ = Rot([sz.enter_context(nc.psum_tensor("s5_pZ%d" % i, [128, 256], F32)) for i in range(2)])
                G(lambda e: e.memset(Wz[:], 0.0), [], [tWz])
                for k in range(8):
                    cmul(TRe[:], TIm[:], BBr[:], BBi[:], slb(PWR + k), slb(PWI + k), TA[:], TB[:])
                    scatter(SPt, 128, 0, TRe); scatter(SPt, 128, 64, TIm)
                    for quad in range(8):
                        T, h2 = quad // 2, quad % 2
                        pz_, tpz = pz.next()
                        for pq in range(2):
                            for ri in range(2):
                                P(lambda e, pq=pq, ri=ri: e.transpose(out=pz_[64 * h2:64 * h2 + 64, pq * 2 + ri, :],
                                                                      in_=SPt[:, 2 * quad + pq, ri, :], identity=ident[:]),
                                  [tS, t_ident], [tpz])
                        o_ = AP(Wz, (64 * h2) * 16384 + T * 4096 + (7 - k) * 256,
                                [[16384, 64], [2048, 2], [128, 2], [1, 128]])
                        i_ = AP(pz_, (64 * h2) * 512, [[512, 64], [256, 2], [128, 2], [1, 128]])
                        A(lambda e: e.copy(out=o_, in_=i_), [tpz], [tWz])
                def z_matmuls():
                    for pair in range(16):
                        quad, pq = pair // 2, pair % 2
                        T, h2 = quad // 2, quad % 2
                        for ri in range(2):
                            pZ_, tpZ = pZ.next()
                            for ip in range(8):
                                P(lambda e, ip=ip: e.matmul(pZ_[:], lhsT=Wz[64 * h2:64 * h2 + 64, T, pq, ip, ri, :],
                                                            rhs=UT[64 * h2:64 * h2 + 64, T, ip, :],
                                                            start=(ip == 0), stop=(ip == 7)), [tWz, tUT], [tpZ])
                            if ri == 0:
                                V(lambda e: e.tensor_copy(Zs[:, ri, pair, :], pZ_[:]), [tpZ], [tZ])
                            else:
                                A(lambda e: e.copy(out=Zs[:, ri, pair, :], in_=pZ_[:]), [tpZ], [tZ])
                def recur():
                    for bb in range(1, 16):
                        zz(rt(0), zv(0, bb - 1), L8r, ALU.mult, [tZ, tS], [tZ])
                        zz(rt(1), zv(1, bb - 1), L8i, ALU.mult, [tZ, tS], [tZ])
                        zz(rt(2), zv(1, bb - 1), L8r, ALU.mult, [tZ, tS], [tZ])
                        zz(rt(3), zv(0, bb - 1), L8i, ALU.mult, [tZ, tS], [tZ])
                        zz(zv(0, bb), zv(0, bb), rt(0), ALU.add, [tZ], [tZ])
                        zz(zv(0, bb), zv(0, bb), rt(1), ALU.subtract, [tZ], [tZ])
                        zz(zv(1, bb), zv(1, bb), rt(2), ALU.add, [tZ], [tZ])
                        zz(zv(1, bb), zv(1, bb), rt(3), ALU.add, [tZ], [tZ])
                phase_all(UT, tUT, Zs, tZ, Eg, tEg, z_matmuls, recur, zv)
                with contextlib.ExitStack() as su:
                    X = norm_ctx(su, 2, g_mix)
                    WU = su.enter_context(nc.sbuf_tensor("WU", [128, 8, 512], BF16)); tWU = Tok()
                    for k in range(8):
                        kb.dma("pool", WU[:, k, :], AP(w_in, k * 128 * INC, [[INC, 128], [1, 512]]), writes=[tWU])
                    pP = Rot([su.enter_context(nc.psum_tensor("u_pP%d" % i, [128, 512], F32)) for i in range(2)])
                    for st in range(8):
                        hT, thT = load_norm_T(x_own, st * 256, 2, X)
                        for T in range(4):
                            pp, tpp = pP.next()
                            for k in range(8):
                                P(lambda e, k=k: e.matmul(pp[:, 0:256], lhsT=WU[:, k, T * 128:(T + 1) * 128], rhs=hT[:, k, :],
                                                          start=(k == 0), stop=(k == 7)), [tWU, thT], [tpp])
                            o_ = AP(UT, T * 2048 + st * 32, [[UTW, 128], [256, 8], [1, 32]])
                            i_ = AP(pp, 0, [[512, 128], [1, 8], [8, 32]])
                            if T % 2 == 0:
                                V(lambda e: e.tensor_copy(o_, i_), [tpp], [tUT])
                            else:
                                A(lambda e: e.copy(out=o_, in_=i_), [tpp], [tUT])
                    kb.barrier()
                z_matmuls()
                recur()
                kb.barrier()
            ck("z")
            if BARRIERS: kb.barrier()
            B4 = a_("s5_B4", [128, 16, 2, 64], BF16); dst_t_pstride[id(B4)] = 16 * 2 * 64
            Wc0 = a_("s5_Wc0", [128, 16, 2, 64], BF16); dst_t_pstride[id(Wc0)] = 16 * 2 * 64
            Wc = a_("s5_Wc", [128, 16, 8, 2, 64], BF16); dst_t_pstride[id(Wc)] = 16 * 8 * 2 * 64
            Kblk = a_("s5_Kblk", [128, 4, 8, 128], BF16)
            for t_ in (B4, Wc0, Kblk):
                V(lambda e, t_=t_: e.memset(t_[:], 0.0), [tS], [tS])
            G(lambda e: e.memset(Wc[:], 0.0), [tS], [tS])
            scatter(B4, 128, 0, BBr); scatter(B4, 128, 64, BBi)
            for k in range(0, 9):
                cmul(TRe[:], TIm[:], Cr[:], Ci[:], slb(PWR + k), slb(PWI + k), TA[:], TB[:])
                if k == 0:
                    scatter(Wc0, 128, 0, TRe); scatter(Wc0, 128, 64, TIm, neg=True)
                else:
                    scatter(Wc, 1024, (k - 1) * 128, TRe); scatter(Wc, 1024, (k - 1) * 128 + 64, TIm, neg=True)
            identf = a_("s5_identf", [128, 128], F32)
            kb.dma("sp", identf[:], identf_d.ap(), writes=[tS])
            ck("tab")
            with contextlib.ExitStack() as sk:
                pK = Rot([sk.enter_context(nc.psum_tensor("s5_pK%d" % i, [128, 128], F32)) for i in range(2)])
                for T in range(4):
                    for tau in range(8):
                        pk, tpk = pK.next()
                        for h2 in range(2):
                            for pq in range(2):
                                pair = 2 * (2 * T + h2) + pq
                                for ri in range(2):
                                    rhs_ = Wc0[:, pair, ri, :] if tau == 0 else Wc[:, pair, tau - 1, ri, :]
                                    P(lambda e, h2=h2, pair=pair, ri=ri, rhs_=rhs_, pq=pq: e.matmul(
                                        pk[64 * h2:64 * h2 + 64, 64 * h2:64 * h2 + 64], lhsT=B4[:, pair, ri, :], rhs=rhs_,
                                        start=(pq == 0 and ri == 0), stop=(pq == 1 and ri == 1)), [tS], [tpk])
                        for h2 in range(2):
                            sl_ = slice(64 * h2, 64 * h2 + 64)
                            if tau == 0:
                                V(lambda e, sl_=sl_: e.scalar_tensor_tensor(out=Kblk[sl_, T, 0, sl_], in0=identf[sl_, sl_],
                                                                            scalar=dcol[sl_, T:T + 1], in1=pk[sl_, sl_],
                                                                            op0=ALU.mult, op1=ALU.add), [tpk, tS], [tS])
                            else:
                                V(lambda e, sl_=sl_: e.tensor_copy(Kblk[sl_, T, tau, sl_], pk[sl_, sl_]), [tpk], [tS])
            ck("kblk")
            if BARRIERS: kb.barrier()
            Xp = a_("s5_Xp", [128, 2, 16, 256], BF16); tXp = Tok()
            with contextlib.ExitStack() as sc:
                def c_(name, shape, dt):
                    return sc.enter_context(nc.sbuf_tensor(name, list(shape), dt))
                Dd = c_("s5_D", [128, 9, 2, 256], F32); tD = Tok()
                Gg = c_("s5_G", [128, 2, 16, 16], F32)
                Cw = c_("s5_Cw", [128, 2, 256], F32)
                Cn = c_("s5_Cn", [128, 2, 256], F32)
                oh = c_("s5_oh", [128, 8], F32)
                kb.dma("sp", oh[:], onehot_d.ap(), writes=[tD])

                def dv(r, ri):
                    return AP(Dd, (r * 2 + ri) * 256, [[9 * 512, 128], [16, 16], [1, 16]])

                def ev(r, ri):
                    return AP(Eg, (r * 2 + ri) * 256, [[8 * 512, 128], [16, 16], [1, 16]])
                L128r, L128i = lb(SQ, 4), lb(SQ, 9)
                for ri in range(2):
                    V(lambda e, ri=ri: e.memset(dv(0, ri), 0.0), [], [tD])
                for r in range(8):
                    zz(rt(0), dv(r, 0), L128r, ALU.mult, [tD, tS], [tD]); zz(rt(1), dv(r, 1), L128i, ALU.mult, [tD, tS], [tD])
                    zz(rt(2), dv(r, 1), L128r, ALU.mult, [tD, tS], [tD]); zz(rt(3), dv(r, 0), L128i, ALU.mult, [tD, tS], [tD])
                    zz(dv(r + 1, 0), rt(0), rt(1), ALU.subtract, [tD], [tD])
                    zz(dv(r + 1, 0), dv(r + 1, 0), ev(r, 0), ALU.add, [tD, tEg], [tD])
                    zz(dv(r + 1, 1), rt(2), rt(3), ALU.add, [tD], [tD])
                    zz(dv(r + 1, 1), dv(r + 1, 1), ev(r, 1), ALU.add, [tD, tEg], [tD])
                def gv(ri, j):
                    return Gg[:, ri, :, j]

                def d8(ri, j):
                    return AP(Dd, (8 * 2 + ri) * 256 + j, [[9 * 512, 128], [16, 16]])
                Lkr, Lki = Q128[:, 8, :], Q128[:, 17, :]
                V(lambda e: e.memset(Gg[:, :, :, 0], 0.0), [], [tD])
                for j in range(15):
                    zz(sl(T1), gv(0, j), Lkr, ALU.mult, [tD, tS], [tS]); zz(sl(T2), gv(1, j), Lki, ALU.mult, [tD, tS], [tS])
                    zz(sl(T3), gv(1, j), Lkr, ALU.mult, [tD, tS], [tS]); zz(sl(T4), gv(0, j), Lki, ALU.mult, [tD, tS], [tS])
                    zz(sl(T1), sl(T1), sl(T2), ALU.subtract, [tS], [tS])
                    zz(gv(0, j + 1), sl(T1), d8(0, j), ALU.add, [tS, tD], [tD])
                    zz(sl(T3), sl(T3), sl(T4), ALU.add, [tS], [tS])
                    zz(gv(1, j + 1), sl(T3), d8(1, j), ALU.add, [tS, tD], [tD])
                Gr = AP(Gg, 0, [[512, 128], [16, 16], [1, 16]]); Gi = AP(Gg, 256, [[512, 128], [16, 16], [1, 16]])
                cw = [AP(Cw, ri * 256, [[512, 128], [16, 16], [1, 16]]) for ri in range(2)]
                for ri in range(2):
                    V(lambda e, ri=ri: e.memset(cw[ri], 0.0), [], [tD])
                for r in range(8):
                    qr, qi = lb(Q128, r), lb(Q128, 9 + r)
                    zz(rt(0), Gr, qr, ALU.mult, [tD, tS], [tD]); zz(rt(1), Gi, qi, ALU.mult, [tD, tS], [tD])
                    zz(rt(2), Gi, qr, ALU.mult, [tD, tS], [tD]); zz(rt(3), Gr, qi, ALU.mult, [tD, tS], [tD])
                    zz(rt(0), rt(0), rt(1), ALU.subtract, [tD], [tD]); zz(rt(0), rt(0), dv(r, 0), ALU.add, [tD], [tD])
                    zz(rt(2), rt(2), rt(3), ALU.add, [tD], [tD]); zz(rt(2), rt(2), dv(r, 1), ALU.add, [tD], [tD])
                    for ri, src in ((0, rt(0)), (1, rt(2))):
                        V(lambda e, ri=ri, src=src: e.scalar_tensor_tensor(out=cw[ri], in0=src, scalar=oh[:, r:r + 1],
                                                                           in1=cw[ri], op0=ALU.mult, op1=ALU.add),
                          [tD], [tD])
                cn = [AP(Cn, ri * 256, [[512, 128], [16, 16], [1, 16]]) for ri in range(2)]

                def xpv(ri, bb):
                    return AP(Xp, ri * 4096 + bb, [[ZW, 128], [256, 16], [16, 16]])
                for bb in range(16):
                    for ri in range(2):
                        if bb == 0:
                            V(lambda e, ri=ri: e.tensor_copy(xpv(ri, 0), cw[ri]), [tD], [tXp])
                        else:
                            zz(xpv(ri, bb), zv(ri, bb - 1), cw[ri], ALU.add, [tZ, tD], [tXp])
                    if bb < 15:
                        zz(rt(0), cw[0], L8r, ALU.mult, [tD, tS], [tD]); zz(rt(1), cw[1], L8i, ALU.mult, [tD, tS], [tD])
                        zz(rt(2), cw[1], L8r, ALU.mult, [tD, tS], [tD]); zz(rt(3), cw[0], L8i, ALU.mult, [tD, tS], [tD])
                        zz(cn[0], rt(0), rt(1), ALU.subtract, [tD], [tD]); zz(cn[1], rt(2), rt(3), ALU.add, [tD], [tD])
                        for ri in range(2):
                            V(lambda e, ri=ri: e.tensor_copy(cw[ri], cn[ri]), [tD], [tD])
            ck("car")
            if BARRIERS: kb.barrier()
            with contextlib.ExitStack() as sy_:
                def y_(name, shape, dt):
                    return sy_.enter_context(nc.sbuf_tensor(name, list(shape), dt))
                zT = y_("s5_zT", [128, 4, 2048], BF16); tzT = Tok()
                pY = Rot([sy_.enter_context(nc.psum_tensor("s5_pY%d" % i, [128, 256], F32)) for i in range(4)])
                for T in range(4):
                    for i in range(8):
                        py, tpy = pY.next()
                        for h2 in range(2):
                            hs = slice(64 * h2, 64 * h2 + 64)
                            for ip in range(i + 1):
                                P(lambda e, ip=ip, hs=hs: e.matmul(py[hs, :], lhsT=Kblk[:, T, i - ip, hs], rhs=UT[:, T, ip, :],
                                                                   start=(ip == 0), stop=False), [tS, tUT], [tpy])
                            n_ = 0
                            for pq in range(2):
                                pair = 2 * (2 * T + h2) + pq
                                for ri in range(2):
                                    n_ += 1
                                    P(lambda e, hs=hs, pair=pair, ri=ri, n_=n_: e.matmul(
                                        py[hs, :], lhsT=Wc[:, pair, i, ri, :], rhs=Xp[:, ri, pair, :],
                                        start=False, stop=(n_ == 4)), [tS, tXp], [tpy])
                        o_ = AP(zT, T * 2048 + i, [[8192, 128], [8, 256]])
                        A(lambda e: e.activation(out=o_, in_=py[:], func=AF.Gelu_apprx_tanh), [tpy], [tzT])
                ck("y")
                if BARRIERS: kb.barrier()
                pG = Rot([sy_.enter_context(nc.psum_tensor("s5_pG%d" % i, [128, 512], F32)) for i in range(2)])
                sg = Rot([y_("s5_sg%d" % i, [128, 512], F32) for i in range(2)])
                zg = Rot([y_("s5_zg%d" % i, [128, 512], BF16) for i in range(2)])
                for ch in range(4):
                    for m in range(4):
                        pg, tpg = pG.next()
                        for k4 in range(4):
                            P(lambda e, k4=k4: e.matmul(pg[:], lhsT=Wgl[:, k4, m * 128:(m + 1) * 128],
                                                        rhs=zT[:, k4, ch * 512:(ch + 1) * 512],
                                                        start=(k4 == 0), stop=(k4 == 3)), [tWgl, tzT], [tpg])
                        s_, ts_ = sg.next()
                        A(lambda e: e.activation(out=s_[:], in_=pg[:], func=AF.Sigmoid), [tpg], [ts_])
                        z_, tz_ = zg.next()
                        V(lambda e: e.tensor_tensor(out=z_[:], in0=s_[:], in1=zT[:, m, ch * 512:(ch + 1) * 512],
                                                    op=ALU.mult), [ts_, tzT], [tz_])
                        kb.dma("sp", AP(zg_d, m * 2048 + ch * 512, [[4 * 2048, 128], [1, 512]]), z_[:],
                               reads=[tz_], writes=[t_zg])


    NEG = -30000.0

    def phase_tables():
        with contextlib.ExitStack() as st:
            def a_(name, shape, dt):
                return st.enter_context(nc.sbuf_tensor(name, list(shape), dt))
            tT = Tok()
            relb = a_("t_relb", [32, 8], F32); rl = a_("t_rl", [32, 8], F32)
            ohr = a_("t_ohr", [32, 256], F32); ohf = a_("t_ohf", [32, 256], F32)
            antiJ = a_("t_antiJ", [128, 128], F32)
            kb.dma("sp", relb[:], rel_bias_d.ap(), writes=[tT])
            kb.dma("sp", rl[:], AP(rel_bias_d, 31 * 8, [[0, 32], [1, 8]]), writes=[tT])
            kb.dma("sp", ohr[:], ohrev_d.ap(), writes=[tT])
            kb.dma("sp", ohf[:], ohfwd_d.ap(), writes=[tT])
            kb.dma("sp", antiJ[:], antij_d.ap(), writes=[tT])
            V(lambda e: e.tensor_tensor(out=relb[:], in0=relb[:], in1=rl[:], op=ALU.subtract), [tT], [tT])
            pt = st.enter_context(nc.psum_tensor("t_pt", [8, 512], F32)); tpt = Tok()
            P(lambda e: e.matmul(pt[:, 0:256], lhsT=relb[:], rhs=ohr[:], start=True, stop=True), [tT], [tpt])
            P(lambda e: e.matmul(pt[:, 256:512], lhsT=relb[:], rhs=ohf[:], start=True, stop=True), [tT], [tpt])
            rowR = a_("t_rowR", [8, 384], F32); rowF = a_("t_rowF", [8, 416], F32)
            V(lambda e: e.memset(rowR[:], NEG), [tT], [tT])
            V(lambda e: e.memset(rowF[:], NEG), [tT], [tT])
            V(lambda e: e.tensor_copy(rowR[:, 0:256], pt[:, 0:256]), [tpt, tT], [tT])
            V(lambda e: e.tensor_copy(rowF[:, 160:416], pt[:, 256:512]), [tpt, tT], [tT])
            tD_ = Tok()
            kb.dma("sp", tabR_d.ap(), rowR[:], reads=[tT], writes=[tD_])
            kb.dma("sp", tabF_d.ap(), rowF[:], reads=[tT], writes=[tD_])
            Hk = a_("t_Hk", [128, 2, 8, 128], F32); tH = Tok()
            kb.dma("sp", Hk[:, 0, :, :], AP(tabR_d, 128, [[1, 128], [384, 8], [1, 128]]), reads=[tD_], writes=[tH])
            kb.dma("sp", Hk[:, 1, :, :], AP(tabR_d, 0, [[1, 128], [384, 8], [1, 128]]), reads=[tD_], writes=[tH])
            pb = Rot([st.enter_context(nc.psum_tensor("t_pb%d" % i, [128, 128], F32)) for i in range(2)])
            for d_ in range(2):
                for h in range(8):
                    p_, tp_ = pb.next()
                    P(lambda e: e.matmul(p_[:], lhsT=Hk[:, d_, h, :], rhs=antiJ[:], start=True, stop=True), [tH, tT], [tp_])
                    V(lambda e: e.tensor_copy(BW[:, d_, h, :], p_[:]), [tp_], [t_BW])
            kb.dma("sp", M4[:], m4_d.ap(), writes=[t_BW])
            V(lambda e: e.memset(BnA[:], 1.0), [], [t_BnA])
            kb.dma("pool", BnA[0:16, :, :], AP(tabF_d, 17, [[16, 16], [416, 8], [1, 128]]), reads=[tD_], writes=[t_BnA])

    def phase_compress():
        with contextlib.ExitStack() as sc:
            def a_(name, shape, dt):
                return sc.enter_context(nc.sbuf_tensor(name, list(shape), dt))
            tW = Tok()
            W1s = a_("c_W1s", [128, 2, 16, 256], BF16)
            W2 = a_("c_W2", [128, 2, 2, 64], BF16)
            PosS = a_("c_PosS", [128, 2, 16], BF16)
            for w, (w1, w2, pos) in enumerate(((cmp_w1_k, cmp_w2_k, cmp_pos_k), (cmp_w1_v, cmp_w2_v, cmp_pos_v))):
                for lp in range(16):
                    kb.dma("pool", W1s[:, w, lp, :], AP(w1, lp * 128 * 256, [[256, 128], [1, 256]]), writes=[tW])
                kb.dma("pool", W2[:, w, :, :], AP(w2, 0, [[64, 128], [128 * 64, 2], [1, 64]]), writes=[tW])
                kb.dma("pool", PosS[:, w, :], AP(pos, 0, [[1, 128], [128, 16]]), writes=[tW], allow_slow_non_contiguous=True)
            biasW = a_("c_biasW", [128, 4], F32); tB = Tok()
            pB = sc.enter_context(nc.psum_tensor("c_pB", [128, 4], F32)); tpB = Tok()
            for w in range(2):
                for ht in range(2):
                    for lp in range(16):
                        P(lambda e, lp=lp: e.matmul(pB[:, w * 2 + ht:w * 2 + ht + 1], lhsT=W1s[:, w, lp, ht * 128:(ht + 1) * 128],
                                                    rhs=PosS[:, w, lp:lp + 1], start=(lp == 0), stop=(lp == 15)), [tW], [tpB])
            V(lambda e: e.tensor_copy(biasW[:], pB[:]), [tpB], [tB])
            CW = 2 * 2 * 2080
            CRs = a_("c_CRs", [128, 2, 2, 2080], BF16); tCR = Tok()
            G1 = a_("c_G1", [128, 2, 2, 2, 128], BF16); tG1 = Tok()
            pH = Rot([sc.enter_context(nc.psum_tensor("c_pH%d" % i, [128, 128], F32)) for i in range(2)])
            pO = Rot([sc.enter_context(nc.psum_tensor("c_pO%d" % i, [128, 128], F32)) for i in range(2)])
            V(lambda e: e.memset(Vc_aug[:], 1.0), [], [t_Vc])
            for nt in range(8):
                t0 = 2048 * nt
                for w, src in enumerate((kcR_d, vcR_d)):
                    for g in range(2):
                        kb.dma("sp", CRs[0:64, w, g, 0:2064], AP(src, (64 * g) * (S + 16) + t0, [[S + 16, 64], [1, 2064]]),
                               reads=kv_toks, writes=[tCR])
                        kb.dma("sp", CRs[64:128, w, g, 0:2063], AP(src, (64 * g) * (S + 16) + t0 + 1, [[S + 16, 64], [1, 2063]]),
                               reads=kv_toks, writes=[tCR])
                for w in range(2):
                    for g in range(2):
                        for ht in range(2):
                            ph, tph = pH.next()
                            for lp in range(16):
                                rhs_ = AP(CRs, (w * 2 + g) * 2080 + 2 * lp, [[CW, 128], [16, 128]])
                                P(lambda e, lp=lp, rhs_=rhs_: e.matmul(ph[:], lhsT=W1s[:, w, lp, ht * 128:(ht + 1) * 128], rhs=rhs_,
                                                                       start=(lp == 0), stop=(lp == 15)), [tW, tCR], [tph])
                            A(lambda e: e.activation(out=G1[:, w, g, ht, :], in_=ph[:], func=AF.Gelu_apprx_tanh,
                                                     bias=biasW[:, w * 2 + ht:w * 2 + ht + 1]), [tph, tB], [tG1])
                po, tpo = pO.next()
                for g in range(2):
                    for ht in range(2):
                        P(lambda e, ht=ht: e.matmul(po[64 * g:64 * g + 64, :], lhsT=W2[:, 0, ht, :], rhs=G1[:, 0, g, ht, :],
                                                    start=(ht == 0), stop=(ht == 1)), [tW, tG1], [tpo])
                V(lambda e: e.tensor_copy(KcT[:, nt * 128:(nt + 1) * 128], po[:]), [tpo], [t_Kc])
                po, tpo = pO.next()
                for g in range(2):
                    for ht in range(2):
                        P(lambda e, ht=ht: e.matmul(po[:, 64 * g:64 * g + 64], lhsT=G1[:, 1, g, ht, :], rhs=W2[:, 1, ht, :],
                                                    start=(ht == 0), stop=(ht == 1)), [tW, tG1], [tpo])
                V(lambda e: e.tensor_copy(Vc_aug[:, nt, :, 0:64], po[:].rearrange("p (g d) -> p g d", g=2)), [tpo], [t_Vc])

    def attn_tile(ps_, tps_, kT_ap, q_ap, deps_r, bias_ap, bias_tok, P_rot, Sb_rot):
        P(lambda e: e.matmul(ps_[:], lhsT=kT_ap, rhs=q_ap, start=True, stop=True), deps_r, [tps_])
        p_, tp_ = P_rot.next()
        if bias_ap is not None:
            sb_, tsb_ = Sb_rot.next()
            V(lambda e: e.tensor_tensor(out=sb_[:], in0=ps_[:], in1=bias_ap, op=ALU.add), [tps_, bias_tok], [tsb_])
            A(lambda e: e.activation(out=p_[:], in_=sb_[:], func=AF.Exp), [tsb_], [tp_])
        else:
            A(lambda e: e.activation(out=p_[:], in_=ps_[:], func=AF.Exp), [tps_], [tp_])
        return p_, tp_

    def combine(po_, tpo_, gcol0, gstride, j, g, acc_, tacc_, first, cf_rot):
        cf, tcf = cf_rot.next()
        rs_ = AP(po_, 64, [[int(np.prod(list(po_.shape)[1:])), 128], [65, 4]])
        V(lambda e: e.tensor_scalar(out=cf[:, 0:4], in0=rs_, scalar1=1e-30, scalar2=None, op0=ALU.max), [tpo_], [tcf])
        V(lambda e: e.reciprocal(out=cf[:, 4:8], in_=cf[:, 0:4]), [tcf], [tcf])
        g_ = AP(gates, j * 24 + 12 * g + gcol0, [[16 * 24, 128], [3, 4]])
        V(lambda e: e.tensor_tensor(out=cf[:, 8:12], in0=cf[:, 4:8], in1=g_, op=ALU.mult), [tcf, t_gates], [tcf])
        for r in range(4):
            h = 4 * g + r
            o_ = acc_[:, h * 64:(h + 1) * 64]
            if first:
                V(lambda e, r=r, o_=o_: e.tensor_scalar(out=o_, in0=po_[:, r * 65:r * 65 + 64], scalar1=cf[:, 8 + r:9 + r],
                                                        scalar2=None, op0=ALU.mult), [tpo_, tcf], [tacc_])
            else:
                V(lambda e, r=r, o_=o_: e.scalar_tensor_tensor(out=o_, in0=po_[:, r * 65:r * 65 + 64], scalar=cf[:, 8 + r:9 + r],
                                                               in1=o_, op0=ALU.mult, op1=ALU.add), [tpo_, tcf], [tacc_])

    def pv_T(poT, tpoT, Pm, tPm, v_ap, tv, first, last):
        P(lambda e: e.matmul(poT[0:65, :], lhsT=v_ap, rhs=Pm[:], start=first, stop=last), [tPm, tv], [tpoT])

    def finish_o(poT, tpoT, po, tpo, osb, tosb):
        V(lambda e: e.tensor_copy(osb[0:65, :], poT[0:65, :]), [tpoT], [tosb])
        for r in range(4):
            P(lambda e, r=r: e.transpose(out=po[:, r * 65:(r + 1) * 65], in_=osb[0:65, r * 128:(r + 1) * 128],
                                         identity=identF[0:65, 0:65]), [tosb, t_identF], [tpo])

    def phase1():
        with contextlib.ExitStack() as s1:
            def a_(name, shape, dt):
                return s1.enter_context(nc.sbuf_tensor(name, list(shape), dt))
            X = norm_ctx(s1, 2, g_mix)
            WinA = a_("WinA", [128, 8, 1048], BF16); tWin = Tok()
            for k in range(8):
                base = k * 128 * INC
                for r in range(4):
                    kb.dma("pool", WinA[:, k, r * 128:(r + 1) * 128].rearrange("p (g d) -> p g d", g=2),
                           AP(w_in, base + 512 + 64 * r, [[INC, 128], [256, 2], [1, 64]]), writes=[tWin])
                for (c0, s0, n) in ((512, 1536, 128), (640, 1280, 128), (768, 1664, 128), (896, 1408, 128), (1024, 1792, 24)):
                    kb.dma("pool", WinA[:, k, c0:c0 + n], AP(w_in, base + s0, [[INC, 128], [1, n]]), writes=[tWin])
            vprev = a_("vprev_sb", [128, 64], F32); tvp = Tok()
            kb.dma("sp", vprev[:], vprev_d.ap(), writes=[tvp])
            KwT = [a_("KwT%d" % i, [128, 5, 128], BF16) for i in range(2)]; tKw = [Tok(), Tok()]
            Vw = [a_("Vw%d" % i, [128, 5, 2, 65], BF16) for i in range(2)]; tVw = [Tok(), Tok()]
            pP = Rot([s1.enter_context(nc.psum_tensor("p1_pP%d" % i, [128, 512], F32)) for i in range(2)])
            pS = Rot([s1.enter_context(nc.psum_tensor("p1_pS%d" % i, [128, 512], F32)) for i in range(2)])
            pO = Rot([s1.enter_context(nc.psum_tensor("p1_pO%d" % i, [128, 512], F32)) for i in range(1)])
            poT = s1.enter_context(nc.psum_tensor("p1_poT", [128, 512], F32)); tpoT = Tok()
            osb = a_("p1_osb", [128, 512], F32); tosb = Tok()
            M4r = a_("p1_M4r", [128, 512], F32)
            for r_ in range(4):
                V(lambda e, r_=r_: e.tensor_copy(M4r[:, r_ * 128:(r_ + 1) * 128], M4[:]), [t_BW], [t_BW])
            Pr = Rot([a_("p1_P%d" % i, [128, 512], BF16) for i in range(2)])
            Sbr = Rot([a_("p1_Sb%d" % i, [128, 512], F32) for i in range(2)])
            cfr = Rot([a_("p1_cf%d" % i, [128, 12], F32) for i in range(2)])
            accw = Rot([a_("p1_acc%d" % i, [128, 512], F32) for i in range(2)])
            V(lambda e: e.memset(VsN[:], 1.0), [], [t_VsN])
            nev = [0]

            def evac(o_, i_, rd, wr, scale=None):
                nev[0] += 1
                if scale is not None:
                    A(lambda e: e.activation(out=o_, in_=i_, func=AF.Copy, scale=scale), rd, wr)
                else:
                    V(lambda e: e.tensor_copy(o_, i_), rd, wr)

            def fm(col0, hT, thT, c0, n):
                pp, tpp = pP.next()
                for k in range(8):
                    P(lambda e, k=k: e.matmul(pp[:, 0:n], lhsT=WinA[:, k, col0:col0 + 128], rhs=hT[:, k, c0:c0 + n],
                                              start=(k == 0), stop=(k == 7)), [tWin, thT], [tpp])
                return pp, tpp

            def tm(col0, ncol, hT, thT, a):
                pp, tpp = pP.next()
                for k in range(8):
                    P(lambda e, k=k: e.matmul(pp[:, 0:ncol], lhsT=hT[:, k, a * 128:(a + 1) * 128], rhs=WinA[:, k, col0:col0 + ncol],
                                              start=(k == 0), stop=(k == 7)), [tWin, thT], [tpp])
                return pp, tpp
            ck("p1w")
            for oc in range(8):
                hT, thT = load_norm_T(x_own, oc * 256, 2, X)
                for r in range(4):
                    pp, tpp = fm(r * 128, hT, thT, 0, 256)
                    evac(AP(Qall, (2 * oc) * 512 + r * 128, [[16 * 512, 128], [512, 2], [1, 128]]),
                         AP(pp, 0, [[512, 128], [128, 2], [1, 128]]), [tpp], [t_Q], scale=0.125)
                ck("p1a")
                pp, tpp = fm(512, hT, thT, 0, 256)
                ck("p1a1")
                for a in range(2):
                    evac(KwT[a][:, 4, :], pp[:, a * 128:(a + 1) * 128], [tpp], [tKw[a]])
                ck("p1a2")
                pp, tpp = fm(640, hT, thT, 0, 256)
                evac(AP(KsN, (2 * oc) * 256 + 128, [[16 * 256, 128], [256, 2], [1, 128]]),
                     AP(pp, 0, [[512, 128], [128, 2], [1, 128]]), [tpp], [t_KsN])
                ck("p1b")
                for a in range(2):
                    j = 2 * oc + a
                    pp, tpp = tm(768, 256, hT, thT, a)
                    evac(Vw[a][:, 4, :, 0:64], pp[:, 0:128].rearrange("p (g d) -> p g d", g=2), [tpp], [tVw[a]])
                    V(lambda e: e.memset(Vw[a][:, 4, :, 64:65], 1.0), [], [tVw[a]])
                    evac(VsN[:, j, 1, :, 0:64], pp[:, 128:256].rearrange("p (g d) -> p g d", g=2), [tpp], [t_VsN])
                    ck("p1c")
                    pp, tpp = tm(1024, 24, hT, thT, a)
                    A(lambda e: e.activation(out=gates[:, j, :], in_=pp[:, 0:24], func=AF.Sigmoid), [tpp], [t_gates])
                ck("p1own")
                for a in range(2):
                    j = 2 * oc + a
                    for pc in range(2):
                        hP, thP = load_norm_T(x_prev, j * 512 + pc * 256, 2, X)
                        pp, tpp = fm(512, hP, thP, 0, 256)
                        evac(KwT[a][:, 2 * pc:2 * pc + 2, :], pp[:, 0:256].rearrange("p (t k) -> p t k", t=2), [tpp], [tKw[a]])
                        for a2 in range(2):
                            p_ = 2 * pc + a2
                            last = (p_ == 3)
                            pp, tpp = tm(768, 256 if last else 128, hP, thP, a2)
                            evac(Vw[a][:, p_, :, 0:64], pp[:, 0:128].rearrange("p (g d) -> p g d", g=2), [tpp], [tVw[a]])
                            vcol = AP(vprev, j * 4 + p_, [[64, 128], [0, 2], [1, 1]])
                            V(lambda e: e.tensor_copy(Vw[a][:, p_, :, 64:65], vcol), [tvp], [tVw[a]])
                            if last:
                                evac(VsN[:, j, 0, :, 0:64], pp[:, 128:256].rearrange("p (g d) -> p g d", g=2), [tpp], [t_VsN])
                                V(lambda e: e.tensor_copy(VsN[:, j, 0, :, 64:65], vcol), [tvp], [t_VsN])
                        if pc == 1:
                            pp, tpp = fm(640, hP, thP, 128, 128)
                            evac(KsN[:, j, 0, :], pp[:, 0:128], [tpp], [t_KsN])
                    ck("p1prev")
                    ac, tac = accw.next()
                    for g in range(2):
                        po, tpo = pO.next()
                        for p_ in range(5):
                            dl = 4 - p_
                            ps_, tps_ = pS.next()
                            if dl == 0:
                                b_ap, b_tok = BW[:, 0, 4 * g:4 * g + 4, :].rearrange("p h q -> p (h q)"), t_BW
                            elif dl == 1:
                                b_ap, b_tok = BW[:, 1, 4 * g:4 * g + 4, :].rearrange("p h q -> p (h q)"), t_BW
                            elif dl == 4:
                                b_ap, b_tok = M4r[:], t_BW
                            else:
                                b_ap, b_tok = None, None
                            if b_ap is not None and dl != 4:
                                pass
                            Pm, tPm = attn_tile(ps_, tps_, KwT[a][64 * g:64 * g + 64, p_, :],
                                                Qall[64 * g:64 * g + 64, j, :, :].rearrange("p r q -> p (r q)"),
                                                [tKw[a], t_Q], b_ap, b_tok, Pr, Sbr)
                            pv_T(poT, tpoT, Pm, tPm, Vw[a][:, p_, g, :], tVw[a], p_ == 0, p_ == 4)
                        finish_o(poT, tpoT, po, tpo, osb, tosb)
                        combine(po, tpo, 2, 3, j, g, ac, tac, True, cfr)
                    kb.dma("sp", acc_d.ap()[j * 128:(j + 1) * 128, :], ac[:], reads=[tac], writes=[t_accd])
                    ck("p1win")

    def phase2():
        with contextlib.ExitStack() as s2:
            def a_(name, shape, dt):
                return s2.enter_context(nc.sbuf_tensor(name, list(shape), dt))
            KsT_all = a_("KsT_all", [128, S], BF16); tKs = Tok()
            for i in range(8):
                kb.dma("sp", KsT_all[:, i * 2048:(i + 1) * 2048], AP(ksT_d, i * 2048, [[S, 128], [1, 2048]]),
                       reads=kv_toks, writes=[tKs])
            Vs_all = a_("Vs_all", [128, 128, 2, 65], BF16); tVs = Tok()
            G(lambda e: e.memset(Vs_all[:], 1.0), [], [tVs])
            for i in range(8):
                for g in range(2):
                    kb.dma("sp", Vs_all[:, i * 16:(i + 1) * 16, g, 0:64],
                           AP(vs_d, i * 16 * 16384 + 64 * g, [[128, 128], [16384, 16], [1, 64]]), reads=kv_toks, writes=[tVs])
            wide = a_("wide_sb", [128, 4096], BF16); tc_ = Tok()
            kb.dma("sp", wide[:], wide_d.ap(), writes=[tc_])
            keepb = a_("keepb", [128, 32], F32)
            kb.dma("sp", keepb[:], keepblk_d.ap(), writes=[tc_])
            candr = Rot([a_("cand%d" % i, [128, 256], BF16) for i in range(2)])
            forcr = Rot([a_("forc%d" % i, [128, 256], BF16) for i in range(2)])
            expnr = Rot([a_("expn%d" % i, [128, 2, 128], BF16) for i in range(2)])
            selcr = Rot([a_("selc%d" % i, [17, 1024], BF16) for i in range(2)])
            accr = Rot([a_("p2_acc%d" % i, [128, 512], F32) for i in range(2)])
            accb = a_("p2_accb", [128, 512], BF16); taccb = Tok()
            imp = a_("p2_imp", [128, 1028], F32); timp = Tok()
            pq = Rot([a_("p2_pq%d" % i, [128, 1024], F32) for i in range(2)])
            rs = Rot([a_("p2_rs%d" % i, [128, 4], F32) for i in range(4)])
            sS = a_("p2_s", [128, 256], F32); sS2 = a_("p2_s2", [128, 256], F32); tS_ = Tok()
            m8 = a_("p2_m8", [128, 16], F32)
            selg = a_("p2_selg", [128, 256], BF16); tselg = Tok()
            selT = a_("p2_selT", [128, 2, 2, 128], BF16); tselT = Tok()
            selTk = a_("p2_selTk", [128, 2, 2, 128], BF16)
            Pr = Rot([a_("p2_P%d" % i, [128, 512], BF16) for i in range(3)])
            Pmr = Rot([a_("p2_Pm%d" % i, [128, 512], BF16) for i in range(3)])
            Sbr = Rot([a_("p2_Sb%d" % i, [128, 512], F32) for i in range(2)])
            cfr = Rot([a_("p2_cf%d" % i, [128, 12], F32) for i in range(2)])
            pS = Rot([s2.enter_context(nc.psum_tensor("p2_pS%d" % i, [128, 512], F32)) for i in range(2)])
            pI = Rot([s2.enter_context(nc.psum_tensor("p2_pI%d" % i, [128, 512], F32)) for i in range(2)])
            pMt = s2.enter_context(nc.psum_tensor("p2_pM", [128, 2, 128], F32))
            pM = Rot([pMt[:, 0, :], pMt[:, 1, :]])
            pO = Rot([s2.enter_context(nc.psum_tensor("p2_pO%d" % i, [128, 260], F32)) for i in range(1)])
            pT = s2.enter_context(nc.psum_tensor("p2_pT", [128, 4, 128], BF16)); tpT = Tok()
            poT = s2.enter_context(nc.psum_tensor("p2_poT", [128, 512], F32)); tpoT = Tok()
            osb = a_("p2_osb", [128, 512], F32); tosb = Tok()
            V(lambda e: e.memset(imp[:], 0.0), [], [timp])

            def pv(po, tpo, Pm, tPm, v_ap, tv, first, last):
                pv_T(poT, tpoT, Pm, tPm, v_ap, tv, first, last)
                if last:
                    finish_o(poT, tpoT, po, tpo, osb, tosb)

            def pm_b(pm):
                i = 0 if pm is pM.aps[0] else 1
                return AP(pMt, i * 128, [[256, 128], [0, 4], [1, 128]])

            def masked(Pt, tPt, pm, tpm):
                Pm, tPm = Pmr.next()
                V(lambda e: e.tensor_tensor(out=Pm[:].rearrange("p (r q) -> p r q", r=4), in0=Pt[:].rearrange("p (r q) -> p r q", r=4),
                                            in1=pm.rearrange("p (o q) -> p o q", o=1).broadcast_to([128, 4, 128]) if False else pm_b(pm), op=ALU.mult), [tPt, tpm], [tPm])
                return Pm, tPm
            for j in range(NQ):
                Wb_ = 16 * (j + 1)
                NCc = 64 * (j + 1)
                ntc = (NCc + 127) // 128
                nbt = (Wb_ + 127) // 128
                cd, tcd = candr.next(); fc, tfc = forcr.next(); ex, tex = expnr.next(); sc_, tsc = selcr.next()
                kb.dma("sp", cd[:], AP(cand_d, j * 128 * 256, [[256, 128], [1, 256]]), writes=[tcd])
                kb.dma("sp", fc[:], AP(forced_d, j * 128 * 256, [[256, 128], [1, 256]]), writes=[tfc])
                kb.dma("sp", ex[:], AP(expn_d, j * 256 * 128, [[128, 128], [128 * 128, 2], [1, 128]]), writes=[tex])
                kb.dma("sp", sc_[:], AP(selc_d, j * 17 * 1024, [[1024, 17], [1, 1024]]), writes=[tsc])
                ac, tac = accr.next()
                kb.dma("sp", ac[:], acc_d.ap()[j * 128:(j + 1) * 128, :], reads=[t_accd], writes=[tac])
                for g in range(2):
                    qg = Qall[64 * g:64 * g + 64, j, :, :].rearrange("p r q -> p (r q)")
                    for r in range(4):
                        h = 4 * g + r
                        pq_, tpq = pq.next(); rs_, trs = rs.next()
                        nch = 0
                        for c0 in range(0, NCc, 512):
                            n = min(512, NCc - c0)
                            pi_, tpi = pI.next()
                            P(lambda e: e.matmul(pi_[:, 0:n], lhsT=Qall[64 * g:64 * g + 64, j, r, :], rhs=KcT[64 * g:64 * g + 64, c0:c0 + n],
                                                 start=True, stop=False), [t_Q, t_Kc], [tpi])
                            P(lambda e: e.matmul(pi_[:, 0:n], lhsT=BnA[0:17, h, :], rhs=sc_[0:17, c0:c0 + n],
                                                 start=False, stop=True), [t_BnA, tsc], [tpi])
                            A(lambda e, nch=nch: e.activation(out=pq_[:, c0:c0 + n], in_=pi_[:, 0:n], func=AF.Exp,
                                                              accum_out=rs_[:, nch:nch + 1]), [tpi], [tpq, trs])
                            nch += 1
                        if nch == 2:
                            V(lambda e: e.tensor_tensor(out=rs_[:, 0:1], in0=rs_[:, 0:1], in1=rs_[:, 1:2], op=ALU.add), [trs], [trs])
                        V(lambda e: e.tensor_scalar(out=rs_[:, 2:3], in0=rs_[:, 0:1], scalar1=1e-30, scalar2=None, op0=ALU.max), [trs], [trs])
                        V(lambda e: e.reciprocal(out=rs_[:, 3:4], in_=rs_[:, 2:3]), [trs], [trs])
                        if r == 0:
                            V(lambda e: e.tensor_scalar(out=imp[:, 1:1 + NCc], in0=pq_[:, 0:NCc], scalar1=rs_[:, 3:4], scalar2=None,
                                                        op0=ALU.mult), [tpq, trs], [timp])
                        else:
                            V(lambda e: e.scalar_tensor_tensor(out=imp[:, 1:1 + NCc], in0=pq_[:, 0:NCc], scalar=rs_[:, 3:4],
                                                               in1=imp[:, 1:1 + NCc], op0=ALU.mult, op1=ALU.add), [tpq, trs], [timp])

                    def iv(o):
                        return AP(imp, o, [[1028, 128], [4, Wb_]])
                    s_ = sS[:, 0:Wb_]
                    V(lambda e: e.tensor_tensor(out=s_, in0=iv(1), in1=iv(2), op=ALU.add), [timp], [tS_])
                    V(lambda e: e.tensor_tensor(out=s_, in0=s_, in1=iv(3), op=ALU.add), [timp, tS_], [tS_])
                    V(lambda e: e.scalar_tensor_tensor(out=s_, in0=s_, scalar=2.0, in1=iv(0), op0=ALU.mult, op1=ALU.add), [timp, tS_], [tS_])
                    V(lambda e: e.tensor_tensor(out=s_, in0=s_, in1=iv(4), op=ALU.add), [timp, tS_], [tS_])
                    V(lambda e: e.tensor_tensor(out=s_, in0=s_, in1=cd[:, 0:Wb_], op=ALU.mult), [tS_, tcd], [tS_])
                    V(lambda e: e.max(out=m8[:, 0:8], in_=s_), [tS_], [tS_])
                    V(lambda e: e.match_replace(out=sS2[:, 0:Wb_], in_to_replace=m8[:, 0:8], in_values=s_, imm_value=-1.0), [tS_], [tS_])
                    V(lambda e: e.max(out=m8[:, 8:16], in_=sS2[:, 0:Wb_]), [tS_], [tS_])
                    V(lambda e: e.tensor_scalar(out=s_, in0=s_, scalar1=m8[:, 12:13], scalar2=None, op0=ALU.is_ge), [tS_], [tS_])
                    V(lambda e: e.tensor_tensor(out=s_, in0=s_, in1=cd[:, 0:Wb_], op=ALU.mult), [tS_, tcd], [tS_])
                    if Wb_ < 256:
                        V(lambda e: e.memset(selg[:, Wb_:256], 0.0), [], [tselg])
                    V(lambda e: e.tensor_tensor(out=selg[:, 0:Wb_], in0=s_, in1=fc[:, 0:Wb_], op=ALU.add), [tS_, tfc], [tselg])
                    for bt in range(nbt):
                        P(lambda e, bt=bt: e.transpose(out=pT[:, bt, :], in_=selg[:, bt * 128:(bt + 1) * 128], identity=ident[:]),
                          [tselg, t_ident], [tpT])
                        V(lambda e, bt=bt: e.tensor_copy(selT[:, g, bt, :], pT[:, bt, :]), [tpT], [tselT])
                        V(lambda e, bt=bt: e.tensor_scalar(out=selTk[:, g, bt, :], in0=pT[:, bt, :], scalar1=keepb[:, j * 2 + bt:j * 2 + bt + 1],
                                                           scalar2=None, op0=ALU.mult), [tpT, tc_], [tselT])
                    po, tpo = pO.next()
                    for nt in range(ntc):
                        ps_, tps_ = pS.next()
                        P(lambda e: e.matmul(ps_[:], lhsT=KcT[64 * g:64 * g + 64, nt * 128:(nt + 1) * 128], rhs=qg,
                                             start=True, stop=False), [t_Kc, t_Q], [tps_])
                        P(lambda e: e.matmul(ps_[:], lhsT=sc_[0:17, nt * 128:(nt + 1) * 128],
                                             rhs=BnA[0:17, 4 * g:4 * g + 4, :].rearrange("p h q -> p (h q)"),
                                             start=False, stop=True), [t_BnA, tsc], [tps_])
                        Pt, tPt = Pr.next()
                        A(lambda e: e.activation(out=Pt[:], in_=ps_[:], func=AF.Exp), [tps_], [tPt])
                        pv(po, tpo, Pt, tPt, Vc_aug[:, nt, g, :], t_Vc, nt == 0, nt == ntc - 1)
                    combine(po, tpo, 0, 3, j, g, ac, tac, False, cfr)
                    po, tpo = pO.next()
                    ntile = 8 * (j + 1)
                    for qb in range(ntile):
                        bt, p0 = divmod(2 * qb, 128)
                        h2, pi2 = p0 // 64, (p0 % 64) // 2
                        ps_, tps_ = pS.next()
                        P(lambda e: e.matmul(ps_[:], lhsT=KsT_all[64 * g:64 * g + 64, qb * 128:(qb + 1) * 128], rhs=qg,
                                             start=True, stop=True), [tKs, t_Q], [tps_])
                        pm, tpm = pM.next()
                        P(lambda e: e.matmul(pm, lhsT=wide[64 * h2:64 * h2 + 64, pi2 * 128:(pi2 + 1) * 128],
                                             rhs=selTk[64 * h2:64 * h2 + 64, g, bt, :], start=True, stop=True), [tc_, tselT], [tpm])
                        Pt, tPt = Pr.next()
                        A(lambda e: e.activation(out=Pt[:], in_=ps_[:], func=AF.Exp), [tps_], [tPt])
                        Pm, tPm = masked(Pt, tPt, pm, tpm)
                        pv(po, tpo, Pm, tPm, Vs_all[:, qb, g, :], tVs, qb == 0, False)
                    ps_, tps_ = pS.next()
                    b_ap = BW[:, 1, 4 * g:4 * g + 4, :].rearrange("p h q -> p (h q)")
                    Pt, tPt = attn_tile(ps_, tps_, KsN[64 * g:64 * g + 64, j, 0, :], qg, [t_KsN, t_Q], b_ap, t_BW, Pr, Sbr)
                    pm, tpm = pM.next()
                    for bt in range(nbt):
                        P(lambda e, bt=bt: e.matmul(pm, lhsT=ex[:, bt, :], rhs=selT[:, g, bt, :], start=(bt == 0), stop=(bt == nbt - 1)),
                          [tex, tselT], [tpm])
                    Pm, tPm = masked(Pt, tPt, pm, tpm)
                    pv(po, tpo, Pm, tPm, VsN[:, j, 0, g, :], t_VsN, False, False)
                    ps_, tps_ = pS.next()
                    b_ap = BW[:, 0, 4 * g:4 * g + 4, :].rearrange("p h q -> p (h q)")
                    Pt, tPt = attn_tile(ps_, tps_, KsN[64 * g:64 * g + 64, j, 1, :], qg, [t_KsN, t_Q], b_ap, t_BW, Pr, Sbr)
                    pv(po, tpo, Pt, tPt, VsN[:, j, 1, g, :], t_VsN, False, True)
                    combine(po, tpo, 1, 3, j, g, ac, tac, False, cfr)
                V(lambda e: e.tensor_copy(accb[:], ac[:]), [tac], [taccb])
                for t in range(4):
                    P(lambda e, t=t: e.transpose(out=pT[:, t, :], in_=accb[:, t * 128:(t + 1) * 128], identity=ident[:]),
                      [taccb, t_ident], [tpT])
                V(lambda e: e.tensor_copy(oT[:, :, j * 128:(j + 1) * 128], pT[:]), [tpT], [t_oT])

    def phase3a():
        with contextlib.ExitStack() as s3:
            def a_(name, shape, dt):
                return s3.enter_context(nc.sbuf_tensor(name, list(shape), dt))
            X = norm_ctx(s3, 2, g_mix)
            tW = Tok()
            Wb = a_("Wbr", [128, 8, 2048], BF16)
            Wus = a_("Wus", [128, 4, 1024], BF16); Wun = a_("Wun", [128, 4, 1024], BF16)
            Wo = a_("Wo", [128, 8, 1024], BF16)
            for k in range(8):
                kb.dma("pool", Wb[:, k, :], AP(w_in, k * 128 * INC + 1816, [[INC, 128], [1, 2048]]), writes=[tW])
                kb.dma("pool", Wo[:, k, :], AP(w_out_d, k * 128 * 1024, [[1024, 128], [1, 1024]]), writes=[tW])
            for k in range(4):
                kb.dma("pool", Wus[:, k, :], AP(w_up_ssm, k * 128 * 1024, [[1024, 128], [1, 1024]]), writes=[tW])
                kb.dma("pool", Wun[:, k, :], AP(w_up_nsa, k * 128 * 1024, [[1024, 128], [1, 1024]]), writes=[tW])
            zgr = Rot([a_("p3_zg%d" % i, [128, 4, 256], BF16) for i in range(2)])
            mix = a_("p3_mix", [128, 8, 256], BF16); tmix = Tok()
            sga = Rot([a_("p3_sa%d" % i, [128, 2, 256], F32) for i in range(2)])
            tt = Rot([a_("p3_t%d" % i, [128, 2, 256], F32) for i in range(2)])
            x1r = Rot([a_("p3_x1%d" % i, [128, 1024], F32) for i in range(2)])
            pA = Rot([s3.enter_context(nc.psum_tensor("p3_pA%d" % i, [128, 2, 256], F32)) for i in range(2)])
            pY = Rot([s3.enter_context(nc.psum_tensor("p3_pY%d" % i, [128, 2, 256], F32)) for i in range(2)])
            pX = Rot([s3.enter_context(nc.psum_tensor("p3_pX%d" % i, [128, 512], F32)) for i in range(2)])
            for oc in range(8):
                hT, thT, xb, txb = load_norm_T(x_own, oc * 256, 2, X, ret_x=True)
                zc, tzc = zgr.next()
                kb.dma("sp", zc[:], AP(zg_d, oc * 256, [[4 * NTOK, 128], [NTOK, 4], [1, 256]]), reads=[t_zg], writes=[tzc])
                for m in range(8):
                    pa, tpa = pA.next(); py, tpy = pY.next()
                    for k in range(8):
                        P(lambda e, k=k: e.matmul(pa[:, 0, :], lhsT=Wb[:, k, m * 128:(m + 1) * 128], rhs=hT[:, k, :],
                                                  start=(k == 0), stop=(k == 7)), [tW, thT], [tpa])
                    for k in range(8):
                        P(lambda e, k=k: e.matmul(pa[:, 1, :], lhsT=Wb[:, k, 1024 + m * 128:1024 + (m + 1) * 128], rhs=hT[:, k, :],
                                                  start=(k == 0), stop=(k == 7)), [tW, thT], [tpa])
                    for k in range(4):
                        P(lambda e, k=k: e.matmul(py[:, 0, :], lhsT=Wus[:, k, m * 128:(m + 1) * 128], rhs=zc[:, k, :],
                                                  start=(k == 0), stop=(k == 3)), [tW, tzc], [tpy])
                    for k in range(4):
                        P(lambda e, k=k: e.matmul(py[:, 1, :], lhsT=Wun[:, k, m * 128:(m + 1) * 128], rhs=oT[:, k, oc * 256:(oc + 1) * 256],
                                                  start=(k == 0), stop=(k == 3)), [tW, t_oT], [tpy])
                    sa, tsa = sga.next(); t_, tt_ = tt.next()
                    A(lambda e: e.activation(out=sa[:], in_=pa[:], func=AF.Sigmoid), [tpa], [tsa])
                    V(lambda e: e.tensor_tensor(out=t_[:], in0=sa[:], in1=py[:], op=ALU.mult), [tsa, tpy], [tt_])
                    V(lambda e: e.tensor_tensor(out=mix[:, m, :], in0=t_[:, 0, :], in1=t_[:, 1, :], op=ALU.add), [tt_], [tmix])
                for a in range(2):
                    x1, tx1 = x1r.next()
                    for hf in range(2):
                        px, tpx = pX.next()
                        for m in range(8):
                            P(lambda e, m=m: e.matmul(px[:], lhsT=mix[:, m, a * 128:(a + 1) * 128], rhs=Wo[:, m, hf * 512:(hf + 1) * 512],
                                                      start=(m == 0), stop=(m == 7)), [tmix, tW], [tpx])
                        V(lambda e: e.tensor_tensor(out=x1[:, hf * 512:(hf + 1) * 512], in0=px[:], in1=xb[:, a, hf * 512:(hf + 1) * 512],
                                                    op=ALU.add), [tpx, txb], [tx1])
                    tk_ = Tok(); x1_toks.append(tk_)
                    kb.dma("sp", x1_d.ap()[oc * 256 + a * 128: oc * 256 + (a + 1) * 128, :], x1[:], reads=[tx1], writes=[tk_])

    def phase_ffn(src_d):
        with contextlib.ExitStack() as fs:
            def fsb(name, shape, dt):
                return fs.enter_context(nc.sbuf_tensor(name, list(shape), dt))

            def fps(name, shape, dt):
                return fs.enter_context(nc.psum_tensor(name, list(shape), dt))
            gF, tgF = load_gain("g_ffn_t", g_ffn) if False else (None, None)
            gF = fsb("gF", [128, D], F32); tgF = Tok()
            kb.dma("sp", gF[:], AP(g_ffn, 0, [[0, 128], [1, D]]), writes=[tgF])
            gL = fsb("gL", [128, D], F32); tgL = Tok()
            kb.dma("sp", gL[:], AP(g_fin, 0, [[0, 128], [1, D]]), writes=[tgL])
            Wg = fsb("Wg", [128, 8, DFF], BF16); tWg = Tok()
            Wu = fsb("Wu", [128, 8, DFF], BF16); tWu = Tok()
            Wd = fsb("Wd", [128, NFT, D], BF16); tWd = Tok()
            lWg, lWu, lWd = [], [], []
            for k in range(8):
                t1_ = Tok(); lWg.append(t1_)
                kb.dma("pool", Wg[:, k, :], w_gate.ap()[k * 128:(k + 1) * 128, :], writes=[t1_])
                t2_ = Tok(); lWu.append(t2_)
                kb.dma("pool", Wu[:, k, :], w_up.ap()[k * 128:(k + 1) * 128, :], writes=[t2_])
            for m in range(NFT):
                t3_ = Tok(); lWd.append(t3_)
                kb.dma("pool", Wd[:, m, :], w_down.ap()[m * 128:(m + 1) * 128, :], writes=[t3_])
            NT = 256
            xt = [fsb("f_x%d" % i, [128, 2, D], F32) for i in range(2)]
            xr = Rot(xt)
            h2 = fsb("f_h2", [128, D], BF16); th2 = Tok()
            h2T = fsb("f_h2T", [128, 8, NT], BF16); th2T = Tok()
            aT = fsb("f_aT", [128, NFT, NT], BF16); taT = Tok()
            sg = Rot([fsb("f_sg%d" % i, [128, NT], F32) for i in range(2)])
            x2 = fsb("f_x2", [128, 2, D], F32); tx2 = Tok()
            oo = Rot([fsb("f_o%d" % i, [128, D], F32) for i in range(2)])
            junk = fsb("f_junk", [128, D], BF16); tjunk = Tok()
            ssr = Rot([fsb("f_ss%d" % i, [128, 4], F32) for i in range(4)])
            pT = Rot([fps("f_pT%d" % i, [128, 8, 128], BF16) for i in range(2)])
            pGU = Rot([fps("f_pGU%d" % i, [128, 2, NT], F32) for i in range(2)])
            pD = Rot([fps("f_pD%d" % i, [128, 512], F32) for i in range(2)])
            for tt in range(NTOK // NT):
                xa, tx = xr.next()
                kb.dma("sp", xa[:], AP(src_d, tt * NT * D, [[D, 128], [128 * D, 2], [1, D]]), reads=x1_toks, writes=[tx])
                for a in range(2):
                    ss, tss = ssr.next()
                    rmsnorm(xa[:, a, :], tx, gF, tgF, h2[:], th2, (junk, tjunk, ss, tss))
                    pt, tpt = pT.next()
                    for k in range(8):
                        kb.op("pe", lambda e, k=k: e.transpose(out=pt[:, k, :], in_=h2[:, k * 128:(k + 1) * 128],
                                                               identity=ident[:]),
                              reads=[th2, t_ident], writes=[tpt])
                    kb.op("act", lambda e: e.copy(out=h2T[:, :, a * 128:(a + 1) * 128], in_=pt[:]),
                          reads=[tpt], writes=[th2T])
                for m in range(NFT):
                    pg, tpg = pGU.next()
                    for k in range(8):
                        kb.op("pe", lambda e, k=k: e.matmul(pg[:, 0, :], lhsT=Wg[:, k, m * 128:(m + 1) * 128],
                                                            rhs=h2T[:, k, :], start=(k == 0), stop=(k == 7)),
                              reads=lWg + [th2T], writes=[tpg])
                    for k in range(8):
                        kb.op("pe", lambda e, k=k: e.matmul(pg[:, 1, :], lhsT=Wu[:, k, m * 128:(m + 1) * 128],
                                                            rhs=h2T[:, k, :], start=(k == 0), stop=(k == 7)),
                              reads=lWu + [th2T], writes=[tpg])
                    s_, ts_ = sg.next()
                    kb.op("act", lambda e: e.activation(out=s_[:], in_=pg[:, 0, :], func=AF.Silu),
                          reads=[tpg], writes=[ts_])
                    kb.op("dve", lambda e: e.tensor_tensor(out=aT[:, m, :], in0=s_[:], in1=pg[:, 1, :], op=ALU.mult),
                          reads=[ts_, tpg], writes=[taT])
                for a in range(2):
                    for hf in range(2):
                        pd, tpd = pD.next()
                        for m in range(NFT):
                            kb.op("pe", lambda e, m=m: e.matmul(pd[:], lhsT=aT[:, m, a * 128:(a + 1) * 128],
                                                                rhs=Wd[:, m, hf * 512:(hf + 1) * 512],
                                                                start=(m == 0), stop=(m == NFT - 1)),
                                  reads=[taT] + lWd, writes=[tpd])
                        kb.op("dve", lambda e: e.tensor_tensor(out=x2[:, a, hf * 512:(hf + 1) * 512], in0=pd[:],
                                                               in1=xa[:, a, hf * 512:(hf + 1) * 512], op=ALU.add),
                              reads=[tpd, tx], writes=[tx2])
                    ss, tss = ssr.next()
                    o_, to_ = oo.next()
                    rmsnorm(x2[:, a, :], tx2, gL, tgL, o_[:], to_, (junk, tjunk, ss, tss))
                    kb.dma("sp", out_d.ap()[tt * NT + a * 128: tt * NT + (a + 1) * 128, :], o_[:],
                           reads=[to_], writes=[t_out])


    t_out = Tok()
    try:
        phase_s5()
    except _Stop:
        return nc
    if debug == "s5":
        kb.finish([t_zg])
        es.close()
        return nc
    es_o = contextlib.ExitStack()
    oT = es_o.enter_context(nc.sbuf_tensor("oT", [128, 4, NTOK], BF16))
    es2 = contextlib.ExitStack()

    def p_(name, shape, dt):
        return es2.enter_context(nc.sbuf_tensor(name, list(shape), dt))
    BW = p_("BW", [128, 2, 8, 128], F32); M4 = p_("M4", [128, 128], F32); BnA = p_("BnA", [17, 8, 128], BF16)
    KcT = p_("KcT", [128, 1024], BF16); Vc_aug = p_("Vc_aug", [128, 8, 2, 65], BF16)
    Qall = p_("Qall", [128, NQ, 4, 128], BF16); gates = p_("gates", [128, NQ, 24], F32)
    KsN = p_("KsN", [128, NQ, 2, 128], BF16); VsN = p_("VsN", [128, NQ, 2, 2, 65], BF16)
    kb.barrier()
    try:
        phase_tables()
        kb.barrier()
        ck("tables")
        phase_compress()
        kb.barrier()
        ck("compress")
        phase1()
        kb.barrier()
        if debug == "dump":
            for nm, t_, tk_, dt_ in (("dbg_bw", BW, t_BW, F32), ("dbg_m4", M4, t_BW, F32), ("dbg_gates", gates, t_gates, F32),
                                     ("dbg_q", Qall, t_Q, BF16), ("dbg_ksn", KsN, t_KsN, BF16), ("dbg_vsn", VsN, t_VsN, BF16),
                                     ("dbg_bna", BnA, t_BnA, BF16), ("dbg_kct", KcT, t_Kc, BF16), ("dbg_vc", Vc_aug, t_Vc, BF16)):
                shp = list(t_.shape)
                n_ = int(np.prod(shp[1:]))
                dd_ = nc.dram_tensor(nm, [shp[0], n_], dt_, kind="ExternalOutput")
                kb.dma("sp", dd_.ap(), AP(t_, 0, [[n_, shp[0]], [1, n_]]), reads=[tk_], writes=[t_out])
        ck("phase1")
        phase2()
        kb.barrier()
        if debug == "dump":
            dd_ = nc.dram_tensor("dbg_oT", [128, 4 * NTOK], BF16, kind="ExternalOutput")
            kb.dma("sp", dd_.ap(), AP(oT, 0, [[4 * NTOK, 128], [1, 4 * NTOK]]), reads=[t_oT], writes=[t_out])
        ck("phase2")
    except _Stop:
        return nc
    es2.close()
    try:
        phase3a()
        kb.barrier()
        ck("phase3a")
    except _Stop:
        return nc
    es_o.close()
    phase_ffn(x1_d)
    kb.finish([t_out])
    es.close()
    return nc


_PROG = {}


def _bf(a):
    return np.ascontiguousarray(a).astype(ml_dtypes.bfloat16)


def make_in_maps(inp):
    x = np.asarray(inp["x"], np.float32)[0]
    xq = x.reshape(S // TQ, TQ, D)
    ident = np.eye(128, dtype=np.float32)
    f = lambda k: np.ascontiguousarray(np.asarray(inp[k], np.float32)[0])
    shared = {
        "norm_mix_g": np.asarray(inp["norm_mix_g"], np.float32).reshape(1, D),
        "norm_ffn_g": np.asarray(inp["norm_ffn_g"], np.float32).reshape(1, D),
        "norm_final_g": np.asarray(inp["norm_final_g"], np.float32).reshape(1, D),
        "w_ffn_gate": f("w_ffn_gate"), "w_ffn_up": f("w_ffn_up"), "w_ffn_down": f("w_ffn_down"),
        "w_in": f("w_in"),
        "ssm_a_re": f("ssm_a_re"), "ssm_a_im": f("ssm_a_im"),
        "ssm_log_dt": np.asarray(inp["ssm_log_dt"], np.float32).reshape(1, 32),
        "ssm_b_re": f("ssm_b_re"), "ssm_b_im": f("ssm_b_im"), "ssm_c_re": f("ssm_c_re"), "ssm_c_im": f("ssm_c_im"),
        "ssm_d": np.asarray(inp["ssm_d"], np.float32).reshape(1, 512),
        "ssm_w_glu": f("ssm_w_glu"),
        "ident_bf": _bf(ident), "ident_f": ident,
    }
    def bucket(n):
        n = np.maximum(n, 0)
        nf = np.maximum(n, 1).astype(np.float32)
        large = 16 + (np.log(nf / np.float32(16)) / np.float32(np.log(8.0)) * np.float32(16)).astype(np.int32)
        large = np.minimum(large, 31)
        return np.where(n < 16, n, large)
    dd = np.arange(256)
    ohf = (bucket(dd)[None, :] == np.arange(32)[:, None]).astype(np.float32)
    ohr = (bucket(255 - dd)[None, :] == np.arange(32)[:, None]).astype(np.float32)
    pp = np.arange(128)
    antij = (pp[:, None] + pp[None, :] == 127).astype(np.float32)
    m4 = np.where(pp[None, :] >= pp[:, None], np.float32(-30000.0), np.float32(0.0)).astype(np.float32)
    mm = np.arange(4096)
    wide = ((pp[:, None] % 64) == (2 * (mm[None, :] // 128) + (mm[None, :] % 128) // 64)).astype(np.float32)
    shared.update({
        "rel_bias": np.asarray(inp["rel_bias"], np.float32), "ohrev": ohr, "ohfwd": ohf, "antij": antij, "m4": m4,
        "cmp_w1_k": f("cmp_w1_k"), "cmp_w1_v": f("cmp_w1_v"), "cmp_w2_k": f("cmp_w2_k"), "cmp_w2_v": f("cmp_w2_v"),
        "cmp_pos_k": f("cmp_pos_k"), "cmp_pos_v": f("cmp_pos_v"), "wide64": _bf(wide),
        "w_out": f("w_out"), "w_up_ssm": f("w_up_ssm"), "w_up_nsa": f("w_up_nsa"), "x_all": x,
    })
    blk = np.arange(256)
    in_maps = []
    for c in range(NCORES):
        m = dict(shared)
        xp = np.zeros((NQ, 512, D), np.float32)
        vp = np.zeros((128, 64), np.float32)
        cand = np.zeros((NQ, 128, 256), np.float32); forced = np.zeros((NQ, 128, 256), np.float32)
        expn = np.zeros((NQ, 256, 128), np.float32); selc = np.zeros((NQ, 17, 1024), np.float32)
        keepb = np.zeros((128, 32), np.float32)
        for j in range(NQ):
            qb = 8 * j + c
            lo = 128 * qb - 512
            s0 = max(lo, 0)
            xp[j, s0 - lo:] = x[s0:128 * qb]
            for a in range(4):
                vp[:, j * 4 + a] = ((lo + a * 128 + pp) >= 0)
            cur = 2 * qb + (pp >= 64).astype(np.int64)
            valid = blk[None, :] <= cur[:, None]
            frc = valid & ((blk[None, :] == 0) | (blk[None, :] >= cur[:, None] - 1))
            cand[j] = valid & ~frc
            forced[j] = frc
            for hh in range(2):
                b_ = 2 * qb - 2 + hh
                if b_ >= 0:
                    expn[j, b_, 64 * hh:64 * hh + 64] = 1.0
            for mp in range(16):
                n = 8 * qb - 8 + (15 - mp)
                if 0 <= n < 1024:
                    selc[j, mp, n] = 1.0
            selc[j, 16, min(8 * qb + 8, 1024):] = -30000.0
            for bt in range(2):
                keepb[:, j * 2 + bt] = ((bt * 128 + pp) <= 2 * qb - 3)
        m["x_prev"] = xp.reshape(NQ * 512, D); m["vprev"] = vp
        m["cand"] = _bf(cand.reshape(NQ * 128, 256)); m["forced"] = _bf(forced.reshape(NQ * 128, 256))
        m["expn"] = _bf(expn.reshape(NQ * 256, 128)); m["selc"] = _bf(selc.reshape(NQ * 17, 1024))
        m["keepblk"] = keepb
        m["x_own"] = np.ascontiguousarray(xq[c::NCORES].reshape(NTOK, D))
        oh = np.zeros((128, 8), np.float32); oh[:, c] = 1.0
        m["onehot_r"] = oh
        in_maps.append(m)
    return in_maps


def kernel(**inp):
    if "main" not in _PROG:
        _PROG["main"] = build_program()
    nc = _PROG["main"]
    in_maps = make_in_maps(inp)
    res = run_bass_kernel_spmd(nc, in_maps, core_ids=list(range(NCORES)))
    out = np.empty((S // TQ, TQ, D), np.float32)
    for c in range(NCORES):
        out[c::NCORES] = np.asarray(res.results[c]["out"], np.float32).reshape(NQ, TQ, D)
    return out.reshape(1, S, D)
```
